# Optimizing a Trainium2 kernel written in Bass

```python
import math
import jax
import jax.numpy as jnp
from jax import lax
import numpy as np

D_MODEL = 1024
BATCH = 4
SEQ = 8192
DEPTH = 1

GRID_W = 64
CTX_LEN = 256
EXPAND = 2
D_MIX = EXPAND * D_MODEL
D_A = D_MIX // 2
D_B = D_MIX - D_A
HA_HEADS = 8
HA_DK = 128
HA_DV = D_A // HA_HEADS
A_QK = HA_HEADS * HA_DK
HB_HEADS = 8
HB_DK = 128
HB_DV = D_B // HB_HEADS
B_QK = HB_HEADS * HB_DK
CONV_K = 3
CONV_CH = 2 * B_QK + D_B
CHUNK = 64
NORM_EPS = 1e-6
COL_SIZES = (A_QK, A_QK, A_QK, D_A, D_A, B_QK, B_QK, D_B, D_B, 2 * HB_HEADS, 2 * HB_HEADS)
N_IN = 3 * A_QK + 2 * D_A + 2 * B_QK + 2 * D_B + 4 * HB_HEADS

kernel_name = "hybrid_hgrn2_gdn_flow_block"


def rmsnorm(x, w):
    xf = x.astype(jnp.float32)
    y = xf * lax.rsqrt(jnp.mean(xf * xf, axis=-1, keepdims=True) + NORM_EPS)
    return (y * w.astype(jnp.float32)).astype(x.dtype)


def l2norm(t):
    return t * lax.rsqrt(jnp.sum(t * t, axis=-1, keepdims=True) + NORM_EPS)


def heads(t, n_heads):
    bsz, length, width = t.shape
    return t.reshape(bsz, length, n_heads, width // n_heads).transpose(0, 2, 1, 3)


def merge_heads(t):
    bsz, nh, length, d = t.shape
    return t.transpose(0, 2, 1, 3).reshape(bsz, length, nh * d)


def split_columns(y):
    out, start = [], 0
    for size in COL_SIZES:
        out.append(y[..., start:start + size])
        start += size
    return out


def adaln(cond, ada_w, ada_b):
    m = jax.nn.silu(cond) @ ada_w + ada_b
    return jnp.split(m, 3, axis=-1)


def grid_conv(t, conv_w, rows):
    bsz, length, ch = t.shape
    img = t.reshape(bsz, rows, length // rows, ch)
    out = lax.conv_general_dilated(img, conv_w.astype(t.dtype)[:, :, None, :], (1, 1), "SAME",
                                   dimension_numbers=("NHWC", "HWIO", "NHWC"),
                                   feature_group_count=ch)
    return out.reshape(bsz, length, ch)


def hgrn2_scan(q, k, v, g, s0, readout):
    bsz, nh, length, dk = k.shape
    dv = v.shape[-1]
    n = length // CHUNK
    blk = lambda t: t.reshape(bsz, nh, n, CHUNK, t.shape[-1])
    q, k, v, g = blk(q), blk(k), blk(v), blk(g)
    b = jnp.cumsum(g, axis=-2)
    b_last = b[..., -1:, :]
    xs = [k * jnp.exp(b_last - b), v, jnp.exp(b_last)]
    if readout:
        xs += [q * jnp.exp(b), q, k, b]
    xs = tuple(jnp.moveaxis(t, 2, 0) for t in xs)
    incl = jnp.tril(jnp.ones((CHUNK, CHUNK), dtype=bool))[:, :, None]

    def step(s, inp):
        k_dec, v_i, d_last = inp[:3]
        s_new = d_last[..., 0, :, None] * s + jnp.einsum("bhsk,bhsv->bhkv", k_dec, v_i)
        if not readout:
            return s_new, None
        q_dec, q_i, k_i, b_i = inp[3:]
        diff = b_i[..., :, None, :] - b_i[..., None, :, :]
        dec = jnp.exp(jnp.where(incl, diff, -jnp.inf))
        scores = jnp.einsum("bhtk,bhsk,bhtsk->bhts", q_i, k_i, dec)
        o = jnp.einsum("bhtk,bhkv->bhtv", q_dec, s) + jnp.einsum("bhts,bhsv->bhtv", scores, v_i)
        return s_new, o

    s_fin, o = lax.scan(step, s0, xs)
    if readout:
        o = jnp.moveaxis(o, 0, 2).reshape(bsz, nh, length, dv)
    return o, s_fin


def gdn_scan(q, k, v, g, beta, s0, readout):
    bsz, nh, length, dk = k.shape
    dv = v.shape[-1]
    n = length // CHUNK
    blk = lambda t: t.reshape(bsz, nh, n, CHUNK, t.shape[-1])
    q, k, v = blk(q), blk(k), blk(v)
    g, beta = g.reshape(bsz, nh, n, CHUNK), beta.reshape(bsz, nh, n, CHUNK)
    b = jnp.cumsum(g, axis=-1)
    diff = b[..., :, None] - b[..., None, :]
    strict = jnp.tril(jnp.ones((CHUNK, CHUNK), dtype=bool), -1)
    incl = jnp.tril(jnp.ones((CHUNK, CHUNK), dtype=bool))
    kk = jnp.einsum("bhntk,bhnsk->bhnts", k, k)
    a_mat = jnp.eye(CHUNK, dtype=k.dtype) + beta[..., :, None] * kk * jnp.exp(jnp.where(strict, diff, -jnp.inf))
    rhs = jnp.concatenate([beta[..., None] * v, (beta * jnp.exp(b))[..., None] * k], axis=-1)
    sol = lax.linalg.triangular_solve(a_mat, rhs, left_side=True, lower=True, unit_diagonal=True)
    u0, w = sol[..., :dv], sol[..., dv:]
    b_last = b[..., -1:]
    xs = [u0, w, k * jnp.exp(b_last - b)[..., None], jnp.exp(b_last[..., 0])]
    if readout:
        p = jnp.einsum("bhntk,bhnsk->bhnts", q, k) * jnp.exp(jnp.where(incl, diff, -jnp.inf))
        xs += [q * jnp.exp(b)[..., None], p]
    xs = tuple(jnp.moveaxis(t, 2, 0) for t in xs)

    def step(s, inp):
        u0_i, w_i, k_dec, d_last = inp[:4]
        v_new = u0_i - jnp.einsum("bhck,bhkv->bhcv", w_i, s)
        s_new = d_last[..., None, None] * s + jnp.einsum("bhck,bhcv->bhkv", k_dec, v_new)
        if not readout:
            return s_new, None
        q_dec, p_i = inp[4:]
        o = jnp.einsum("bhtk,bhkv->bhtv", q_dec, s) + jnp.einsum("bhts,bhsv->bhtv", p_i, v_new)
        return s_new, o

    s_fin, o = lax.scan(step, s0, xs)
    if readout:
        o = jnp.moveaxis(o, 0, 2).reshape(bsz, nh, length, dv)
    return o, s_fin


def bidirectional(scan_fn, ctx_args, lat_args, s0, readout_ctx):
    flip = lambda t: jnp.flip(t, axis=2)
    (ctx_f, ctx_b), (lat_f, lat_b) = ctx_args, lat_args
    oc_f, sc_f = scan_fn(*ctx_f, s0, readout_ctx)
    oc_b, sc_b = scan_fn(*(flip(t) for t in ctx_b), s0, readout_ctx)
    ol_f, _ = scan_fn(*lat_f, sc_f, True)
    ol_b, _ = scan_fn(*(flip(t) for t in lat_b), sc_b, True)
    o_ctx = oc_f + flip(oc_b) if readout_ctx else None
    return ol_f + flip(ol_b), o_ctx


def project(h, w_in, conv_w, lb, a_log, dt_bias, rows):
    f32 = jnp.float32
    y = jnp.einsum("bld,dn->bln", h, w_in)
    a_q, a_f_fwd, a_f_bwd, a_i, a_z, b_q, b_k, b_v, b_z, b_a, b_b = split_columns(y)
    bsz, length, _ = y.shape
    q_a = heads(jax.nn.silu(a_q.astype(f32)), HA_HEADS) * HA_DK ** -0.5
    v_a = heads(a_i.astype(f32), HA_HEADS)

    def forget(fz, lb_d):
        fz = fz.astype(f32)
        log_f = jnp.logaddexp(jnp.log(lb_d), jnp.log1p(-lb_d) + jax.nn.log_sigmoid(fz))
        k = (1.0 - lb_d) * jax.nn.sigmoid(-fz)
        return heads(k, HA_HEADS), heads(log_f, HA_HEADS)

    k_af, g_af = forget(a_f_fwd, lb[0])
    k_ab, g_ab = forget(a_f_bwd, lb[1])
    hg_args = ((q_a, k_af, v_a, g_af), (q_a, k_ab, v_a, g_ab))
    qkv = jax.nn.silu(grid_conv(jnp.concatenate([b_q, b_k, b_v], axis=-1), conv_w, rows).astype(f32))
    q_b = l2norm(heads(qkv[..., :B_QK], HB_HEADS)) * HB_DK ** -0.5
    k_b = l2norm(heads(qkv[..., B_QK:2 * B_QK], HB_HEADS))
    v_b = heads(qkv[..., 2 * B_QK:], HB_HEADS)
    alpha_logit = b_a.astype(f32).reshape(bsz, length, 2, HB_HEADS)
    beta = jax.nn.sigmoid(b_b.astype(f32).reshape(bsz, length, 2, HB_HEADS))
    log_alpha = -jnp.exp(a_log.astype(f32)) * jax.nn.softplus(alpha_logit + dt_bias.astype(f32))
    log_alpha = log_alpha.transpose(2, 0, 3, 1)
    beta = beta.transpose(2, 0, 3, 1)
    gdn_args = ((q_b, k_b, v_b, log_alpha[0], beta[0]), (q_b, k_b, v_b, log_alpha[1], beta[1]))
    return hg_args, gdn_args, a_z.astype(f32), b_z.astype(f32)


def group_out(o_a, o_b, z_a, z_b, na_w, nb_w, w_out, dtype):
    def head_norm(o, w):
        o = o * lax.rsqrt(jnp.mean(o * o, axis=-1, keepdims=True) + NORM_EPS)
        return merge_heads(o * w.astype(jnp.float32)[None, :, None, :])

    y_a = jax.nn.silu(z_a) * head_norm(o_a, na_w)
    y_b = jax.nn.silu(z_b) * head_norm(o_b, nb_w)
    y = jnp.concatenate([y_a, y_b], axis=-1).astype(dtype)
    return y @ w_out


def setup_inputs(seed: int = 0) -> dict:
    key = jax.random.key(seed)
    ks = jax.random.split(key, 16)
    f32 = jnp.float32
    nrm = lambda k, shape, s: s * jax.random.normal(k, shape, f32)
    x = nrm(ks[0], (BATCH, SEQ, D_MODEL), 1.0)
    c = nrm(ks[1], (BATCH, D_MODEL), 1.0)
    ctx = nrm(ks[2], (BATCH, CTX_LEN, D_MODEL), 1.0)
    c_ctx = nrm(ks[3], (D_MODEL,), 1.0)
    norm_w = 1.0 + nrm(ks[4], (DEPTH, D_MODEL), 0.02)
    ada_w = nrm(ks[5], (DEPTH, D_MODEL, 3 * D_MODEL), 0.5 * D_MODEL ** -0.5)
    ada_b = nrm(ks[6], (DEPTH, 3 * D_MODEL), 0.01)
    w_in = nrm(ks[7], (DEPTH, D_MODEL, N_IN), D_MODEL ** -0.5)
    conv_w = nrm(ks[8], (DEPTH, CONV_K, CONV_K, CONV_CH), 1.0 / CONV_K)
    hg_lb_logits = nrm(ks[9], (DEPTH + 1, 2, A_QK), 0.5)
    gdn_a_log = jnp.log(jax.random.uniform(ks[10], (DEPTH, 2, HB_HEADS), f32, 1.0, 16.0))
    dt = jnp.exp(jax.random.uniform(ks[11], (DEPTH, 2, HB_HEADS), f32, math.log(1e-3), math.log(1e-1)))
    gdn_dt_bias = dt + jnp.log(-jnp.expm1(-dt))
    ha_norm_w = 1.0 + nrm(ks[12], (DEPTH, HA_HEADS, HA_DV), 0.02)
    hb_norm_w = 1.0 + nrm(ks[13], (DEPTH, HB_HEADS, HB_DV), 0.02)
    w_out = nrm(ks[14], (DEPTH, D_MIX, D_MODEL), D_MIX ** -0.5)
    final_norm_w = 1.0 + nrm(ks[15], (D_MODEL,), 0.02)
    return {"x": x, "c": c, "ctx": ctx, "c_ctx": c_ctx, "norm_w": norm_w, "ada_w": ada_w,
            "ada_b": ada_b, "w_in": w_in, "conv_w": conv_w, "hg_lb_logits": hg_lb_logits,
            "gdn_a_log": gdn_a_log, "gdn_dt_bias": gdn_dt_bias, "ha_norm_w": ha_norm_w,
            "hb_norm_w": hb_norm_w, "w_out": w_out, "final_norm_w": final_norm_w}


def reference(x, c, ctx, c_ctx, norm_w, ada_w, ada_b, w_in, conv_w, hg_lb_logits, gdn_a_log,
              gdn_dt_bias, ha_norm_w, hb_norm_w, w_out, final_norm_w):
    bsz = x.shape[0]
    rows = x.shape[1] // GRID_W
    lb_all = jnp.cumsum(jax.nn.softmax(hg_lb_logits.astype(jnp.float32), axis=0), axis=0)
    s0_a = jnp.zeros((bsz, HA_HEADS, HA_DK, HA_DV), jnp.float32)
    s0_b = jnp.zeros((bsz, HB_HEADS, HB_DK, HB_DV), jnp.float32)
    for layer in range(DEPTH):
        readout_ctx = layer + 1 < DEPTH
        sh_l, sc_l, gt_l = adaln(c, ada_w[layer], ada_b[layer])
        sh_c, sc_c, gt_c = adaln(c_ctx, ada_w[layer], ada_b[layer])
        h_lat = rmsnorm(x, norm_w[layer]) * (1.0 + sc_l[:, None, :]) + sh_l[:, None, :]
        h_ctx = rmsnorm(ctx, norm_w[layer]) * (1.0 + sc_c) + sh_c
        lat_hg, lat_gdn, lz_a, lz_b = project(h_lat, w_in[layer], conv_w[layer], lb_all[layer],
                                              gdn_a_log[layer], gdn_dt_bias[layer], rows)
        ctx_hg, ctx_gdn, cz_a, cz_b = project(h_ctx, w_in[layer], conv_w[layer], lb_all[layer],
                                              gdn_a_log[layer], gdn_dt_bias[layer], 1)
        oa_lat, oa_ctx = bidirectional(hgrn2_scan, ctx_hg, lat_hg, s0_a, readout_ctx)
        ob_lat, ob_ctx = bidirectional(gdn_scan, ctx_gdn, lat_gdn, s0_b, readout_ctx)
        mix_lat = group_out(oa_lat, ob_lat, lz_a, lz_b, ha_norm_w[layer], hb_norm_w[layer], w_out[layer], x.dtype)
        if readout_ctx:
            mix_ctx = group_out(oa_ctx, ob_ctx, cz_a, cz_b, ha_norm_w[layer], hb_norm_w[layer], w_out[layer], ctx.dtype)
            ctx = ctx + gt_c * mix_ctx
        x = x + gt_l[:, None, :] * mix_lat
    return rmsnorm(x, final_norm_w)
```

```python
import contextlib
import numpy as np
import concourse.bass as bass
import concourse.mybir as mybir
from concourse.bass_utils import run_bass_kernel_spmd

F32 = mybir.dt.float32
BF16 = mybir.dt.bfloat16
AF = mybir.ActivationFunctionType
ALU = mybir.AluOpType

ENGS = ("pe", "act", "dve", "pool", "sp")

D = 1024
SEQ = 8192
CTXL = 256
NIN = 9248
EPS = 1e-6
C_AQ, C_AF0, C_AF1, C_AI, C_AZ = 0, 1024, 2048, 3072, 4096
C_BQ, C_BK, C_BV, C_BZ, C_BA, C_BB = 5120, 6144, 7168, 8192, 9216, 9232


class Res:
    __slots__ = ("name", "w", "rs")

    def __init__(self, name):
        self.name = name
        self.w = None
        self.rs = {}


class Op:
    __slots__ = ("eng", "fn", "deps", "chan", "chan_idx", "need_sig", "sig", "idx")

    def __init__(self, eng, fn, chan):
        self.eng = eng
        self.fn = fn
        self.chan = chan
        self.chan_idx = 0
        self.deps = ()
        self.need_sig = False
        self.sig = 0
        self.idx = 0


class Prog:
    def __init__(self, nc):
        self.nc = nc
        self.eng_ops = {e: [] for e in ENGS}
        self.chan_cnt = {}
        self.n = 0

    def op(self, eng, fn, reads=(), writes=(), chan=None):
        o = Op(eng, fn, chan)
        o.idx = self.n
        self.n += 1
        if chan is not None:
            c = self.chan_cnt.get(chan, 0) + 1
            self.chan_cnt[chan] = c
            o.chan_idx = c
        deps = {}
        for r in reads:
            if r.w is not None:
                deps[id(r.w)] = r.w
        for w in writes:
            if w.w is not None:
                deps[id(w.w)] = w.w
            for x in w.rs.values():
                deps[id(x)] = x
        o.deps = list(deps.values())
        for r in reads:
            key = eng if chan is None else ("dma", o.idx)
            r.rs[key] = o
        for w in writes:
            w.w = o
            w.rs = {}
        self.eng_ops[eng].append(o)
        return o

    def emit(self, st, final_waits=()):
        nc = self.nc
        esem = {e: st.enter_context(nc.semaphore("s_" + e)) for e in ENGS}
        csem = {c: st.enter_context(nc.semaphore("c_" + str(c))) for c in self.chan_cnt}
        for e in ENGS:
            for o in self.eng_ops[e]:
                for d in o.deps:
                    if d.chan is None:
                        if d.eng == "pe" and o.eng == "pe":
                            continue
                        d.need_sig = True
        for e in ENGS:
            c = 0
            for o in self.eng_ops[e]:
                if o.chan is None and o.need_sig:
                    c += 1
                    o.sig = c
        block = st.enter_context(nc.Block())

        def run(e, h):
            known = {}
            for o in self.eng_ops[e]:
                for d in o.deps:
                    if d.chan is not None:
                        key, val, sem = ("c", d.chan), 16 * d.chan_idx, csem[d.chan]
                    else:
                        if d.eng == "pe" and e == "pe":
                            continue
                        key, val, sem = ("e", d.eng), d.sig, esem[d.eng]
                    if known.get(key, 0) >= val:
                        continue
                    known[key] = val
                    h.wait_ge(sem, val)
                ins = o.fn(h)
                if o.chan is not None:
                    ins.then_inc(csem[o.chan], 16)
                elif o.need_sig:
                    ins.then_inc(esem[e], 1)
            if e == "sp":
                for ch in final_waits:
                    h.wait_ge(csem[ch], 16 * self.chan_cnt[ch])

        @block.tensor
        def _(h):
            run("pe", h)

        @block.scalar
        def _(h):
            run("act", h)

        @block.vector
        def _(h):
            run("dve", h)

        @block.gpsimd
        def _(h):
            run("pool", h)

        @block.sync
        def _(h):
            run("sp", h)


def build(ng=32, debug=False, do_gdn=True, do_s2=True):
    nc = bass.Bass("TRN2", target_bir_lowering=False)
    NT = ng * 256
    dram = lambda n, s, dt=F32, kind="ExternalInput": nc.dram_tensor(n, s, dt, kind=kind).ap()
    NTH = NT // 2
    xs = [dram("xs1", [NTH + 128, D]), dram("xs2", [NT + 128, D])]
    cx = [dram("ctx1", [CTXL, D]), dram("ctx2", [CTXL, D])]
    ccol_d = dram("ccol", [128, 16])
    normw_d = dram("norm_w", [D])
    fnw_d = dram("final_norm_w", [D])
    adaw_d = dram("ada_w", [D, 3 * D])
    adab_d = dram("ada_b", [3 * D])
    win_d = dram("w_in", [D, NIN])
    wout_d = dram("w_out", [2 * D, D])
    lbl_d = dram("lbl", [128, 32])
    cw_d = dram("cw", [128, 2, 24, 9])
    hnw_d = dram("hnwc", [128, 16])
    gpar_d = dram("gpar", [16, 16])
    gdr_d = dram("gdr", [32])
    out_d = dram("out", [NTH, D], kind="ExternalOutput")
    winb = dram("winb", [D, NIN], BF16, kind="Internal")
    woutb = dram("woutb", [2 * D, D], BF16, kind="Internal")
    o1_d = dram("o1", [NTH, 2 * D], F32, kind="ExternalOutput" if debug else "Internal")
    dbg_d = dram("dbg", [128, 4096], F32, kind="ExternalOutput") if debug else None

    P = Prog(nc)
    R_o1 = Res("o1")
    st = contextlib.ExitStack()
    with st:
        def sb(name, shape, dt=F32):
            return st.enter_context(nc.sbuf_tensor("sb_" + name, shape, dt)), Res(name)

        def ps(name, shape, dt=F32):
            return st.enter_context(nc.psum_tensor("ps_" + name, shape, dt)), Res(name)

        A = P.op

        idf, R_idf = sb("idf", [128, 128])
        idb, R_idb = sb("idb", [128, 128], BF16)
        jf, R_jf = sb("jf", [128, 128])
        m01, R_m01 = sb("m01", [128, 128])
        rmask, R_rmask = sb("rmask", [128, 256])
        A("pool", lambda e: e.memset(idf[:], 1.0), writes=[R_idf])
        A("pool", lambda e: e.affine_select(out=idf[:], in_=idf[:], pattern=[[-1, 128]], compare_op=ALU.is_equal,
                                            fill=0.0, base=0, channel_multiplier=1), reads=[R_idf], writes=[R_idf])
        A("dve", lambda e: e.tensor_copy(out=idb[:], in_=idf[:]), reads=[R_idf], writes=[R_idb])
        A("pool", lambda e: e.memset(jf[:], 1.0), writes=[R_jf])
        A("pool", lambda e: e.affine_select(out=jf[:], in_=jf[:], pattern=[[1, 128]], compare_op=ALU.is_equal,
                                            fill=0.0, base=-127, channel_multiplier=1), reads=[R_jf], writes=[R_jf])
        A("pool", lambda e: e.memset(m01[:], 1.0), writes=[R_m01])
        A("pool", lambda e: e.affine_select(out=m01[:], in_=m01[:], pattern=[[1, 128]], compare_op=ALU.is_ge,
                                            fill=0.0, base=0, channel_multiplier=-1), reads=[R_m01], writes=[R_m01])
        A("pool", lambda e: e.memset(m01[0:64, 64:128], 0.0), reads=[R_m01], writes=[R_m01])
        A("pool", lambda e: e.memset(rmask[:], 1.0), writes=[R_rmask])
        A("pool", lambda e: e.memset(rmask[:].rearrange("p (c t) -> p c t", t=64)[:, :, 0:1], 0.0),
          reads=[R_rmask], writes=[R_rmask])

        R_winb, R_woutb = Res("winb"), Res("woutb")
        for i in range(8):
            A("pool", lambda e, i=i: e.dma_start(out=winb[128 * i:128 * i + 128, :], in_=win_d[128 * i:128 * i + 128, :]),
              writes=[R_winb], chan="wcv")

        xt = [sb("xt%d" % i, [128, D]) for i in range(2)]
        qA, R_qA = sb("qA", [128, 8, 256])
        osb, R_osb = sb("osb", [128, 2 * D])
        ccol, R_ccol = sb("ccol", [128, 16])
        scb, R_scb = qA[:].rearrange("p a b -> p (a b)").rearrange("p (j t) -> p j t", t=128), R_qA
        adaw, R_adaw = xt[0][0][:].rearrange("p (k n) -> p k n", n=128), xt[0][1]
        nwb, R_nwb = osb[:, 0:D], R_osb
        adab, R_adab = osb[:, D:D + 128], R_osb
        Amod, R_Amod = sb("Amod", [128, 2, D])
        shmod, R_shmod = sb("shmod", [128, 2, D])
        gate, R_gate = sb("gate", [128, D])
        A("sp", lambda e: e.dma_start(out=ccol[:], in_=ccol_d), writes=[R_ccol], chan="p0")
        A("sp", lambda e: e.dma_start(out=nwb, in_=normw_d.partition_broadcast(128)), writes=[R_nwb], chan="p2")
        A("act", lambda e: e.activation(out=ccol[:], in_=ccol[:], func=AF.Silu), reads=[R_ccol], writes=[R_ccol])
        A("dve", lambda e: e.tensor_copy(out=scb, in_=ccol[:].unsqueeze(2).broadcast_to([128, 16, 128])),
          reads=[R_ccol], writes=[R_scb])
        pm = [ps("pm%d" % i, [128, 512]) for i in range(8)]
        for blk in range(24):
            A("sp", lambda e, blk=blk: e.dma_start(out=adaw, in_=adaw_d[:, 128 * blk:128 * blk + 128].rearrange("(k p) n -> p k n", p=128)),
              writes=[R_adaw], chan="adaw")
            A("sp", lambda e, blk=blk: e.dma_start(out=adab, in_=adab_d[128 * blk:128 * blk + 128].partition_broadcast(128)), writes=[R_adab], chan="p1")
            which, off = blk // 8, (blk % 8) * 128
            for j in range(2):
                pt, R_pt = pm[j]
                for k in range(8):
                    A("pe", lambda e, j=j, k=k, pt=pt: e.matmul(pt[:, 0:128], lhsT=scb[:, j * 8 + k, :], rhs=adaw[:, k, :], start=(k == 0), stop=(k == 7)),
                      reads=[R_scb, R_adaw], writes=[R_pt])
                if which == 0:
                    A("dve", lambda e, j=j, pt=pt, off=off: e.tensor_tensor(out=shmod[:, j, off:off + 128], in0=pt[:, 0:128], in1=adab, op=ALU.add),
                      reads=[R_pt, R_adab], writes=[R_shmod])
                elif which == 1:
                    A("dve", lambda e, j=j, pt=pt, off=off: e.scalar_tensor_tensor(out=Amod[:, j, off:off + 128], in0=pt[:, 0:128], scalar=1.0, in1=adab, op0=ALU.add, op1=ALU.add),
                      reads=[R_pt, R_adab], writes=[R_Amod])
                    A("dve", lambda e, j=j, off=off: e.tensor_tensor(out=Amod[:, j, off:off + 128], in0=Amod[:, j, off:off + 128], in1=nwb[:, off:off + 128], op=ALU.mult),
                      reads=[R_Amod, R_nwb], writes=[R_Amod])
                elif j == 0:
                    A("dve", lambda e, pt=pt, off=off: e.tensor_tensor(out=gate[:, off:off + 128], in0=pt[:, 0:128], in1=adab, op=ALU.add),
                      reads=[R_pt, R_adab], writes=[R_gate])

        lbl, R_lbl = sb("lbl", [128, 32])
        lb, R_lb = sb("lb", [128, 16])
        oml, R_oml = sb("oml", [128, 16])
        lbm1, R_lbm1 = sb("lbm1", [128, 16])
        A("sp", lambda e: e.dma_start(out=lbl[:], in_=lbl_d), writes=[R_lbl], chan="p3")
        A("dve", lambda e: e.tensor_tensor(out=lb[:], in0=lbl[:, 0:16], in1=lbl[:, 16:32], op=ALU.subtract), reads=[R_lbl], writes=[R_lb])
        A("act", lambda e: e.activation(out=lb[:], in_=lb[:], func=AF.Sigmoid), reads=[R_lb], writes=[R_lb])
        A("dve", lambda e: e.tensor_scalar(out=oml[:], in0=lb[:], scalar1=-1.0, scalar2=1.0, op0=ALU.mult, op1=ALU.add), reads=[R_lb], writes=[R_oml])
        A("dve", lambda e: e.tensor_scalar(out=lbm1[:], in0=lb[:], scalar1=-1.0, scalar2=None, op0=ALU.add), reads=[R_lb], writes=[R_lbm1])

        junk, R_junk = sb("junk", [128, D], BF16)
        ssq, R_ssq = sb("ssq", [128, 2])
        hn, R_hn = sb("hn", [128, D])
        hb, R_hb = sb("hb", [128, D], BF16)
        hT, R_hT = sb("hT", [128, 8, 384], BF16)
        wt = [sb("wt%d" % i, [128, 8, 512], BF16) for i in range(2)]
        wctr = [0]
        tf = [sb("tf%d" % i, [128, 256]) for i in range(4)]
        qt, R_qt = sb("qt", [128, 8, 256], BF16)
        kdT, R_kdT = sb("kdT", [128, 8, 256], BF16)
        dA, R_dA = sb("dA", [128, 8, 4])
        kdm, R_kdm = sb("kdm", [128, 2, D], BF16)
        vA, R_vA = sb("vA", [128, 2, D], BF16)
        scT, R_scT = sb("scT", [128, 8, 128], BF16)
        SA, R_SA = sb("SA", [128, 8, 128])
        SAd, R_SAd = sb("SAd", [128, 8, 128])
        SAb, R_SAb = sb("SAb", [128, 8, 128], BF16)
        A("dve", lambda e: e.memset(SA[:], 0.0), writes=[R_SA])
        cws, R_cws = sb("cws", [128, 2, 24, 9])
        gp, R_gp = sb("gp", [16, 16])
        nA16, R_nA16 = sb("nA16", [16, 2])
        gdtr, R_gdtr = sb("gdtr", [128, 4, 8])
        nAr, R_nAr = sb("nAr", [128, 2, 8])
        onesb, R_onesb = sb("onesb", [128, 128], BF16)
        capI, R_capI = sb("capI", [128, 128])
        capS, R_capS = sb("capS", [128, 128])
        mU, R_mU = sb("mU", [128, 128])
        cind, R_cind = sb("cind", [128, 2, 128])
        Esel, R_Esel = sb("Esel", [16, 8, 128])
        A("sp", lambda e: e.dma_start(out=cws[:], in_=cw_d), writes=[R_cws], chan="p4")
        A("sp", lambda e: e.dma_start(out=gp[:], in_=gpar_d), writes=[R_gp], chan="p5")
        A("sp", lambda e: e.dma_start(out=gdtr[:].rearrange("p a h -> p (a h)"), in_=gdr_d.partition_broadcast(128)), writes=[R_gdtr], chan="p6")
        A("act", lambda e: e.activation(out=nA16[:, 0:1], in_=gp[:, 0:1], func=AF.Exp), reads=[R_gp], writes=[R_nA16])
        A("act", lambda e: e.activation(out=nA16[:, 1:2], in_=gp[:, 2:3], func=AF.Exp), reads=[R_gp], writes=[R_nA16])
        A("dve", lambda e: e.tensor_scalar(out=nA16[:], in0=nA16[:], scalar1=-1.0, scalar2=None, op0=ALU.mult), reads=[R_nA16], writes=[R_nA16])
        A("act", lambda e: e.activation(out=nAr[:], in_=gdtr[:, 0:2, :], func=AF.Exp), reads=[R_gdtr], writes=[R_nAr])
        A("dve", lambda e: e.tensor_scalar(out=nAr[:], in0=nAr[:], scalar1=-1.0, scalar2=None, op0=ALU.mult), reads=[R_nAr], writes=[R_nAr])
        A("pool", lambda e: e.memset(onesb[:], 1.0), writes=[R_onesb])
        A("dve", lambda e: e.tensor_scalar(out=capI[:], in0=m01[:], scalar1=-1.0, scalar2=30000.0, op0=ALU.add, op1=ALU.mult), reads=[R_m01], writes=[R_capI])
        A("dve", lambda e: e.tensor_tensor(out=capS[:], in0=m01[:], in1=idf[:], op=ALU.subtract), reads=[R_m01, R_idf], writes=[R_capS])
        A("dve", lambda e: e.tensor_scalar(out=capS[:], in0=capS[:], scalar1=-1.0, scalar2=30000.0, op0=ALU.add, op1=ALU.mult), reads=[R_capS], writes=[R_capS])
        A("pool", lambda e: e.memset(mU[:], 0.0), writes=[R_mU])
        A("pool", lambda e: e.memset(mU[0:64, 0:64], 1.0), reads=[R_mU], writes=[R_mU])
        A("pool", lambda e: e.memset(mU[64:128, 64:128], 1.0), reads=[R_mU], writes=[R_mU])
        A("dve", lambda e: e.tensor_tensor(out=mU[:], in0=mU[:], in1=m01[:], op=ALU.subtract), reads=[R_mU, R_m01], writes=[R_mU])
        A("pool", lambda e: e.memset(cind[:], 0.0), writes=[R_cind])
        A("pool", lambda e: e.memset(cind[0:64, 0, :], 1.0), reads=[R_cind], writes=[R_cind])
        A("pool", lambda e: e.memset(cind[64:128, 1, :], 1.0), reads=[R_cind], writes=[R_cind])
        A("dve", lambda e: e.tensor_scalar(out=Esel[:], in0=gp[:, 8:16].unsqueeze(2).broadcast_to([16, 8, 128]), scalar1=gp[:, 6:7], scalar2=None, op0=ALU.mult),
          reads=[R_gp], writes=[R_Esel])
        wab, R_wab = sb("wab", [128, 8, 32], BF16)
        dw, R_dw = sb("dw", [128, 9, 128], BF16)
        cpad, R_cpad = sb("cpad", [128, 400], BF16)
        cs, R_cs = sb("cs", [128, 256])
        sq, R_sq = sb("sq", [128, 256], BF16)
        rn, R_rn = sb("rn", [128, 256])
        qBf, R_qBf = sb("qBf", [128, 8, 256], BF16)
        kBf, R_kBf = sb("kBf", [128, 8, 256], BF16)
        vBf, R_vBf = sb("vBf", [128, 8, 256], BF16)
        qdB, R_qdB = sb("qdB", [128, 8, 256], BF16)
        rf = [sb("rf%d" % i, [16, 256]) for i in range(6)]
        rhsP, R_rhsP = sb("rhsP", [16, 8, 128])
        rhsA, R_rhsA = sb("rhsA", [16, 8, 128])
        tms = [sb("tms%d" % i, [128, 16]) for i in range(4)]
        bet, R_bet = sb("bet", [128, 8])
        bee, R_bee = sb("bee", [128, 8])
        ksf, R_ksf = sb("ksf", [128, 8])
        dB, R_dB = sb("dB", [128, 2, 8])
        kbe, R_kbe = sb("kbe", [128, 8, 128], BF16)
        kdc, R_kdc = sb("kdc", [128, 8, 128], BF16)
        bvt, R_bvt = sb("bvt", [128, 8, 128], BF16)
        eA, R_eA = sb("eA", [128, 8, 128])
        eP, R_eP = sb("eP", [128, 8, 128])
        XT = [sb("XT%d" % i, [128, 8, 128], BF16) for i in range(2)]
        XX = [sb("XX%d" % i, [128, 8, 128], BF16) for i in range(2)]
        QQ = [sb("QQ%d" % i, [128, 8, 128], BF16) for i in range(2)]
        PT, R_PT = sb("PT", [128, 8, 128], BF16)
        u0, R_u0 = sb("u0", [128, 8, 128])
        nwT, R_nwT = sb("nwT", [128, 8, 128], BF16)
        vnew, R_vnew = sb("vnew", [128, 8, 128], BF16)
        SB, R_SB = sb("SB", [128, 8, 128])
        SBb, R_SBb = sb("SBb", [128, 8, 128], BF16)
        A("dve", lambda e: e.memset(SB[:], 0.0), writes=[R_SB])
        A("pool", lambda e: e.memset(SBb[:], 0.0), writes=[R_SBb])
        fnw, R_fnw = sb("fnw", [128, D])
        hnwc, R_hnwc = sb("hnwc", [128, 16])
        hst, R_hst = sb("hst", [128, 32])
        A("sp", lambda e: e.dma_start(out=fnw[:], in_=fnw_d.partition_broadcast(128)), writes=[R_fnw], chan="p7")
        A("sp", lambda e: e.dma_start(out=hnwc[:], in_=hnw_d), writes=[R_hnwc], chan="p8")
        for kk in range(16):
            x_t, R_x = xt[1]
            A("sp", lambda e, kk=kk: e.dma_start(out=x_t[:], in_=wout_d[128 * kk:128 * kk + 128, :]), writes=[R_x], chan="x1")
            A("dve", lambda e, kk=kk: e.tensor_scalar(out=hb[:], in0=x_t[:], scalar1=hnwc[:, kk:kk + 1], scalar2=None, op0=ALU.mult), reads=[R_x, R_hnwc], writes=[R_hb])
            A("sp", lambda e, kk=kk: e.dma_start(out=woutb[128 * kk:128 * kk + 128, :], in_=hb[:]), reads=[R_hb], writes=[R_woutb], chan="wo_st")
        szs = [(Amod[:, 1, :].bitcast(BF16), R_Amod), (shmod[:, 1, :].bitcast(BF16), R_shmod)]
        yT2 = [XT[1], XX[1]]
        yb2 = [(kbe[:].rearrange("p h c -> p (h c)"), R_kbe), (kdc[:].rearrange("p h c -> p (h c)"), R_kdc)]
        dw2, R_dw2 = sb("dw2", [128, 9, 128], BF16)
        DW = [(dw, R_dw), (dw2, R_dw2)]
        print("sbuf bytes remaining", nc.sbuf_bytes_remaining)


        def load_w(c0, ncol):
            i = wctr[0] % 2
            wctr[0] += 1
            t, R = wt[i]
            A("sp", lambda e: e.dma_start(out=t[:, :, 0:ncol], in_=winb[:, c0:c0 + ncol].rearrange("(k p) n -> p k n", p=128)),
              reads=[R_winb], writes=[R], chan="w%d" % i)
            return t, R

        def xprep(src, row0, i, col0, j):
            x_t, R_x = xt[i % 2]
            A("sp", lambda e: e.dma_start(out=x_t[:], in_=src[row0:row0 + 128, :]), writes=[R_x], chan="x%d" % (i % 2))
            A("act", lambda e: e.activation(out=junk[:], in_=x_t[:], func=AF.Square, accum_out=ssq[:, 0:1]),
              reads=[R_x], writes=[R_junk, R_ssq])
            A("act", lambda e: e.activation(out=ssq[:, 1:2], in_=ssq[:, 0:1], func=AF.Ln, scale=1.0 / D, bias=EPS), reads=[R_ssq], writes=[R_ssq])
            A("act", lambda e: e.activation(out=ssq[:, 1:2], in_=ssq[:, 1:2], func=AF.Exp, scale=-0.5), reads=[R_ssq], writes=[R_ssq])
            A("dve", lambda e: e.scalar_tensor_tensor(out=hn[:], in0=x_t[:], scalar=ssq[:, 1:2], in1=Amod[:, j, :], op0=ALU.mult, op1=ALU.mult),
              reads=[R_x, R_ssq, R_Amod], writes=[R_hn])
            A("pool", lambda e: e.tensor_tensor(out=hb[:], in0=hn[:], in1=shmod[:, j, :], op=ALU.add), reads=[R_hn, R_shmod], writes=[R_hb])
            pt, R_pt = pm[7]
            ptb = pt[:].bitcast(BF16)
            for k in range(8):
                A("pe", lambda e, k=k: e.transpose(out=ptb[:, 128 * k:128 * k + 128], in_=hb[:, 128 * k:128 * k + 128], identity=idb[:]),
                  reads=[R_hb, R_idb], writes=[R_pt])
            A("act", lambda e: e.activation(out=hT[:, :, col0:col0 + 128], in_=ptb.rearrange("p (k t) -> p k t", t=128), func=AF.Copy),
              reads=[R_pt], writes=[R_hT])

        def group(s, src, row0, ntile_w, off, j, readout, gidx):
            T = 256
            for i in range(ntile_w):
                if ntile_w == 3 and gidx >= 1 and i == 0:
                    A("pool", lambda e: e.tensor_copy(out=hT[:, :, 0:128], in_=hT[:, :, 256:384]), reads=[R_hT], writes=[R_hT])
                    continue
                xprep(src, row0 + 128 * i, i, 128 * i, j)
            main = slice(off, off + T)
            for half in (range(2) if readout else ()):
                w, R_w = load_w(C_AQ + 512 * half, 512)
                for hh in range(4):
                    h = 4 * half + hh
                    pt, R_pt = pm[hh % 2]
                    for k in range(8):
                        A("pe", lambda e, k=k, hh=hh, pt=pt, w=w: e.matmul(pt[:, 0:T], lhsT=w[:, k, 128 * hh:128 * hh + 128], rhs=hT[:, k, main], start=(k == 0), stop=(k == 7)),
                          reads=[R_w, R_hT], writes=[R_pt])
                    A("act", lambda e, h=h, pt=pt: e.activation(out=qA[:, h, :], in_=pt[:, 0:T], func=AF.Silu), reads=[R_pt], writes=[R_qA])
            caf = C_AF0 if s == 0 else C_AF1
            for half in range(2):
                w, R_w = load_w(caf + 512 * half, 512)
                for hh in range(4):
                    h = 4 * half + hh
                    li = s * 8 + h
                    pt, R_pt = pm[2 + hh % 2]
                    (sig, R_sig), (g, R_g), (b, R_b), (dl, R_dl) = tf
                    for k in range(8):
                        A("pe", lambda e, k=k, hh=hh, pt=pt, w=w: e.matmul(pt[:, 0:T], lhsT=w[:, k, 128 * hh:128 * hh + 128], rhs=hT[:, k, main], start=(k == 0), stop=(k == 7)),
                          reads=[R_w, R_hT], writes=[R_pt])
                    A("act", lambda e, pt=pt: e.activation(out=sig[:], in_=pt[:, 0:T], func=AF.Sigmoid), reads=[R_pt], writes=[R_sig])
                    A("act", lambda e, li=li: e.activation(out=g[:], in_=sig[:], func=AF.Ln, scale=oml[:, li:li + 1], bias=lb[:, li:li + 1]),
                      reads=[R_sig, R_oml, R_lb], writes=[R_g])
                    A("dve", lambda e, li=li: e.tensor_scalar(out=sig[:], in0=sig[:], scalar1=-1.0, scalar2=lbm1[:, li:li + 1], op0=ALU.add, op1=ALU.mult),
                      reads=[R_sig, R_lbm1], writes=[R_sig])
                    A("dve", lambda e: e.tensor_tensor_scan(out=b[:], data0=rmask[:, 0:T], data1=g[:], initial=0.0, op0=ALU.mult, op1=ALU.add),
                      reads=[R_rmask, R_g], writes=[R_b])
                    b3 = b[:].rearrange("p (c t) -> p c t", t=64)
                    A("dve", lambda e, b3=b3: e.tensor_tensor(out=dl[:].rearrange("p (c t) -> p c t", t=64), in0=b3, in1=b3[:, :, 63:64].broadcast_to([128, 4, 64]), op=ALU.subtract),
                      reads=[R_b], writes=[R_dl])
                    A("act", lambda e, h=h, b3=b3: e.activation(out=dA[:, h, :], in_=b3[:, :, 63], func=AF.Exp), reads=[R_b], writes=[R_dA])
                    if readout:
                        A("act", lambda e: e.activation(out=g[:], in_=dl[:], func=AF.Exp), reads=[R_dl], writes=[R_g])
                    A("act", lambda e: e.activation(out=b[:], in_=dl[:], func=AF.Exp, scale=-1.0), reads=[R_dl], writes=[R_b])
                    if readout:
                        A("dve", lambda e, h=h: e.scalar_tensor_tensor(out=qt[:, h, :], in0=qA[:, h, :], scalar=float(128 ** -0.5), in1=g[:], op0=ALU.mult, op1=ALU.mult),
                          reads=[R_qA, R_g], writes=[R_qt])
                    A("pool", lambda e, h=h: e.tensor_tensor(out=kdT[:, h, :], in0=sig[:], in1=b[:], op=ALU.mult),
                      reads=[R_sig, R_b], writes=[R_kdT])
            for half in range(2):
                w, R_w = load_w(C_AI + 512 * half, 512)
                for ti in range(2):
                    pt, R_pt = pm[4 + ti]
                    for k in range(8):
                        A("pe", lambda e, k=k, ti=ti, pt=pt, w=w: e.matmul(pt[:], lhsT=hT[:, k, off + 128 * ti:off + 128 * ti + 128], rhs=w[:, k, :], start=(k == 0), stop=(k == 7)),
                          reads=[R_w, R_hT], writes=[R_pt])
                    A("act", lambda e, ti=ti, pt=pt, half=half: e.activation(out=vA[:, ti, 512 * half:512 * half + 512], in_=pt[:], func=AF.Copy),
                      reads=[R_pt], writes=[R_vA])
            for ti in range(2):
                pt, R_pt = pm[6]
                ptb = pt[:].bitcast(BF16)
                for h in range(8):
                    A("pe", lambda e, h=h, ti=ti, ptb=ptb: e.transpose(out=ptb[:, 128 * h:128 * h + 128], in_=kdT[:, h, 128 * ti:128 * ti + 128], identity=idb[:]),
                      reads=[R_kdT, R_idb], writes=[R_pt])
                A("dve", lambda e, ti=ti, ptb=ptb: e.tensor_copy(out=kdm[:, ti, :], in_=ptb), reads=[R_pt], writes=[R_kdm])
            if do_gdn:
                lat = (ntile_w == 3)
                WN = 128 * ntile_w
                if gidx <= 0:
                    A("pool", lambda e: e.memset(cpad[:], 0.0), writes=[R_cpad])
                if lat:
                    cpv = cpad[:, 0:396].rearrange("p (r c) -> p r c", c=66)
                for m in range(6):
                    if m < 2 and not readout:
                        continue
                    w, R_w = load_w(C_BQ + 512 * m, 512)
                    for hh in range(4):
                        blk = 4 * m + hh
                        kind, h = blk // 8, blk % 8
                        pt, R_pt = pm[hh % 2]
                        for k in range(8):
                            A("pe", lambda e, k=k, hh=hh, pt=pt, w=w: e.matmul(pt[:, 0:WN], lhsT=w[:, k, 128 * hh:128 * hh + 128], rhs=hT[:, k, 0:WN], start=(k == 0), stop=(k == 7)),
                              reads=[R_w, R_hT], writes=[R_pt])
                        if lat:
                            A("act", lambda e, pt=pt: e.activation(out=cpv[:, :, 1:65], in_=pt[:, 0:384].rearrange("p (r c) -> p r c", c=64), func=AF.Copy),
                              reads=[R_pt], writes=[R_cpad])
                            if gidx == 0:
                                A("pool", lambda e: e.memset(cpv[:, 0, :], 0.0), reads=[R_cpad], writes=[R_cpad])
                            if s == 1 and gidx == ng - 1:
                                A("pool", lambda e: e.memset(cpv[:, 5, :], 0.0), reads=[R_cpad], writes=[R_cpad])
                        else:
                            A("act", lambda e, pt=pt: e.activation(out=cpad[:, 1:257], in_=pt[:, 0:256], func=AF.Copy), reads=[R_pt], writes=[R_cpad])
                        dwb, Rdw = DW[blk % 2]
                        A("pool", lambda e, blk=blk, dwb=dwb: e.tensor_tensor(out=dwb[:], in0=idb[:].unsqueeze(1).broadcast_to([128, 9, 128]),
                                                                              in1=cws[:, s, blk, :].unsqueeze(2).broadcast_to([128, 9, 128]), op=ALU.mult),
                          reads=[R_idb, R_cws], writes=[Rdw])
                        pc, R_pc = pm[2 + hh % 2]
                        taps = [(i, jj) for i in range(3) for jj in range(3)] if lat else [(1, jj) for jj in range(3)]
                        for n, (i, jj) in enumerate(taps):
                            if lat:
                                A("pe", lambda e, i=i, jj=jj, n=n, pc=pc, nt=len(taps), dwb=dwb: e.matmul(pc[:, 0:256].rearrange("p (r c) -> p r c", c=64), lhsT=dwb[:, 3 * i + jj, :], rhs=cpv[:, i:i + 4, jj:jj + 64], start=(n == 0), stop=(n == nt - 1)),
                                  reads=[Rdw, R_cpad], writes=[R_pc])
                            else:
                                A("pe", lambda e, i=i, jj=jj, n=n, pc=pc, nt=len(taps), dwb=dwb: e.matmul(pc[:, 0:256], lhsT=dwb[:, 3 * i + jj, :], rhs=cpad[:, jj:jj + 256], start=(n == 0), stop=(n == nt - 1)),
                                  reads=[Rdw, R_cpad], writes=[R_pc])
                        if kind == 2:
                            A("act", lambda e, h=h, pc=pc: e.activation(out=vBf[:, h, :], in_=pc[:, 0:256], func=AF.Silu), reads=[R_pc], writes=[R_vBf])
                        else:
                            A("act", lambda e, pc=pc: e.activation(out=cs[:], in_=pc[:, 0:256], func=AF.Silu), reads=[R_pc], writes=[R_cs])
                            A("act", lambda e: e.activation(out=sq[:], in_=cs[:], func=AF.Square), reads=[R_cs], writes=[R_sq])
                            pn, R_pn = pm[4 + hh % 2]
                            A("pe", lambda e, pn=pn: e.matmul(pn[:, 0:256], lhsT=onesb[:], rhs=sq[:], start=True, stop=True), reads=[R_onesb, R_sq], writes=[R_pn])
                            A("act", lambda e, pn=pn: e.activation(out=rn[:], in_=pn[:, 0:256], func=AF.Ln, bias=EPS), reads=[R_pn], writes=[R_rn])
                            A("act", lambda e: e.activation(out=rn[:], in_=rn[:], func=AF.Exp, scale=-0.5), reads=[R_rn], writes=[R_rn])
                            dst, R_dst = (qBf, R_qBf) if kind == 0 else (kBf, R_kBf)
                            sc_ = float(128 ** -0.5) if kind == 0 else 1.0
                            A("dve", lambda e, h=h, dst=dst, sc_=sc_: e.scalar_tensor_tensor(out=dst[:, h, :], in0=cs[:], scalar=sc_, in1=rn[:], op0=ALU.mult, op1=ALU.mult),
                              reads=[R_cs, R_rn], writes=[R_dst])
                if gidx <= 0:
                    ca_, cb_ = C_BA + 8 * s, C_BB + 8 * s
                    for q_, c_ in enumerate((ca_, ca_, cb_, cb_)):
                        A("sp", lambda e, q_=q_, c_=c_: e.dma_start(out=wab[:, :, 8 * q_:8 * q_ + 8], in_=winb[:, c_:c_ + 8].rearrange("(k p) n -> p k n", p=128)),
                          reads=[R_winb], writes=[R_wab], chan="wab")
                pz, R_pz = pm[6]
                for k in range(8):
                    A("pe", lambda e, k=k: e.matmul(pz[0:16, 0:256], lhsT=wab[:, k, 0:16], rhs=hT[:, k, main], start=(k == 0), stop=(k == 7)), reads=[R_wab, R_hT], writes=[R_pz])
                for k in range(8):
                    A("pe", lambda e, k=k: e.matmul(pz[0:16, 256:512], lhsT=wab[:, k, 16:32], rhs=hT[:, k, main], start=(k == 0), stop=(k == 7)), reads=[R_wab, R_hT], writes=[R_pz])
                (r0, R_r0), (r1, R_r1), (r2, R_r2), (r3, R_r3), (r4, R_r4), (r5, R_r5) = rf
                A("act", lambda e: e.activation(out=r0[:], in_=pz[0:16, 0:256], func=AF.Exp, bias=gp[:, 1 + 2 * s:2 + 2 * s]), reads=[R_pz, R_gp], writes=[R_r0])
                A("act", lambda e: e.activation(out=r0[:], in_=r0[:], func=AF.Ln, bias=1.0), reads=[R_r0], writes=[R_r0])
                A("dve", lambda e: e.tensor_scalar(out=r0[:], in0=r0[:], scalar1=nA16[:, s:s + 1], scalar2=None, op0=ALU.mult), reads=[R_r0, R_nA16], writes=[R_r0])
                A("act", lambda e: e.activation(out=r1[:], in_=pz[0:16, 256:512], func=AF.Exp, scale=-1.0), reads=[R_pz], writes=[R_r1])
                A("act", lambda e: e.activation(out=r1[:], in_=r1[:], func=AF.Ln, bias=1.0), reads=[R_r1], writes=[R_r1])
                A("dve", lambda e: e.tensor_tensor_scan(out=r2[:], data0=rmask[0:16, 0:256], data1=r0[:], initial=0.0, op0=ALU.mult, op1=ALU.add), reads=[R_rmask, R_r0], writes=[R_r2])
                A("dve", lambda e: e.tensor_tensor(out=r1[:], in0=r2[:], in1=r1[:], op=ALU.subtract), reads=[R_r2, R_r1], writes=[R_r1])
                A("dve", lambda e: e.tensor_scalar(out=r3[:], in0=r2[:], scalar1=gp[:, 4:5], scalar2=gp[:, 5:6], op0=ALU.mult, op1=ALU.add), reads=[R_r2, R_gp], writes=[R_r3])
                A("dve", lambda e: e.tensor_scalar(out=r4[:], in0=r2[:], scalar1=gp[:, 6:7], scalar2=gp[:, 7:8], op0=ALU.mult, op1=ALU.add), reads=[R_r2, R_gp], writes=[R_r4])
                A("dve", lambda e: e.tensor_scalar(out=r5[:], in0=r1[:], scalar1=gp[:, 6:7], scalar2=gp[:, 7:8], op0=ALU.mult, op1=ALU.add), reads=[R_r1, R_gp], writes=[R_r5])
                for h in (range(8) if readout else ()):
                    pb, R_pb = pm[h % 2]
                    A("pe", lambda e, h=h, pb=pb: e.matmul(pb[:, 0:256], lhsT=Esel[:, h, :], rhs=r4[:], start=True, stop=True), reads=[R_Esel, R_r4], writes=[R_pb])
                    A("act", lambda e, pb=pb: e.activation(out=cs[:], in_=pb[:, 0:256], func=AF.Exp), reads=[R_pb], writes=[R_cs])
                    A("dve", lambda e, h=h: e.tensor_tensor(out=qdB[:, h, :], in0=qBf[:, h, :], in1=cs[:], op=ALU.mult), reads=[R_qBf, R_cs], writes=[R_qdB])
            if s == 1 and readout:
                for q_ in range(4):
                    w, R_w = load_w((C_AZ if q_ < 2 else C_BZ) + 512 * (q_ % 2), 512)
                    for ti in range(2):
                        pt, R_pt = pm[4 + ti]
                        for k in range(8):
                            A("pe", lambda e, k=k, ti=ti, pt=pt, w=w: e.matmul(pt[:], lhsT=hT[:, k, off + 128 * ti:off + 128 * ti + 128], rhs=w[:, k, :], start=(k == 0), stop=(k == 7)),
                              reads=[R_w, R_hT], writes=[R_pt])
                        A("act", lambda e, ti=ti, pt=pt, q_=q_: e.activation(out=szs[ti][0][:, 512 * q_:512 * q_ + 512], in_=pt[:], func=AF.Silu), reads=[R_pt], writes=[szs[ti][1]])
            for ti in range(2):
                tok = slice(128 * ti, 128 * ti + 128)
                if readout:
                    for hq in range(2):
                        pt, R_pt = pm[hq]
                        for hh in range(4):
                            h = 4 * hq + hh
                            A("pe", lambda e, h=h, hh=hh, pt=pt, tok=tok: e.matmul(pt[:, 128 * hh:128 * hh + 128], lhsT=kdT[:, h, tok], rhs=qt[:, h, tok], start=True, stop=True),
                              reads=[R_kdT, R_qt], writes=[R_pt])
                        A("dve", lambda e, hq=hq, pt=pt: e.tensor_tensor(out=scT[:, 4 * hq:4 * hq + 4, :], in0=pt[:].rearrange("p (h t) -> p h t", t=128),
                                                                         in1=m01[:].unsqueeze(1).broadcast_to([128, 4, 128]), op=ALU.mult),
                          reads=[R_pt, R_m01], writes=[R_scT])
                po = [pm[2], pm[3]]
                pre = (s == 1 and readout)
                if pre:
                    nr = NTH - 128 - ((gidx - ng // 2) * 256 + 128 * ti)
                    A("sp", lambda e, nr=nr: e.dma_start(out=osb[:], in_=o1_d[nr:nr + 128, :]), reads=[R_o1], writes=[R_osb], chan="o1l")
                    for q_, (pt, R_pt) in enumerate((pm[2], pm[3], pm[6], pm[7])):
                        A("pe", lambda e, q_=q_, pt=pt: e.matmul(pt[:], lhsT=jf[:], rhs=osb[:, 512 * q_:512 * q_ + 512], start=True, stop=False), reads=[R_jf, R_osb], writes=[R_pt])
                for c in range(2):
                    rows = slice(64 * c, 64 * c + 64)
                    ctok = slice(128 * ti + 64 * c, 128 * ti + 64 * c + 64)
                    cg = 2 * ti + c
                    A("dve", lambda e, cg=cg: e.tensor_tensor(out=SAd[:], in0=SA[:], in1=dA[:, :, cg:cg + 1].broadcast_to([128, 8, 128]), op=ALU.mult),
                      reads=[R_SA, R_dA], writes=[R_SAd])
                    if readout:
                        A("act", lambda e: e.activation(out=SAb[:], in_=SAd[:], func=AF.Copy), reads=[R_SAd], writes=[R_SAb])
                        for h in range(8):
                            pt, R_pt = po[h // 4]
                            cols = slice(128 * (h % 4), 128 * (h % 4) + 128)
                            A("pe", lambda e, h=h, pt=pt, cols=cols, rows=rows, ctok=ctok: e.matmul(pt[rows, cols], lhsT=qt[:, h, ctok], rhs=SAb[:, h, :], start=(not pre), stop=False),
                              reads=[R_qt, R_SAb], writes=[R_pt])
                            A("pe", lambda e, h=h, pt=pt, cols=cols, rows=rows, ti=ti: e.matmul(pt[rows, cols], lhsT=scT[rows, h, rows], rhs=vA[rows, ti, 128 * h:128 * h + 128], start=False, stop=True),
                              reads=[R_scT, R_vA], writes=[R_pt])
                    for hq in range(2):
                        pt, R_pt = pm[4 + hq]
                        for hh in range(4):
                            h = 4 * hq + hh
                            A("pe", lambda e, h=h, hh=hh, pt=pt, rows=rows, ti=ti: e.matmul(pt[:, 128 * hh:128 * hh + 128], lhsT=kdm[rows, ti, 128 * h:128 * h + 128], rhs=vA[rows, ti, 128 * h:128 * h + 128], start=True, stop=True),
                              reads=[R_kdm, R_vA], writes=[R_pt])
                        A("dve", lambda e, hq=hq, pt=pt: e.tensor_tensor(out=SA[:, 4 * hq:4 * hq + 4, :], in0=pt[:].rearrange("p (h v) -> p h v", v=128), in1=SAd[:, 4 * hq:4 * hq + 4, :], op=ALU.add),
                          reads=[R_pt, R_SAd], writes=[R_SA])
                if readout:
                    for hq in range(2):
                        pt, R_pt = po[hq]
                        A("act", lambda e, hq=hq, pt=pt: e.activation(out=osb[:, 512 * hq:512 * hq + 512], in_=pt[:], func=AF.Copy), reads=[R_pt], writes=[R_osb])
                if do_gdn:
                    (t0, R_t0), (t1, R_t1), (t2, R_t2), (t3, R_t3) = tms
                    mtok = slice(off + 128 * ti, off + 128 * ti + 128)
                    pz, R_pz = pm[0]
                    for k in range(8):
                        A("pe", lambda e, k=k, mtok=mtok: e.matmul(pz[:, 0:16], lhsT=hT[:, k, mtok], rhs=wab[:, k, 8:24], start=(k == 0), stop=(k == 7)), reads=[R_wab, R_hT], writes=[R_pz])
                    A("dve", lambda e: e.tensor_tensor(out=t0[:, 0:8], in0=pz[:, 0:8], in1=gdtr[:, 2 + s, :], op=ALU.add), reads=[R_pz, R_gdtr], writes=[R_t0])
                    A("act", lambda e: e.activation(out=t0[:, 0:8], in_=t0[:, 0:8], func=AF.Exp), reads=[R_t0], writes=[R_t0])
                    A("act", lambda e: e.activation(out=t0[:, 0:8], in_=t0[:, 0:8], func=AF.Ln, bias=1.0), reads=[R_t0], writes=[R_t0])
                    A("dve", lambda e: e.tensor_tensor(out=t0[:, 0:8], in0=t0[:, 0:8], in1=nAr[:, s, :], op=ALU.mult), reads=[R_t0, R_nAr], writes=[R_t0])
                    A("act", lambda e: e.activation(out=t0[:, 8:16], in_=pz[:, 8:16], func=AF.Exp, scale=-1.0), reads=[R_pz], writes=[R_t0])
                    A("dve", lambda e: e.tensor_scalar(out=t0[:, 8:16], in0=t0[:, 8:16], scalar1=1.0, scalar2=None, op0=ALU.add), reads=[R_t0], writes=[R_t0])
                    A("dve", lambda e: e.reciprocal(out=bet[:], in_=t0[:, 8:16]), reads=[R_t0], writes=[R_bet])
                    pz2, R_pz2 = pm[1]
                    A("pe", lambda e: e.matmul(pz2[:, 0:8], lhsT=m01[:], rhs=t0[:, 0:8], start=True, stop=True), reads=[R_m01, R_t0], writes=[R_pz2])
                    A("pe", lambda e: e.matmul(pz2[:, 8:16], lhsT=mU[:], rhs=t0[:, 0:8], start=True, stop=True), reads=[R_mU, R_t0], writes=[R_pz2])
                    for c in range(2):
                        A("pe", lambda e, c=c: e.matmul(pz2[:, 16 + 8 * c:24 + 8 * c], lhsT=cind[:, c, :], rhs=t0[:, 0:8], start=True, stop=True), reads=[R_cind, R_t0], writes=[R_pz2])
                    A("act", lambda e: e.activation(out=bee[:], in_=pz2[:, 0:8], func=AF.Exp), reads=[R_pz2], writes=[R_bee])
                    A("act", lambda e: e.activation(out=ksf[:], in_=pz2[:, 8:16], func=AF.Exp), reads=[R_pz2], writes=[R_ksf])
                    A("act", lambda e: e.activation(out=dB[:].rearrange("p c h -> p (c h)"), in_=pz2[:, 16:32], func=AF.Exp), reads=[R_pz2], writes=[R_dB])
                    A("dve", lambda e: e.tensor_tensor(out=bee[:], in0=bee[:], in1=bet[:], op=ALU.mult), reads=[R_bee, R_bet], writes=[R_bee])
                    pk, R_pk = pm[0]
                    pkb = pk[:].bitcast(BF16)
                    for h in range(8):
                        A("pe", lambda e, h=h, tok=tok: e.transpose(out=pkb[:, 128 * h:128 * h + 128], in_=kBf[:, h, tok], identity=idb[:]), reads=[R_kBf, R_idb], writes=[R_pk])
                    pk3 = pkb.rearrange("p (h c) -> p h c", c=128)
                    A("dve", lambda e: e.tensor_tensor(out=kbe[:], in0=pk3, in1=bee[:].unsqueeze(2).broadcast_to([128, 8, 128]), op=ALU.mult), reads=[R_pk, R_bee], writes=[R_kbe])
                    A("dve", lambda e: e.tensor_tensor(out=kdc[:], in0=pk3, in1=ksf[:].unsqueeze(2).broadcast_to([128, 8, 128]), op=ALU.mult), reads=[R_pk, R_ksf], writes=[R_kdc])
                    pv_, R_pv_ = pm[1]
                    pvb = pv_[:].bitcast(BF16)
                    for h in range(8):
                        A("pe", lambda e, h=h, tok=tok: e.transpose(out=pvb[:, 128 * h:128 * h + 128], in_=vBf[:, h, tok], identity=idb[:]), reads=[R_vBf, R_idb], writes=[R_pv_])
                    A("dve", lambda e: e.tensor_tensor(out=bvt[:], in0=pvb.rearrange("p (h c) -> p h c", c=128), in1=bet[:].unsqueeze(2).broadcast_to([128, 8, 128]), op=ALU.mult),
                      reads=[R_pv_, R_bet], writes=[R_bvt])
                    A("dve", lambda e, tok=tok: e.tensor_tensor(out=rhsP[:], in0=rf[4][0][:, tok].unsqueeze(1).broadcast_to([16, 8, 128]), in1=gp[:, 8:16].unsqueeze(2).broadcast_to([16, 8, 128]), op=ALU.mult),
                      reads=[rf[4][1], R_gp], writes=[R_rhsP])
                    A("dve", lambda e, tok=tok: e.tensor_tensor(out=rhsA[:], in0=rf[5][0][:, tok].unsqueeze(1).broadcast_to([16, 8, 128]), in1=gp[:, 8:16].unsqueeze(2).broadcast_to([16, 8, 128]), op=ALU.mult),
                      reads=[rf[5][1], R_gp], writes=[R_rhsA])
                    for (rh, R_rh, cap, R_cap, ee, R_ee, b0) in (((rhsA, R_rhsA, capS, R_capS, eA, R_eA, 2), (rhsP, R_rhsP, capI, R_capI, eP, R_eP, 4)) if readout else ((rhsA, R_rhsA, capS, R_capS, eA, R_eA, 2),)):
                        for hq in range(2):
                            pd, R_pd = pm[b0 + hq]
                            for hh in range(4):
                                h = 4 * hq + hh
                                A("pe", lambda e, h=h, hh=hh, pd=pd, rh=rh, tok=tok: e.matmul(pd[:, 128 * hh:128 * hh + 128], lhsT=rf[3][0][:, tok], rhs=rh[:, h, :], start=True, stop=True),
                                  reads=[rf[3][1], R_rh], writes=[R_pd])
                            A("dve", lambda e, hq=hq, pd=pd, cap=cap, ee=ee: e.tensor_tensor(out=ee[:, 4 * hq:4 * hq + 4, :], in0=pd[:].rearrange("p (h t) -> p h t", t=128),
                                                                                           in1=cap[:].unsqueeze(1).broadcast_to([128, 4, 128]), op=ALU.min),
                              reads=[R_pd, R_cap], writes=[R_ee])
                        A("act", lambda e, ee=ee: e.activation(out=ee[:], in_=ee[:], func=AF.Exp), reads=[R_ee], writes=[R_ee])
                    (xt0_, R_xt0), (xx0_, R_xx0), (q0_, R_q0) = XT[0], XX[0], QQ[0]
                    for hq in range(2):
                        pa, R_pa = pm[2 + hq]
                        pb, R_pb = pm[4 + hq]
                        for hh in range(4):
                            h = 4 * hq + hh
                            A("pe", lambda e, h=h, hh=hh, pa=pa, tok=tok: e.matmul(pa[:, 128 * hh:128 * hh + 128], lhsT=kBf[:, h, tok], rhs=kBf[:, h, tok], start=True, stop=True), reads=[R_kBf], writes=[R_pa])
                            if readout:
                                A("pe", lambda e, h=h, hh=hh, pb=pb, tok=tok: e.matmul(pb[:, 128 * hh:128 * hh + 128], lhsT=kBf[:, h, tok], rhs=qBf[:, h, tok], start=True, stop=True), reads=[R_kBf, R_qBf], writes=[R_pb])
                        A("dve", lambda e, hq=hq, pa=pa: e.scalar_tensor_tensor(out=xt0_[:, 4 * hq:4 * hq + 4, :], in0=pa[:].rearrange("p (h t) -> p h t", t=128), scalar=-1.0, in1=eA[:, 4 * hq:4 * hq + 4, :], op0=ALU.mult, op1=ALU.mult),
                          reads=[R_pa, R_eA], writes=[R_xt0])
                        if readout:
                            A("dve", lambda e, hq=hq, pb=pb: e.tensor_tensor(out=PT[:, 4 * hq:4 * hq + 4, :], in0=pb[:].rearrange("p (h t) -> p h t", t=128), in1=eP[:, 4 * hq:4 * hq + 4, :], op=ALU.mult),
                              reads=[R_pb, R_eP], writes=[R_PT])
                    px, R_px = pm[0]
                    pxb = px[:].bitcast(BF16)
                    for h in range(8):
                        A("pe", lambda e, h=h: e.transpose(out=pxb[:, 128 * h:128 * h + 128], in_=xt0_[:, h, :], identity=idb[:]), reads=[R_xt0, R_idb], writes=[R_px])
                    A("act", lambda e: e.activation(out=xx0_[:], in_=pxb.rearrange("p (h c) -> p h c", c=128), func=AF.Copy), reads=[R_px], writes=[R_xx0])
                    A("pool", lambda e: e.tensor_tensor(out=q0_[:], in0=xt0_[:], in1=idb[:].unsqueeze(1).broadcast_to([128, 8, 128]), op=ALU.add), reads=[R_xt0, R_idb], writes=[R_q0])
                    for l in range(6):
                        (xtc, R_xtc), (xxc, R_xxc) = XT[l % 2], XX[l % 2]
                        (xtn, R_xtn), (xxn, R_xxn) = XT[(l + 1) % 2], XX[(l + 1) % 2]
                        (qc, R_qc), (qn, R_qn) = QQ[l % 2], QQ[(l + 1) % 2]
                        for hq in range(2):
                            hs = slice(4 * hq, 4 * hq + 4)
                            if l < 5:
                                p1, R_p1 = pm[2 + hq]
                                p2, R_p2 = pm[4 + hq]
                                for hh in range(4):
                                    h = 4 * hq + hh
                                    A("pe", lambda e, h=h, hh=hh, p1=p1, xxc=xxc, xtc=xtc: e.matmul(p1[:, 128 * hh:128 * hh + 128], lhsT=xxc[:, h, :], rhs=xtc[:, h, :], start=True, stop=True), reads=[R_xxc, R_xtc], writes=[R_p1])
                                    A("pe", lambda e, h=h, hh=hh, p2=p2, xxc=xxc, xtc=xtc: e.matmul(p2[:, 128 * hh:128 * hh + 128], lhsT=xtc[:, h, :], rhs=xxc[:, h, :], start=True, stop=True), reads=[R_xxc, R_xtc], writes=[R_p2])
                            if l >= 1:
                                p3, R_p3 = pm[hq]
                                for hh in range(4):
                                    h = 4 * hq + hh
                                    A("pe", lambda e, h=h, hh=hh, p3=p3, xxc=xxc, qc=qc: e.matmul(p3[:, 128 * hh:128 * hh + 128], lhsT=xxc[:, h, :], rhs=qc[:, h, :], start=True, stop=True), reads=[R_xxc, R_qc], writes=[R_p3])
                                A("dve", lambda e, hs=hs, p3=p3, qc=qc, qn=qn: e.tensor_tensor(out=qn[:, hs, :], in0=p3[:].rearrange("p (h t) -> p h t", t=128), in1=qc[:, hs, :], op=ALU.add),
                                  reads=[R_p3, R_qc], writes=[R_qn])
                            if l < 5:
                                A("act", lambda e, hs=hs, p1=p1, xtn=xtn: e.activation(out=xtn[:, hs, :], in_=p1[:].rearrange("p (h t) -> p h t", t=128), func=AF.Copy), reads=[R_p1], writes=[R_xtn])
                                A("pool" if False else "dve", lambda e, hs=hs, p2=p2, xxn=xxn: e.tensor_copy(out=xxn[:, hs, :], in_=p2[:].rearrange("p (h t) -> p h t", t=128)), reads=[R_p2], writes=[R_xxn])
                        if l == 0:
                            A("pool", lambda e: e.tensor_copy(out=QQ[1][0][:], in_=QQ[0][0][:]), reads=[QQ[0][1]], writes=[QQ[1][1]])
                    Qf, R_Qf = QQ[0]
                    for hq in range(2):
                        p1, R_p1 = pm[2 + hq]
                        p2, R_p2 = pm[4 + hq]
                        for hh in range(4):
                            h = 4 * hq + hh
                            A("pe", lambda e, h=h, hh=hh, p1=p1: e.matmul(p1[:, 128 * hh:128 * hh + 128], lhsT=Qf[:, h, :], rhs=bvt[:, h, :], start=True, stop=True), reads=[R_Qf, R_bvt], writes=[R_p1])
                            A("pe", lambda e, h=h, hh=hh, p2=p2: e.matmul(p2[:, 128 * hh:128 * hh + 128], lhsT=kbe[:, h, :], rhs=Qf[:, h, :], start=True, stop=True), reads=[R_Qf, R_kbe], writes=[R_p2])
                        A("act", lambda e, hq=hq, p1=p1: e.activation(out=u0[:, 4 * hq:4 * hq + 4, :], in_=p1[:].rearrange("p (h t) -> p h t", t=128), func=AF.Copy), reads=[R_p1], writes=[R_u0])
                        A("act", lambda e, hq=hq, p2=p2: e.activation(out=nwT[:, 4 * hq:4 * hq + 4, :], in_=p2[:].rearrange("p (h t) -> p h t", t=128), func=AF.Copy, scale=-1.0), reads=[R_p2], writes=[R_nwT])
                    pob = [pm[6], pm[7]]
                    for c in range(2):
                        rows = slice(64 * c, 64 * c + 64)
                        ctok = slice(128 * ti + 64 * c, 128 * ti + 64 * c + 64)
                        for hq in range(2):
                            p1, R_p1 = pm[hq]
                            for hh in range(4):
                                h = 4 * hq + hh
                                A("pe", lambda e, h=h, hh=hh, p1=p1, rows=rows: e.matmul(p1[rows, 128 * hh:128 * hh + 128], lhsT=nwT[:, h, rows], rhs=SBb[:, h, :], start=True, stop=True), reads=[R_nwT, R_SBb], writes=[R_p1])
                            A("dve", lambda e, hq=hq, p1=p1, rows=rows: e.tensor_tensor(out=vnew[rows, 4 * hq:4 * hq + 4, :], in0=p1[rows, :].rearrange("p (h t) -> p h t", t=128), in1=u0[rows, 4 * hq:4 * hq + 4, :], op=ALU.add),
                              reads=[R_p1, R_u0], writes=[R_vnew])
                        for hq in range(2):
                            if readout:
                                po_, R_po = pob[hq]
                                for hh in range(4):
                                    h = 4 * hq + hh
                                    A("pe", lambda e, h=h, hh=hh, po_=po_, rows=rows, ctok=ctok: e.matmul(po_[rows, 128 * hh:128 * hh + 128], lhsT=qdB[:, h, ctok], rhs=SBb[:, h, :], start=(not pre), stop=False), reads=[R_qdB, R_SBb], writes=[R_po])
                                    A("pe", lambda e, h=h, hh=hh, po_=po_, rows=rows: e.matmul(po_[rows, 128 * hh:128 * hh + 128], lhsT=PT[rows, h, rows], rhs=vnew[rows, h, :], start=False, stop=True), reads=[R_PT, R_vnew], writes=[R_po])
                            p2, R_p2 = pm[2 + hq]
                            for hh in range(4):
                                h = 4 * hq + hh
                                A("pe", lambda e, h=h, hh=hh, p2=p2, rows=rows: e.matmul(p2[:, 128 * hh:128 * hh + 128], lhsT=kdc[rows, h, :], rhs=vnew[rows, h, :], start=True, stop=True), reads=[R_kdc, R_vnew], writes=[R_p2])
                            hs = slice(4 * hq, 4 * hq + 4)
                            A("dve", lambda e, hs=hs, c=c: e.tensor_tensor(out=SB[:, hs, :], in0=SB[:, hs, :], in1=dB[:, c, hs].unsqueeze(2).broadcast_to([128, 4, 128]), op=ALU.mult), reads=[R_SB, R_dB], writes=[R_SB])
                            A("dve", lambda e, hs=hs, p2=p2: e.tensor_tensor(out=SB[:, hs, :], in0=p2[:].rearrange("p (h t) -> p h t", t=128), in1=SB[:, hs, :], op=ALU.add), reads=[R_p2, R_SB], writes=[R_SB])
                        A("act", lambda e: e.activation(out=SBb[:], in_=SB[:], func=AF.Copy), reads=[R_SB], writes=[R_SBb])
                    if readout:
                        for hq in range(2):
                            po_, R_po = pob[hq]
                            A("act", lambda e, hq=hq, po_=po_: e.activation(out=osb[:, D + 512 * hq:D + 512 * hq + 512], in_=po_[:], func=AF.Copy), reads=[R_po], writes=[R_osb])
                if readout and s == 1:
                    for hd in range(16):
                        A("act", lambda e, hd=hd: e.activation(out=junk[:, 0:128], in_=osb[:, 128 * hd:128 * hd + 128], func=AF.Square, accum_out=hst[:, hd:hd + 1]), reads=[R_osb], writes=[R_junk, R_hst])
                    A("act", lambda e: e.activation(out=hst[:, 16:32], in_=hst[:, 0:16], func=AF.Ln, scale=1.0 / 128, bias=EPS), reads=[R_hst], writes=[R_hst])
                    A("act", lambda e: e.activation(out=hst[:, 16:32], in_=hst[:, 16:32], func=AF.Exp, scale=-0.5), reads=[R_hst], writes=[R_hst])
                    szt, R_szt = szs[ti]
                    for q_ in range(4):
                        yb_, R_yb = yb2[q_ // 2]
                        cb0 = 512 * (q_ % 2)
                        A("dve", lambda e, q_=q_, szt=szt: e.tensor_tensor(out=osb[:, 512 * q_:512 * q_ + 512], in0=osb[:, 512 * q_:512 * q_ + 512], in1=szt[:, 512 * q_:512 * q_ + 512], op=ALU.mult), reads=[R_osb, R_szt], writes=[R_osb])
                        A("dve", lambda e, q_=q_, yb_=yb_, cb0=cb0: e.tensor_tensor(out=yb_[:, cb0:cb0 + 512].rearrange("p (h c) -> p h c", c=128), in0=osb[:, 512 * q_:512 * q_ + 512].rearrange("p (h c) -> p h c", c=128),
                                                                                in1=hst[:, 16 + 4 * q_:20 + 4 * q_].unsqueeze(2).broadcast_to([128, 4, 128]), op=ALU.mult), reads=[R_osb, R_hst], writes=[R_yb])
                    for half in range(2):
                        yb_, R_yb = yb2[half]
                        yT_, R_yT = yT2[half]
                        pt, R_pt = pm[half]
                        ptb = pt[:].bitcast(BF16)
                        for kk in range(8):
                            A("pe", lambda e, kk=kk, yb_=yb_, ptb=ptb: e.transpose(out=ptb[:, 128 * kk:128 * kk + 128], in_=yb_[:, 128 * kk:128 * kk + 128], identity=idb[:]), reads=[R_yb, R_idb], writes=[R_pt])
                        A("act", lambda e, yT_=yT_, ptb=ptb: e.activation(out=yT_[:], in_=ptb.rearrange("p (k t) -> p k t", t=128), func=AF.Copy), reads=[R_pt], writes=[R_yT])
                    x_t, R_x = xt[ti % 2]
                    xrow = 64 + gidx * 256 + 128 * ti
                    A("sp", lambda e, xrow=xrow, x_t=x_t: e.dma_start(out=x_t[:], in_=src[xrow:xrow + 128, :]), writes=[R_x], chan="x%d" % (ti % 2))
                    for ch in range(2):
                        pt, R_pt = pm[2 + ch]
                        for kh in range(2):
                            i = wctr[0] % 2
                            wctr[0] += 1
                            wt_, R_wt = wt[i]
                            A("sp", lambda e, kh=kh, ch=ch, wt_=wt_: e.dma_start(out=wt_[:], in_=woutb[1024 * kh:1024 * kh + 1024, 512 * ch:512 * ch + 512].rearrange("(k p) n -> p k n", p=128)),
                              reads=[R_woutb], writes=[R_wt], chan="w%d" % i)
                            yT_, R_yT = yT2[kh]
                            for k in range(8):
                                A("pe", lambda e, k=k, kh=kh, pt=pt, yT_=yT_, wt_=wt_: e.matmul(pt[:], lhsT=yT_[:, k, :], rhs=wt_[:, k, :], start=(kh == 0 and k == 0), stop=(kh == 1 and k == 7)),
                                  reads=[R_yT, R_wt], writes=[R_pt])
                        A("dve", lambda e, ch=ch, pt=pt: e.tensor_tensor(out=hn[:, 512 * ch:512 * ch + 512], in0=pt[:], in1=gate[:, 512 * ch:512 * ch + 512], op=ALU.mult), reads=[R_pt, R_gate], writes=[R_hn])
                    A("dve", lambda e, x_t=x_t: e.tensor_tensor(out=hn[:], in0=hn[:], in1=x_t[:], op=ALU.add), reads=[R_hn, R_x], writes=[R_hn])
                    A("act", lambda e: e.activation(out=junk[:], in_=hn[:], func=AF.Square, accum_out=ssq[:, 0:1]), reads=[R_hn], writes=[R_junk, R_ssq])
                    A("act", lambda e: e.activation(out=ssq[:, 1:2], in_=ssq[:, 0:1], func=AF.Ln, scale=1.0 / D, bias=EPS), reads=[R_ssq], writes=[R_ssq])
                    A("act", lambda e: e.activation(out=ssq[:, 1:2], in_=ssq[:, 1:2], func=AF.Exp, scale=-0.5), reads=[R_ssq], writes=[R_ssq])
                    A("dve", lambda e: e.scalar_tensor_tensor(out=hn[:], in0=hn[:], scalar=ssq[:, 1:2], in1=fnw[:], op0=ALU.mult, op1=ALU.mult), reads=[R_hn, R_ssq, R_fnw], writes=[R_hn])
                    orow = (gidx - ng // 2) * 256 + 128 * ti
                    A("sp", lambda e, orow=orow: e.dma_start(out=out_d[orow:orow + 128, :], in_=hn[:]), reads=[R_hn], chan="out")
                if readout and s == 0:
                    srow = gidx * 256 + 128 * ti
                    A("sp", lambda e, srow=srow: e.dma_start(out=o1_d[srow:srow + 128, :], in_=osb[:]), reads=[R_osb], writes=[R_o1], chan="o1")

        for s in range(2 if do_s2 else 1):
            if s == 1:
                A("dve", lambda e: e.memset(SA[:], 0.0), writes=[R_SA])
                A("dve", lambda e: e.memset(SB[:], 0.0), writes=[R_SB])
                A("pool", lambda e: e.memset(SBb[:], 0.0), writes=[R_SBb])
            group(s, cx[s], 0, 2, 0, 1, False, -1)
            if s == 0:
                for g in range(ng // 2):
                    group(s, xs[s], 256 * g, 3, 64, 0, True, g)
            else:
                for g in range(ng):
                    group(s, xs[s], 256 * g, 3, 64, 0, g >= ng // 2, g)

        P.emit(st, final_waits=["o1"] + (["out"] if do_s2 else []))
    return nc


def make_inputs(x, c, ctx, c_ctx, norm_w, ada_w, ada_b, w_in, conv_w, hg_lb_logits, gdn_a_log,
                gdn_dt_bias, ha_norm_w, hb_norm_w, w_out, final_norm_w, ncores=8):
    f = lambda a: np.ascontiguousarray(np.asarray(a, dtype=np.float32))
    S = x.shape[1]
    Hh = S // 2
    cw = conv_w[0].reshape(9, 3072)
    cwn = cw.T.reshape(24, 128, 9).transpose(1, 0, 2)
    cwf = cw[::-1].T.reshape(24, 128, 9).transpose(1, 0, 2)
    lbl4 = hg_lb_logits.reshape(2, 2, 8, 128).transpose(3, 0, 1, 2)
    hnw = np.concatenate([ha_norm_w[0].reshape(-1), hb_norm_w[0].reshape(-1)])
    z = np.zeros((64, D), np.float32)
    in_maps = []
    for core in range(ncores):
        b, half = core // 2, core % 2
        d1, d2 = (0, 1) if half == 0 else (1, 0)
        xb = x[b]
        if half == 0:
            xs1 = np.concatenate([z, xb[:Hh], xb[Hh:Hh + 64]], axis=0)
            xs2 = np.concatenate([z, xb[::-1], z], axis=0)
            ctx1, ctx2 = ctx[b], ctx[b][::-1]
            cwp = np.stack([cwn, cwf], axis=1)
        else:
            xs1 = np.concatenate([z, xb[Hh:][::-1], xb[Hh - 64:Hh][::-1]], axis=0)
            xs2 = np.concatenate([z, xb, z], axis=0)
            ctx1, ctx2 = ctx[b][::-1], ctx[b]
            cwp = np.stack([cwf, cwn], axis=1)
        w = np.array(w_in[0], dtype=np.float32, copy=True)
        if half == 1:
            w[:, C_AF0:C_AF0 + 1024], w[:, C_AF1:C_AF1 + 1024] = w_in[0][:, C_AF1:C_AF1 + 1024], w_in[0][:, C_AF0:C_AF0 + 1024]
            w[:, C_BA:C_BA + 8], w[:, C_BA + 8:C_BA + 16] = w_in[0][:, C_BA + 8:C_BA + 16], w_in[0][:, C_BA:C_BA + 8]
            w[:, C_BB:C_BB + 8], w[:, C_BB + 8:C_BB + 16] = w_in[0][:, C_BB + 8:C_BB + 16], w_in[0][:, C_BB:C_BB + 8]
        lbl = lbl4[:, :, [d1, d2], :].reshape(128, 32)
        gpar = np.zeros((16, 16), np.float32)
        for si, d in enumerate((d1, d2)):
            gpar[:, 2 * si] = np.tile(gdn_a_log[0, d], 2)
            gpar[:, 2 * si + 1] = np.tile(gdn_dt_bias[0, d], 2)
        gpar[0:8, 4] = -1.0
        gpar[8:16, 5] = 1.0
        gpar[8:16, 6] = 1.0
        gpar[0:8, 7] = 1.0
        for k in range(16):
            gpar[k, 8 + k % 8] = 1.0
        gdr = np.concatenate([gdn_a_log[0, d1], gdn_a_log[0, d2], gdn_dt_bias[0, d1], gdn_dt_bias[0, d2]])
        ccol = np.concatenate([c[b].reshape(8, 128).T, c_ctx.reshape(8, 128).T], axis=1)
        in_maps.append({
            "xs1": f(xs1), "xs2": f(xs2), "ctx1": f(ctx1), "ctx2": f(ctx2),
            "ccol": f(ccol), "norm_w": f(norm_w[0]), "final_norm_w": f(final_norm_w),
            "ada_w": f(ada_w[0]), "ada_b": f(ada_b[0]), "w_in": f(w), "w_out": f(w_out[0]),
            "lbl": f(lbl), "cw": f(cwp), "hnwc": f(hnw.reshape(16, 128).T), "gpar": f(gpar), "gdr": f(gdr),
        })
    return in_maps


def assemble(results, nb):
    outs = []
    for b in range(nb):
        outs.append(np.concatenate([results[2 * b]["out"][::-1], results[2 * b + 1]["out"]], axis=0))
    return np.ascontiguousarray(np.stack(outs, axis=0).astype(np.float32))


_NC_CACHE = {}


def kernel(**inputs):
    inputs = {k: np.asarray(v) for k, v in inputs.items()}
    in_maps = make_inputs(**inputs)
    if "nc" not in _NC_CACHE:
        _NC_CACHE["nc"] = build()
    res = run_bass_kernel_spmd(_NC_CACHE["nc"], in_maps, core_ids=list(range(8)))
    return assemble(res.results, 4)
```

```python
import contextlib
import numpy as np
import concourse.bass as bass
import concourse.mybir as mybir
from concourse.bass_utils import run_bass_kernel_spmd

F32 = mybir.dt.float32
BF16 = mybir.dt.bfloat16
AF = mybir.ActivationFunctionType
ALU = mybir.AluOpType

ENGS = ("pe", "act", "dve", "pool", "sp")

D = 1024
SEQ = 8192
CTXL = 256
NIN = 9248
EPS = 1e-6
C_AQ, C_AF0, C_AF1, C_AI, C_AZ = 0, 1024, 2048, 3072, 4096
C_BQ, C_BK, C_BV, C_BZ, C_BA, C_BB = 5120, 6144, 7168, 8192, 9216, 9232


class Res:
    __slots__ = ("name", "w", "rs")

    def __init__(self, name):
        self.name = name
        self.w = None
        self.rs = {}


class Op:
    __slots__ = ("eng", "fn", "deps", "chan", "chan_idx", "need_sig", "sig", "idx")

    def __init__(self, eng, fn, chan):
        self.eng = eng
        self.fn = fn
        self.chan = chan
        self.chan_idx = 0
        self.deps = ()
        self.need_sig = False
        self.sig = 0
        self.idx = 0


class Prog:
    def __init__(self, nc):
        self.nc = nc
        self.eng_ops = {e: [] for e in ENGS}
        self.chan_cnt = {}
        self.n = 0

    def op(self, eng, fn, reads=(), writes=(), chan=None):
        o = Op(eng, fn, chan)
        o.idx = self.n
        self.n += 1
        if chan is not None:
            c = self.chan_cnt.get(chan, 0) + 1
            self.chan_cnt[chan] = c
            o.chan_idx = c
        deps = {}
        for r in reads:
            if r.w is not None:
                deps[id(r.w)] = r.w
        for w in writes:
            if w.w is not None:
                deps[id(w.w)] = w.w
            for x in w.rs.values():
                deps[id(x)] = x
        o.deps = list(deps.values())
        for r in reads:
            key = eng if chan is None else ("dma", o.idx)
            r.rs[key] = o
        for w in writes:
            w.w = o
            w.rs = {}
        self.eng_ops[eng].append(o)
        return o

    def emit(self, st, final_waits=()):
        nc = self.nc
        esem = {e: st.enter_context(nc.semaphore("s_" + e)) for e in ENGS}
        csem = {c: st.enter_context(nc.semaphore("c_" + str(c))) for c in self.chan_cnt}
        for e in ENGS:
            for o in self.eng_ops[e]:
                for d in o.deps:
                    if d.chan is None:
                        if d.eng == "pe" and o.eng == "pe":
                            continue
                        d.need_sig = True
        for e in ENGS:
            c = 0
            for o in self.eng_ops[e]:
                if o.chan is None and o.need_sig:
                    c += 1
                    o.sig = c
        block = st.enter_context(nc.Block())

        def run(e, h):
            known = {}
            for o in self.eng_ops[e]:
                for d in o.deps:
                    if d.chan is not None:
                        key, val, sem = ("c", d.chan), 16 * d.chan_idx, csem[d.chan]
                    else:
                        if d.eng == "pe" and e == "pe":
                            continue
                        key, val, sem = ("e", d.eng), d.sig, esem[d.eng]
                    if known.get(key, 0) >= val:
                        continue
                    known[key] = val
                    h.wait_ge(sem, val)
                ins = o.fn(h)
                if o.chan is not None:
                    ins.then_inc(csem[o.chan], 16)
                elif o.need_sig:
                    ins.then_inc(esem[e], 1)
            if e == "sp":
                for ch in final_waits:
                    h.wait_ge(csem[ch], 16 * self.chan_cnt[ch])

        @block.tensor
        def _(h):
            run("pe", h)

        @block.scalar
        def _(h):
            run("act", h)

        @block.vector
        def _(h):
            run("dve", h)

        @block.gpsimd
        def _(h):
            run("pool", h)

        @block.sync
        def _(h):
            run("sp", h)


def build(ng=32, debug=False, do_gdn=True, do_s2=True):
    nc = bass.Bass("TRN2", target_bir_lowering=False)
    NT = ng * 256
    dram = lambda n, s, dt=F32, kind="ExternalInput": nc.dram_tensor(n, s, dt, kind=kind).ap()
    NTH = NT // 2
    xs = [dram("xs1", [NTH + 128, D]), dram("xs2", [NT + 128, D])]
    cx = [dram("ctx1", [CTXL, D]), dram("ctx2", [CTXL, D])]
    ccol_d = dram("ccol", [128, 16])
    normw_d = dram("norm_w", [D])
    fnw_d = dram("final_norm_w", [D])
    adaw_d = dram("ada_w", [D, 3 * D])
    adab_d = dram("ada_b", [3 * D])
    win_d = dram("w_in", [D, NIN])
    wout_d = dram("w_out", [2 * D, D])
    lbl_d = dram("lbl", [128, 32])
    cw_d = dram("cw", [128, 2, 24, 9])
    hnw_d = dram("hnwc", [128, 16])
    gpar_d = dram("gpar", [16, 16])
    gdr_d = dram("gdr", [32])
    out_d = dram("out", [NTH, D], kind="ExternalOutput")
    winb = dram("winb", [D, NIN], BF16, kind="Internal")
    woutb = dram("woutb", [2 * D, D], BF16, kind="Internal")
    o1_d = dram("o1", [NTH, 2 * D], F32, kind="ExternalOutput" if debug else "Internal")
    dbg_d = dram("dbg", [128, 4096], F32, kind="ExternalOutput") if debug else None

    P = Prog(nc)
    R_o1 = Res("o1")
    st = contextlib.ExitStack()
    with st:
        def sb(name, shape, dt=F32):
            return st.enter_context(nc.sbuf_tensor("sb_" + name, shape, dt)), Res(name)

        def ps(name, shape, dt=F32):
            return st.enter_context(nc.psum_tensor("ps_" + name, shape, dt)), Res(name)

        A = P.op

        idf, R_idf = sb("idf", [128, 128])
        idb, R_idb = sb("idb", [128, 128], BF16)
        jf, R_jf = sb("jf", [128, 128])
        m01, R_m01 = sb("m01", [128, 128])
        rmask, R_rmask = sb("rmask", [128, 256])
        A("pool", lambda e: e.memset(idf[:], 1.0), writes=[R_idf])
        A("pool", lambda e: e.affine_select(out=idf[:], in_=idf[:], pattern=[[-1, 128]], compare_op=ALU.is_equal,
                                            fill=0.0, base=0, channel_multiplier=1), reads=[R_idf], writes=[R_idf])
        A("dve", lambda e: e.tensor_copy(out=idb[:], in_=idf[:]), reads=[R_idf], writes=[R_idb])
        A("pool", lambda e: e.memset(jf[:], 1.0), writes=[R_jf])
        A("pool", lambda e: e.affine_select(out=jf[:], in_=jf[:], pattern=[[1, 128]], compare_op=ALU.is_equal,
                                            fill=0.0, base=-127, channel_multiplier=1), reads=[R_jf], writes=[R_jf])
        A("pool", lambda e: e.memset(m01[:], 1.0), writes=[R_m01])
        A("pool", lambda e: e.affine_select(out=m01[:], in_=m01[:], pattern=[[1, 128]], compare_op=ALU.is_ge,
                                            fill=0.0, base=0, channel_multiplier=-1), reads=[R_m01], writes=[R_m01])
        A("pool", lambda e: e.memset(m01[0:64, 64:128], 0.0), reads=[R_m01], writes=[R_m01])
        A("pool", lambda e: e.memset(rmask[:], 1.0), writes=[R_rmask])
        A("pool", lambda e: e.memset(rmask[:].rearrange("p (c t) -> p c t", t=64)[:, :, 0:1], 0.0),
          reads=[R_rmask], writes=[R_rmask])

        R_winb, R_woutb = Res("winb"), Res("woutb")
        for i in range(8):
            A("pool", lambda e, i=i: e.dma_start(out=winb[128 * i:128 * i + 128, :], in_=win_d[128 * i:128 * i + 128, :]),
              writes=[R_winb], chan="wcv")

        xt = [sb("xt%d" % i, [128, D]) for i in range(2)]
        qA, R_qA = sb("qA", [128, 8, 256])
        osb, R_osb = sb("osb", [128, 2 * D])
        ccol, R_ccol = sb("ccol", [128, 16])
        scb, R_scb = qA[:].rearrange("p a b -> p (a b)").rearrange("p (j t) -> p j t", t=128), R_qA
        adaw, R_adaw = xt[0][0][:].rearrange("p (k n) -> p k n", n=128), xt[0][1]
        nwb, R_nwb = osb[:, 0:D], R_osb
        adab, R_adab = osb[:, D:D + 128], R_osb
        Amod, R_Amod = sb("Amod", [128, 2, D])
        shmod, R_shmod = sb("shmod", [128, 2, D])
        gate, R_gate = sb("gate", [128, D])
        A("sp", lambda e: e.dma_start(out=ccol[:], in_=ccol_d), writes=[R_ccol], chan="p0")
        A("sp", lambda e: e.dma_start(out=nwb, in_=normw_d.partition_broadcast(128)), writes=[R_nwb], chan="p2")
        A("act", lambda e: e.activation(out=ccol[:], in_=ccol[:], func=AF.Silu), reads=[R_ccol], writes=[R_ccol])
        A("dve", lambda e: e.tensor_copy(out=scb, in_=ccol[:].unsqueeze(2).broadcast_to([128, 16, 128])),
          reads=[R_ccol], writes=[R_scb])
        pm = [ps("pm%d" % i, [128, 512]) for i in range(8)]
        for blk in range(24):
            A("sp", lambda e, blk=blk: e.dma_start(out=adaw, in_=adaw_d[:, 128 * blk:128 * blk + 128].rearrange("(k p) n -> p k n", p=128)),
              writes=[R_adaw], chan="adaw")
            A("sp", lambda e, blk=blk: e.dma_start(out=adab, in_=adab_d[128 * blk:128 * blk + 128].partition_broadcast(128)), writes=[R_adab], chan="p1")
            which, off = blk // 8, (blk % 8) * 128
            for j in range(2):
                pt, R_pt = pm[j]
                for k in range(8):
                    A("pe", lambda e, j=j, k=k, pt=pt: e.matmul(pt[:, 0:128], lhsT=scb[:, j * 8 + k, :], rhs=adaw[:, k, :], start=(k == 0), stop=(k == 7)),
                      reads=[R_scb, R_adaw], writes=[R_pt])
                if which == 0:
                    A("dve", lambda e, j=j, pt=pt, off=off: e.tensor_tensor(out=shmod[:, j, off:off + 128], in0=pt[:, 0:128], in1=adab, op=ALU.add),
                      reads=[R_pt, R_adab], writes=[R_shmod])
                elif which == 1:
                    A("dve", lambda e, j=j, pt=pt, off=off: e.scalar_tensor_tensor(out=Amod[:, j, off:off + 128], in0=pt[:, 0:128], scalar=1.0, in1=adab, op0=ALU.add, op1=ALU.add),
                      reads=[R_pt, R_adab], writes=[R_Amod])
                    A("dve", lambda e, j=j, off=off: e.tensor_tensor(out=Amod[:, j, off:off + 128], in0=Amod[:, j, off:off + 128], in1=nwb[:, off:off + 128], op=ALU.mult),
                      reads=[R_Amod, R_nwb], writes=[R_Amod])
                elif j == 0:
                    A("dve", lambda e, pt=pt, off=off: e.tensor_tensor(out=gate[:, off:off + 128], in0=pt[:, 0:128], in1=adab, op=ALU.add),
                      reads=[R_pt, R_adab], writes=[R_gate])

        lbl, R_lbl = sb("lbl", [128, 32])
        lb, R_lb = sb("lb", [128, 16])
        oml, R_oml = sb("oml", [128, 16])
        lbm1, R_lbm1 = sb("lbm1", [128, 16])
        A("sp", lambda e: e.dma_start(out=lbl[:], in_=lbl_d), writes=[R_lbl], chan="p3")
        A("dve", lambda e: e.tensor_tensor(out=lb[:], in0=lbl[:, 0:16], in1=lbl[:, 16:32], op=ALU.subtract), reads=[R_lbl], writes=[R_lb])
        A("act", lambda e: e.activation(out=lb[:], in_=lb[:], func=AF.Sigmoid), reads=[R_lb], writes=[R_lb])
        A("dve", lambda e: e.tensor_scalar(out=oml[:], in0=lb[:], scalar1=-1.0, scalar2=1.0, op0=ALU.mult, op1=ALU.add), reads=[R_lb], writes=[R_oml])
        A("dve", lambda e: e.tensor_scalar(out=lbm1[:], in0=lb[:], scalar1=-1.0, scalar2=None, op0=ALU.add), reads=[R_lb], writes=[R_lbm1])

        junk, R_junk = sb("junk", [128, D], BF16)
        ssq, R_ssq = sb("ssq", [128, 2])
        hn, R_hn = sb("hn", [128, D])
        hb, R_hb = sb("hb", [128, D], BF16)
        hT, R_hT = sb("hT", [128, 8, 384], BF16)
        wt = [sb("wt%d" % i, [128, 8, 512], BF16) for i in range(2)]
        wctr = [0]
        tf = [sb("tf%d" % i, [128, 256]) for i in range(4)]
        qt, R_qt = sb("qt", [128, 8, 256], BF16)
        kdT, R_kdT = sb("kdT", [128, 8, 256], BF16)
        dA, R_dA = sb("dA", [128, 8, 4])
        kdm, R_kdm = sb("kdm", [128, 2, D], BF16)
        vA, R_vA = sb("vA", [128, 2, D], BF16)
        scT, R_scT = sb("scT", [128, 8, 128], BF16)
        SA, R_SA = sb("SA", [128, 8, 128])
        SAd, R_SAd = sb("SAd", [128, 8, 128])
        SAb, R_SAb = sb("SAb", [128, 8, 128], BF16)
        A("dve", lambda e: e.memset(SA[:], 0.0), writes=[R_SA])
        cws, R_cws = sb("cws", [128, 2, 24, 9])
        gp, R_gp = sb("gp", [16, 16])
        nA16, R_nA16 = sb("nA16", [16, 2])
        gdtr, R_gdtr = sb("gdtr", [128, 4, 8])
        nAr, R_nAr = sb("nAr", [128, 2, 8])
        onesb, R_onesb = sb("onesb", [128, 128], BF16)
        capI, R_capI = sb("capI", [128, 128])
        capS, R_capS = sb("capS", [128, 128])
        mU, R_mU = sb("mU", [128, 128])
        cind, R_cind = sb("cind", [128, 2, 128])
        Esel, R_Esel = sb("Esel", [16, 8, 128])
        A("sp", lambda e: e.dma_start(out=cws[:], in_=cw_d), writes=[R_cws], chan="p4")
        A("sp", lambda e: e.dma_start(out=gp[:], in_=gpar_d), writes=[R_gp], chan="p5")
        A("sp", lambda e: e.dma_start(out=gdtr[:].rearrange("p a h -> p (a h)"), in_=gdr_d.partition_broadcast(128)), writes=[R_gdtr], chan="p6")
        A("act", lambda e: e.activation(out=nA16[:, 0:1], in_=gp[:, 0:1], func=AF.Exp), reads=[R_gp], writes=[R_nA16])
        A("act", lambda e: e.activation(out=nA16[:, 1:2], in_=gp[:, 2:3], func=AF.Exp), reads=[R_gp], writes=[R_nA16])
        A("dve", lambda e: e.tensor_scalar(out=nA16[:], in0=nA16[:], scalar1=-1.0, scalar2=None, op0=ALU.mult), reads=[R_nA16], writes=[R_nA16])
        A("act", lambda e: e.activation(out=nAr[:], in_=gdtr[:, 0:2, :], func=AF.Exp), reads=[R_gdtr], writes=[R_nAr])
        A("dve", lambda e: e.tensor_scalar(out=nAr[:], in0=nAr[:], scalar1=-1.0, scalar2=None, op0=ALU.mult), reads=[R_nAr], writes=[R_nAr])
        A("pool", lambda e: e.memset(onesb[:], 1.0), writes=[R_onesb])
        A("dve", lambda e: e.tensor_scalar(out=capI[:], in0=m01[:], scalar1=-1.0, scalar2=30000.0, op0=ALU.add, op1=ALU.mult), reads=[R_m01], writes=[R_capI])
        A("dve", lambda e: e.tensor_tensor(out=capS[:], in0=m01[:], in1=idf[:], op=ALU.subtract), reads=[R_m01, R_idf], writes=[R_capS])
        A("dve", lambda e: e.tensor_scalar(out=capS[:], in0=capS[:], scalar1=-1.0, scalar2=30000.0, op0=ALU.add, op1=ALU.mult), reads=[R_capS], writes=[R_capS])
        A("pool", lambda e: e.memset(mU[:], 0.0), writes=[R_mU])
        A("pool", lambda e: e.memset(mU[0:64, 0:64], 1.0), reads=[R_mU], writes=[R_mU])
        A("pool", lambda e: e.memset(mU[64:128, 64:128], 1.0), reads=[R_mU], writes=[R_mU])
        A("dve", lambda e: e.tensor_tensor(out=mU[:], in0=mU[:], in1=m01[:], op=ALU.subtract), reads=[R_mU, R_m01], writes=[R_mU])
        A("pool", lambda e: e.memset(cind[:], 0.0), writes=[R_cind])
        A("pool", lambda e: e.memset(cind[0:64, 0, :], 1.0), reads=[R_cind], writes=[R_cind])
        A("pool", lambda e: e.memset(cind[64:128, 1, :], 1.0), reads=[R_cind], writes=[R_cind])
        A("dve", lambda e: e.tensor_scalar(out=Esel[:], in0=gp[:, 8:16].unsqueeze(2).broadcast_to([16, 8, 128]), scalar1=gp[:, 6:7], scalar2=None, op0=ALU.mult),
          reads=[R_gp], writes=[R_Esel])
        wab, R_wab = sb("wab", [128, 8, 32], BF16)
        dw, R_dw = sb("dw", [128, 9, 128], BF16)
        cpad, R_cpad = sb("cpad", [128, 400], BF16)
        cs, R_cs = sb("cs", [128, 256])
        sq, R_sq = sb("sq", [128, 256], BF16)
        rn, R_rn = sb("rn", [128, 256])
        qBf, R_qBf = sb("qBf", [128, 8, 256], BF16)
        kBf, R_kBf = sb("kBf", [128, 8, 256], BF16)
        vBf, R_vBf = sb("vBf", [128, 8, 256], BF16)
        qdB, R_qdB = sb("qdB", [128, 8, 256], BF16)
        rf = [sb("rf%d" % i, [16, 256]) for i in range(6)]
        rhsP, R_rhsP = sb("rhsP", [16, 8, 128])
        rhsA, R_rhsA = sb("rhsA", [16, 8, 128])
        tms = [sb("tms%d" % i, [128, 16]) for i in range(4)]
        bet, R_bet = sb("bet", [128, 8])
        bee, R_bee = sb("bee", [128, 8])
        ksf, R_ksf = sb("ksf", [128, 8])
        dB, R_dB = sb("dB", [128, 2, 8])
        kbe, R_kbe = sb("kbe", [128, 8, 128], BF16)
        kdc, R_kdc = sb("kdc", [128, 8, 128], BF16)
        bvt, R_bvt = sb("bvt", [128, 8, 128], BF16)
        eA, R_eA = sb("eA", [128, 8, 128])
        eP, R_eP = sb("eP", [128, 8, 128])
        XT = [sb("XT%d" % i, [128, 8, 128], BF16) for i in range(2)]
        XX = [sb("XX%d" % i, [128, 8, 128], BF16) for i in range(2)]
        QQ = [sb("QQ%d" % i, [128, 8, 128], BF16) for i in range(2)]
        PT, R_PT = sb("PT", [128, 8, 128], BF16)
        u0, R_u0 = sb("u0", [128, 8, 128])
        nwT, R_nwT = sb("nwT", [128, 8, 128], BF16)
        vnew, R_vnew = sb("vnew", [128, 8, 128], BF16)
        SB, R_SB = sb("SB", [128, 8, 128])
        SBb, R_SBb = sb("SBb", [128, 8, 128], BF16)
        A("dve", lambda e: e.memset(SB[:], 0.0), writes=[R_SB])
        A("pool", lambda e: e.memset(SBb[:], 0.0), writes=[R_SBb])
        fnw, R_fnw = sb("fnw", [128, D])
        hnwc, R_hnwc = sb("hnwc", [128, 16])
        hst, R_hst = sb("hst", [128, 32])
        A("sp", lambda e: e.dma_start(out=fnw[:], in_=fnw_d.partition_broadcast(128)), writes=[R_fnw], chan="p7")
        A("sp", lambda e: e.dma_start(out=hnwc[:], in_=hnw_d), writes=[R_hnwc], chan="p8")
        for kk in range(16):
            x_t, R_x = xt[1]
            A("sp", lambda e, kk=kk: e.dma_start(out=x_t[:], in_=wout_d[128 * kk:128 * kk + 128, :]), writes=[R_x], chan="x1")
            A("dve", lambda e, kk=kk: e.tensor_scalar(out=hb[:], in0=x_t[:], scalar1=hnwc[:, kk:kk + 1], scalar2=None, op0=ALU.mult), reads=[R_x, R_hnwc], writes=[R_hb])
            A("sp", lambda e, kk=kk: e.dma_start(out=woutb[128 * kk:128 * kk + 128, :], in_=hb[:]), reads=[R_hb], writes=[R_woutb], chan="wo_st")
        szs = [(Amod[:, 1, :].bitcast(BF16), R_Amod), (shmod[:, 1, :].bitcast(BF16), R_shmod)]
        yT2 = [XT[1], XX[1]]
        yb2 = [(kbe[:].rearrange("p h c -> p (h c)"), R_kbe), (kdc[:].rearrange("p h c -> p (h c)"), R_kdc)]
        dw2, R_dw2 = sb("dw2", [128, 9, 128], BF16)
        cpad2, R_cpad2 = sb("cpad2", [128, 400], BF16)
        CP = [(cpad, R_cpad), (cpad2, R_cpad2)]
        DW = [(dw, R_dw), (dw2, R_dw2)]
        print("sbuf bytes remaining", nc.sbuf_bytes_remaining)


        def load_w(c0, ncol):
            i = wctr[0] % 2
            wctr[0] += 1
            t, R = wt[i]
            A("sp", lambda e: e.dma_start(out=t[:, :, 0:ncol], in_=winb[:, c0:c0 + ncol].rearrange("(k p) n -> p k n", p=128)),
              reads=[R_winb], writes=[R], chan="w%d" % i)
            return t, R

        def xprep(src, row0, i, col0, j):
            x_t, R_x = xt[i % 2]
            A("sp", lambda e: e.dma_start(out=x_t[:], in_=src[row0:row0 + 128, :]), writes=[R_x], chan="x%d" % (i % 2))
            A("act", lambda e: e.activation(out=junk[:], in_=x_t[:], func=AF.Square, accum_out=ssq[:, 0:1]),
              reads=[R_x], writes=[R_junk, R_ssq])
            A("act", lambda e: e.activation(out=ssq[:, 1:2], in_=ssq[:, 0:1], func=AF.Ln, scale=1.0 / D, bias=EPS), reads=[R_ssq], writes=[R_ssq])
            A("act", lambda e: e.activation(out=ssq[:, 1:2], in_=ssq[:, 1:2], func=AF.Exp, scale=-0.5), reads=[R_ssq], writes=[R_ssq])
            A("dve", lambda e: e.scalar_tensor_tensor(out=hn[:], in0=x_t[:], scalar=ssq[:, 1:2], in1=Amod[:, j, :], op0=ALU.mult, op1=ALU.mult),
              reads=[R_x, R_ssq, R_Amod], writes=[R_hn])
            A("pool", lambda e: e.tensor_tensor(out=hb[:], in0=hn[:], in1=shmod[:, j, :], op=ALU.add), reads=[R_hn, R_shmod], writes=[R_hb])
            pt, R_pt = pm[7]
            ptb = pt[:].bitcast(BF16)
            for k in range(8):
                A("pe", lambda e, k=k: e.transpose(out=ptb[:, 128 * k:128 * k + 128], in_=hb[:, 128 * k:128 * k + 128], identity=idb[:]),
                  reads=[R_hb, R_idb], writes=[R_pt])
            A("act", lambda e: e.activation(out=hT[:, :, col0:col0 + 128], in_=ptb.rearrange("p (k t) -> p k t", t=128), func=AF.Copy),
              reads=[R_pt], writes=[R_hT])

        def group(s, src, row0, ntile_w, off, j, readout, gidx):
            T = 256
            for i in range(ntile_w):
                if ntile_w == 3 and gidx >= 1 and i == 0:
                    A("pool", lambda e: e.tensor_copy(out=hT[:, :, 0:128], in_=hT[:, :, 256:384]), reads=[R_hT], writes=[R_hT])
                    continue
                xprep(src, row0 + 128 * i, i, 128 * i, j)
            main = slice(off, off + T)
            for half in (range(2) if readout else ()):
                w, R_w = load_w(C_AQ + 512 * half, 512)
                for hh in range(4):
                    h = 4 * half + hh
                    pt, R_pt = pm[hh % 2]
                    for k in range(8):
                        A("pe", lambda e, k=k, hh=hh, pt=pt, w=w: e.matmul(pt[:, 0:T], lhsT=w[:, k, 128 * hh:128 * hh + 128], rhs=hT[:, k, main], start=(k == 0), stop=(k == 7)),
                          reads=[R_w, R_hT], writes=[R_pt])
                    A("act", lambda e, h=h, pt=pt: e.activation(out=qA[:, h, :], in_=pt[:, 0:T], func=AF.Silu), reads=[R_pt], writes=[R_qA])
            caf = C_AF0 if s == 0 else C_AF1
            for half in range(2):
                w, R_w = load_w(caf + 512 * half, 512)
                for hh in range(4):
                    h = 4 * half + hh
                    li = s * 8 + h
                    pt, R_pt = pm[2 + hh % 2]
                    (sig, R_sig), (g, R_g), (b, R_b), (dl, R_dl) = tf
                    for k in range(8):
                        A("pe", lambda e, k=k, hh=hh, pt=pt, w=w: e.matmul(pt[:, 0:T], lhsT=w[:, k, 128 * hh:128 * hh + 128], rhs=hT[:, k, main], start=(k == 0), stop=(k == 7)),
                          reads=[R_w, R_hT], writes=[R_pt])
                    A("act", lambda e, pt=pt: e.activation(out=sig[:], in_=pt[:, 0:T], func=AF.Sigmoid), reads=[R_pt], writes=[R_sig])
                    A("act", lambda e, li=li: e.activation(out=g[:], in_=sig[:], func=AF.Ln, scale=oml[:, li:li + 1], bias=lb[:, li:li + 1]),
                      reads=[R_sig, R_oml, R_lb], writes=[R_g])
                    A("dve", lambda e, li=li: e.tensor_scalar(out=sig[:], in0=sig[:], scalar1=-1.0, scalar2=lbm1[:, li:li + 1], op0=ALU.add, op1=ALU.mult),
                      reads=[R_sig, R_lbm1], writes=[R_sig])
                    A("dve", lambda e: e.tensor_tensor_scan(out=b[:], data0=rmask[:, 0:T], data1=g[:], initial=0.0, op0=ALU.mult, op1=ALU.add),
                      reads=[R_rmask, R_g], writes=[R_b])
                    b3 = b[:].rearrange("p (c t) -> p c t", t=64)
                    A("dve", lambda e, b3=b3: e.tensor_tensor(out=dl[:].rearrange("p (c t) -> p c t", t=64), in0=b3, in1=b3[:, :, 63:64].broadcast_to([128, 4, 64]), op=ALU.subtract),
                      reads=[R_b], writes=[R_dl])
                    A("act", lambda e, h=h, b3=b3: e.activation(out=dA[:, h, :], in_=b3[:, :, 63], func=AF.Exp), reads=[R_b], writes=[R_dA])
                    if readout:
                        A("act", lambda e: e.activation(out=g[:], in_=dl[:], func=AF.Exp), reads=[R_dl], writes=[R_g])
                    A("act", lambda e: e.activation(out=b[:], in_=dl[:], func=AF.Exp, scale=-1.0), reads=[R_dl], writes=[R_b])
                    if readout:
                        A("dve", lambda e, h=h: e.scalar_tensor_tensor(out=qt[:, h, :], in0=qA[:, h, :], scalar=float(128 ** -0.5), in1=g[:], op0=ALU.mult, op1=ALU.mult),
                          reads=[R_qA, R_g], writes=[R_qt])
                    A("pool", lambda e, h=h: e.tensor_tensor(out=kdT[:, h, :], in0=sig[:], in1=b[:], op=ALU.mult),
                      reads=[R_sig, R_b], writes=[R_kdT])
            for half in range(2):
                w, R_w = load_w(C_AI + 512 * half, 512)
                for ti in range(2):
                    pt, R_pt = pm[4 + ti]
                    for k in range(8):
                        A("pe", lambda e, k=k, ti=ti, pt=pt, w=w: e.matmul(pt[:], lhsT=hT[:, k, off + 128 * ti:off + 128 * ti + 128], rhs=w[:, k, :], start=(k == 0), stop=(k == 7)),
                          reads=[R_w, R_hT], writes=[R_pt])
                    A("act", lambda e, ti=ti, pt=pt, half=half: e.activation(out=vA[:, ti, 512 * half:512 * half + 512], in_=pt[:], func=AF.Copy),
                      reads=[R_pt], writes=[R_vA])
            for ti in range(2):
                pt, R_pt = pm[6]
                ptb = pt[:].bitcast(BF16)
                for h in range(8):
                    A("pe", lambda e, h=h, ti=ti, ptb=ptb: e.transpose(out=ptb[:, 128 * h:128 * h + 128], in_=kdT[:, h, 128 * ti:128 * ti + 128], identity=idb[:]),
                      reads=[R_kdT, R_idb], writes=[R_pt])
                A("dve", lambda e, ti=ti, ptb=ptb: e.tensor_copy(out=kdm[:, ti, :], in_=ptb), reads=[R_pt], writes=[R_kdm])
            if do_gdn:
                lat = (ntile_w == 3)
                WN = 128 * ntile_w
                if gidx <= 0:
                    A("pool", lambda e: e.memset(cpad[:], 0.0), writes=[R_cpad])
                    A("pool", lambda e: e.memset(cpad2[:], 0.0), writes=[R_cpad2])
                if lat:
                    cpv = cpad[:, 0:396].rearrange("p (r c) -> p r c", c=66)
                def stageA(blk, w, R_w, hh):
                    cpd, Rcp = CP[blk % 2]
                    cpv_ = cpd[:, 0:396].rearrange("p (r c) -> p r c", c=66)
                    pt, R_pt = pm[blk % 2]
                    for k in range(8):
                        A("pe", lambda e, k=k, hh=hh, pt=pt, w=w: e.matmul(pt[:, 0:WN], lhsT=w[:, k, 128 * hh:128 * hh + 128], rhs=hT[:, k, 0:WN], start=(k == 0), stop=(k == 7)),
                          reads=[R_w, R_hT], writes=[R_pt])
                    if lat:
                        A("act", lambda e, pt=pt, cpv_=cpv_: e.activation(out=cpv_[:, :, 1:65], in_=pt[:, 0:384].rearrange("p (r c) -> p r c", c=64), func=AF.Copy),
                          reads=[R_pt], writes=[Rcp])
                        if gidx == 0:
                            A("pool", lambda e, cpv_=cpv_: e.memset(cpv_[:, 0, :], 0.0), reads=[Rcp], writes=[Rcp])
                        if s == 1 and gidx == ng - 1:
                            A("pool", lambda e, cpv_=cpv_: e.memset(cpv_[:, 5, :], 0.0), reads=[Rcp], writes=[Rcp])
                    else:
                        A("act", lambda e, pt=pt, cpd=cpd: e.activation(out=cpd[:, 1:257], in_=pt[:, 0:256], func=AF.Copy), reads=[R_pt], writes=[Rcp])
                    dwb, Rdw = DW[blk % 2]
                    A("pool", lambda e, blk=blk, dwb=dwb: e.tensor_tensor(out=dwb[:], in0=idb[:].unsqueeze(1).broadcast_to([128, 9, 128]),
                                                                          in1=cws[:, s, blk, :].unsqueeze(2).broadcast_to([128, 9, 128]), op=ALU.mult),
                      reads=[R_idb, R_cws], writes=[Rdw])
                    pc, R_pc = pm[2 + blk % 2]
                    taps = [(i, jj) for i in range(3) for jj in range(3)] if lat else [(1, jj) for jj in range(3)]
                    for n, (i, jj) in enumerate(taps):
                        if lat:
                            A("pe", lambda e, i=i, jj=jj, n=n, pc=pc, nt=len(taps), dwb=dwb, cpv_=cpv_: e.matmul(pc[:, 0:256].rearrange("p (r c) -> p r c", c=64), lhsT=dwb[:, 3 * i + jj, :], rhs=cpv_[:, i:i + 4, jj:jj + 64], start=(n == 0), stop=(n == nt - 1)),
                              reads=[Rdw, Rcp], writes=[R_pc])
                        else:
                            A("pe", lambda e, i=i, jj=jj, n=n, pc=pc, nt=len(taps), dwb=dwb, cpd=cpd: e.matmul(pc[:, 0:256], lhsT=dwb[:, 3 * i + jj, :], rhs=cpd[:, jj:jj + 256], start=(n == 0), stop=(n == nt - 1)),
                              reads=[Rdw, Rcp], writes=[R_pc])

                def stageB(blk):
                    kind, h = blk // 8, blk % 8
                    pc, R_pc = pm[2 + blk % 2]
                    if kind == 2:
                        A("act", lambda e, h=h, pc=pc: e.activation(out=vBf[:, h, :], in_=pc[:, 0:256], func=AF.Silu), reads=[R_pc], writes=[R_vBf])
                    else:
                        A("act", lambda e, pc=pc: e.activation(out=cs[:], in_=pc[:, 0:256], func=AF.Silu), reads=[R_pc], writes=[R_cs])
                        A("act", lambda e: e.activation(out=sq[:], in_=cs[:], func=AF.Square), reads=[R_cs], writes=[R_sq])
                        pn, R_pn = pm[4 + blk % 2]
                        A("pe", lambda e, pn=pn: e.matmul(pn[:, 0:256], lhsT=onesb[:], rhs=sq[:], start=True, stop=True), reads=[R_onesb, R_sq], writes=[R_pn])
                        A("act", lambda e, pn=pn: e.activation(out=rn[:], in_=pn[:, 0:256], func=AF.Ln, bias=EPS), reads=[R_pn], writes=[R_rn])
                        A("act", lambda e: e.activation(out=rn[:], in_=rn[:], func=AF.Exp, scale=-0.5), reads=[R_rn], writes=[R_rn])
                        dst, R_dst = (qBf, R_qBf) if kind == 0 else (kBf, R_kBf)
                        sc_ = float(128 ** -0.5) if kind == 0 else 1.0
                        A("dve", lambda e, h=h, dst=dst, sc_=sc_: e.scalar_tensor_tensor(out=dst[:, h, :], in0=cs[:], scalar=sc_, in1=rn[:], op0=ALU.mult, op1=ALU.mult),
                          reads=[R_cs, R_rn], writes=[R_dst])

                pend = None
                for m in range(6):
                    if m < 2 and not readout:
                        continue
                    w, R_w = load_w(C_BQ + 512 * m, 512)
                    for hh in range(4):
                        blk = 4 * m + hh
                        stageA(blk, w, R_w, hh)
                        if pend is not None:
                            stageB(pend)
                        pend = blk
                if pend is not None:
                    stageB(pend)
                if gidx <= 0:
                    ca_, cb_ = C_BA + 8 * s, C_BB + 8 * s
                    for q_, c_ in enumerate((ca_, ca_, cb_, cb_)):
                        A("sp", lambda e, q_=q_, c_=c_: e.dma_start(out=wab[:, :, 8 * q_:8 * q_ + 8], in_=winb[:, c_:c_ + 8].rearrange("(k p) n -> p k n", p=128)),
                          reads=[R_winb], writes=[R_wab], chan="wab")
                pz, R_pz = pm[6]
                for k in range(8):
                    A("pe", lambda e, k=k: e.matmul(pz[0:16, 0:256], lhsT=wab[:, k, 0:16], rhs=hT[:, k, main], start=(k == 0), stop=(k == 7)), reads=[R_wab, R_hT], writes=[R_pz])
                for k in range(8):
                    A("pe", lambda e, k=k: e.matmul(pz[0:16, 256:512], lhsT=wab[:, k, 16:32], rhs=hT[:, k, main], start=(k == 0), stop=(k == 7)), reads=[R_wab, R_hT], writes=[R_pz])
                (r0, R_r0), (r1, R_r1), (r2, R_r2), (r3, R_r3), (r4, R_r4), (r5, R_r5) = rf
                A("act", lambda e: e.activation(out=r0[:], in_=pz[0:16, 0:256], func=AF.Exp, bias=gp[:, 1 + 2 * s:2 + 2 * s]), reads=[R_pz, R_gp], writes=[R_r0])
                A("act", lambda e: e.activation(out=r0[:], in_=r0[:], func=AF.Ln, bias=1.0), reads=[R_r0], writes=[R_r0])
                A("dve", lambda e: e.tensor_scalar(out=r0[:], in0=r0[:], scalar1=nA16[:, s:s + 1], scalar2=None, op0=ALU.mult), reads=[R_r0, R_nA16], writes=[R_r0])
                A("act", lambda e: e.activation(out=r1[:], in_=pz[0:16, 256:512], func=AF.Exp, scale=-1.0), reads=[R_pz], writes=[R_r1])
                A("act", lambda e: e.activation(out=r1[:], in_=r1[:], func=AF.Ln, bias=1.0), reads=[R_r1], writes=[R_r1])
                A("dve", lambda e: e.tensor_tensor_scan(out=r2[:], data0=rmask[0:16, 0:256], data1=r0[:], initial=0.0, op0=ALU.mult, op1=ALU.add), reads=[R_rmask, R_r0], writes=[R_r2])
                A("dve", lambda e: e.tensor_tensor(out=r1[:], in0=r2[:], in1=r1[:], op=ALU.subtract), reads=[R_r2, R_r1], writes=[R_r1])
                A("dve", lambda e: e.tensor_scalar(out=r3[:], in0=r2[:], scalar1=gp[:, 4:5], scalar2=gp[:, 5:6], op0=ALU.mult, op1=ALU.add), reads=[R_r2, R_gp], writes=[R_r3])
                A("dve", lambda e: e.tensor_scalar(out=r4[:], in0=r2[:], scalar1=gp[:, 6:7], scalar2=gp[:, 7:8], op0=ALU.mult, op1=ALU.add), reads=[R_r2, R_gp], writes=[R_r4])
                A("dve", lambda e: e.tensor_scalar(out=r5[:], in0=r1[:], scalar1=gp[:, 6:7], scalar2=gp[:, 7:8], op0=ALU.mult, op1=ALU.add), reads=[R_r1, R_gp], writes=[R_r5])
                for h in (range(8) if readout else ()):
                    pb, R_pb = pm[h % 2]
                    A("pe", lambda e, h=h, pb=pb: e.matmul(pb[:, 0:256], lhsT=Esel[:, h, :], rhs=r4[:], start=True, stop=True), reads=[R_Esel, R_r4], writes=[R_pb])
                    A("act", lambda e, pb=pb: e.activation(out=cs[:], in_=pb[:, 0:256], func=AF.Exp), reads=[R_pb], writes=[R_cs])
                    A("dve", lambda e, h=h: e.tensor_tensor(out=qdB[:, h, :], in0=qBf[:, h, :], in1=cs[:], op=ALU.mult), reads=[R_qBf, R_cs], writes=[R_qdB])
            if s == 1 and readout:
                for q_ in range(4):
                    w, R_w = load_w((C_AZ if q_ < 2 else C_BZ) + 512 * (q_ % 2), 512)
                    for ti in range(2):
                        pt, R_pt = pm[4 + ti]
                        for k in range(8):
                            A("pe", lambda e, k=k, ti=ti, pt=pt, w=w: e.matmul(pt[:], lhsT=hT[:, k, off + 128 * ti:off + 128 * ti + 128], rhs=w[:, k, :], start=(k == 0), stop=(k == 7)),
                              reads=[R_w, R_hT], writes=[R_pt])
                        A("act", lambda e, ti=ti, pt=pt, q_=q_: e.activation(out=szs[ti][0][:, 512 * q_:512 * q_ + 512], in_=pt[:], func=AF.Silu), reads=[R_pt], writes=[szs[ti][1]])
            for ti in range(2):
                tok = slice(128 * ti, 128 * ti + 128)
                if readout:
                    for hq in range(2):
                        pt, R_pt = pm[hq]
                        for hh in range(4):
                            h = 4 * hq + hh
                            A("pe", lambda e, h=h, hh=hh, pt=pt, tok=tok: e.matmul(pt[:, 128 * hh:128 * hh + 128], lhsT=kdT[:, h, tok], rhs=qt[:, h, tok], start=True, stop=True),
                              reads=[R_kdT, R_qt], writes=[R_pt])
                        A("dve", lambda e, hq=hq, pt=pt: e.tensor_tensor(out=scT[:, 4 * hq:4 * hq + 4, :], in0=pt[:].rearrange("p (h t) -> p h t", t=128),
                                                                         in1=m01[:].unsqueeze(1).broadcast_to([128, 4, 128]), op=ALU.mult),
                          reads=[R_pt, R_m01], writes=[R_scT])
                po = [pm[2], pm[3]]
                pre = (s == 1 and readout)
                if pre:
                    nr = NTH - 128 - ((gidx - ng // 2) * 256 + 128 * ti)
                    A("sp", lambda e, nr=nr: e.dma_start(out=osb[:], in_=o1_d[nr:nr + 128, :]), reads=[R_o1], writes=[R_osb], chan="o1l")
                    for q_, (pt, R_pt) in enumerate((pm[2], pm[3], pm[6], pm[7])):
                        A("pe", lambda e, q_=q_, pt=pt: e.matmul(pt[:], lhsT=jf[:], rhs=osb[:, 512 * q_:512 * q_ + 512], start=True, stop=False), reads=[R_jf, R_osb], writes=[R_pt])
                for c in range(2):
                    rows = slice(64 * c, 64 * c + 64)
                    ctok = slice(128 * ti + 64 * c, 128 * ti + 64 * c + 64)
                    cg = 2 * ti + c
                    A("dve", lambda e, cg=cg: e.tensor_tensor(out=SAd[:], in0=SA[:], in1=dA[:, :, cg:cg + 1].broadcast_to([128, 8, 128]), op=ALU.mult),
                      reads=[R_SA, R_dA], writes=[R_SAd])
                    if readout:
                        A("act", lambda e: e.activation(out=SAb[:], in_=SAd[:], func=AF.Copy), reads=[R_SAd], writes=[R_SAb])
                        for h in range(8):
                            pt, R_pt = po[h // 4]
                            cols = slice(128 * (h % 4), 128 * (h % 4) + 128)
                            A("pe", lambda e, h=h, pt=pt, cols=cols, rows=rows, ctok=ctok: e.matmul(pt[rows, cols], lhsT=qt[:, h, ctok], rhs=SAb[:, h, :], start=(not pre), stop=False),
                              reads=[R_qt, R_SAb], writes=[R_pt])
                            A("pe", lambda e, h=h, pt=pt, cols=cols, rows=rows, ti=ti: e.matmul(pt[rows, cols], lhsT=scT[rows, h, rows], rhs=vA[rows, ti, 128 * h:128 * h + 128], start=False, stop=True),
                              reads=[R_scT, R_vA], writes=[R_pt])
                    for hq in range(2):
                        pt, R_pt = pm[4 + hq]
                        for hh in range(4):
                            h = 4 * hq + hh
                            A("pe", lambda e, h=h, hh=hh, pt=pt, rows=rows, ti=ti: e.matmul(pt[:, 128 * hh:128 * hh + 128], lhsT=kdm[rows, ti, 128 * h:128 * h + 128], rhs=vA[rows, ti, 128 * h:128 * h + 128], start=True, stop=True),
                              reads=[R_kdm, R_vA], writes=[R_pt])
                        A("dve", lambda e, hq=hq, pt=pt: e.tensor_tensor(out=SA[:, 4 * hq:4 * hq + 4, :], in0=pt[:].rearrange("p (h v) -> p h v", v=128), in1=SAd[:, 4 * hq:4 * hq + 4, :], op=ALU.add),
                          reads=[R_pt, R_SAd], writes=[R_SA])
                if readout:
                    for hq in range(2):
                        pt, R_pt = po[hq]
                        A("act", lambda e, hq=hq, pt=pt: e.activation(out=osb[:, 512 * hq:512 * hq + 512], in_=pt[:], func=AF.Copy), reads=[R_pt], writes=[R_osb])
                if do_gdn:
                    (t0, R_t0), (t1, R_t1), (t2, R_t2), (t3, R_t3) = tms
                    mtok = slice(off + 128 * ti, off + 128 * ti + 128)
                    pz, R_pz = pm[0]
                    for k in range(8):
                        A("pe", lambda e, k=k, mtok=mtok: e.matmul(pz[:, 0:16], lhsT=hT[:, k, mtok], rhs=wab[:, k, 8:24], start=(k == 0), stop=(k == 7)), reads=[R_wab, R_hT], writes=[R_pz])
                    A("dve", lambda e: e.tensor_tensor(out=t0[:, 0:8], in0=pz[:, 0:8], in1=gdtr[:, 2 + s, :], op=ALU.add), reads=[R_pz, R_gdtr], writes=[R_t0])
                    A("act", lambda e: e.activation(out=t0[:, 0:8], in_=t0[:, 0:8], func=AF.Exp), reads=[R_t0], writes=[R_t0])
                    A("act", lambda e: e.activation(out=t0[:, 0:8], in_=t0[:, 0:8], func=AF.Ln, bias=1.0), reads=[R_t0], writes=[R_t0])
                    A("dve", lambda e: e.tensor_tensor(out=t0[:, 0:8], in0=t0[:, 0:8], in1=nAr[:, s, :], op=ALU.mult), reads=[R_t0, R_nAr], writes=[R_t0])
                    A("act", lambda e: e.activation(out=t0[:, 8:16], in_=pz[:, 8:16], func=AF.Exp, scale=-1.0), reads=[R_pz], writes=[R_t0])
                    A("dve", lambda e: e.tensor_scalar(out=t0[:, 8:16], in0=t0[:, 8:16], scalar1=1.0, scalar2=None, op0=ALU.add), reads=[R_t0], writes=[R_t0])
                    A("dve", lambda e: e.reciprocal(out=bet[:], in_=t0[:, 8:16]), reads=[R_t0], writes=[R_bet])
                    pz2, R_pz2 = pm[1]
                    A("pe", lambda e: e.matmul(pz2[:, 0:8], lhsT=m01[:], rhs=t0[:, 0:8], start=True, stop=True), reads=[R_m01, R_t0], writes=[R_pz2])
                    A("pe", lambda e: e.matmul(pz2[:, 8:16], lhsT=mU[:], rhs=t0[:, 0:8], start=True, stop=True), reads=[R_mU, R_t0], writes=[R_pz2])
                    for c in range(2):
                        A("pe", lambda e, c=c: e.matmul(pz2[:, 16 + 8 * c:24 + 8 * c], lhsT=cind[:, c, :], rhs=t0[:, 0:8], start=True, stop=True), reads=[R_cind, R_t0], writes=[R_pz2])
                    A("act", lambda e: e.activation(out=bee[:], in_=pz2[:, 0:8], func=AF.Exp), reads=[R_pz2], writes=[R_bee])
                    A("act", lambda e: e.activation(out=ksf[:], in_=pz2[:, 8:16], func=AF.Exp), reads=[R_pz2], writes=[R_ksf])
                    A("act", lambda e: e.activation(out=dB[:].rearrange("p c h -> p (c h)"), in_=pz2[:, 16:32], func=AF.Exp), reads=[R_pz2], writes=[R_dB])
                    A("dve", lambda e: e.tensor_tensor(out=bee[:], in0=bee[:], in1=bet[:], op=ALU.mult), reads=[R_bee, R_bet], writes=[R_bee])
                    pk, R_pk = pm[0]
                    pkb = pk[:].bitcast(BF16)
                    for h in range(8):
                        A("pe", lambda e, h=h, tok=tok: e.transpose(out=pkb[:, 128 * h:128 * h + 128], in_=kBf[:, h, tok], identity=idb[:]), reads=[R_kBf, R_idb], writes=[R_pk])
                    pk3 = pkb.rearrange("p (h c) -> p h c", c=128)
                    A("dve", lambda e: e.tensor_tensor(out=kbe[:], in0=pk3, in1=bee[:].unsqueeze(2).broadcast_to([128, 8, 128]), op=ALU.mult), reads=[R_pk, R_bee], writes=[R_kbe])
                    A("dve", lambda e: e.tensor_tensor(out=kdc[:], in0=pk3, in1=ksf[:].unsqueeze(2).broadcast_to([128, 8, 128]), op=ALU.mult), reads=[R_pk, R_ksf], writes=[R_kdc])
                    pv_, R_pv_ = pm[1]
                    pvb = pv_[:].bitcast(BF16)
                    for h in range(8):
                        A("pe", lambda e, h=h, tok=tok: e.transpose(out=pvb[:, 128 * h:128 * h + 128], in_=vBf[:, h, tok], identity=idb[:]), reads=[R_vBf, R_idb], writes=[R_pv_])
                    A("dve", lambda e: e.tensor_tensor(out=bvt[:], in0=pvb.rearrange("p (h c) -> p h c", c=128), in1=bet[:].unsqueeze(2).broadcast_to([128, 8, 128]), op=ALU.mult),
                      reads=[R_pv_, R_bet], writes=[R_bvt])
                    A("dve", lambda e, tok=tok: e.tensor_tensor(out=rhsP[:], in0=rf[4][0][:, tok].unsqueeze(1).broadcast_to([16, 8, 128]), in1=gp[:, 8:16].unsqueeze(2).broadcast_to([16, 8, 128]), op=ALU.mult),
                      reads=[rf[4][1], R_gp], writes=[R_rhsP])
                    A("dve", lambda e, tok=tok: e.tensor_tensor(out=rhsA[:], in0=rf[5][0][:, tok].unsqueeze(1).broadcast_to([16, 8, 128]), in1=gp[:, 8:16].unsqueeze(2).broadcast_to([16, 8, 128]), op=ALU.mult),
                      reads=[rf[5][1], R_gp], writes=[R_rhsA])
                    for (rh, R_rh, cap, R_cap, ee, R_ee, b0) in (((rhsA, R_rhsA, capS, R_capS, eA, R_eA, 2), (rhsP, R_rhsP, capI, R_capI, eP, R_eP, 4)) if readout else ((rhsA, R_rhsA, capS, R_capS, eA, R_eA, 2),)):
                        for hq in range(2):
                            pd, R_pd = pm[b0 + hq]
                            for hh in range(4):
                                h = 4 * hq + hh
                                A("pe", lambda e, h=h, hh=hh, pd=pd, rh=rh, tok=tok: e.matmul(pd[:, 128 * hh:128 * hh + 128], lhsT=rf[3][0][:, tok], rhs=rh[:, h, :], start=True, stop=True),
                                  reads=[rf[3][1], R_rh], writes=[R_pd])
                            A("dve", lambda e, hq=hq, pd=pd, cap=cap, ee=ee: e.tensor_tensor(out=ee[:, 4 * hq:4 * hq + 4, :], in0=pd[:].rearrange("p (h t) -> p h t", t=128),
                                                                                           in1=cap[:].unsqueeze(1).broadcast_to([128, 4, 128]), op=ALU.min),
                              reads=[R_pd, R_cap], writes=[R_ee])
                        A("act", lambda e, ee=ee: e.activation(out=ee[:], in_=ee[:], func=AF.Exp), reads=[R_ee], writes=[R_ee])
                    (xt0_, R_xt0), (xx0_, R_xx0), (q0_, R_q0) = XT[0], XX[0], QQ[0]
                    for hq in range(2):
                        pa, R_pa = pm[2 + hq]
                        pb, R_pb = pm[4 + hq]
                        for hh in range(4):
                            h = 4 * hq + hh
                            A("pe", lambda e, h=h, hh=hh, pa=pa, tok=tok: e.matmul(pa[:, 128 * hh:128 * hh + 128], lhsT=kBf[:, h, tok], rhs=kBf[:, h, tok], start=True, stop=True), reads=[R_kBf], writes=[R_pa])
                            if readout:
                                A("pe", lambda e, h=h, hh=hh, pb=pb, tok=tok: e.matmul(pb[:, 128 * hh:128 * hh + 128], lhsT=kBf[:, h, tok], rhs=qBf[:, h, tok], start=True, stop=True), reads=[R_kBf, R_qBf], writes=[R_pb])
                        A("dve", lambda e, hq=hq, pa=pa: e.scalar_tensor_tensor(out=xt0_[:, 4 * hq:4 * hq + 4, :], in0=pa[:].rearrange("p (h t) -> p h t", t=128), scalar=-1.0, in1=eA[:, 4 * hq:4 * hq + 4, :], op0=ALU.mult, op1=ALU.mult),
                          reads=[R_pa, R_eA], writes=[R_xt0])
                        if readout:
                            A("dve", lambda e, hq=hq, pb=pb: e.tensor_tensor(out=PT[:, 4 * hq:4 * hq + 4, :], in0=pb[:].rearrange("p (h t) -> p h t", t=128), in1=eP[:, 4 * hq:4 * hq + 4, :], op=ALU.mult),
                              reads=[R_pb, R_eP], writes=[R_PT])
                    px, R_px = pm[0]
                    pxb = px[:].bitcast(BF16)
                    for h in range(8):
                        A("pe", lambda e, h=h: e.transpose(out=pxb[:, 128 * h:128 * h + 128], in_=xt0_[:, h, :], identity=idb[:]), reads=[R_xt0, R_idb], writes=[R_px])
                    A("act", lambda e: e.activation(out=xx0_[:], in_=pxb.rearrange("p (h c) -> p h c", c=128), func=AF.Copy), reads=[R_px], writes=[R_xx0])
                    A("pool", lambda e: e.tensor_tensor(out=q0_[:], in0=xt0_[:], in1=idb[:].unsqueeze(1).broadcast_to([128, 8, 128]), op=ALU.add), reads=[R_xt0, R_idb], writes=[R_q0])
                    for l in range(6):
                        (xtc, R_xtc), (xxc, R_xxc) = XT[l % 2], XX[l % 2]
                        (xtn, R_xtn), (xxn, R_xxn) = XT[(l + 1) % 2], XX[(l + 1) % 2]
                        (qc, R_qc), (qn, R_qn) = QQ[l % 2], QQ[(l + 1) % 2]
                        for hq in range(2):
                            hs = slice(4 * hq, 4 * hq + 4)
                            if l < 5:
                                p1, R_p1 = pm[2 + hq]
                                p2, R_p2 = pm[4 + hq]
                                for hh in range(4):
                                    h = 4 * hq + hh
                                    A("pe", lambda e, h=h, hh=hh, p1=p1, xxc=xxc, xtc=xtc: e.matmul(p1[:, 128 * hh:128 * hh + 128], lhsT=xxc[:, h, :], rhs=xtc[:, h, :], start=True, stop=True), reads=[R_xxc, R_xtc], writes=[R_p1])
                                    A("pe", lambda e, h=h, hh=hh, p2=p2, xxc=xxc, xtc=xtc: e.matmul(p2[:, 128 * hh:128 * hh + 128], lhsT=xtc[:, h, :], rhs=xxc[:, h, :], start=True, stop=True), reads=[R_xxc, R_xtc], writes=[R_p2])
                            if l >= 1:
                                p3, R_p3 = pm[hq]
                                for hh in range(4):
                                    h = 4 * hq + hh
                                    A("pe", lambda e, h=h, hh=hh, p3=p3, xxc=xxc, qc=qc: e.matmul(p3[:, 128 * hh:128 * hh + 128], lhsT=xxc[:, h, :], rhs=qc[:, h, :], start=True, stop=True), reads=[R_xxc, R_qc], writes=[R_p3])
                                A("dve", lambda e, hs=hs, p3=p3, qc=qc, qn=qn: e.tensor_tensor(out=qn[:, hs, :], in0=p3[:].rearrange("p (h t) -> p h t", t=128), in1=qc[:, hs, :], op=ALU.add),
                                  reads=[R_p3, R_qc], writes=[R_qn])
                            if l < 5:
                                A("act", lambda e, hs=hs, p1=p1, xtn=xtn: e.activation(out=xtn[:, hs, :], in_=p1[:].rearrange("p (h t) -> p h t", t=128), func=AF.Copy), reads=[R_p1], writes=[R_xtn])
                                A("pool" if False else "dve", lambda e, hs=hs, p2=p2, xxn=xxn: e.tensor_copy(out=xxn[:, hs, :], in_=p2[:].rearrange("p (h t) -> p h t", t=128)), reads=[R_p2], writes=[R_xxn])
                        if l == 0:
                            A("pool", lambda e: e.tensor_copy(out=QQ[1][0][:], in_=QQ[0][0][:]), reads=[QQ[0][1]], writes=[QQ[1][1]])
                    Qf, R_Qf = QQ[0]
                    for hq in range(2):
                        p1, R_p1 = pm[2 + hq]
                        p2, R_p2 = pm[4 + hq]
                        for hh in range(4):
                            h = 4 * hq + hh
                            A("pe", lambda e, h=h, hh=hh, p1=p1: e.matmul(p1[:, 128 * hh:128 * hh + 128], lhsT=Qf[:, h, :], rhs=bvt[:, h, :], start=True, stop=True), reads=[R_Qf, R_bvt], writes=[R_p1])
                            A("pe", lambda e, h=h, hh=hh, p2=p2: e.matmul(p2[:, 128 * hh:128 * hh + 128], lhsT=kbe[:, h, :], rhs=Qf[:, h, :], start=True, stop=True), reads=[R_Qf, R_kbe], writes=[R_p2])
                        A("act", lambda e, hq=hq, p1=p1: e.activation(out=u0[:, 4 * hq:4 * hq + 4, :], in_=p1[:].rearrange("p (h t) -> p h t", t=128), func=AF.Copy), reads=[R_p1], writes=[R_u0])
                        A("act", lambda e, hq=hq, p2=p2: e.activation(out=nwT[:, 4 * hq:4 * hq + 4, :], in_=p2[:].rearrange("p (h t) -> p h t", t=128), func=AF.Copy, scale=-1.0), reads=[R_p2], writes=[R_nwT])
                    pob = [pm[6], pm[7]]
                    for c in range(2):
                        rows = slice(64 * c, 64 * c + 64)
                        ctok = slice(128 * ti + 64 * c, 128 * ti + 64 * c + 64)
                        for hq in range(2):
                            p1, R_p1 = pm[hq]
                            for hh in range(4):
                                h = 4 * hq + hh
                                A("pe", lambda e, h=h, hh=hh, p1=p1, rows=rows: e.matmul(p1[rows, 128 * hh:128 * hh + 128], lhsT=nwT[:, h, rows], rhs=SBb[:, h, :], start=True, stop=True), reads=[R_nwT, R_SBb], writes=[R_p1])
                            A("dve", lambda e, hq=hq, p1=p1, rows=rows: e.tensor_tensor(out=vnew[rows, 4 * hq:4 * hq + 4, :], in0=p1[rows, :].rearrange("p (h t) -> p h t", t=128), in1=u0[rows, 4 * hq:4 * hq + 4, :], op=ALU.add),
                              reads=[R_p1, R_u0], writes=[R_vnew])
                        for hq in range(2):
                            if readout:
                                po_, R_po = pob[hq]
                                for hh in range(4):
                                    h = 4 * hq + hh
                                    A("pe", lambda e, h=h, hh=hh, po_=po_, rows=rows, ctok=ctok: e.matmul(po_[rows, 128 * hh:128 * hh + 128], lhsT=qdB[:, h, ctok], rhs=SBb[:, h, :], start=(not pre), stop=False), reads=[R_qdB, R_SBb], writes=[R_po])
                                    A("pe", lambda e, h=h, hh=hh, po_=po_, rows=rows: e.matmul(po_[rows, 128 * hh:128 * hh + 128], lhsT=PT[rows, h, rows], rhs=vnew[rows, h, :], start=False, stop=True), reads=[R_PT, R_vnew], writes=[R_po])
                            p2, R_p2 = pm[2 + hq]
                            for hh in range(4):
                                h = 4 * hq + hh
                                A("pe", lambda e, h=h, hh=hh, p2=p2, rows=rows: e.matmul(p2[:, 128 * hh:128 * hh + 128], lhsT=kdc[rows, h, :], rhs=vnew[rows, h, :], start=True, stop=True), reads=[R_kdc, R_vnew], writes=[R_p2])
                            hs = slice(4 * hq, 4 * hq + 4)
                            A("dve", lambda e, hs=hs, c=c: e.tensor_tensor(out=SB[:, hs, :], in0=SB[:, hs, :], in1=dB[:, c, hs].unsqueeze(2).broadcast_to([128, 4, 128]), op=ALU.mult), reads=[R_SB, R_dB], writes=[R_SB])
                            A("dve", lambda e, hs=hs, p2=p2: e.tensor_tensor(out=SB[:, hs, :], in0=p2[:].rearrange("p (h t) -> p h t", t=128), in1=SB[:, hs, :], op=ALU.add), reads=[R_p2, R_SB], writes=[R_SB])
                        A("act", lambda e: e.activation(out=SBb[:], in_=SB[:], func=AF.Copy), reads=[R_SB], writes=[R_SBb])
                    if readout:
                        for hq in range(2):
                            po_, R_po = pob[hq]
                            A("act", lambda e, hq=hq, po_=po_: e.activation(out=osb[:, D + 512 * hq:D + 512 * hq + 512], in_=po_[:], func=AF.Copy), reads=[R_po], writes=[R_osb])
                if readout and s == 1:
                    for hd in range(16):
                        A("act", lambda e, hd=hd: e.activation(out=junk[:, 0:128], in_=osb[:, 128 * hd:128 * hd + 128], func=AF.Square, accum_out=hst[:, hd:hd + 1]), reads=[R_osb], writes=[R_junk, R_hst])
                    A("act", lambda e: e.activation(out=hst[:, 16:32], in_=hst[:, 0:16], func=AF.Ln, scale=1.0 / 128, bias=EPS), reads=[R_hst], writes=[R_hst])
                    A("act", lambda e: e.activation(out=hst[:, 16:32], in_=hst[:, 16:32], func=AF.Exp, scale=-0.5), reads=[R_hst], writes=[R_hst])
                    szt, R_szt = szs[ti]
                    for q_ in range(4):
                        yb_, R_yb = yb2[q_ // 2]
                        cb0 = 512 * (q_ % 2)
                        A("dve", lambda e, q_=q_, szt=szt: e.tensor_tensor(out=osb[:, 512 * q_:512 * q_ + 512], in0=osb[:, 512 * q_:512 * q_ + 512], in1=szt[:, 512 * q_:512 * q_ + 512], op=ALU.mult), reads=[R_osb, R_szt], writes=[R_osb])
                        A("dve", lambda e, q_=q_, yb_=yb_, cb0=cb0: e.tensor_tensor(out=yb_[:, cb0:cb0 + 512].rearrange("p (h c) -> p h c", c=128), in0=osb[:, 512 * q_:512 * q_ + 512].rearrange("p (h c) -> p h c", c=128),
                                                                                in1=hst[:, 16 + 4 * q_:20 + 4 * q_].unsqueeze(2).broadcast_to([128, 4, 128]), op=ALU.mult), reads=[R_osb, R_hst], writes=[R_yb])
                    for half in range(2):
                        yb_, R_yb = yb2[half]
                        yT_, R_yT = yT2[half]
                        pt, R_pt = pm[half]
                        ptb = pt[:].bitcast(BF16)
                        for kk in range(8):
                            A("pe", lambda e, kk=kk, yb_=yb_, ptb=ptb: e.transpose(out=ptb[:, 128 * kk:128 * kk + 128], in_=yb_[:, 128 * kk:128 * kk + 128], identity=idb[:]), reads=[R_yb, R_idb], writes=[R_pt])
                        A("act", lambda e, yT_=yT_, ptb=ptb: e.activation(out=yT_[:], in_=ptb.rearrange("p (k t) -> p k t", t=128), func=AF.Copy), reads=[R_pt], writes=[R_yT])
                    x_t, R_x = xt[ti % 2]
                    xrow = 64 + gidx * 256 + 128 * ti
                    A("sp", lambda e, xrow=xrow, x_t=x_t: e.dma_start(out=x_t[:], in_=src[xrow:xrow + 128, :]), writes=[R_x], chan="x%d" % (ti % 2))
                    for ch in range(2):
                        pt, R_pt = pm[2 + ch]
                        for kh in range(2):
                            i = wctr[0] % 2
                            wctr[0] += 1
                            wt_, R_wt = wt[i]
                            A("sp", lambda e, kh=kh, ch=ch, wt_=wt_: e.dma_start(out=wt_[:], in_=woutb[1024 * kh:1024 * kh + 1024, 512 * ch:512 * ch + 512].rearrange("(k p) n -> p k n", p=128)),
                              reads=[R_woutb], writes=[R_wt], chan="w%d" % i)
                            yT_, R_yT = yT2[kh]
                            for k in range(8):
                                A("pe", lambda e, k=k, kh=kh, pt=pt, yT_=yT_, wt_=wt_: e.matmul(pt[:], lhsT=yT_[:, k, :], rhs=wt_[:, k, :], start=(kh == 0 and k == 0), stop=(kh == 1 and k == 7)),
                                  reads=[R_yT, R_wt], writes=[R_pt])
                        A("dve", lambda e, ch=ch, pt=pt: e.tensor_tensor(out=hn[:, 512 * ch:512 * ch + 512], in0=pt[:], in1=gate[:, 512 * ch:512 * ch + 512], op=ALU.mult), reads=[R_pt, R_gate], writes=[R_hn])
                    A("dve", lambda e, x_t=x_t: e.tensor_tensor(out=hn[:], in0=hn[:], in1=x_t[:], op=ALU.add), reads=[R_hn, R_x], writes=[R_hn])
                    A("act", lambda e: e.activation(out=junk[:], in_=hn[:], func=AF.Square, accum_out=ssq[:, 0:1]), reads=[R_hn], writes=[R_junk, R_ssq])
                    A("act", lambda e: e.activation(out=ssq[:, 1:2], in_=ssq[:, 0:1], func=AF.Ln, scale=1.0 / D, bias=EPS), reads=[R_ssq], writes=[R_ssq])
                    A("act", lambda e: e.activation(out=ssq[:, 1:2], in_=ssq[:, 1:2], func=AF.Exp, scale=-0.5), reads=[R_ssq], writes=[R_ssq])
                    A("dve", lambda e: e.scalar_tensor_tensor(out=hn[:], in0=hn[:], scalar=ssq[:, 1:2], in1=fnw[:], op0=ALU.mult, op1=ALU.mult), reads=[R_hn, R_ssq, R_fnw], writes=[R_hn])
                    orow = (gidx - ng // 2) * 256 + 128 * ti
                    A("sp", lambda e, orow=orow: e.dma_start(out=out_d[orow:orow + 128, :], in_=hn[:]), reads=[R_hn], chan="out")
                if readout and s == 0:
                    srow = gidx * 256 + 128 * ti
                    A("sp", lambda e, srow=srow: e.dma_start(out=o1_d[srow:srow + 128, :], in_=osb[:]), reads=[R_osb], writes=[R_o1], chan="o1")

        for s in range(2 if do_s2 else 1):
            if s == 1:
                A("dve", lambda e: e.memset(SA[:], 0.0), writes=[R_SA])
                A("dve", lambda e: e.memset(SB[:], 0.0), writes=[R_SB])
                A("pool", lambda e: e.memset(SBb[:], 0.0), writes=[R_SBb])
            group(s, cx[s], 0, 2, 0, 1, False, -1)
            if s == 0:
                for g in range(ng // 2):
                    group(s, xs[s], 256 * g, 3, 64, 0, True, g)
            else:
                for g in range(ng):
                    group(s, xs[s], 256 * g, 3, 64, 0, g >= ng // 2, g)

        P.emit(st, final_waits=["o1"] + (["out"] if do_s2 else []))
    return nc


def make_inputs(x, c, ctx, c_ctx, norm_w, ada_w, ada_b, w_in, conv_w, hg_lb_logits, gdn_a_log,
                gdn_dt_bias, ha_norm_w, hb_norm_w, w_out, final_norm_w, ncores=8):
    f = lambda a: np.ascontiguousarray(np.asarray(a, dtype=np.float32))
    S = x.shape[1]
    Hh = S // 2
    cw = conv_w[0].reshape(9, 3072)
    cwn = cw.T.reshape(24, 128, 9).transpose(1, 0, 2)
    cwf = cw[::-1].T.reshape(24, 128, 9).transpose(1, 0, 2)
    lbl4 = hg_lb_logits.reshape(2, 2, 8, 128).transpose(3, 0, 1, 2)
    hnw = np.concatenate([ha_norm_w[0].reshape(-1), hb_norm_w[0].reshape(-1)])
    z = np.zeros((64, D), np.float32)
    in_maps = []
    for core in range(ncores):
        b, half = core // 2, core % 2
        d1, d2 = (0, 1) if half == 0 else (1, 0)
        xb = x[b]
        if half == 0:
            xs1 = np.concatenate([z, xb[:Hh], xb[Hh:Hh + 64]], axis=0)
            xs2 = np.concatenate([z, xb[::-1], z], axis=0)
            ctx1, ctx2 = ctx[b], ctx[b][::-1]
            cwp = np.stack([cwn, cwf], axis=1)
        else:
            xs1 = np.concatenate([z, xb[Hh:][::-1], xb[Hh - 64:Hh][::-1]], axis=0)
            xs2 = np.concatenate([z, xb, z], axis=0)
            ctx1, ctx2 = ctx[b][::-1], ctx[b]
            cwp = np.stack([cwf, cwn], axis=1)
        w = np.array(w_in[0], dtype=np.float32, copy=True)
        if half == 1:
            w[:, C_AF0:C_AF0 + 1024], w[:, C_AF1:C_AF1 + 1024] = w_in[0][:, C_AF1:C_AF1 + 1024], w_in[0][:, C_AF0:C_AF0 + 1024]
            w[:, C_BA:C_BA + 8], w[:, C_BA + 8:C_BA + 16] = w_in[0][:, C_BA + 8:C_BA + 16], w_in[0][:, C_BA:C_BA + 8]
            w[:, C_BB:C_BB + 8], w[:, C_BB + 8:C_BB + 16] = w_in[0][:, C_BB + 8:C_BB + 16], w_in[0][:, C_BB:C_BB + 8]
        lbl = lbl4[:, :, [d1, d2], :].reshape(128, 32)
        gpar = np.zeros((16, 16), np.float32)
        for si, d in enumerate((d1, d2)):
            gpar[:, 2 * si] = np.tile(gdn_a_log[0, d], 2)
            gpar[:, 2 * si + 1] = np.tile(gdn_dt_bias[0, d], 2)
        gpar[0:8, 4] = -1.0
        gpar[8:16, 5] = 1.0
        gpar[8:16, 6] = 1.0
        gpar[0:8, 7] = 1.0
        for k in range(16):
            gpar[k, 8 + k % 8] = 1.0
        gdr = np.concatenate([gdn_a_log[0, d1], gdn_a_log[0, d2], gdn_dt_bias[0, d1], gdn_dt_bias[0, d2]])
        ccol = np.concatenate([c[b].reshape(8, 128).T, c_ctx.reshape(8, 128).T], axis=1)
        in_maps.append({
            "xs1": f(xs1), "xs2": f(xs2), "ctx1": f(ctx1), "ctx2": f(ctx2),
            "ccol": f(ccol), "norm_w": f(norm_w[0]), "final_norm_w": f(final_norm_w),
            "ada_w": f(ada_w[0]), "ada_b": f(ada_b[0]), "w_in": f(w), "w_out": f(w_out[0]),
            "lbl": f(lbl), "cw": f(cwp), "hnwc": f(hnw.reshape(16, 128).T), "gpar": f(gpar), "gdr": f(gdr),
        })
    return in_maps


def assemble(results, nb):
    outs = []
    for b in range(nb):
        outs.append(np.concatenate([results[2 * b]["out"][::-1], results[2 * b + 1]["out"]], axis=0))
    return np.ascontiguousarray(np.stack(outs, axis=0).astype(np.float32))


_NC_CACHE = {}


def kernel(**inputs):
    inputs = {k: np.asarray(v) for k, v in inputs.items()}
    in_maps = make_inputs(**inputs)
    if "nc" not in _NC_CACHE:
        _NC_CACHE["nc"] = build()
    res = run_bass_kernel_spmd(_NC_CACHE["nc"], in_maps, core_ids=list(range(8)))
    return assemble(res.results, 4)
```

```python
import contextlib
import numpy as np
import concourse.bass as bass
import concourse.mybir as mybir
from concourse.bass_utils import run_bass_kernel_spmd

F32 = mybir.dt.float32
BF16 = mybir.dt.bfloat16
AF = mybir.ActivationFunctionType
ALU = mybir.AluOpType

ENGS = ("pe", "act", "dve", "pool", "sp")

D = 1024
SEQ = 8192
CTXL = 256
NIN = 9248
EPS = 1e-6
C_AQ, C_AF0, C_AF1, C_AI, C_AZ = 0, 1024, 2048, 3072, 4096
C_BQ, C_BK, C_BV, C_BZ, C_BA, C_BB = 5120, 6144, 7168, 8192, 9216, 9232


class Res:
    __slots__ = ("name", "w", "rs")

    def __init__(self, name):
        self.name = name
        self.w = None
        self.rs = {}


class Op:
    __slots__ = ("eng", "fn", "deps", "chan", "chan_idx", "need_sig", "sig", "idx")

    def __init__(self, eng, fn, chan):
        self.eng = eng
        self.fn = fn
        self.chan = chan
        self.chan_idx = 0
        self.deps = ()
        self.need_sig = False
        self.sig = 0
        self.idx = 0


class Prog:
    def __init__(self, nc):
        self.nc = nc
        self.eng_ops = {e: [] for e in ENGS}
        self.chan_cnt = {}
        self.n = 0

    def op(self, eng, fn, reads=(), writes=(), chan=None):
        o = Op(eng, fn, chan)
        o.idx = self.n
        self.n += 1
        if chan is not None:
            c = self.chan_cnt.get(chan, 0) + 1
            self.chan_cnt[chan] = c
            o.chan_idx = c
        deps = {}
        for r in reads:
            if r.w is not None:
                deps[id(r.w)] = r.w
        for w in writes:
            if w.w is not None:
                deps[id(w.w)] = w.w
            for x in w.rs.values():
                deps[id(x)] = x
        o.deps = list(deps.values())
        for r in reads:
            key = eng if chan is None else ("dma", o.idx)
            r.rs[key] = o
        for w in writes:
            w.w = o
            w.rs = {}
        self.eng_ops[eng].append(o)
        return o

    def emit(self, st, final_waits=()):
        nc = self.nc
        esem = {e: st.enter_context(nc.semaphore("s_" + e)) for e in ENGS}
        csem = {c: st.enter_context(nc.semaphore("c_" + str(c))) for c in self.chan_cnt}
        for e in ENGS:
            for o in self.eng_ops[e]:
                for d in o.deps:
                    if d.chan is None:
                        if d.eng == "pe" and o.eng == "pe":
                            continue
                        d.need_sig = True
        for e in ENGS:
            c = 0
            for o in self.eng_ops[e]:
                if o.chan is None and o.need_sig:
                    c += 1
                    o.sig = c
        block = st.enter_context(nc.Block())

        def run(e, h):
            known = {}
            for o in self.eng_ops[e]:
                for d in o.deps:
                    if d.chan is not None:
                        key, val, sem = ("c", d.chan), 16 * d.chan_idx, csem[d.chan]
                    else:
                        if d.eng == "pe" and e == "pe":
                            continue
                        key, val, sem = ("e", d.eng), d.sig, esem[d.eng]
                    if known.get(key, 0) >= val:
                        continue
                    known[key] = val
                    h.wait_ge(sem, val)
                ins = o.fn(h)
                if o.chan is not None:
                    ins.then_inc(csem[o.chan], 16)
                elif o.need_sig:
                    ins.then_inc(esem[e], 1)
            if e == "sp":
                for ch in final_waits:
                    h.wait_ge(csem[ch], 16 * self.chan_cnt[ch])

        @block.tensor
        def _(h):
            run("pe", h)

        @block.scalar
        def _(h):
            run("act", h)

        @block.vector
        def _(h):
            run("dve", h)

        @block.gpsimd
        def _(h):
            run("pool", h)

        @block.sync
        def _(h):
            run("sp", h)


def build(ng=32, debug=False, do_gdn=True, do_s2=True):
    nc = bass.Bass("TRN2", target_bir_lowering=False)
    NT = ng * 256
    dram = lambda n, s, dt=F32, kind="ExternalInput": nc.dram_tensor(n, s, dt, kind=kind).ap()
    NTH = NT // 2
    xs = [dram("xs1", [NTH + 128, D]), dram("xs2", [NT + 128, D])]
    cx = [dram("ctx1", [CTXL, D]), dram("ctx2", [CTXL, D])]
    ccol_d = dram("ccol", [128, 16])
    normw_d = dram("norm_w", [D])
    fnw_d = dram("final_norm_w", [D])
    adaw_d = dram("ada_w", [D, 3 * D])
    adab_d = dram("ada_b", [3 * D])
    win_d = dram("w_in", [D, NIN])
    wout_d = dram("w_out", [2 * D, D])
    lbl_d = dram("lbl", [128, 32])
    cw_d = dram("cw", [128, 2, 24, 9])
    hnw_d = dram("hnwc", [128, 16])
    gpar_d = dram("gpar", [16, 16])
    gdr_d = dram("gdr", [32])
    out_d = dram("out", [NTH, D], kind="ExternalOutput")
    winb = dram("winb", [D, NIN], BF16, kind="Internal")
    woutb = dram("woutb", [2 * D, D], BF16, kind="Internal")
    o1_d = dram("o1", [NTH, 2 * D], F32, kind="ExternalOutput" if debug else "Internal")
    dbg_d = dram("dbg", [128, 4096], F32, kind="ExternalOutput") if debug else None

    P = Prog(nc)
    R_o1 = Res("o1")
    st = contextlib.ExitStack()
    with st:
        def sb(name, shape, dt=F32):
            return st.enter_context(nc.sbuf_tensor("sb_" + name, shape, dt)), Res(name)

        def ps(name, shape, dt=F32):
            return st.enter_context(nc.psum_tensor("ps_" + name, shape, dt)), Res(name)

        A = P.op

        idf, R_idf = sb("idf", [128, 128])
        idb, R_idb = sb("idb", [128, 128], BF16)
        jf, R_jf = sb("jf", [128, 128])
        m01, R_m01 = sb("m01", [128, 128])
        rmask, R_rmask = sb("rmask", [128, 256])
        A("pool", lambda e: e.memset(idf[:], 1.0), writes=[R_idf])
        A("pool", lambda e: e.affine_select(out=idf[:], in_=idf[:], pattern=[[-1, 128]], compare_op=ALU.is_equal,
                                            fill=0.0, base=0, channel_multiplier=1), reads=[R_idf], writes=[R_idf])
        A("dve", lambda e: e.tensor_copy(out=idb[:], in_=idf[:]), reads=[R_idf], writes=[R_idb])
        A("pool", lambda e: e.memset(jf[:], 1.0), writes=[R_jf])
        A("pool", lambda e: e.affine_select(out=jf[:], in_=jf[:], pattern=[[1, 128]], compare_op=ALU.is_equal,
                                            fill=0.0, base=-127, channel_multiplier=1), reads=[R_jf], writes=[R_jf])
        A("pool", lambda e: e.memset(m01[:], 1.0), writes=[R_m01])
        A("pool", lambda e: e.affine_select(out=m01[:], in_=m01[:], pattern=[[1, 128]], compare_op=ALU.is_ge,
                                            fill=0.0, base=0, channel_multiplier=-1), reads=[R_m01], writes=[R_m01])
        A("pool", lambda e: e.memset(m01[0:64, 64:128], 0.0), reads=[R_m01], writes=[R_m01])
        A("pool", lambda e: e.memset(rmask[:], 1.0), writes=[R_rmask])
        A("pool", lambda e: e.memset(rmask[:].rearrange("p (c t) -> p c t", t=64)[:, :, 0:1], 0.0),
          reads=[R_rmask], writes=[R_rmask])

        R_winb, R_woutb = Res("winb"), Res("woutb")
        for i in range(8):
            A("pool", lambda e, i=i: e.dma_start(out=winb[128 * i:128 * i + 128, :], in_=win_d[128 * i:128 * i + 128, :]),
              writes=[R_winb], chan="wcv")

        xt = [sb("xt%d" % i, [128, D]) for i in range(2)]
        qA, R_qA = sb("qA", [128, 8, 256])
        osb, R_osb = sb("osb", [128, 2 * D])
        ccol, R_ccol = sb("ccol", [128, 16])
        scb, R_scb = qA[:].rearrange("p a b -> p (a b)").rearrange("p (j t) -> p j t", t=128), R_qA
        adaw, R_adaw = xt[0][0][:].rearrange("p (k n) -> p k n", n=128), xt[0][1]
        nwb, R_nwb = osb[:, 0:D], R_osb
        adab, R_adab = osb[:, D:D + 128], R_osb
        Amod, R_Amod = sb("Amod", [128, 2, D])
        shmod, R_shmod = sb("shmod", [128, 2, D])
        gate, R_gate = sb("gate", [128, D])
        A("sp", lambda e: e.dma_start(out=ccol[:], in_=ccol_d), writes=[R_ccol], chan="p0")
        A("sp", lambda e: e.dma_start(out=nwb, in_=normw_d.partition_broadcast(128)), writes=[R_nwb], chan="p2")
        A("act", lambda e: e.activation(out=ccol[:], in_=ccol[:], func=AF.Silu), reads=[R_ccol], writes=[R_ccol])
        A("dve", lambda e: e.tensor_copy(out=scb, in_=ccol[:].unsqueeze(2).broadcast_to([128, 16, 128])),
          reads=[R_ccol], writes=[R_scb])
        pm = [ps("pm%d" % i, [128, 512]) for i in range(8)]
        for blk in range(24):
            A("sp", lambda e, blk=blk: e.dma_start(out=adaw, in_=adaw_d[:, 128 * blk:128 * blk + 128].rearrange("(k p) n -> p k n", p=128)),
              writes=[R_adaw], chan="adaw")
            A("sp", lambda e, blk=blk: e.dma_start(out=adab, in_=adab_d[128 * blk:128 * blk + 128].partition_broadcast(128)), writes=[R_adab], chan="p1")
            which, off = blk // 8, (blk % 8) * 128
            for j in range(2):
                pt, R_pt = pm[j]
                for k in range(8):
                    A("pe", lambda e, j=j, k=k, pt=pt: e.matmul(pt[:, 0:128], lhsT=scb[:, j * 8 + k, :], rhs=adaw[:, k, :], start=(k == 0), stop=(k == 7)),
                      reads=[R_scb, R_adaw], writes=[R_pt])
                if which == 0:
                    A("dve", lambda e, j=j, pt=pt, off=off: e.tensor_tensor(out=shmod[:, j, off:off + 128], in0=pt[:, 0:128], in1=adab, op=ALU.add),
                      reads=[R_pt, R_adab], writes=[R_shmod])
                elif which == 1:
                    A("dve", lambda e, j=j, pt=pt, off=off: e.scalar_tensor_tensor(out=Amod[:, j, off:off + 128], in0=pt[:, 0:128], scalar=1.0, in1=adab, op0=ALU.add, op1=ALU.add),
                      reads=[R_pt, R_adab], writes=[R_Amod])
                    A("dve", lambda e, j=j, off=off: e.tensor_tensor(out=Amod[:, j, off:off + 128], in0=Amod[:, j, off:off + 128], in1=nwb[:, off:off + 128], op=ALU.mult),
                      reads=[R_Amod, R_nwb], writes=[R_Amod])
                elif j == 0:
                    A("dve", lambda e, pt=pt, off=off: e.tensor_tensor(out=gate[:, off:off + 128], in0=pt[:, 0:128], in1=adab, op=ALU.add),
                      reads=[R_pt, R_adab], writes=[R_gate])

        lbl, R_lbl = sb("lbl", [128, 32])
        lb, R_lb = sb("lb", [128, 16])
        oml, R_oml = sb("oml", [128, 16])
        lbm1, R_lbm1 = sb("lbm1", [128, 16])
        A("sp", lambda e: e.dma_start(out=lbl[:], in_=lbl_d), writes=[R_lbl], chan="p3")
        A("dve", lambda e: e.tensor_tensor(out=lb[:], in0=lbl[:, 0:16], in1=lbl[:, 16:32], op=ALU.subtract), reads=[R_lbl], writes=[R_lb])
        A("act", lambda e: e.activation(out=lb[:], in_=lb[:], func=AF.Sigmoid), reads=[R_lb], writes=[R_lb])
        A("dve", lambda e: e.tensor_scalar(out=oml[:], in0=lb[:], scalar1=-1.0, scalar2=1.0, op0=ALU.mult, op1=ALU.add), reads=[R_lb], writes=[R_oml])
        A("dve", lambda e: e.tensor_scalar(out=lbm1[:], in0=lb[:], scalar1=-1.0, scalar2=None, op0=ALU.add), reads=[R_lb], writes=[R_lbm1])

        junk, R_junk = sb("junk", [128, D], BF16)
        ssq, R_ssq = sb("ssq", [128, 2])
        hn, R_hn = sb("hn", [128, D])
        hb, R_hb = sb("hb", [128, D], BF16)
        hT, R_hT = sb("hT", [128, 8, 384], BF16)
        wt = [sb("wt%d" % i, [128, 8, 512], BF16) for i in range(2)]
        wctr = [0]
        tf = [sb("tf%d" % i, [128, 256]) for i in range(4)]
        tf2 = [sb("tfb%d" % i, [128, 256]) for i in range(2)]
        SG = [(tf[0], tf[1]), (tf2[0], tf2[1])]
        qt, R_qt = sb("qt", [128, 8, 256], BF16)
        kdT, R_kdT = sb("kdT", [128, 8, 256], BF16)
        dA, R_dA = sb("dA", [128, 8, 4])
        kdm, R_kdm = sb("kdm", [128, 2, D], BF16)
        vA, R_vA = sb("vA", [128, 2, D], BF16)
        scT, R_scT = sb("scT", [128, 8, 128], BF16)
        SA, R_SA = sb("SA", [128, 8, 128])
        SAd, R_SAd = sb("SAd", [128, 8, 128])
        SAb, R_SAb = sb("SAb", [128, 8, 128], BF16)
        A("dve", lambda e: e.memset(SA[:], 0.0), writes=[R_SA])
        cws, R_cws = sb("cws", [128, 2, 24, 9])
        gp, R_gp = sb("gp", [16, 16])
        nA16, R_nA16 = sb("nA16", [16, 2])
        gdtr, R_gdtr = sb("gdtr", [128, 4, 8])
        nAr, R_nAr = sb("nAr", [128, 2, 8])
        onesb, R_onesb = sb("onesb", [128, 128], BF16)
        capI, R_capI = sb("capI", [128, 128])
        capS, R_capS = sb("capS", [128, 128])
        mU, R_mU = sb("mU", [128, 128])
        cind, R_cind = sb("cind", [128, 2, 128])
        Esel, R_Esel = sb("Esel", [16, 8, 128])
        A("sp", lambda e: e.dma_start(out=cws[:], in_=cw_d), writes=[R_cws], chan="p4")
        A("sp", lambda e: e.dma_start(out=gp[:], in_=gpar_d), writes=[R_gp], chan="p5")
        A("sp", lambda e: e.dma_start(out=gdtr[:].rearrange("p a h -> p (a h)"), in_=gdr_d.partition_broadcast(128)), writes=[R_gdtr], chan="p6")
        A("act", lambda e: e.activation(out=nA16[:, 0:1], in_=gp[:, 0:1], func=AF.Exp), reads=[R_gp], writes=[R_nA16])
        A("act", lambda e: e.activation(out=nA16[:, 1:2], in_=gp[:, 2:3], func=AF.Exp), reads=[R_gp], writes=[R_nA16])
        A("dve", lambda e: e.tensor_scalar(out=nA16[:], in0=nA16[:], scalar1=-1.0, scalar2=None, op0=ALU.mult), reads=[R_nA16], writes=[R_nA16])
        A("act", lambda e: e.activation(out=nAr[:], in_=gdtr[:, 0:2, :], func=AF.Exp), reads=[R_gdtr], writes=[R_nAr])
        A("dve", lambda e: e.tensor_scalar(out=nAr[:], in0=nAr[:], scalar1=-1.0, scalar2=None, op0=ALU.mult), reads=[R_nAr], writes=[R_nAr])
        A("pool", lambda e: e.memset(onesb[:], 1.0), writes=[R_onesb])
        A("dve", lambda e: e.tensor_scalar(out=capI[:], in0=m01[:], scalar1=-1.0, scalar2=30000.0, op0=ALU.add, op1=ALU.mult), reads=[R_m01], writes=[R_capI])
        A("dve", lambda e: e.tensor_tensor(out=capS[:], in0=m01[:], in1=idf[:], op=ALU.subtract), reads=[R_m01, R_idf], writes=[R_capS])
        A("dve", lambda e: e.tensor_scalar(out=capS[:], in0=capS[:], scalar1=-1.0, scalar2=30000.0, op0=ALU.add, op1=ALU.mult), reads=[R_capS], writes=[R_capS])
        A("pool", lambda e: e.memset(mU[:], 0.0), writes=[R_mU])
        A("pool", lambda e: e.memset(mU[0:64, 0:64], 1.0), reads=[R_mU], writes=[R_mU])
        A("pool", lambda e: e.memset(mU[64:128, 64:128], 1.0), reads=[R_mU], writes=[R_mU])
        A("dve", lambda e: e.tensor_tensor(out=mU[:], in0=mU[:], in1=m01[:], op=ALU.subtract), reads=[R_mU, R_m01], writes=[R_mU])
        A("pool", lambda e: e.memset(cind[:], 0.0), writes=[R_cind])
        A("pool", lambda e: e.memset(cind[0:64, 0, :], 1.0), reads=[R_cind], writes=[R_cind])
        A("pool", lambda e: e.memset(cind[64:128, 1, :], 1.0), reads=[R_cind], writes=[R_cind])
        A("dve", lambda e: e.tensor_scalar(out=Esel[:], in0=gp[:, 8:16].unsqueeze(2).broadcast_to([16, 8, 128]), scalar1=gp[:, 6:7], scalar2=None, op0=ALU.mult),
          reads=[R_gp], writes=[R_Esel])
        wab, R_wab = sb("wab", [128, 8, 32], BF16)
        dw, R_dw = sb("dw", [128, 9, 128], BF16)
        cpad, R_cpad = sb("cpad", [128, 400], BF16)
        cs, R_cs = sb("cs", [128, 256])
        sq, R_sq = sb("sq", [128, 256], BF16)
        rn, R_rn = sb("rn", [128, 256])
        qBf, R_qBf = sb("qBf", [128, 8, 256], BF16)
        kBf, R_kBf = sb("kBf", [128, 8, 256], BF16)
        vBf, R_vBf = sb("vBf", [128, 8, 256], BF16)
        qdB, R_qdB = sb("qdB", [128, 8, 256], BF16)
        rf = [sb("rf%d" % i, [16, 256]) for i in range(6)]
        rhsP, R_rhsP = sb("rhsP", [16, 8, 128])
        rhsA, R_rhsA = sb("rhsA", [16, 8, 128])
        tms = [sb("tms%d" % i, [128, 16]) for i in range(4)]
        bet, R_bet = sb("bet", [128, 8])
        bee, R_bee = sb("bee", [128, 8])
        ksf, R_ksf = sb("ksf", [128, 8])
        dB, R_dB = sb("dB", [128, 2, 8])
        kbe, R_kbe = sb("kbe", [128, 8, 128], BF16)
        kdc, R_kdc = sb("kdc", [128, 8, 128], BF16)
        bvt, R_bvt = sb("bvt", [128, 8, 128], BF16)
        eA, R_eA = sb("eA", [128, 8, 128])
        eP, R_eP = sb("eP", [128, 8, 128])
        XT = [sb("XT%d" % i, [128, 8, 128], BF16) for i in range(2)]
        XX = [sb("XX%d" % i, [128, 8, 128], BF16) for i in range(2)]
        QQ = [sb("QQ%d" % i, [128, 8, 128], BF16) for i in range(2)]
        PT, R_PT = sb("PT", [128, 8, 128], BF16)
        u0, R_u0 = sb("u0", [128, 8, 128])
        nwT, R_nwT = sb("nwT", [128, 8, 128], BF16)
        vnew, R_vnew = sb("vnew", [128, 8, 128], BF16)
        SB, R_SB = sb("SB", [128, 8, 128])
        SBb, R_SBb = sb("SBb", [128, 8, 128], BF16)
        A("dve", lambda e: e.memset(SB[:], 0.0), writes=[R_SB])
        A("pool", lambda e: e.memset(SBb[:], 0.0), writes=[R_SBb])
        fnw, R_fnw = sb("fnw", [128, D])
        hnwc, R_hnwc = sb("hnwc", [128, 16])
        hst, R_hst = sb("hst", [128, 32])
        A("sp", lambda e: e.dma_start(out=fnw[:], in_=fnw_d.partition_broadcast(128)), writes=[R_fnw], chan="p7")
        A("sp", lambda e: e.dma_start(out=hnwc[:], in_=hnw_d), writes=[R_hnwc], chan="p8")
        for kk in range(16):
            x_t, R_x = xt[1]
            A("sp", lambda e, kk=kk: e.dma_start(out=x_t[:], in_=wout_d[128 * kk:128 * kk + 128, :]), writes=[R_x], chan="x1")
            A("dve", lambda e, kk=kk: e.tensor_scalar(out=hb[:], in0=x_t[:], scalar1=hnwc[:, kk:kk + 1], scalar2=None, op0=ALU.mult), reads=[R_x, R_hnwc], writes=[R_hb])
            A("sp", lambda e, kk=kk: e.dma_start(out=woutb[128 * kk:128 * kk + 128, :], in_=hb[:]), reads=[R_hb], writes=[R_woutb], chan="wo_st")
        szs = [(Amod[:, 1, :].bitcast(BF16), R_Amod), (shmod[:, 1, :].bitcast(BF16), R_shmod)]
        yT2 = [XT[1], XX[1]]
        yb2 = [(kbe[:].rearrange("p h c -> p (h c)"), R_kbe), (kdc[:].rearrange("p h c -> p (h c)"), R_kdc)]
        dw2, R_dw2 = sb("dw2", [128, 9, 128], BF16)
        cpad2, R_cpad2 = sb("cpad2", [128, 400], BF16)
        CP = [(cpad, R_cpad), (cpad2, R_cpad2)]
        DW = [(dw, R_dw), (dw2, R_dw2)]
        print("sbuf bytes remaining", nc.sbuf_bytes_remaining)


        def load_w(c0, ncol):
            i = wctr[0] % 2
            wctr[0] += 1
            t, R = wt[i]
            A("sp", lambda e: e.dma_start(out=t[:, :, 0:ncol], in_=winb[:, c0:c0 + ncol].rearrange("(k p) n -> p k n", p=128)),
              reads=[R_winb], writes=[R], chan="w%d" % i)
            return t, R

        def xprep(src, row0, i, col0, j):
            x_t, R_x = xt[i % 2]
            A("sp", lambda e: e.dma_start(out=x_t[:], in_=src[row0:row0 + 128, :]), writes=[R_x], chan="x%d" % (i % 2))
            A("act", lambda e: e.activation(out=junk[:], in_=x_t[:], func=AF.Square, accum_out=ssq[:, 0:1]),
              reads=[R_x], writes=[R_junk, R_ssq])
            A("act", lambda e: e.activation(out=ssq[:, 1:2], in_=ssq[:, 0:1], func=AF.Ln, scale=1.0 / D, bias=EPS), reads=[R_ssq], writes=[R_ssq])
            A("act", lambda e: e.activation(out=ssq[:, 1:2], in_=ssq[:, 1:2], func=AF.Exp, scale=-0.5), reads=[R_ssq], writes=[R_ssq])
            A("dve", lambda e: e.scalar_tensor_tensor(out=hn[:], in0=x_t[:], scalar=ssq[:, 1:2], in1=Amod[:, j, :], op0=ALU.mult, op1=ALU.mult),
              reads=[R_x, R_ssq, R_Amod], writes=[R_hn])
            A("pool", lambda e: e.tensor_tensor(out=hb[:], in0=hn[:], in1=shmod[:, j, :], op=ALU.add), reads=[R_hn, R_shmod], writes=[R_hb])
            pt, R_pt = pm[7]
            ptb = pt[:].bitcast(BF16)
            for k in range(8):
                A("pe", lambda e, k=k: e.transpose(out=ptb[:, 128 * k:128 * k + 128], in_=hb[:, 128 * k:128 * k + 128], identity=idb[:]),
                  reads=[R_hb, R_idb], writes=[R_pt])
            A("act", lambda e: e.activation(out=hT[:, :, col0:col0 + 128], in_=ptb.rearrange("p (k t) -> p k t", t=128), func=AF.Copy),
              reads=[R_pt], writes=[R_hT])

        def group(s, src, row0, ntile_w, off, j, readout, gidx):
            T = 256
            for i in range(ntile_w):
                if ntile_w == 3 and gidx >= 1 and i == 0:
                    A("pool", lambda e: e.tensor_copy(out=hT[:, :, 0:128], in_=hT[:, :, 256:384]), reads=[R_hT], writes=[R_hT])
                    continue
                xprep(src, row0 + 128 * i, i, 128 * i, j)
            main = slice(off, off + T)
            for half in (range(2) if readout else ()):
                w, R_w = load_w(C_AQ + 512 * half, 512)
                for hh in range(4):
                    h = 4 * half + hh
                    pt, R_pt = pm[hh % 2]
                    for k in range(8):
                        A("pe", lambda e, k=k, hh=hh, pt=pt, w=w: e.matmul(pt[:, 0:T], lhsT=w[:, k, 128 * hh:128 * hh + 128], rhs=hT[:, k, main], start=(k == 0), stop=(k == 7)),
                          reads=[R_w, R_hT], writes=[R_pt])
                    A("act", lambda e, h=h, pt=pt: e.activation(out=qA[:, h, :], in_=pt[:, 0:T], func=AF.Silu), reads=[R_pt], writes=[R_qA])
            caf = C_AF0 if s == 0 else C_AF1
            (b, R_b), (dl, R_dl) = tf[2], tf[3]

            def fgA(h, w, R_w, hh):
                li = s * 8 + h
                pt, R_pt = pm[2 + h % 2]
                sig, R_sig = SG[h % 2][0]
                g, R_g = SG[h % 2][1]
                for k in range(8):
                    A("pe", lambda e, k=k, hh=hh, pt=pt, w=w: e.matmul(pt[:, 0:T], lhsT=w[:, k, 128 * hh:128 * hh + 128], rhs=hT[:, k, main], start=(k == 0), stop=(k == 7)),
                      reads=[R_w, R_hT], writes=[R_pt])
                A("act", lambda e, pt=pt, sig=sig: e.activation(out=sig[:], in_=pt[:, 0:T], func=AF.Sigmoid), reads=[R_pt], writes=[R_sig])
                A("act", lambda e, li=li, sig=sig, g=g: e.activation(out=g[:], in_=sig[:], func=AF.Ln, scale=oml[:, li:li + 1], bias=lb[:, li:li + 1]),
                  reads=[R_sig, R_oml, R_lb], writes=[R_g])
                A("dve", lambda e, li=li, sig=sig: e.tensor_scalar(out=sig[:], in0=sig[:], scalar1=-1.0, scalar2=lbm1[:, li:li + 1], op0=ALU.add, op1=ALU.mult),
                  reads=[R_sig, R_lbm1], writes=[R_sig])

            def fgB(h):
                sig, R_sig = SG[h % 2][0]
                g, R_g = SG[h % 2][1]
                A("dve", lambda e, g=g: e.tensor_tensor_scan(out=b[:], data0=rmask[:, 0:T], data1=g[:], initial=0.0, op0=ALU.mult, op1=ALU.add),
                  reads=[R_rmask, R_g], writes=[R_b])
                b3 = b[:].rearrange("p (c t) -> p c t", t=64)
                A("dve", lambda e, b3=b3: e.tensor_tensor(out=dl[:].rearrange("p (c t) -> p c t", t=64), in0=b3, in1=b3[:, :, 63:64].broadcast_to([128, 4, 64]), op=ALU.subtract),
                  reads=[R_b], writes=[R_dl])
                A("act", lambda e, h=h, b3=b3: e.activation(out=dA[:, h, :], in_=b3[:, :, 63], func=AF.Exp), reads=[R_b], writes=[R_dA])
                if readout:
                    A("act", lambda e, g=g: e.activation(out=g[:], in_=dl[:], func=AF.Exp), reads=[R_dl], writes=[R_g])
                A("act", lambda e: e.activation(out=b[:], in_=dl[:], func=AF.Exp, scale=-1.0), reads=[R_dl], writes=[R_b])
                if readout:
                    A("dve", lambda e, h=h, g=g: e.scalar_tensor_tensor(out=qt[:, h, :], in0=qA[:, h, :], scalar=float(128 ** -0.5), in1=g[:], op0=ALU.mult, op1=ALU.mult),
                      reads=[R_qA, R_g], writes=[R_qt])
                A("pool", lambda e, h=h, sig=sig: e.tensor_tensor(out=kdT[:, h, :], in0=sig[:], in1=b[:], op=ALU.mult),
                  reads=[R_sig, R_b], writes=[R_kdT])

            pendf = None
            for half in range(2):
                w, R_w = load_w(caf + 512 * half, 512)
                for hh in range(4):
                    h = 4 * half + hh
                    fgA(h, w, R_w, hh)
                    if pendf is not None:
                        fgB(pendf)
                    pendf = h
            fgB(pendf)
            for half in range(2):
                w, R_w = load_w(C_AI + 512 * half, 512)
                for ti in range(2):
                    pt, R_pt = pm[4 + ti]
                    for k in range(8):
                        A("pe", lambda e, k=k, ti=ti, pt=pt, w=w: e.matmul(pt[:], lhsT=hT[:, k, off + 128 * ti:off + 128 * ti + 128], rhs=w[:, k, :], start=(k == 0), stop=(k == 7)),
                          reads=[R_w, R_hT], writes=[R_pt])
                    A("act", lambda e, ti=ti, pt=pt, half=half: e.activation(out=vA[:, ti, 512 * half:512 * half + 512], in_=pt[:], func=AF.Copy),
                      reads=[R_pt], writes=[R_vA])
            for ti in range(2):
                pt, R_pt = pm[6]
                ptb = pt[:].bitcast(BF16)
                for h in range(8):
                    A("pe", lambda e, h=h, ti=ti, ptb=ptb: e.transpose(out=ptb[:, 128 * h:128 * h + 128], in_=kdT[:, h, 128 * ti:128 * ti + 128], identity=idb[:]),
                      reads=[R_kdT, R_idb], writes=[R_pt])
                A("dve", lambda e, ti=ti, ptb=ptb: e.tensor_copy(out=kdm[:, ti, :], in_=ptb), reads=[R_pt], writes=[R_kdm])
            if do_gdn:
                lat = (ntile_w == 3)
                WN = 128 * ntile_w
                if gidx <= 0:
                    A("pool", lambda e: e.memset(cpad[:], 0.0), writes=[R_cpad])
                    A("pool", lambda e: e.memset(cpad2[:], 0.0), writes=[R_cpad2])
                if lat:
                    cpv = cpad[:, 0:396].rearrange("p (r c) -> p r c", c=66)
                def stageA(blk, w, R_w, hh):
                    cpd, Rcp = CP[blk % 2]
                    cpv_ = cpd[:, 0:396].rearrange("p (r c) -> p r c", c=66)
                    pt, R_pt = pm[blk % 2]
                    for k in range(8):
                        A("pe", lambda e, k=k, hh=hh, pt=pt, w=w: e.matmul(pt[:, 0:WN], lhsT=w[:, k, 128 * hh:128 * hh + 128], rhs=hT[:, k, 0:WN], start=(k == 0), stop=(k == 7)),
                          reads=[R_w, R_hT], writes=[R_pt])
                    if lat:
                        A("act", lambda e, pt=pt, cpv_=cpv_: e.activation(out=cpv_[:, :, 1:65], in_=pt[:, 0:384].rearrange("p (r c) -> p r c", c=64), func=AF.Copy),
                          reads=[R_pt], writes=[Rcp])
                        if gidx == 0:
                            A("pool", lambda e, cpv_=cpv_: e.memset(cpv_[:, 0, :], 0.0), reads=[Rcp], writes=[Rcp])
                        if s == 1 and gidx == ng - 1:
                            A("pool", lambda e, cpv_=cpv_: e.memset(cpv_[:, 5, :], 0.0), reads=[Rcp], writes=[Rcp])
                    else:
                        A("act", lambda e, pt=pt, cpd=cpd: e.activation(out=cpd[:, 1:257], in_=pt[:, 0:256], func=AF.Copy), reads=[R_pt], writes=[Rcp])
                    dwb, Rdw = DW[blk % 2]
                    A("pool", lambda e, blk=blk, dwb=dwb: e.tensor_tensor(out=dwb[:], in0=idb[:].unsqueeze(1).broadcast_to([128, 9, 128]),
                                                                          in1=cws[:, s, blk, :].unsqueeze(2).broadcast_to([128, 9, 128]), op=ALU.mult),
                      reads=[R_idb, R_cws], writes=[Rdw])
                    pc, R_pc = pm[2 + blk % 2]
                    taps = [(i, jj) for i in range(3) for jj in range(3)] if lat else [(1, jj) for jj in range(3)]
                    for n, (i, jj) in enumerate(taps):
                        if lat:
                            A("pe", lambda e, i=i, jj=jj, n=n, pc=pc, nt=len(taps), dwb=dwb, cpv_=cpv_: e.matmul(pc[:, 0:256].rearrange("p (r c) -> p r c", c=64), lhsT=dwb[:, 3 * i + jj, :], rhs=cpv_[:, i:i + 4, jj:jj + 64], start=(n == 0), stop=(n == nt - 1)),
                              reads=[Rdw, Rcp], writes=[R_pc])
                        else:
                            A("pe", lambda e, i=i, jj=jj, n=n, pc=pc, nt=len(taps), dwb=dwb, cpd=cpd: e.matmul(pc[:, 0:256], lhsT=dwb[:, 3 * i + jj, :], rhs=cpd[:, jj:jj + 256], start=(n == 0), stop=(n == nt - 1)),
                              reads=[Rdw, Rcp], writes=[R_pc])

                def stageB(blk):
                    kind, h = blk // 8, blk % 8
                    pc, R_pc = pm[2 + blk % 2]
                    if kind == 2:
                        A("act", lambda e, h=h, pc=pc: e.activation(out=vBf[:, h, :], in_=pc[:, 0:256], func=AF.Silu), reads=[R_pc], writes=[R_vBf])
                    else:
                        A("act", lambda e, pc=pc: e.activation(out=cs[:], in_=pc[:, 0:256], func=AF.Silu), reads=[R_pc], writes=[R_cs])
                        A("act", lambda e: e.activation(out=sq[:], in_=cs[:], func=AF.Square), reads=[R_cs], writes=[R_sq])
                        pn, R_pn = pm[4 + blk % 2]
                        A("pe", lambda e, pn=pn: e.matmul(pn[:, 0:256], lhsT=onesb[:], rhs=sq[:], start=True, stop=True), reads=[R_onesb, R_sq], writes=[R_pn])
                        A("act", lambda e, pn=pn: e.activation(out=rn[:], in_=pn[:, 0:256], func=AF.Ln, bias=EPS), reads=[R_pn], writes=[R_rn])
                        A("act", lambda e: e.activation(out=rn[:], in_=rn[:], func=AF.Exp, scale=-0.5), reads=[R_rn], writes=[R_rn])
                        dst, R_dst = (qBf, R_qBf) if kind == 0 else (kBf, R_kBf)
                        sc_ = float(128 ** -0.5) if kind == 0 else 1.0
                        A("dve", lambda e, h=h, dst=dst, sc_=sc_: e.scalar_tensor_tensor(out=dst[:, h, :], in0=cs[:], scalar=sc_, in1=rn[:], op0=ALU.mult, op1=ALU.mult),
                          reads=[R_cs, R_rn], writes=[R_dst])

                pend = None
                for m in range(6):
                    if m < 2 and not readout:
                        continue
                    w, R_w = load_w(C_BQ + 512 * m, 512)
                    for hh in range(4):
                        blk = 4 * m + hh
                        stageA(blk, w, R_w, hh)
                        if pend is not None:
                            stageB(pend)
                        pend = blk
                if pend is not None:
                    stageB(pend)
                if gidx <= 0:
                    ca_, cb_ = C_BA + 8 * s, C_BB + 8 * s
                    for q_, c_ in enumerate((ca_, ca_, cb_, cb_)):
                        A("sp", lambda e, q_=q_, c_=c_: e.dma_start(out=wab[:, :, 8 * q_:8 * q_ + 8], in_=winb[:, c_:c_ + 8].rearrange("(k p) n -> p k n", p=128)),
                          reads=[R_winb], writes=[R_wab], chan="wab")
                pz, R_pz = pm[6]
                for k in range(8):
                    A("pe", lambda e, k=k: e.matmul(pz[0:16, 0:256], lhsT=wab[:, k, 0:16], rhs=hT[:, k, main], start=(k == 0), stop=(k == 7)), reads=[R_wab, R_hT], writes=[R_pz])
                for k in range(8):
                    A("pe", lambda e, k=k: e.matmul(pz[0:16, 256:512], lhsT=wab[:, k, 16:32], rhs=hT[:, k, main], start=(k == 0), stop=(k == 7)), reads=[R_wab, R_hT], writes=[R_pz])
                (r0, R_r0), (r1, R_r1), (r2, R_r2), (r3, R_r3), (r4, R_r4), (r5, R_r5) = rf
                A("act", lambda e: e.activation(out=r0[:], in_=pz[0:16, 0:256], func=AF.Exp, bias=gp[:, 1 + 2 * s:2 + 2 * s]), reads=[R_pz, R_gp], writes=[R_r0])
                A("act", lambda e: e.activation(out=r0[:], in_=r0[:], func=AF.Ln, bias=1.0), reads=[R_r0], writes=[R_r0])
                A("dve", lambda e: e.tensor_scalar(out=r0[:], in0=r0[:], scalar1=nA16[:, s:s + 1], scalar2=None, op0=ALU.mult), reads=[R_r0, R_nA16], writes=[R_r0])
                A("act", lambda e: e.activation(out=r1[:], in_=pz[0:16, 256:512], func=AF.Exp, scale=-1.0), reads=[R_pz], writes=[R_r1])
                A("act", lambda e: e.activation(out=r1[:], in_=r1[:], func=AF.Ln, bias=1.0), reads=[R_r1], writes=[R_r1])
                A("dve", lambda e: e.tensor_tensor_scan(out=r2[:], data0=rmask[0:16, 0:256], data1=r0[:], initial=0.0, op0=ALU.mult, op1=ALU.add), reads=[R_rmask, R_r0], writes=[R_r2])
                A("dve", lambda e: e.tensor_tensor(out=r1[:], in0=r2[:], in1=r1[:], op=ALU.subtract), reads=[R_r2, R_r1], writes=[R_r1])
                A("dve", lambda e: e.tensor_scalar(out=r3[:], in0=r2[:], scalar1=gp[:, 4:5], scalar2=gp[:, 5:6], op0=ALU.mult, op1=ALU.add), reads=[R_r2, R_gp], writes=[R_r3])
                A("dve", lambda e: e.tensor_scalar(out=r4[:], in0=r2[:], scalar1=gp[:, 6:7], scalar2=gp[:, 7:8], op0=ALU.mult, op1=ALU.add), reads=[R_r2, R_gp], writes=[R_r4])
                A("dve", lambda e: e.tensor_scalar(out=r5[:], in0=r1[:], scalar1=gp[:, 6:7], scalar2=gp[:, 7:8], op0=ALU.mult, op1=ALU.add), reads=[R_r1, R_gp], writes=[R_r5])
                for h in (range(8) if readout else ()):
                    pb, R_pb = pm[h % 2]
                    A("pe", lambda e, h=h, pb=pb: e.matmul(pb[:, 0:256], lhsT=Esel[:, h, :], rhs=r4[:], start=True, stop=True), reads=[R_Esel, R_r4], writes=[R_pb])
                    A("act", lambda e, pb=pb: e.activation(out=cs[:], in_=pb[:, 0:256], func=AF.Exp), reads=[R_pb], writes=[R_cs])
                    A("dve", lambda e, h=h: e.tensor_tensor(out=qdB[:, h, :], in0=qBf[:, h, :], in1=cs[:], op=ALU.mult), reads=[R_qBf, R_cs], writes=[R_qdB])
            if s == 1 and readout:
                for q_ in range(4):
                    w, R_w = load_w((C_AZ if q_ < 2 else C_BZ) + 512 * (q_ % 2), 512)
                    for ti in range(2):
                        pt, R_pt = pm[4 + ti]
                        for k in range(8):
                            A("pe", lambda e, k=k, ti=ti, pt=pt, w=w: e.matmul(pt[:], lhsT=hT[:, k, off + 128 * ti:off + 128 * ti + 128], rhs=w[:, k, :], start=(k == 0), stop=(k == 7)),
                              reads=[R_w, R_hT], writes=[R_pt])
                        A("act", lambda e, ti=ti, pt=pt, q_=q_: e.activation(out=szs[ti][0][:, 512 * q_:512 * q_ + 512], in_=pt[:], func=AF.Silu), reads=[R_pt], writes=[szs[ti][1]])
            for ti in range(2):
                tok = slice(128 * ti, 128 * ti + 128)
                if readout:
                    for hq in range(2):
                        pt, R_pt = pm[hq]
                        for hh in range(4):
                            h = 4 * hq + hh
                            A("pe", lambda e, h=h, hh=hh, pt=pt, tok=tok: e.matmul(pt[:, 128 * hh:128 * hh + 128], lhsT=kdT[:, h, tok], rhs=qt[:, h, tok], start=True, stop=True),
                              reads=[R_kdT, R_qt], writes=[R_pt])
                        A("dve", lambda e, hq=hq, pt=pt: e.tensor_tensor(out=scT[:, 4 * hq:4 * hq + 4, :], in0=pt[:].rearrange("p (h t) -> p h t", t=128),
                                                                         in1=m01[:].unsqueeze(1).broadcast_to([128, 4, 128]), op=ALU.mult),
                          reads=[R_pt, R_m01], writes=[R_scT])
                po = [pm[2], pm[3]]
                pre = (s == 1 and readout)
                if pre:
                    nr = NTH - 128 - ((gidx - ng // 2) * 256 + 128 * ti)
                    A("sp", lambda e, nr=nr: e.dma_start(out=osb[:], in_=o1_d[nr:nr + 128, :]), reads=[R_o1], writes=[R_osb], chan="o1l")
                    for q_, (pt, R_pt) in enumerate((pm[2], pm[3], pm[6], pm[7])):
                        A("pe", lambda e, q_=q_, pt=pt: e.matmul(pt[:], lhsT=jf[:], rhs=osb[:, 512 * q_:512 * q_ + 512], start=True, stop=False), reads=[R_jf, R_osb], writes=[R_pt])
                for c in range(2):
                    rows = slice(64 * c, 64 * c + 64)
                    ctok = slice(128 * ti + 64 * c, 128 * ti + 64 * c + 64)
                    cg = 2 * ti + c
                    A("dve", lambda e, cg=cg: e.tensor_tensor(out=SAd[:], in0=SA[:], in1=dA[:, :, cg:cg + 1].broadcast_to([128, 8, 128]), op=ALU.mult),
                      reads=[R_SA, R_dA], writes=[R_SAd])
                    if readout:
                        A("act", lambda e: e.activation(out=SAb[:], in_=SAd[:], func=AF.Copy), reads=[R_SAd], writes=[R_SAb])
                        for h in range(8):
                            pt, R_pt = po[h // 4]
                            cols = slice(128 * (h % 4), 128 * (h % 4) + 128)
                            A("pe", lambda e, h=h, pt=pt, cols=cols, rows=rows, ctok=ctok: e.matmul(pt[rows, cols], lhsT=qt[:, h, ctok], rhs=SAb[:, h, :], start=(not pre), stop=False),
                              reads=[R_qt, R_SAb], writes=[R_pt])
                            A("pe", lambda e, h=h, pt=pt, cols=cols, rows=rows, ti=ti: e.matmul(pt[rows, cols], lhsT=scT[rows, h, rows], rhs=vA[rows, ti, 128 * h:128 * h + 128], start=False, stop=True),
                              reads=[R_scT, R_vA], writes=[R_pt])
                    for hq in range(2):
                        pt, R_pt = pm[4 + hq]
                        for hh in range(4):
                            h = 4 * hq + hh
                            A("pe", lambda e, h=h, hh=hh, pt=pt, rows=rows, ti=ti: e.matmul(pt[:, 128 * hh:128 * hh + 128], lhsT=kdm[rows, ti, 128 * h:128 * h + 128], rhs=vA[rows, ti, 128 * h:128 * h + 128], start=True, stop=True),
                              reads=[R_kdm, R_vA], writes=[R_pt])
                        A("dve", lambda e, hq=hq, pt=pt: e.tensor_tensor(out=SA[:, 4 * hq:4 * hq + 4, :], in0=pt[:].rearrange("p (h v) -> p h v", v=128), in1=SAd[:, 4 * hq:4 * hq + 4, :], op=ALU.add),
                          reads=[R_pt, R_SAd], writes=[R_SA])
                if readout:
                    for hq in range(2):
                        pt, R_pt = po[hq]
                        A("act", lambda e, hq=hq, pt=pt: e.activation(out=osb[:, 512 * hq:512 * hq + 512], in_=pt[:], func=AF.Copy), reads=[R_pt], writes=[R_osb])
                if do_gdn:
                    (t0, R_t0), (t1, R_t1), (t2, R_t2), (t3, R_t3) = tms
                    mtok = slice(off + 128 * ti, off + 128 * ti + 128)
                    pz, R_pz = pm[0]
                    for k in range(8):
                        A("pe", lambda e, k=k, mtok=mtok: e.matmul(pz[:, 0:16], lhsT=hT[:, k, mtok], rhs=wab[:, k, 8:24], start=(k == 0), stop=(k == 7)), reads=[R_wab, R_hT], writes=[R_pz])
                    A("dve", lambda e: e.tensor_tensor(out=t0[:, 0:8], in0=pz[:, 0:8], in1=gdtr[:, 2 + s, :], op=ALU.add), reads=[R_pz, R_gdtr], writes=[R_t0])
                    A("act", lambda e: e.activation(out=t0[:, 0:8], in_=t0[:, 0:8], func=AF.Exp), reads=[R_t0], writes=[R_t0])
                    A("act", lambda e: e.activation(out=t0[:, 0:8], in_=t0[:, 0:8], func=AF.Ln, bias=1.0), reads=[R_t0], writes=[R_t0])
                    A("dve", lambda e: e.tensor_tensor(out=t0[:, 0:8], in0=t0[:, 0:8], in1=nAr[:, s, :], op=ALU.mult), reads=[R_t0, R_nAr], writes=[R_t0])
                    A("act", lambda e: e.activation(out=t0[:, 8:16], in_=pz[:, 8:16], func=AF.Exp, scale=-1.0), reads=[R_pz], writes=[R_t0])
                    A("dve", lambda e: e.tensor_scalar(out=t0[:, 8:16], in0=t0[:, 8:16], scalar1=1.0, scalar2=None, op0=ALU.add), reads=[R_t0], writes=[R_t0])
                    A("dve", lambda e: e.reciprocal(out=bet[:], in_=t0[:, 8:16]), reads=[R_t0], writes=[R_bet])
                    pz2, R_pz2 = pm[1]
                    A("pe", lambda e: e.matmul(pz2[:, 0:8], lhsT=m01[:], rhs=t0[:, 0:8], start=True, stop=True), reads=[R_m01, R_t0], writes=[R_pz2])
                    A("pe", lambda e: e.matmul(pz2[:, 8:16], lhsT=mU[:], rhs=t0[:, 0:8], start=True, stop=True), reads=[R_mU, R_t0], writes=[R_pz2])
                    for c in range(2):
                        A("pe", lambda e, c=c: e.matmul(pz2[:, 16 + 8 * c:24 + 8 * c], lhsT=cind[:, c, :], rhs=t0[:, 0:8], start=True, stop=True), reads=[R_cind, R_t0], writes=[R_pz2])
                    A("act", lambda e: e.activation(out=bee[:], in_=pz2[:, 0:8], func=AF.Exp), reads=[R_pz2], writes=[R_bee])
                    A("act", lambda e: e.activation(out=ksf[:], in_=pz2[:, 8:16], func=AF.Exp), reads=[R_pz2], writes=[R_ksf])
                    A("act", lambda e: e.activation(out=dB[:].rearrange("p c h -> p (c h)"), in_=pz2[:, 16:32], func=AF.Exp), reads=[R_pz2], writes=[R_dB])
                    A("dve", lambda e: e.tensor_tensor(out=bee[:], in0=bee[:], in1=bet[:], op=ALU.mult), reads=[R_bee, R_bet], writes=[R_bee])
                    pk, R_pk = pm[0]
                    pkb = pk[:].bitcast(BF16)
                    for h in range(8):
                        A("pe", lambda e, h=h, tok=tok: e.transpose(out=pkb[:, 128 * h:128 * h + 128], in_=kBf[:, h, tok], identity=idb[:]), reads=[R_kBf, R_idb], writes=[R_pk])
                    pk3 = pkb.rearrange("p (h c) -> p h c", c=128)
                    A("dve", lambda e: e.tensor_tensor(out=kbe[:], in0=pk3, in1=bee[:].unsqueeze(2).broadcast_to([128, 8, 128]), op=ALU.mult), reads=[R_pk, R_bee], writes=[R_kbe])
                    A("dve", lambda e: e.tensor_tensor(out=kdc[:], in0=pk3, in1=ksf[:].unsqueeze(2).broadcast_to([128, 8, 128]), op=ALU.mult), reads=[R_pk, R_ksf], writes=[R_kdc])
                    pv_, R_pv_ = pm[1]
                    pvb = pv_[:].bitcast(BF16)
                    for h in range(8):
                        A("pe", lambda e, h=h, tok=tok: e.transpose(out=pvb[:, 128 * h:128 * h + 128], in_=vBf[:, h, tok], identity=idb[:]), reads=[R_vBf, R_idb], writes=[R_pv_])
                    A("dve", lambda e: e.tensor_tensor(out=bvt[:], in0=pvb.rearrange("p (h c) -> p h c", c=128), in1=bet[:].unsqueeze(2).broadcast_to([128, 8, 128]), op=ALU.mult),
                      reads=[R_pv_, R_bet], writes=[R_bvt])
                    A("dve", lambda e, tok=tok: e.tensor_tensor(out=rhsP[:], in0=rf[4][0][:, tok].unsqueeze(1).broadcast_to([16, 8, 128]), in1=gp[:, 8:16].unsqueeze(2).broadcast_to([16, 8, 128]), op=ALU.mult),
                      reads=[rf[4][1], R_gp], writes=[R_rhsP])
                    A("dve", lambda e, tok=tok: e.tensor_tensor(out=rhsA[:], in0=rf[5][0][:, tok].unsqueeze(1).broadcast_to([16, 8, 128]), in1=gp[:, 8:16].unsqueeze(2).broadcast_to([16, 8, 128]), op=ALU.mult),
                      reads=[rf[5][1], R_gp], writes=[R_rhsA])
                    for (rh, R_rh, cap, R_cap, ee, R_ee, b0) in (((rhsA, R_rhsA, capS, R_capS, eA, R_eA, 2), (rhsP, R_rhsP, capI, R_capI, eP, R_eP, 4)) if readout else ((rhsA, R_rhsA, capS, R_capS, eA, R_eA, 2),)):
                        for hq in range(2):
                            pd, R_pd = pm[b0 + hq]
                            for hh in range(4):
                                h = 4 * hq + hh
                                A("pe", lambda e, h=h, hh=hh, pd=pd, rh=rh, tok=tok: e.matmul(pd[:, 128 * hh:128 * hh + 128], lhsT=rf[3][0][:, tok], rhs=rh[:, h, :], start=True, stop=True),
                                  reads=[rf[3][1], R_rh], writes=[R_pd])
                            A("dve", lambda e, hq=hq, pd=pd, cap=cap, ee=ee: e.tensor_tensor(out=ee[:, 4 * hq:4 * hq + 4, :], in0=pd[:].rearrange("p (h t) -> p h t", t=128),
                                                                                           in1=cap[:].unsqueeze(1).broadcast_to([128, 4, 128]), op=ALU.min),
                              reads=[R_pd, R_cap], writes=[R_ee])
                        A("act", lambda e, ee=ee: e.activation(out=ee[:], in_=ee[:], func=AF.Exp), reads=[R_ee], writes=[R_ee])
                    (xt0_, R_xt0), (xx0_, R_xx0), (q0_, R_q0) = XT[0], XX[0], QQ[0]
                    for hq in range(2):
                        pa, R_pa = pm[2 + hq]
                        pb, R_pb = pm[4 + hq]
                        for hh in range(4):
                            h = 4 * hq + hh
                            A("pe", lambda e, h=h, hh=hh, pa=pa, tok=tok: e.matmul(pa[:, 128 * hh:128 * hh + 128], lhsT=kBf[:, h, tok], rhs=kBf[:, h, tok], start=True, stop=True), reads=[R_kBf], writes=[R_pa])
                            if readout:
                                A("pe", lambda e, h=h, hh=hh, pb=pb, tok=tok: e.matmul(pb[:, 128 * hh:128 * hh + 128], lhsT=kBf[:, h, tok], rhs=qBf[:, h, tok], start=True, stop=True), reads=[R_kBf, R_qBf], writes=[R_pb])
                        A("dve", lambda e, hq=hq, pa=pa: e.scalar_tensor_tensor(out=xt0_[:, 4 * hq:4 * hq + 4, :], in0=pa[:].rearrange("p (h t) -> p h t", t=128), scalar=-1.0, in1=eA[:, 4 * hq:4 * hq + 4, :], op0=ALU.mult, op1=ALU.mult),
                          reads=[R_pa, R_eA], writes=[R_xt0])
                        if readout:
                            A("dve", lambda e, hq=hq, pb=pb: e.tensor_tensor(out=PT[:, 4 * hq:4 * hq + 4, :], in0=pb[:].rearrange("p (h t) -> p h t", t=128), in1=eP[:, 4 * hq:4 * hq + 4, :], op=ALU.mult),
                              reads=[R_pb, R_eP], writes=[R_PT])
                    px, R_px = pm[0]
                    pxb = px[:].bitcast(BF16)
                    for h in range(8):
                        A("pe", lambda e, h=h: e.transpose(out=pxb[:, 128 * h:128 * h + 128], in_=xt0_[:, h, :], identity=idb[:]), reads=[R_xt0, R_idb], writes=[R_px])
                    A("act", lambda e: e.activation(out=xx0_[:], in_=pxb.rearrange("p (h c) -> p h c", c=128), func=AF.Copy), reads=[R_px], writes=[R_xx0])
                    A("pool", lambda e: e.tensor_tensor(out=q0_[:], in0=xt0_[:], in1=idb[:].unsqueeze(1).broadcast_to([128, 8, 128]), op=ALU.add), reads=[R_xt0, R_idb], writes=[R_q0])
                    for l in range(6):
                        (xtc, R_xtc), (xxc, R_xxc) = XT[l % 2], XX[l % 2]
                        (xtn, R_xtn), (xxn, R_xxn) = XT[(l + 1) % 2], XX[(l + 1) % 2]
                        (qc, R_qc), (qn, R_qn) = QQ[l % 2], QQ[(l + 1) % 2]
                        for hq in range(2):
                            hs = slice(4 * hq, 4 * hq + 4)
                            if l < 5:
                                p1, R_p1 = pm[2 + hq]
                                p2, R_p2 = pm[4 + hq]
                                for hh in range(4):
                                    h = 4 * hq + hh
                                    A("pe", lambda e, h=h, hh=hh, p1=p1, xxc=xxc, xtc=xtc: e.matmul(p1[:, 128 * hh:128 * hh + 128], lhsT=xxc[:, h, :], rhs=xtc[:, h, :], start=True, stop=True), reads=[R_xxc, R_xtc], writes=[R_p1])
                                    A("pe", lambda e, h=h, hh=hh, p2=p2, xxc=xxc, xtc=xtc: e.matmul(p2[:, 128 * hh:128 * hh + 128], lhsT=xtc[:, h, :], rhs=xxc[:, h, :], start=True, stop=True), reads=[R_xxc, R_xtc], writes=[R_p2])
                            if l >= 1:
                                p3, R_p3 = pm[hq]
                                for hh in range(4):
                                    h = 4 * hq + hh
                                    A("pe", lambda e, h=h, hh=hh, p3=p3, xxc=xxc, qc=qc: e.matmul(p3[:, 128 * hh:128 * hh + 128], lhsT=xxc[:, h, :], rhs=qc[:, h, :], start=True, stop=True), reads=[R_xxc, R_qc], writes=[R_p3])
                                A("dve", lambda e, hs=hs, p3=p3, qc=qc, qn=qn: e.tensor_tensor(out=qn[:, hs, :], in0=p3[:].rearrange("p (h t) -> p h t", t=128), in1=qc[:, hs, :], op=ALU.add),
                                  reads=[R_p3, R_qc], writes=[R_qn])
                            if l < 5:
                                A("act", lambda e, hs=hs, p1=p1, xtn=xtn: e.activation(out=xtn[:, hs, :], in_=p1[:].rearrange("p (h t) -> p h t", t=128), func=AF.Copy), reads=[R_p1], writes=[R_xtn])
                                A("pool" if False else "dve", lambda e, hs=hs, p2=p2, xxn=xxn: e.tensor_copy(out=xxn[:, hs, :], in_=p2[:].rearrange("p (h t) -> p h t", t=128)), reads=[R_p2], writes=[R_xxn])
                        if l == 0:
                            A("pool", lambda e: e.tensor_copy(out=QQ[1][0][:], in_=QQ[0][0][:]), reads=[QQ[0][1]], writes=[QQ[1][1]])
                    Qf, R_Qf = QQ[0]
                    for hq in range(2):
                        p1, R_p1 = pm[2 + hq]
                        p2, R_p2 = pm[4 + hq]
                        for hh in range(4):
                            h = 4 * hq + hh
                            A("pe", lambda e, h=h, hh=hh, p1=p1: e.matmul(p1[:, 128 * hh:128 * hh + 128], lhsT=Qf[:, h, :], rhs=bvt[:, h, :], start=True, stop=True), reads=[R_Qf, R_bvt], writes=[R_p1])
                            A("pe", lambda e, h=h, hh=hh, p2=p2: e.matmul(p2[:, 128 * hh:128 * hh + 128], lhsT=kbe[:, h, :], rhs=Qf[:, h, :], start=True, stop=True), reads=[R_Qf, R_kbe], writes=[R_p2])
                        A("act", lambda e, hq=hq, p1=p1: e.activation(out=u0[:, 4 * hq:4 * hq + 4, :], in_=p1[:].rearrange("p (h t) -> p h t", t=128), func=AF.Copy), reads=[R_p1], writes=[R_u0])
                        A("act", lambda e, hq=hq, p2=p2: e.activation(out=nwT[:, 4 * hq:4 * hq + 4, :], in_=p2[:].rearrange("p (h t) -> p h t", t=128), func=AF.Copy, scale=-1.0), reads=[R_p2], writes=[R_nwT])
                    pob = [pm[6], pm[7]]
                    for c in range(2):
                        rows = slice(64 * c, 64 * c + 64)
                        ctok = slice(128 * ti + 64 * c, 128 * ti + 64 * c + 64)
                        for hq in range(2):
                            p1, R_p1 = pm[hq]
                            for hh in range(4):
                                h = 4 * hq + hh
                                A("pe", lambda e, h=h, hh=hh, p1=p1, rows=rows: e.matmul(p1[rows, 128 * hh:128 * hh + 128], lhsT=nwT[:, h, rows], rhs=SBb[:, h, :], start=True, stop=True), reads=[R_nwT, R_SBb], writes=[R_p1])
                            A("dve", lambda e, hq=hq, p1=p1, rows=rows: e.tensor_tensor(out=vnew[rows, 4 * hq:4 * hq + 4, :], in0=p1[rows, :].rearrange("p (h t) -> p h t", t=128), in1=u0[rows, 4 * hq:4 * hq + 4, :], op=ALU.add),
                              reads=[R_p1, R_u0], writes=[R_vnew])
                        for hq in range(2):
                            if readout:
                                po_, R_po = pob[hq]
                                for hh in range(4):
                                    h = 4 * hq + hh
                                    A("pe", lambda e, h=h, hh=hh, po_=po_, rows=rows, ctok=ctok: e.matmul(po_[rows, 128 * hh:128 * hh + 128], lhsT=qdB[:, h, ctok], rhs=SBb[:, h, :], start=(not pre), stop=False), reads=[R_qdB, R_SBb], writes=[R_po])
                                    A("pe", lambda e, h=h, hh=hh, po_=po_, rows=rows: e.matmul(po_[rows, 128 * hh:128 * hh + 128], lhsT=PT[rows, h, rows], rhs=vnew[rows, h, :], start=False, stop=True), reads=[R_PT, R_vnew], writes=[R_po])
                            p2, R_p2 = pm[2 + hq]
                            for hh in range(4):
                                h = 4 * hq + hh
                                A("pe", lambda e, h=h, hh=hh, p2=p2, rows=rows: e.matmul(p2[:, 128 * hh:128 * hh + 128], lhsT=kdc[rows, h, :], rhs=vnew[rows, h, :], start=True, stop=True), reads=[R_kdc, R_vnew], writes=[R_p2])
                            hs = slice(4 * hq, 4 * hq + 4)
                            A("dve", lambda e, hs=hs, c=c: e.tensor_tensor(out=SB[:, hs, :], in0=SB[:, hs, :], in1=dB[:, c, hs].unsqueeze(2).broadcast_to([128, 4, 128]), op=ALU.mult), reads=[R_SB, R_dB], writes=[R_SB])
                            A("dve", lambda e, hs=hs, p2=p2: e.tensor_tensor(out=SB[:, hs, :], in0=p2[:].rearrange("p (h t) -> p h t", t=128), in1=SB[:, hs, :], op=ALU.add), reads=[R_p2, R_SB], writes=[R_SB])
                        A("act", lambda e: e.activation(out=SBb[:], in_=SB[:], func=AF.Copy), reads=[R_SB], writes=[R_SBb])
                    if readout:
                        for hq in range(2):
                            po_, R_po = pob[hq]
                            A("act", lambda e, hq=hq, po_=po_: e.activation(out=osb[:, D + 512 * hq:D + 512 * hq + 512], in_=po_[:], func=AF.Copy), reads=[R_po], writes=[R_osb])
                if readout and s == 1:
                    for hd in range(16):
                        A("act", lambda e, hd=hd: e.activation(out=junk[:, 0:128], in_=osb[:, 128 * hd:128 * hd + 128], func=AF.Square, accum_out=hst[:, hd:hd + 1]), reads=[R_osb], writes=[R_junk, R_hst])
                    A("act", lambda e: e.activation(out=hst[:, 16:32], in_=hst[:, 0:16], func=AF.Ln, scale=1.0 / 128, bias=EPS), reads=[R_hst], writes=[R_hst])
                    A("act", lambda e: e.activation(out=hst[:, 16:32], in_=hst[:, 16:32], func=AF.Exp, scale=-0.5), reads=[R_hst], writes=[R_hst])
                    szt, R_szt = szs[ti]
                    for q_ in range(4):
                        yb_, R_yb = yb2[q_ // 2]
                        cb0 = 512 * (q_ % 2)
                        A("dve", lambda e, q_=q_, szt=szt: e.tensor_tensor(out=osb[:, 512 * q_:512 * q_ + 512], in0=osb[:, 512 * q_:512 * q_ + 512], in1=szt[:, 512 * q_:512 * q_ + 512], op=ALU.mult), reads=[R_osb, R_szt], writes=[R_osb])
                        A("dve", lambda e, q_=q_, yb_=yb_, cb0=cb0: e.tensor_tensor(out=yb_[:, cb0:cb0 + 512].rearrange("p (h c) -> p h c", c=128), in0=osb[:, 512 * q_:512 * q_ + 512].rearrange("p (h c) -> p h c", c=128),
                                                                                in1=hst[:, 16 + 4 * q_:20 + 4 * q_].unsqueeze(2).broadcast_to([128, 4, 128]), op=ALU.mult), reads=[R_osb, R_hst], writes=[R_yb])
                    for half in range(2):
                        yb_, R_yb = yb2[half]
                        yT_, R_yT = yT2[half]
                        pt, R_pt = pm[half]
                        ptb = pt[:].bitcast(BF16)
                        for kk in range(8):
                            A("pe", lambda e, kk=kk, yb_=yb_, ptb=ptb: e.transpose(out=ptb[:, 128 * kk:128 * kk + 128], in_=yb_[:, 128 * kk:128 * kk + 128], identity=idb[:]), reads=[R_yb, R_idb], writes=[R_pt])
                        A("act", lambda e, yT_=yT_, ptb=ptb: e.activation(out=yT_[:], in_=ptb.rearrange("p (k t) -> p k t", t=128), func=AF.Copy), reads=[R_pt], writes=[R_yT])
                    x_t, R_x = xt[ti % 2]
                    xrow = 64 + gidx * 256 + 128 * ti
                    A("sp", lambda e, xrow=xrow, x_t=x_t: e.dma_start(out=x_t[:], in_=src[xrow:xrow + 128, :]), writes=[R_x], chan="x%d" % (ti % 2))
                    for ch in range(2):
                        pt, R_pt = pm[2 + ch]
                        for kh in range(2):
                            i = wctr[0] % 2
                            wctr[0] += 1
                            wt_, R_wt = wt[i]
                            A("sp", lambda e, kh=kh, ch=ch, wt_=wt_: e.dma_start(out=wt_[:], in_=woutb[1024 * kh:1024 * kh + 1024, 512 * ch:512 * ch + 512].rearrange("(k p) n -> p k n", p=128)),
                              reads=[R_woutb], writes=[R_wt], chan="w%d" % i)
                            yT_, R_yT = yT2[kh]
                            for k in range(8):
                                A("pe", lambda e, k=k, kh=kh, pt=pt, yT_=yT_, wt_=wt_: e.matmul(pt[:], lhsT=yT_[:, k, :], rhs=wt_[:, k, :], start=(kh == 0 and k == 0), stop=(kh == 1 and k == 7)),
                                  reads=[R_yT, R_wt], writes=[R_pt])
                        A("dve", lambda e, ch=ch, pt=pt: e.tensor_tensor(out=hn[:, 512 * ch:512 * ch + 512], in0=pt[:], in1=gate[:, 512 * ch:512 * ch + 512], op=ALU.mult), reads=[R_pt, R_gate], writes=[R_hn])
                    A("dve", lambda e, x_t=x_t: e.tensor_tensor(out=hn[:], in0=hn[:], in1=x_t[:], op=ALU.add), reads=[R_hn, R_x], writes=[R_hn])
                    A("act", lambda e: e.activation(out=junk[:], in_=hn[:], func=AF.Square, accum_out=ssq[:, 0:1]), reads=[R_hn], writes=[R_junk, R_ssq])
                    A("act", lambda e: e.activation(out=ssq[:, 1:2], in_=ssq[:, 0:1], func=AF.Ln, scale=1.0 / D, bias=EPS), reads=[R_ssq], writes=[R_ssq])
                    A("act", lambda e: e.activation(out=ssq[:, 1:2], in_=ssq[:, 1:2], func=AF.Exp, scale=-0.5), reads=[R_ssq], writes=[R_ssq])
                    A("dve", lambda e: e.scalar_tensor_tensor(out=hn[:], in0=hn[:], scalar=ssq[:, 1:2], in1=fnw[:], op0=ALU.mult, op1=ALU.mult), reads=[R_hn, R_ssq, R_fnw], writes=[R_hn])
                    orow = (gidx - ng // 2) * 256 + 128 * ti
                    A("sp", lambda e, orow=orow: e.dma_start(out=out_d[orow:orow + 128, :], in_=hn[:]), reads=[R_hn], chan="out")
                if readout and s == 0:
                    srow = gidx * 256 + 128 * ti
                    A("sp", lambda e, srow=srow: e.dma_start(out=o1_d[srow:srow + 128, :], in_=osb[:]), reads=[R_osb], writes=[R_o1], chan="o1")

        for s in range(2 if do_s2 else 1):
            if s == 1:
                A("dve", lambda e: e.memset(SA[:], 0.0), writes=[R_SA])
                A("dve", lambda e: e.memset(SB[:], 0.0), writes=[R_SB])
                A("pool", lambda e: e.memset(SBb[:], 0.0), writes=[R_SBb])
            group(s, cx[s], 0, 2, 0, 1, False, -1)
            if s == 0:
                for g in range(ng // 2):
                    group(s, xs[s], 256 * g, 3, 64, 0, True, g)
            else:
                for g in range(ng):
                    group(s, xs[s], 256 * g, 3, 64, 0, g >= ng // 2, g)

        P.emit(st, final_waits=["o1"] + (["out"] if do_s2 else []))
    return nc


def make_inputs(x, c, ctx, c_ctx, norm_w, ada_w, ada_b, w_in, conv_w, hg_lb_logits, gdn_a_log,
                gdn_dt_bias, ha_norm_w, hb_norm_w, w_out, final_norm_w, ncores=8):
    f = lambda a: np.ascontiguousarray(np.asarray(a, dtype=np.float32))
    S = x.shape[1]
    Hh = S // 2
    cw = conv_w[0].reshape(9, 3072)
    cwn = cw.T.reshape(24, 128, 9).transpose(1, 0, 2)
    cwf = cw[::-1].T.reshape(24, 128, 9).transpose(1, 0, 2)
    lbl4 = hg_lb_logits.reshape(2, 2, 8, 128).transpose(3, 0, 1, 2)
    hnw = np.concatenate([ha_norm_w[0].reshape(-1), hb_norm_w[0].reshape(-1)])
    z = np.zeros((64, D), np.float32)
    in_maps = []
    for core in range(ncores):
        b, half = core // 2, core % 2
        d1, d2 = (0, 1) if half == 0 else (1, 0)
        xb = x[b]
        if half == 0:
            xs1 = np.concatenate([z, xb[:Hh], xb[Hh:Hh + 64]], axis=0)
            xs2 = np.concatenate([z, xb[::-1], z], axis=0)
            ctx1, ctx2 = ctx[b], ctx[b][::-1]
            cwp = np.stack([cwn, cwf], axis=1)
        else:
            xs1 = np.concatenate([z, xb[Hh:][::-1], xb[Hh - 64:Hh][::-1]], axis=0)
            xs2 = np.concatenate([z, xb, z], axis=0)
            ctx1, ctx2 = ctx[b][::-1], ctx[b]
            cwp = np.stack([cwf, cwn], axis=1)
        w = np.array(w_in[0], dtype=np.float32, copy=True)
        if half == 1:
            w[:, C_AF0:C_AF0 + 1024], w[:, C_AF1:C_AF1 + 1024] = w_in[0][:, C_AF1:C_AF1 + 1024], w_in[0][:, C_AF0:C_AF0 + 1024]
            w[:, C_BA:C_BA + 8], w[:, C_BA + 8:C_BA + 16] = w_in[0][:, C_BA + 8:C_BA + 16], w_in[0][:, C_BA:C_BA + 8]
            w[:, C_BB:C_BB + 8], w[:, C_BB + 8:C_BB + 16] = w_in[0][:, C_BB + 8:C_BB + 16], w_in[0][:, C_BB:C_BB + 8]
        lbl = lbl4[:, :, [d1, d2], :].reshape(128, 32)
        gpar = np.zeros((16, 16), np.float32)
        for si, d in enumerate((d1, d2)):
            gpar[:, 2 * si] = np.tile(gdn_a_log[0, d], 2)
            gpar[:, 2 * si + 1] = np.tile(gdn_dt_bias[0, d], 2)
        gpar[0:8, 4] = -1.0
        gpar[8:16, 5] = 1.0
        gpar[8:16, 6] = 1.0
        gpar[0:8, 7] = 1.0
        for k in range(16):
            gpar[k, 8 + k % 8] = 1.0
        gdr = np.concatenate([gdn_a_log[0, d1], gdn_a_log[0, d2], gdn_dt_bias[0, d1], gdn_dt_bias[0, d2]])
        ccol = np.concatenate([c[b].reshape(8, 128).T, c_ctx.reshape(8, 128).T], axis=1)
        in_maps.append({
            "xs1": f(xs1), "xs2": f(xs2), "ctx1": f(ctx1), "ctx2": f(ctx2),
            "ccol": f(ccol), "norm_w": f(norm_w[0]), "final_norm_w": f(final_norm_w),
            "ada_w": f(ada_w[0]), "ada_b": f(ada_b[0]), "w_in": f(w), "w_out": f(w_out[0]),
            "lbl": f(lbl), "cw": f(cwp), "hnwc": f(hnw.reshape(16, 128).T), "gpar": f(gpar), "gdr": f(gdr),
        })
    return in_maps


def assemble(results, nb):
    outs = []
    for b in range(nb):
        outs.append(np.concatenate([results[2 * b]["out"][::-1], results[2 * b + 1]["out"]], axis=0))
    return np.ascontiguousarray(np.stack(outs, axis=0).astype(np.float32))


_NC_CACHE = {}


def kernel(**inputs):
    inputs = {k: np.asarray(v) for k, v in inputs.items()}
    in_maps = make_inputs(**inputs)
    if "nc" not in _NC_CACHE:
        _NC_CACHE["nc"] = build()
    res = run_bass_kernel_spmd(_NC_CACHE["nc"], in_maps, core_ids=list(range(8)))
    return assemble(res.results, 4)
```

```python
import contextlib
import numpy as np
import concourse.bass as bass
import concourse.mybir as mybir
from concourse.bass_utils import run_bass_kernel_spmd

F32 = mybir.dt.float32
BF16 = mybir.dt.bfloat16
AF = mybir.ActivationFunctionType
ALU = mybir.AluOpType

ENGS = ("pe", "act", "dve", "pool", "sp")

D = 1024
SEQ = 8192
CTXL = 256
NIN = 9248
EPS = 1e-6
C_AQ, C_AF0, C_AF1, C_AI, C_AZ = 0, 1024, 2048, 3072, 4096
C_BQ, C_BK, C_BV, C_BZ, C_BA, C_BB = 5120, 6144, 7168, 8192, 9216, 9232


class Res:
    __slots__ = ("name", "w", "rs")

    def __init__(self, name):
        self.name = name
        self.w = None
        self.rs = {}


class Op:
    __slots__ = ("eng", "fn", "deps", "chan", "chan_idx", "need_sig", "sig", "idx")

    def __init__(self, eng, fn, chan):
        self.eng = eng
        self.fn = fn
        self.chan = chan
        self.chan_idx = 0
        self.deps = ()
        self.need_sig = False
        self.sig = 0
        self.idx = 0


class Prog:
    def __init__(self, nc):
        self.nc = nc
        self.eng_ops = {e: [] for e in ENGS}
        self.chan_cnt = {}
        self.n = 0

    def op(self, eng, fn, reads=(), writes=(), chan=None):
        o = Op(eng, fn, chan)
        o.idx = self.n
        self.n += 1
        if chan is not None:
            c = self.chan_cnt.get(chan, 0) + 1
            self.chan_cnt[chan] = c
            o.chan_idx = c
        deps = {}
        for r in reads:
            if r.w is not None:
                deps[id(r.w)] = r.w
        for w in writes:
            if w.w is not None:
                deps[id(w.w)] = w.w
            for x in w.rs.values():
                deps[id(x)] = x
        o.deps = list(deps.values())
        for r in reads:
            key = eng if chan is None else ("dma", o.idx)
            r.rs[key] = o
        for w in writes:
            w.w = o
            w.rs = {}
        self.eng_ops[eng].append(o)
        return o

    def emit(self, st, final_waits=()):
        nc = self.nc
        esem = {e: st.enter_context(nc.semaphore("s_" + e)) for e in ENGS}
        csem = {c: st.enter_context(nc.semaphore("c_" + str(c))) for c in self.chan_cnt}
        for e in ENGS:
            for o in self.eng_ops[e]:
                for d in o.deps:
                    if d.chan is None:
                        if d.eng == "pe" and o.eng == "pe":
                            continue
                        d.need_sig = True
        for e in ENGS:
            c = 0
            for o in self.eng_ops[e]:
                if o.chan is None and o.need_sig:
                    c += 1
                    o.sig = c
        block = st.enter_context(nc.Block())

        def run(e, h):
            known = {}
            for o in self.eng_ops[e]:
                for d in o.deps:
                    if d.chan is not None:
                        key, val, sem = ("c", d.chan), 16 * d.chan_idx, csem[d.chan]
                    else:
                        if d.eng == "pe" and e == "pe":
                            continue
                        key, val, sem = ("e", d.eng), d.sig, esem[d.eng]
                    if known.get(key, 0) >= val:
                        continue
                    known[key] = val
                    h.wait_ge(sem, val)
                ins = o.fn(h)
                if o.chan is not None:
                    ins.then_inc(csem[o.chan], 16)
                elif o.need_sig:
                    ins.then_inc(esem[e], 1)
            if e == "sp":
                for ch in final_waits:
                    h.wait_ge(csem[ch], 16 * self.chan_cnt[ch])

        @block.tensor
        def _(h):
            run("pe", h)

        @block.scalar
        def _(h):
            run("act", h)

        @block.vector
        def _(h):
            run("dve", h)

        @block.gpsimd
        def _(h):
            run("pool", h)

        @block.sync
        def _(h):
            run("sp", h)


def build(ng=32, debug=False, do_gdn=True, do_s2=True):
    nc = bass.Bass("TRN2", target_bir_lowering=False)
    NT = ng * 256
    dram = lambda n, s, dt=F32, kind="ExternalInput": nc.dram_tensor(n, s, dt, kind=kind).ap()
    NTH = NT // 2
    xs = [dram("xs1", [NTH + 128, D]), dram("xs2", [NT + 128, D])]
    cx = [dram("ctx1", [CTXL, D]), dram("ctx2", [CTXL, D])]
    ccol_d = dram("ccol", [128, 16])
    normw_d = dram("norm_w", [D])
    fnw_d = dram("final_norm_w", [D])
    adaw_d = dram("ada_w", [D, 3 * D])
    adab_d = dram("ada_b", [3 * D])
    win_d = dram("w_in", [D, NIN])
    wout_d = dram("w_out", [2 * D, D])
    lbl_d = dram("lbl", [128, 32])
    cw_d = dram("cw", [128, 2, 24, 9])
    hnw_d = dram("hnwc", [128, 16])
    gpar_d = dram("gpar", [16, 16])
    gdr_d = dram("gdr", [32])
    out_d = dram("out", [NTH, D], kind="ExternalOutput")
    winb = dram("winb", [D, NIN], BF16, kind="Internal")
    woutb = dram("woutb", [2 * D, D], BF16, kind="Internal")
    o1_d = dram("o1", [NTH, 2 * D], F32, kind="ExternalOutput" if debug else "Internal")
    dbg_d = dram("dbg", [128, 4096], F32, kind="ExternalOutput") if debug else None

    P = Prog(nc)
    R_o1 = Res("o1")
    st = contextlib.ExitStack()
    with st:
        def sb(name, shape, dt=F32):
            return st.enter_context(nc.sbuf_tensor("sb_" + name, shape, dt)), Res(name)

        def ps(name, shape, dt=F32):
            return st.enter_context(nc.psum_tensor("ps_" + name, shape, dt)), Res(name)

        A = P.op

        idf, R_idf = sb("idf", [128, 128])
        idb, R_idb = sb("idb", [128, 128], BF16)
        jf, R_jf = sb("jf", [128, 128])
        m01, R_m01 = sb("m01", [128, 128])
        rmask, R_rmask = sb("rmask", [128, 256])
        A("pool", lambda e: e.memset(idf[:], 1.0), writes=[R_idf])
        A("pool", lambda e: e.affine_select(out=idf[:], in_=idf[:], pattern=[[-1, 128]], compare_op=ALU.is_equal,
                                            fill=0.0, base=0, channel_multiplier=1), reads=[R_idf], writes=[R_idf])
        A("dve", lambda e: e.tensor_copy(out=idb[:], in_=idf[:]), reads=[R_idf], writes=[R_idb])
        A("pool", lambda e: e.memset(jf[:], 1.0), writes=[R_jf])
        A("pool", lambda e: e.affine_select(out=jf[:], in_=jf[:], pattern=[[1, 128]], compare_op=ALU.is_equal,
                                            fill=0.0, base=-127, channel_multiplier=1), reads=[R_jf], writes=[R_jf])
        A("pool", lambda e: e.memset(m01[:], 1.0), writes=[R_m01])
        A("pool", lambda e: e.affine_select(out=m01[:], in_=m01[:], pattern=[[1, 128]], compare_op=ALU.is_ge,
                                            fill=0.0, base=0, channel_multiplier=-1), reads=[R_m01], writes=[R_m01])
        A("pool", lambda e: e.memset(m01[0:64, 64:128], 0.0), reads=[R_m01], writes=[R_m01])
        A("pool", lambda e: e.memset(rmask[:], 1.0), writes=[R_rmask])
        A("pool", lambda e: e.memset(rmask[:].rearrange("p (c t) -> p c t", t=64)[:, :, 0:1], 0.0),
          reads=[R_rmask], writes=[R_rmask])

        R_winb, R_woutb = Res("winb"), Res("woutb")
        for i in range(8):
            A("pool", lambda e, i=i: e.dma_start(out=winb[128 * i:128 * i + 128, :], in_=win_d[128 * i:128 * i + 128, :]),
              writes=[R_winb], chan="wcv")

        xt = [sb("xt%d" % i, [128, D]) for i in range(2)]
        qA, R_qA = sb("qA", [128, 8, 256])
        osb, R_osb = sb("osb", [128, 2 * D])
        ccol, R_ccol = sb("ccol", [128, 16])
        scb, R_scb = qA[:].rearrange("p a b -> p (a b)").rearrange("p (j t) -> p j t", t=128), R_qA
        adaw, R_adaw = xt[0][0][:].rearrange("p (k n) -> p k n", n=128), xt[0][1]
        nwb, R_nwb = osb[:, 0:D], R_osb
        adab, R_adab = osb[:, D:D + 128], R_osb
        Amod, R_Amod = sb("Amod", [128, 2, D])
        shmod, R_shmod = sb("shmod", [128, 2, D])
        gate, R_gate = sb("gate", [128, D])
        A("sp", lambda e: e.dma_start(out=ccol[:], in_=ccol_d), writes=[R_ccol], chan="p0")
        A("sp", lambda e: e.dma_start(out=nwb, in_=normw_d.partition_broadcast(128)), writes=[R_nwb], chan="p2")
        A("act", lambda e: e.activation(out=ccol[:], in_=ccol[:], func=AF.Silu), reads=[R_ccol], writes=[R_ccol])
        A("dve", lambda e: e.tensor_copy(out=scb, in_=ccol[:].unsqueeze(2).broadcast_to([128, 16, 128])),
          reads=[R_ccol], writes=[R_scb])
        pm = [ps("pm%d" % i, [128, 512]) for i in range(8)]
        for blk in range(24):
            A("sp", lambda e, blk=blk: e.dma_start(out=adaw, in_=adaw_d[:, 128 * blk:128 * blk + 128].rearrange("(k p) n -> p k n", p=128)),
              writes=[R_adaw], chan="adaw")
            A("sp", lambda e, blk=blk: e.dma_start(out=adab, in_=adab_d[128 * blk:128 * blk + 128].partition_broadcast(128)), writes=[R_adab], chan="p1")
            which, off = blk // 8, (blk % 8) * 128
            for j in range(2):
                pt, R_pt = pm[j]
                for k in range(8):
                    A("pe", lambda e, j=j, k=k, pt=pt: e.matmul(pt[:, 0:128], lhsT=scb[:, j * 8 + k, :], rhs=adaw[:, k, :], start=(k == 0), stop=(k == 7)),
                      reads=[R_scb, R_adaw], writes=[R_pt])
                if which == 0:
                    A("dve", lambda e, j=j, pt=pt, off=off: e.tensor_tensor(out=shmod[:, j, off:off + 128], in0=pt[:, 0:128], in1=adab, op=ALU.add),
                      reads=[R_pt, R_adab], writes=[R_shmod])
                elif which == 1:
                    A("dve", lambda e, j=j, pt=pt, off=off: e.scalar_tensor_tensor(out=Amod[:, j, off:off + 128], in0=pt[:, 0:128], scalar=1.0, in1=adab, op0=ALU.add, op1=ALU.add),
                      reads=[R_pt, R_adab], writes=[R_Amod])
                    A("dve", lambda e, j=j, off=off: e.tensor_tensor(out=Amod[:, j, off:off + 128], in0=Amod[:, j, off:off + 128], in1=nwb[:, off:off + 128], op=ALU.mult),
                      reads=[R_Amod, R_nwb], writes=[R_Amod])
                elif j == 0:
                    A("dve", lambda e, pt=pt, off=off: e.tensor_tensor(out=gate[:, off:off + 128], in0=pt[:, 0:128], in1=adab, op=ALU.add),
                      reads=[R_pt, R_adab], writes=[R_gate])

        lbl, R_lbl = sb("lbl", [128, 32])
        lb, R_lb = sb("lb", [128, 16])
        oml, R_oml = sb("oml", [128, 16])
        lbm1, R_lbm1 = sb("lbm1", [128, 16])
        A("sp", lambda e: e.dma_start(out=lbl[:], in_=lbl_d), writes=[R_lbl], chan="p3")
        A("dve", lambda e: e.tensor_tensor(out=lb[:], in0=lbl[:, 0:16], in1=lbl[:, 16:32], op=ALU.subtract), reads=[R_lbl], writes=[R_lb])
        A("act", lambda e: e.activation(out=lb[:], in_=lb[:], func=AF.Sigmoid), reads=[R_lb], writes=[R_lb])
        A("dve", lambda e: e.tensor_scalar(out=oml[:], in0=lb[:], scalar1=-1.0, scalar2=1.0, op0=ALU.mult, op1=ALU.add), reads=[R_lb], writes=[R_oml])
        A("dve", lambda e: e.tensor_scalar(out=lbm1[:], in0=lb[:], scalar1=-1.0, scalar2=None, op0=ALU.add), reads=[R_lb], writes=[R_lbm1])

        junk, R_junk = sb("junk", [128, D], BF16)
        ssq, R_ssq = sb("ssq", [128, 2])
        hn, R_hn = sb("hn", [128, D])
        hb, R_hb = sb("hb", [128, D], BF16)
        hT, R_hT = sb("hT", [128, 8, 384], BF16)
        wt = [sb("wt%d" % i, [128, 8, 512], BF16) for i in range(2)]
        wctr = [0]
        tf = [sb("tf%d" % i, [128, 256]) for i in range(4)]
        qt, R_qt = sb("qt", [128, 8, 256], BF16)
        kdT, R_kdT = sb("kdT", [128, 8, 256], BF16)
        dA, R_dA = sb("dA", [128, 8, 4])
        kdm, R_kdm = sb("kdm", [128, 2, D], BF16)
        vA, R_vA = sb("vA", [128, 2, D], BF16)
        scT, R_scT = sb("scT", [128, 8, 128], BF16)
        SA, R_SA = sb("SA", [128, 8, 128])
        SAd, R_SAd = sb("SAd", [128, 8, 128])
        SAb, R_SAb = sb("SAb", [128, 8, 128], BF16)
        A("dve", lambda e: e.memset(SA[:], 0.0), writes=[R_SA])
        cws, R_cws = sb("cws", [128, 2, 24, 9])
        gp, R_gp = sb("gp", [16, 16])
        nA16, R_nA16 = sb("nA16", [16, 2])
        gdtr, R_gdtr = sb("gdtr", [128, 4, 8])
        nAr, R_nAr = sb("nAr", [128, 2, 8])
        onesb, R_onesb = sb("onesb", [128, 128], BF16)
        capI, R_capI = sb("capI", [128, 128])
        capS, R_capS = sb("capS", [128, 128])
        mU, R_mU = sb("mU", [128, 128])
        cind, R_cind = sb("cind", [128, 2, 128])
        Esel, R_Esel = sb("Esel", [16, 8, 128])
        A("sp", lambda e: e.dma_start(out=cws[:], in_=cw_d), writes=[R_cws], chan="p4")
        A("sp", lambda e: e.dma_start(out=gp[:], in_=gpar_d), writes=[R_gp], chan="p5")
        A("sp", lambda e: e.dma_start(out=gdtr[:].rearrange("p a h -> p (a h)"), in_=gdr_d.partition_broadcast(128)), writes=[R_gdtr], chan="p6")
        A("act", lambda e: e.activation(out=nA16[:, 0:1], in_=gp[:, 0:1], func=AF.Exp), reads=[R_gp], writes=[R_nA16])
        A("act", lambda e: e.activation(out=nA16[:, 1:2], in_=gp[:, 2:3], func=AF.Exp), reads=[R_gp], writes=[R_nA16])
        A("dve", lambda e: e.tensor_scalar(out=nA16[:], in0=nA16[:], scalar1=-1.0, scalar2=None, op0=ALU.mult), reads=[R_nA16], writes=[R_nA16])
        A("act", lambda e: e.activation(out=nAr[:], in_=gdtr[:, 0:2, :], func=AF.Exp), reads=[R_gdtr], writes=[R_nAr])
        A("dve", lambda e: e.tensor_scalar(out=nAr[:], in0=nAr[:], scalar1=-1.0, scalar2=None, op0=ALU.mult), reads=[R_nAr], writes=[R_nAr])
        A("pool", lambda e: e.memset(onesb[:], 1.0), writes=[R_onesb])
        A("dve", lambda e: e.tensor_scalar(out=capI[:], in0=m01[:], scalar1=-1.0, scalar2=30000.0, op0=ALU.add, op1=ALU.mult), reads=[R_m01], writes=[R_capI])
        A("dve", lambda e: e.tensor_tensor(out=capS[:], in0=m01[:], in1=idf[:], op=ALU.subtract), reads=[R_m01, R_idf], writes=[R_capS])
        A("dve", lambda e: e.tensor_scalar(out=capS[:], in0=capS[:], scalar1=-1.0, scalar2=30000.0, op0=ALU.add, op1=ALU.mult), reads=[R_capS], writes=[R_capS])
        A("pool", lambda e: e.memset(mU[:], 0.0), writes=[R_mU])
        A("pool", lambda e: e.memset(mU[0:64, 0:64], 1.0), reads=[R_mU], writes=[R_mU])
        A("pool", lambda e: e.memset(mU[64:128, 64:128], 1.0), reads=[R_mU], writes=[R_mU])
        A("dve", lambda e: e.tensor_tensor(out=mU[:], in0=mU[:], in1=m01[:], op=ALU.subtract), reads=[R_mU, R_m01], writes=[R_mU])
        A("pool", lambda e: e.memset(cind[:], 0.0), writes=[R_cind])
        A("pool", lambda e: e.memset(cind[0:64, 0, :], 1.0), reads=[R_cind], writes=[R_cind])
        A("pool", lambda e: e.memset(cind[64:128, 1, :], 1.0), reads=[R_cind], writes=[R_cind])
        A("dve", lambda e: e.tensor_scalar(out=Esel[:], in0=gp[:, 8:16].unsqueeze(2).broadcast_to([16, 8, 128]), scalar1=gp[:, 6:7], scalar2=None, op0=ALU.mult),
          reads=[R_gp], writes=[R_Esel])
        wab, R_wab = sb("wab", [128, 8, 32], BF16)
        dw, R_dw = sb("dw", [128, 9, 128], BF16)
        cpad, R_cpad = sb("cpad", [128, 400], BF16)
        cs, R_cs = sb("cs", [128, 256])
        sq, R_sq = sb("sq", [128, 256], BF16)
        rn, R_rn = sb("rn", [128, 256])
        qBf, R_qBf = sb("qBf", [128, 8, 256], BF16)
        kBf, R_kBf = sb("kBf", [128, 8, 256], BF16)
        vBf, R_vBf = sb("vBf", [128, 8, 256], BF16)
        qdB, R_qdB = sb("qdB", [128, 8, 256], BF16)
        rf = [sb("rf%d" % i, [16, 256]) for i in range(6)]
        rhsP, R_rhsP = sb("rhsP", [16, 8, 128])
        rhsA, R_rhsA = sb("rhsA", [16, 8, 128])
        tms = [sb("tms%d" % i, [128, 16]) for i in range(4)]
        bet, R_bet = sb("bet", [128, 8])
        bee, R_bee = sb("bee", [128, 8])
        ksf, R_ksf = sb("ksf", [128, 8])
        dB, R_dB = sb("dB", [128, 2, 8])
        kbe, R_kbe = sb("kbe", [128, 8, 128], BF16)
        kdc, R_kdc = sb("kdc", [128, 8, 128], BF16)
        bvt, R_bvt = sb("bvt", [128, 8, 128], BF16)
        eA, R_eA = sb("eA", [128, 8, 128])
        eP, R_eP = sb("eP", [128, 8, 128])
        XT = [sb("XT%d" % i, [128, 8, 128], BF16) for i in range(2)]
        XX = [sb("XX%d" % i, [128, 8, 128], BF16) for i in range(2)]
        QQ = [sb("QQ%d" % i, [128, 8, 128], BF16) for i in range(2)]
        PT, R_PT = sb("PT", [128, 8, 128], BF16)
        u0, R_u0 = sb("u0", [128, 8, 128])
        nwT, R_nwT = sb("nwT", [128, 8, 128], BF16)
        vnew, R_vnew = sb("vnew", [128, 8, 128], BF16)
        SB, R_SB = sb("SB", [128, 8, 128])
        SBb, R_SBb = sb("SBb", [128, 8, 128], BF16)
        A("dve", lambda e: e.memset(SB[:], 0.0), writes=[R_SB])
        A("pool", lambda e: e.memset(SBb[:], 0.0), writes=[R_SBb])
        fnw, R_fnw = sb("fnw", [128, D])
        hnwc, R_hnwc = sb("hnwc", [128, 16])
        hst, R_hst = sb("hst", [128, 32])
        A("sp", lambda e: e.dma_start(out=fnw[:], in_=fnw_d.partition_broadcast(128)), writes=[R_fnw], chan="p7")
        A("sp", lambda e: e.dma_start(out=hnwc[:], in_=hnw_d), writes=[R_hnwc], chan="p8")
        for kk in range(16):
            x_t, R_x = xt[1]
            A("sp", lambda e, kk=kk: e.dma_start(out=x_t[:], in_=wout_d[128 * kk:128 * kk + 128, :]), writes=[R_x], chan="x1")
            A("dve", lambda e, kk=kk: e.tensor_scalar(out=hb[:], in0=x_t[:], scalar1=hnwc[:, kk:kk + 1], scalar2=None, op0=ALU.mult), reads=[R_x, R_hnwc], writes=[R_hb])
            A("sp", lambda e, kk=kk: e.dma_start(out=woutb[128 * kk:128 * kk + 128, :], in_=hb[:]), reads=[R_hb], writes=[R_woutb], chan="wo_st")
        szs = [(Amod[:, 1, :].bitcast(BF16), R_Amod), (shmod[:, 1, :].bitcast(BF16), R_shmod)]
        yT2 = [XT[1], XX[1]]
        yb2 = [(kbe[:].rearrange("p h c -> p (h c)"), R_kbe), (kdc[:].rearrange("p h c -> p (h c)"), R_kdc)]
        dw2, R_dw2 = sb("dw2", [128, 9, 128], BF16)
        cpad2, R_cpad2 = sb("cpad2", [128, 400], BF16)
        cs2, R_cs2 = sb("cs2", [128, 256])
        sq2, R_sq2 = sb("sq2", [128, 256], BF16)
        rn2, R_rn2 = sb("rn2", [128, 256])
        NB = [(cs, R_cs, sq, R_sq, rn, R_rn), (cs2, R_cs2, sq2, R_sq2, rn2, R_rn2)]
        CP = [(cpad, R_cpad), (cpad2, R_cpad2)]
        DW = [(dw, R_dw), (dw2, R_dw2)]
        print("sbuf bytes remaining", nc.sbuf_bytes_remaining)


        def load_w(c0, ncol):
            i = wctr[0] % 2
            wctr[0] += 1
            t, R = wt[i]
            A("sp", lambda e: e.dma_start(out=t[:, :, 0:ncol], in_=winb[:, c0:c0 + ncol].rearrange("(k p) n -> p k n", p=128)),
              reads=[R_winb], writes=[R], chan="w%d" % i)
            return t, R

        def xprep(src, row0, i, col0, j):
            x_t, R_x = xt[i % 2]
            A("sp", lambda e: e.dma_start(out=x_t[:], in_=src[row0:row0 + 128, :]), writes=[R_x], chan="x%d" % (i % 2))
            A("act", lambda e: e.activation(out=junk[:], in_=x_t[:], func=AF.Square, accum_out=ssq[:, 0:1]),
              reads=[R_x], writes=[R_junk, R_ssq])
            A("act", lambda e: e.activation(out=ssq[:, 1:2], in_=ssq[:, 0:1], func=AF.Ln, scale=1.0 / D, bias=EPS), reads=[R_ssq], writes=[R_ssq])
            A("act", lambda e: e.activation(out=ssq[:, 1:2], in_=ssq[:, 1:2], func=AF.Exp, scale=-0.5), reads=[R_ssq], writes=[R_ssq])
            A("dve", lambda e: e.scalar_tensor_tensor(out=hn[:], in0=x_t[:], scalar=ssq[:, 1:2], in1=Amod[:, j, :], op0=ALU.mult, op1=ALU.mult),
              reads=[R_x, R_ssq, R_Amod], writes=[R_hn])
            A("pool", lambda e: e.tensor_tensor(out=hb[:], in0=hn[:], in1=shmod[:, j, :], op=ALU.add), reads=[R_hn, R_shmod], writes=[R_hb])
            pt, R_pt = pm[7]
            ptb = pt[:].bitcast(BF16)
            for k in range(8):
                A("pe", lambda e, k=k: e.transpose(out=ptb[:, 128 * k:128 * k + 128], in_=hb[:, 128 * k:128 * k + 128], identity=idb[:]),
                  reads=[R_hb, R_idb], writes=[R_pt])
            A("act", lambda e: e.activation(out=hT[:, :, col0:col0 + 128], in_=ptb.rearrange("p (k t) -> p k t", t=128), func=AF.Copy),
              reads=[R_pt], writes=[R_hT])

        def group(s, src, row0, ntile_w, off, j, readout, gidx):
            T = 256
            for i in range(ntile_w):
                if ntile_w == 3 and gidx >= 1 and i == 0:
                    A("pool", lambda e: e.tensor_copy(out=hT[:, :, 0:128], in_=hT[:, :, 256:384]), reads=[R_hT], writes=[R_hT])
                    continue
                xprep(src, row0 + 128 * i, i, 128 * i, j)
            main = slice(off, off + T)
            for half in (range(2) if readout else ()):
                w, R_w = load_w(C_AQ + 512 * half, 512)
                for hh in range(4):
                    h = 4 * half + hh
                    pt, R_pt = pm[hh % 2]
                    for k in range(8):
                        A("pe", lambda e, k=k, hh=hh, pt=pt, w=w: e.matmul(pt[:, 0:T], lhsT=w[:, k, 128 * hh:128 * hh + 128], rhs=hT[:, k, main], start=(k == 0), stop=(k == 7)),
                          reads=[R_w, R_hT], writes=[R_pt])
                    A("act", lambda e, h=h, pt=pt: e.activation(out=qA[:, h, :], in_=pt[:, 0:T], func=AF.Silu), reads=[R_pt], writes=[R_qA])
            caf = C_AF0 if s == 0 else C_AF1
            for half in range(2):
                w, R_w = load_w(caf + 512 * half, 512)
                for hh in range(4):
                    h = 4 * half + hh
                    li = s * 8 + h
                    pt, R_pt = pm[2 + hh % 2]
                    (sig, R_sig), (g, R_g), (b, R_b), (dl, R_dl) = tf
                    for k in range(8):
                        A("pe", lambda e, k=k, hh=hh, pt=pt, w=w: e.matmul(pt[:, 0:T], lhsT=w[:, k, 128 * hh:128 * hh + 128], rhs=hT[:, k, main], start=(k == 0), stop=(k == 7)),
                          reads=[R_w, R_hT], writes=[R_pt])
                    A("act", lambda e, pt=pt: e.activation(out=sig[:], in_=pt[:, 0:T], func=AF.Sigmoid), reads=[R_pt], writes=[R_sig])
                    A("act", lambda e, li=li: e.activation(out=g[:], in_=sig[:], func=AF.Ln, scale=oml[:, li:li + 1], bias=lb[:, li:li + 1]),
                      reads=[R_sig, R_oml, R_lb], writes=[R_g])
                    A("dve", lambda e, li=li: e.tensor_scalar(out=sig[:], in0=sig[:], scalar1=-1.0, scalar2=lbm1[:, li:li + 1], op0=ALU.add, op1=ALU.mult),
                      reads=[R_sig, R_lbm1], writes=[R_sig])
                    A("dve", lambda e: e.tensor_tensor_scan(out=b[:], data0=rmask[:, 0:T], data1=g[:], initial=0.0, op0=ALU.mult, op1=ALU.add),
                      reads=[R_rmask, R_g], writes=[R_b])
                    b3 = b[:].rearrange("p (c t) -> p c t", t=64)
                    A("dve", lambda e, b3=b3: e.tensor_tensor(out=dl[:].rearrange("p (c t) -> p c t", t=64), in0=b3, in1=b3[:, :, 63:64].broadcast_to([128, 4, 64]), op=ALU.subtract),
                      reads=[R_b], writes=[R_dl])
                    A("act", lambda e, h=h, b3=b3: e.activation(out=dA[:, h, :], in_=b3[:, :, 63], func=AF.Exp), reads=[R_b], writes=[R_dA])
                    if readout:
                        A("act", lambda e: e.activation(out=g[:], in_=dl[:], func=AF.Exp), reads=[R_dl], writes=[R_g])
                    A("act", lambda e: e.activation(out=b[:], in_=dl[:], func=AF.Exp, scale=-1.0), reads=[R_dl], writes=[R_b])
                    if readout:
                        A("dve", lambda e, h=h: e.scalar_tensor_tensor(out=qt[:, h, :], in0=qA[:, h, :], scalar=float(128 ** -0.5), in1=g[:], op0=ALU.mult, op1=ALU.mult),
                          reads=[R_qA, R_g], writes=[R_qt])
                    A("pool", lambda e, h=h: e.tensor_tensor(out=kdT[:, h, :], in0=sig[:], in1=b[:], op=ALU.mult),
                      reads=[R_sig, R_b], writes=[R_kdT])
            for half in range(2):
                w, R_w = load_w(C_AI + 512 * half, 512)
                for ti in range(2):
                    pt, R_pt = pm[4 + ti]
                    for k in range(8):
                        A("pe", lambda e, k=k, ti=ti, pt=pt, w=w: e.matmul(pt[:], lhsT=hT[:, k, off + 128 * ti:off + 128 * ti + 128], rhs=w[:, k, :], start=(k == 0), stop=(k == 7)),
                          reads=[R_w, R_hT], writes=[R_pt])
                    A("act", lambda e, ti=ti, pt=pt, half=half: e.activation(out=vA[:, ti, 512 * half:512 * half + 512], in_=pt[:], func=AF.Copy),
                      reads=[R_pt], writes=[R_vA])
            for ti in range(2):
                pt, R_pt = pm[6]
                ptb = pt[:].bitcast(BF16)
                for h in range(8):
                    A("pe", lambda e, h=h, ti=ti, ptb=ptb: e.transpose(out=ptb[:, 128 * h:128 * h + 128], in_=kdT[:, h, 128 * ti:128 * ti + 128], identity=idb[:]),
                      reads=[R_kdT, R_idb], writes=[R_pt])
                A("dve", lambda e, ti=ti, ptb=ptb: e.tensor_copy(out=kdm[:, ti, :], in_=ptb), reads=[R_pt], writes=[R_kdm])
            if do_gdn:
                lat = (ntile_w == 3)
                WN = 128 * ntile_w
                if gidx <= 0:
                    A("pool", lambda e: e.memset(cpad[:], 0.0), writes=[R_cpad])
                    A("pool", lambda e: e.memset(cpad2[:], 0.0), writes=[R_cpad2])
                if lat:
                    cpv = cpad[:, 0:396].rearrange("p (r c) -> p r c", c=66)
                def stageA(blk, w, R_w, hh):
                    cpd, Rcp = CP[blk % 2]
                    cpv_ = cpd[:, 0:396].rearrange("p (r c) -> p r c", c=66)
                    pt, R_pt = pm[blk % 2]
                    for k in range(8):
                        A("pe", lambda e, k=k, hh=hh, pt=pt, w=w: e.matmul(pt[:, 0:WN], lhsT=w[:, k, 128 * hh:128 * hh + 128], rhs=hT[:, k, 0:WN], start=(k == 0), stop=(k == 7)),
                          reads=[R_w, R_hT], writes=[R_pt])
                    if lat:
                        A("act", lambda e, pt=pt, cpv_=cpv_: e.activation(out=cpv_[:, :, 1:65], in_=pt[:, 0:384].rearrange("p (r c) -> p r c", c=64), func=AF.Copy),
                          reads=[R_pt], writes=[Rcp])
                        if gidx == 0:
                            A("pool", lambda e, cpv_=cpv_: e.memset(cpv_[:, 0, :], 0.0), reads=[Rcp], writes=[Rcp])
                        if s == 1 and gidx == ng - 1:
                            A("pool", lambda e, cpv_=cpv_: e.memset(cpv_[:, 5, :], 0.0), reads=[Rcp], writes=[Rcp])
                    else:
                        A("act", lambda e, pt=pt, cpd=cpd: e.activation(out=cpd[:, 1:257], in_=pt[:, 0:256], func=AF.Copy), reads=[R_pt], writes=[Rcp])
                    dwb, Rdw = DW[blk % 2]
                    A("pool", lambda e, blk=blk, dwb=dwb: e.tensor_tensor(out=dwb[:], in0=idb[:].unsqueeze(1).broadcast_to([128, 9, 128]),
                                                                          in1=cws[:, s, blk, :].unsqueeze(2).broadcast_to([128, 9, 128]), op=ALU.mult),
                      reads=[R_idb, R_cws], writes=[Rdw])
                    pc, R_pc = pm[2 + blk % 2]
                    taps = [(i, jj) for i in range(3) for jj in range(3)] if lat else [(1, jj) for jj in range(3)]
                    for n, (i, jj) in enumerate(taps):
                        if lat:
                            A("pe", lambda e, i=i, jj=jj, n=n, pc=pc, nt=len(taps), dwb=dwb, cpv_=cpv_: e.matmul(pc[:, 0:256].rearrange("p (r c) -> p r c", c=64), lhsT=dwb[:, 3 * i + jj, :], rhs=cpv_[:, i:i + 4, jj:jj + 64], start=(n == 0), stop=(n == nt - 1)),
                              reads=[Rdw, Rcp], writes=[R_pc])
                        else:
                            A("pe", lambda e, i=i, jj=jj, n=n, pc=pc, nt=len(taps), dwb=dwb, cpd=cpd: e.matmul(pc[:, 0:256], lhsT=dwb[:, 3 * i + jj, :], rhs=cpd[:, jj:jj + 256], start=(n == 0), stop=(n == nt - 1)),
                              reads=[Rdw, Rcp], writes=[R_pc])

                def stageB(blk):
                    kind, h = blk // 8, blk % 8
                    pc, R_pc = pm[2 + blk % 2]
                    if kind == 2:
                        A("act", lambda e, h=h, pc=pc: e.activation(out=vBf[:, h, :], in_=pc[:, 0:256], func=AF.Silu), reads=[R_pc], writes=[R_vBf])
                    else:
                        c_, Rc_, q_s, Rq_, r_, Rr_ = NB[blk % 2]
                        A("act", lambda e, pc=pc, c_=c_: e.activation(out=c_[:], in_=pc[:, 0:256], func=AF.Silu), reads=[R_pc], writes=[Rc_])
                        A("act", lambda e, c_=c_, q_s=q_s: e.activation(out=q_s[:], in_=c_[:], func=AF.Square), reads=[Rc_], writes=[Rq_])
                        pn, R_pn = pm[4 + blk % 2]
                        A("pe", lambda e, pn=pn, q_s=q_s: e.matmul(pn[:, 0:256], lhsT=onesb[:], rhs=q_s[:], start=True, stop=True), reads=[R_onesb, Rq_], writes=[R_pn])
                        A("act", lambda e, pn=pn, r_=r_: e.activation(out=r_[:], in_=pn[:, 0:256], func=AF.Ln, bias=EPS), reads=[R_pn], writes=[Rr_])
                        A("act", lambda e, r_=r_: e.activation(out=r_[:], in_=r_[:], func=AF.Exp, scale=-0.5), reads=[Rr_], writes=[Rr_])
                        dst, R_dst = (qBf, R_qBf) if kind == 0 else (kBf, R_kBf)
                        sc_ = float(128 ** -0.5) if kind == 0 else 1.0
                        A("dve", lambda e, h=h, dst=dst, sc_=sc_, c_=c_, r_=r_: e.scalar_tensor_tensor(out=dst[:, h, :], in0=c_[:], scalar=sc_, in1=r_[:], op0=ALU.mult, op1=ALU.mult),
                          reads=[Rc_, Rr_], writes=[R_dst])

                pend = None
                for m in range(6):
                    if m < 2 and not readout:
                        continue
                    w, R_w = load_w(C_BQ + 512 * m, 512)
                    for hh in range(4):
                        blk = 4 * m + hh
                        stageA(blk, w, R_w, hh)
                        if pend is not None:
                            stageB(pend)
                        pend = blk
                if pend is not None:
                    stageB(pend)
                if gidx <= 0:
                    ca_, cb_ = C_BA + 8 * s, C_BB + 8 * s
                    for q_, c_ in enumerate((ca_, ca_, cb_, cb_)):
                        A("sp", lambda e, q_=q_, c_=c_: e.dma_start(out=wab[:, :, 8 * q_:8 * q_ + 8], in_=winb[:, c_:c_ + 8].rearrange("(k p) n -> p k n", p=128)),
                          reads=[R_winb], writes=[R_wab], chan="wab")
                pz, R_pz = pm[6]
                for k in range(8):
                    A("pe", lambda e, k=k: e.matmul(pz[0:16, 0:256], lhsT=wab[:, k, 0:16], rhs=hT[:, k, main], start=(k == 0), stop=(k == 7)), reads=[R_wab, R_hT], writes=[R_pz])
                for k in range(8):
                    A("pe", lambda e, k=k: e.matmul(pz[0:16, 256:512], lhsT=wab[:, k, 16:32], rhs=hT[:, k, main], start=(k == 0), stop=(k == 7)), reads=[R_wab, R_hT], writes=[R_pz])
                (r0, R_r0), (r1, R_r1), (r2, R_r2), (r3, R_r3), (r4, R_r4), (r5, R_r5) = rf
                A("act", lambda e: e.activation(out=r0[:], in_=pz[0:16, 0:256], func=AF.Exp, bias=gp[:, 1 + 2 * s:2 + 2 * s]), reads=[R_pz, R_gp], writes=[R_r0])
                A("act", lambda e: e.activation(out=r0[:], in_=r0[:], func=AF.Ln, bias=1.0), reads=[R_r0], writes=[R_r0])
                A("dve", lambda e: e.tensor_scalar(out=r0[:], in0=r0[:], scalar1=nA16[:, s:s + 1], scalar2=None, op0=ALU.mult), reads=[R_r0, R_nA16], writes=[R_r0])
                A("act", lambda e: e.activation(out=r1[:], in_=pz[0:16, 256:512], func=AF.Exp, scale=-1.0), reads=[R_pz], writes=[R_r1])
                A("act", lambda e: e.activation(out=r1[:], in_=r1[:], func=AF.Ln, bias=1.0), reads=[R_r1], writes=[R_r1])
                A("dve", lambda e: e.tensor_tensor_scan(out=r2[:], data0=rmask[0:16, 0:256], data1=r0[:], initial=0.0, op0=ALU.mult, op1=ALU.add), reads=[R_rmask, R_r0], writes=[R_r2])
                A("dve", lambda e: e.tensor_tensor(out=r1[:], in0=r2[:], in1=r1[:], op=ALU.subtract), reads=[R_r2, R_r1], writes=[R_r1])
                A("dve", lambda e: e.tensor_scalar(out=r3[:], in0=r2[:], scalar1=gp[:, 4:5], scalar2=gp[:, 5:6], op0=ALU.mult, op1=ALU.add), reads=[R_r2, R_gp], writes=[R_r3])
                A("dve", lambda e: e.tensor_scalar(out=r4[:], in0=r2[:], scalar1=gp[:, 6:7], scalar2=gp[:, 7:8], op0=ALU.mult, op1=ALU.add), reads=[R_r2, R_gp], writes=[R_r4])
                A("dve", lambda e: e.tensor_scalar(out=r5[:], in0=r1[:], scalar1=gp[:, 6:7], scalar2=gp[:, 7:8], op0=ALU.mult, op1=ALU.add), reads=[R_r1, R_gp], writes=[R_r5])
                for h in (range(8) if readout else ()):
                    pb, R_pb = pm[h % 2]
                    A("pe", lambda e, h=h, pb=pb: e.matmul(pb[:, 0:256], lhsT=Esel[:, h, :], rhs=r4[:], start=True, stop=True), reads=[R_Esel, R_r4], writes=[R_pb])
                    A("act", lambda e, pb=pb: e.activation(out=cs[:], in_=pb[:, 0:256], func=AF.Exp), reads=[R_pb], writes=[R_cs])
                    A("dve", lambda e, h=h: e.tensor_tensor(out=qdB[:, h, :], in0=qBf[:, h, :], in1=cs[:], op=ALU.mult), reads=[R_qBf, R_cs], writes=[R_qdB])
            if s == 1 and readout:
                for q_ in range(4):
                    w, R_w = load_w((C_AZ if q_ < 2 else C_BZ) + 512 * (q_ % 2), 512)
                    for ti in range(2):
                        pt, R_pt = pm[4 + ti]
                        for k in range(8):
                            A("pe", lambda e, k=k, ti=ti, pt=pt, w=w: e.matmul(pt[:], lhsT=hT[:, k, off + 128 * ti:off + 128 * ti + 128], rhs=w[:, k, :], start=(k == 0), stop=(k == 7)),
                              reads=[R_w, R_hT], writes=[R_pt])
                        A("act", lambda e, ti=ti, pt=pt, q_=q_: e.activation(out=szs[ti][0][:, 512 * q_:512 * q_ + 512], in_=pt[:], func=AF.Silu), reads=[R_pt], writes=[szs[ti][1]])
            for ti in range(2):
                tok = slice(128 * ti, 128 * ti + 128)
                if readout:
                    for hq in range(2):
                        pt, R_pt = pm[hq]
                        for hh in range(4):
                            h = 4 * hq + hh
                            A("pe", lambda e, h=h, hh=hh, pt=pt, tok=tok: e.matmul(pt[:, 128 * hh:128 * hh + 128], lhsT=kdT[:, h, tok], rhs=qt[:, h, tok], start=True, stop=True),
                              reads=[R_kdT, R_qt], writes=[R_pt])
                        A("dve", lambda e, hq=hq, pt=pt: e.tensor_tensor(out=scT[:, 4 * hq:4 * hq + 4, :], in0=pt[:].rearrange("p (h t) -> p h t", t=128),
                                                                         in1=m01[:].unsqueeze(1).broadcast_to([128, 4, 128]), op=ALU.mult),
                          reads=[R_pt, R_m01], writes=[R_scT])
                po = [pm[2], pm[3]]
                pre = (s == 1 and readout)
                if pre:
                    nr = NTH - 128 - ((gidx - ng // 2) * 256 + 128 * ti)
                    A("sp", lambda e, nr=nr: e.dma_start(out=osb[:], in_=o1_d[nr:nr + 128, :]), reads=[R_o1], writes=[R_osb], chan="o1l")
                    for q_, (pt, R_pt) in enumerate((pm[2], pm[3], pm[6], pm[7])):
                        A("pe", lambda e, q_=q_, pt=pt: e.matmul(pt[:], lhsT=jf[:], rhs=osb[:, 512 * q_:512 * q_ + 512], start=True, stop=False), reads=[R_jf, R_osb], writes=[R_pt])
                for c in range(2):
                    rows = slice(64 * c, 64 * c + 64)
                    ctok = slice(128 * ti + 64 * c, 128 * ti + 64 * c + 64)
                    cg = 2 * ti + c
                    A("dve", lambda e, cg=cg: e.tensor_tensor(out=SAd[:], in0=SA[:], in1=dA[:, :, cg:cg + 1].broadcast_to([128, 8, 128]), op=ALU.mult),
                      reads=[R_SA, R_dA], writes=[R_SAd])
                    if readout:
                        A("act", lambda e: e.activation(out=SAb[:], in_=SAd[:], func=AF.Copy), reads=[R_SAd], writes=[R_SAb])
                        for h in range(8):
                            pt, R_pt = po[h // 4]
                            cols = slice(128 * (h % 4), 128 * (h % 4) + 128)
                            A("pe", lambda e, h=h, pt=pt, cols=cols, rows=rows, ctok=ctok: e.matmul(pt[rows, cols], lhsT=qt[:, h, ctok], rhs=SAb[:, h, :], start=(not pre), stop=False),
                              reads=[R_qt, R_SAb], writes=[R_pt])
                            A("pe", lambda e, h=h, pt=pt, cols=cols, rows=rows, ti=ti: e.matmul(pt[rows, cols], lhsT=scT[rows, h, rows], rhs=vA[rows, ti, 128 * h:128 * h + 128], start=False, stop=True),
                              reads=[R_scT, R_vA], writes=[R_pt])
                    for hq in range(2):
                        pt, R_pt = pm[4 + hq]
                        for hh in range(4):
                            h = 4 * hq + hh
                            A("pe", lambda e, h=h, hh=hh, pt=pt, rows=rows, ti=ti: e.matmul(pt[:, 128 * hh:128 * hh + 128], lhsT=kdm[rows, ti, 128 * h:128 * h + 128], rhs=vA[rows, ti, 128 * h:128 * h + 128], start=True, stop=True),
                              reads=[R_kdm, R_vA], writes=[R_pt])
                        A("dve", lambda e, hq=hq, pt=pt: e.tensor_tensor(out=SA[:, 4 * hq:4 * hq + 4, :], in0=pt[:].rearrange("p (h v) -> p h v", v=128), in1=SAd[:, 4 * hq:4 * hq + 4, :], op=ALU.add),
                          reads=[R_pt, R_SAd], writes=[R_SA])
                if readout:
                    for hq in range(2):
                        pt, R_pt = po[hq]
                        A("act", lambda e, hq=hq, pt=pt: e.activation(out=osb[:, 512 * hq:512 * hq + 512], in_=pt[:], func=AF.Copy), reads=[R_pt], writes=[R_osb])
                if do_gdn:
                    (t0, R_t0), (t1, R_t1), (t2, R_t2), (t3, R_t3) = tms
                    mtok = slice(off + 128 * ti, off + 128 * ti + 128)
                    pz, R_pz = pm[0]
                    for k in range(8):
                        A("pe", lambda e, k=k, mtok=mtok: e.matmul(pz[:, 0:16], lhsT=hT[:, k, mtok], rhs=wab[:, k, 8:24], start=(k == 0), stop=(k == 7)), reads=[R_wab, R_hT], writes=[R_pz])
                    A("dve", lambda e: e.tensor_tensor(out=t0[:, 0:8], in0=pz[:, 0:8], in1=gdtr[:, 2 + s, :], op=ALU.add), reads=[R_pz, R_gdtr], writes=[R_t0])
                    A("act", lambda e: e.activation(out=t0[:, 0:8], in_=t0[:, 0:8], func=AF.Exp), reads=[R_t0], writes=[R_t0])
                    A("act", lambda e: e.activation(out=t0[:, 0:8], in_=t0[:, 0:8], func=AF.Ln, bias=1.0), reads=[R_t0], writes=[R_t0])
                    A("dve", lambda e: e.tensor_tensor(out=t0[:, 0:8], in0=t0[:, 0:8], in1=nAr[:, s, :], op=ALU.mult), reads=[R_t0, R_nAr], writes=[R_t0])
                    A("act", lambda e: e.activation(out=t0[:, 8:16], in_=pz[:, 8:16], func=AF.Exp, scale=-1.0), reads=[R_pz], writes=[R_t0])
                    A("dve", lambda e: e.tensor_scalar(out=t0[:, 8:16], in0=t0[:, 8:16], scalar1=1.0, scalar2=None, op0=ALU.add), reads=[R_t0], writes=[R_t0])
                    A("dve", lambda e: e.reciprocal(out=bet[:], in_=t0[:, 8:16]), reads=[R_t0], writes=[R_bet])
                    pz2, R_pz2 = pm[1]
                    A("pe", lambda e: e.matmul(pz2[:, 0:8], lhsT=m01[:], rhs=t0[:, 0:8], start=True, stop=True), reads=[R_m01, R_t0], writes=[R_pz2])
                    A("pe", lambda e: e.matmul(pz2[:, 8:16], lhsT=mU[:], rhs=t0[:, 0:8], start=True, stop=True), reads=[R_mU, R_t0], writes=[R_pz2])
                    for c in range(2):
                        A("pe", lambda e, c=c: e.matmul(pz2[:, 16 + 8 * c:24 + 8 * c], lhsT=cind[:, c, :], rhs=t0[:, 0:8], start=True, stop=True), reads=[R_cind, R_t0], writes=[R_pz2])
                    A("act", lambda e: e.activation(out=bee[:], in_=pz2[:, 0:8], func=AF.Exp), reads=[R_pz2], writes=[R_bee])
                    A("act", lambda e: e.activation(out=ksf[:], in_=pz2[:, 8:16], func=AF.Exp), reads=[R_pz2], writes=[R_ksf])
                    A("act", lambda e: e.activation(out=dB[:].rearrange("p c h -> p (c h)"), in_=pz2[:, 16:32], func=AF.Exp), reads=[R_pz2], writes=[R_dB])
                    A("dve", lambda e: e.tensor_tensor(out=bee[:], in0=bee[:], in1=bet[:], op=ALU.mult), reads=[R_bee, R_bet], writes=[R_bee])
                    pk, R_pk = pm[0]
                    pkb = pk[:].bitcast(BF16)
                    for h in range(8):
                        A("pe", lambda e, h=h, tok=tok: e.transpose(out=pkb[:, 128 * h:128 * h + 128], in_=kBf[:, h, tok], identity=idb[:]), reads=[R_kBf, R_idb], writes=[R_pk])
                    pk3 = pkb.rearrange("p (h c) -> p h c", c=128)
                    A("dve", lambda e: e.tensor_tensor(out=kbe[:], in0=pk3, in1=bee[:].unsqueeze(2).broadcast_to([128, 8, 128]), op=ALU.mult), reads=[R_pk, R_bee], writes=[R_kbe])
                    A("dve", lambda e: e.tensor_tensor(out=kdc[:], in0=pk3, in1=ksf[:].unsqueeze(2).broadcast_to([128, 8, 128]), op=ALU.mult), reads=[R_pk, R_ksf], writes=[R_kdc])
                    pv_, R_pv_ = pm[1]
                    pvb = pv_[:].bitcast(BF16)
                    for h in range(8):
                        A("pe", lambda e, h=h, tok=tok: e.transpose(out=pvb[:, 128 * h:128 * h + 128], in_=vBf[:, h, tok], identity=idb[:]), reads=[R_vBf, R_idb], writes=[R_pv_])
                    A("dve", lambda e: e.tensor_tensor(out=bvt[:], in0=pvb.rearrange("p (h c) -> p h c", c=128), in1=bet[:].unsqueeze(2).broadcast_to([128, 8, 128]), op=ALU.mult),
                      reads=[R_pv_, R_bet], writes=[R_bvt])
                    A("dve", lambda e, tok=tok: e.tensor_tensor(out=rhsP[:], in0=rf[4][0][:, tok].unsqueeze(1).broadcast_to([16, 8, 128]), in1=gp[:, 8:16].unsqueeze(2).broadcast_to([16, 8, 128]), op=ALU.mult),
                      reads=[rf[4][1], R_gp], writes=[R_rhsP])
                    A("dve", lambda e, tok=tok: e.tensor_tensor(out=rhsA[:], in0=rf[5][0][:, tok].unsqueeze(1).broadcast_to([16, 8, 128]), in1=gp[:, 8:16].unsqueeze(2).broadcast_to([16, 8, 128]), op=ALU.mult),
                      reads=[rf[5][1], R_gp], writes=[R_rhsA])
                    for (rh, R_rh, cap, R_cap, ee, R_ee, b0) in (((rhsA, R_rhsA, capS, R_capS, eA, R_eA, 2), (rhsP, R_rhsP, capI, R_capI, eP, R_eP, 4)) if readout else ((rhsA, R_rhsA, capS, R_capS, eA, R_eA, 2),)):
                        for hq in range(2):
                            pd, R_pd = pm[b0 + hq]
                            for hh in range(4):
                                h = 4 * hq + hh
                                A("pe", lambda e, h=h, hh=hh, pd=pd, rh=rh, tok=tok: e.matmul(pd[:, 128 * hh:128 * hh + 128], lhsT=rf[3][0][:, tok], rhs=rh[:, h, :], start=True, stop=True),
                                  reads=[rf[3][1], R_rh], writes=[R_pd])
                            A("dve", lambda e, hq=hq, pd=pd, cap=cap, ee=ee: e.tensor_tensor(out=ee[:, 4 * hq:4 * hq + 4, :], in0=pd[:].rearrange("p (h t) -> p h t", t=128),
                                                                                           in1=cap[:].unsqueeze(1).broadcast_to([128, 4, 128]), op=ALU.min),
                              reads=[R_pd, R_cap], writes=[R_ee])
                        A("act", lambda e, ee=ee: e.activation(out=ee[:], in_=ee[:], func=AF.Exp), reads=[R_ee], writes=[R_ee])
                    (xt0_, R_xt0), (xx0_, R_xx0), (q0_, R_q0) = XT[0], XX[0], QQ[0]
                    for hq in range(2):
                        pa, R_pa = pm[2 + hq]
                        pb, R_pb = pm[4 + hq]
                        for hh in range(4):
                            h = 4 * hq + hh
                            A("pe", lambda e, h=h, hh=hh, pa=pa, tok=tok: e.matmul(pa[:, 128 * hh:128 * hh + 128], lhsT=kBf[:, h, tok], rhs=kBf[:, h, tok], start=True, stop=True), reads=[R_kBf], writes=[R_pa])
                            if readout:
                                A("pe", lambda e, h=h, hh=hh, pb=pb, tok=tok: e.matmul(pb[:, 128 * hh:128 * hh + 128], lhsT=kBf[:, h, tok], rhs=qBf[:, h, tok], start=True, stop=True), reads=[R_kBf, R_qBf], writes=[R_pb])
                        A("dve", lambda e, hq=hq, pa=pa: e.scalar_tensor_tensor(out=xt0_[:, 4 * hq:4 * hq + 4, :], in0=pa[:].rearrange("p (h t) -> p h t", t=128), scalar=-1.0, in1=eA[:, 4 * hq:4 * hq + 4, :], op0=ALU.mult, op1=ALU.mult),
                          reads=[R_pa, R_eA], writes=[R_xt0])
                        if readout:
                            A("dve", lambda e, hq=hq, pb=pb: e.tensor_tensor(out=PT[:, 4 * hq:4 * hq + 4, :], in0=pb[:].rearrange("p (h t) -> p h t", t=128), in1=eP[:, 4 * hq:4 * hq + 4, :], op=ALU.mult),
                              reads=[R_pb, R_eP], writes=[R_PT])
                    px, R_px = pm[0]
                    pxb = px[:].bitcast(BF16)
                    for h in range(8):
                        A("pe", lambda e, h=h: e.transpose(out=pxb[:, 128 * h:128 * h + 128], in_=xt0_[:, h, :], identity=idb[:]), reads=[R_xt0, R_idb], writes=[R_px])
                    A("act", lambda e: e.activation(out=xx0_[:], in_=pxb.rearrange("p (h c) -> p h c", c=128), func=AF.Copy), reads=[R_px], writes=[R_xx0])
                    A("pool", lambda e: e.tensor_tensor(out=q0_[:], in0=xt0_[:], in1=idb[:].unsqueeze(1).broadcast_to([128, 8, 128]), op=ALU.add), reads=[R_xt0, R_idb], writes=[R_q0])
                    for l in range(6):
                        (xtc, R_xtc), (xxc, R_xxc) = XT[l % 2], XX[l % 2]
                        (xtn, R_xtn), (xxn, R_xxn) = XT[(l + 1) % 2], XX[(l + 1) % 2]
                        (qc, R_qc), (qn, R_qn) = QQ[l % 2], QQ[(l + 1) % 2]
                        for hq in range(2):
                            hs = slice(4 * hq, 4 * hq + 4)
                            if l < 5:
                                p1, R_p1 = pm[2 + hq]
                                p2, R_p2 = pm[4 + hq]
                                for hh in range(4):
                                    h = 4 * hq + hh
                                    A("pe", lambda e, h=h, hh=hh, p1=p1, xxc=xxc, xtc=xtc: e.matmul(p1[:, 128 * hh:128 * hh + 128], lhsT=xxc[:, h, :], rhs=xtc[:, h, :], start=True, stop=True), reads=[R_xxc, R_xtc], writes=[R_p1])
                                    A("pe", lambda e, h=h, hh=hh, p2=p2, xxc=xxc, xtc=xtc: e.matmul(p2[:, 128 * hh:128 * hh + 128], lhsT=xtc[:, h, :], rhs=xxc[:, h, :], start=True, stop=True), reads=[R_xxc, R_xtc], writes=[R_p2])
                            if l >= 1:
                                p3, R_p3 = pm[hq]
                                for hh in range(4):
                                    h = 4 * hq + hh
                                    A("pe", lambda e, h=h, hh=hh, p3=p3, xxc=xxc, qc=qc: e.matmul(p3[:, 128 * hh:128 * hh + 128], lhsT=xxc[:, h, :], rhs=qc[:, h, :], start=True, stop=True), reads=[R_xxc, R_qc], writes=[R_p3])
                                A("dve", lambda e, hs=hs, p3=p3, qc=qc, qn=qn: e.tensor_tensor(out=qn[:, hs, :], in0=p3[:].rearrange("p (h t) -> p h t", t=128), in1=qc[:, hs, :], op=ALU.add),
                                  reads=[R_p3, R_qc], writes=[R_qn])
                            if l < 5:
                                A("act", lambda e, hs=hs, p1=p1, xtn=xtn: e.activation(out=xtn[:, hs, :], in_=p1[:].rearrange("p (h t) -> p h t", t=128), func=AF.Copy), reads=[R_p1], writes=[R_xtn])
                                A("pool" if False else "dve", lambda e, hs=hs, p2=p2, xxn=xxn: e.tensor_copy(out=xxn[:, hs, :], in_=p2[:].rearrange("p (h t) -> p h t", t=128)), reads=[R_p2], writes=[R_xxn])
                        if l == 0:
                            A("pool", lambda e: e.tensor_copy(out=QQ[1][0][:], in_=QQ[0][0][:]), reads=[QQ[0][1]], writes=[QQ[1][1]])
                    Qf, R_Qf = QQ[0]
                    for hq in range(2):
                        p1, R_p1 = pm[2 + hq]
                        p2, R_p2 = pm[4 + hq]
                        for hh in range(4):
                            h = 4 * hq + hh
                            A("pe", lambda e, h=h, hh=hh, p1=p1: e.matmul(p1[:, 128 * hh:128 * hh + 128], lhsT=Qf[:, h, :], rhs=bvt[:, h, :], start=True, stop=True), reads=[R_Qf, R_bvt], writes=[R_p1])
                            A("pe", lambda e, h=h, hh=hh, p2=p2: e.matmul(p2[:, 128 * hh:128 * hh + 128], lhsT=kbe[:, h, :], rhs=Qf[:, h, :], start=True, stop=True), reads=[R_Qf, R_kbe], writes=[R_p2])
                        A("act", lambda e, hq=hq, p1=p1: e.activation(out=u0[:, 4 * hq:4 * hq + 4, :], in_=p1[:].rearrange("p (h t) -> p h t", t=128), func=AF.Copy), reads=[R_p1], writes=[R_u0])
                        A("act", lambda e, hq=hq, p2=p2: e.activation(out=nwT[:, 4 * hq:4 * hq + 4, :], in_=p2[:].rearrange("p (h t) -> p h t", t=128), func=AF.Copy, scale=-1.0), reads=[R_p2], writes=[R_nwT])
                    pob = [pm[6], pm[7]]
                    for c in range(2):
                        rows = slice(64 * c, 64 * c + 64)
                        ctok = slice(128 * ti + 64 * c, 128 * ti + 64 * c + 64)
                        for hq in range(2):
                            p1, R_p1 = pm[hq]
                            for hh in range(4):
                                h = 4 * hq + hh
                                A("pe", lambda e, h=h, hh=hh, p1=p1, rows=rows: e.matmul(p1[rows, 128 * hh:128 * hh + 128], lhsT=nwT[:, h, rows], rhs=SBb[:, h, :], start=True, stop=True), reads=[R_nwT, R_SBb], writes=[R_p1])
                            A("dve", lambda e, hq=hq, p1=p1, rows=rows: e.tensor_tensor(out=vnew[rows, 4 * hq:4 * hq + 4, :], in0=p1[rows, :].rearrange("p (h t) -> p h t", t=128), in1=u0[rows, 4 * hq:4 * hq + 4, :], op=ALU.add),
                              reads=[R_p1, R_u0], writes=[R_vnew])
                        for hq in range(2):
                            if readout:
                                po_, R_po = pob[hq]
                                for hh in range(4):
                                    h = 4 * hq + hh
                                    A("pe", lambda e, h=h, hh=hh, po_=po_, rows=rows, ctok=ctok: e.matmul(po_[rows, 128 * hh:128 * hh + 128], lhsT=qdB[:, h, ctok], rhs=SBb[:, h, :], start=(not pre), stop=False), reads=[R_qdB, R_SBb], writes=[R_po])
                                    A("pe", lambda e, h=h, hh=hh, po_=po_, rows=rows: e.matmul(po_[rows, 128 * hh:128 * hh + 128], lhsT=PT[rows, h, rows], rhs=vnew[rows, h, :], start=False, stop=True), reads=[R_PT, R_vnew], writes=[R_po])
                            p2, R_p2 = pm[2 + hq]
                            for hh in range(4):
                                h = 4 * hq + hh
                                A("pe", lambda e, h=h, hh=hh, p2=p2, rows=rows: e.matmul(p2[:, 128 * hh:128 * hh + 128], lhsT=kdc[rows, h, :], rhs=vnew[rows, h, :], start=True, stop=True), reads=[R_kdc, R_vnew], writes=[R_p2])
                            hs = slice(4 * hq, 4 * hq + 4)
                            A("dve", lambda e, hs=hs, c=c: e.tensor_tensor(out=SB[:, hs, :], in0=SB[:, hs, :], in1=dB[:, c, hs].unsqueeze(2).broadcast_to([128, 4, 128]), op=ALU.mult), reads=[R_SB, R_dB], writes=[R_SB])
                            A("dve", lambda e, hs=hs, p2=p2: e.tensor_tensor(out=SB[:, hs, :], in0=p2[:].rearrange("p (h t) -> p h t", t=128), in1=SB[:, hs, :], op=ALU.add), reads=[R_p2, R_SB], writes=[R_SB])
                        A("act", lambda e: e.activation(out=SBb[:], in_=SB[:], func=AF.Copy), reads=[R_SB], writes=[R_SBb])
                    if readout:
                        for hq in range(2):
                            po_, R_po = pob[hq]
                            A("act", lambda e, hq=hq, po_=po_: e.activation(out=osb[:, D + 512 * hq:D + 512 * hq + 512], in_=po_[:], func=AF.Copy), reads=[R_po], writes=[R_osb])
                if readout and s == 1:
                    for hd in range(16):
                        A("act", lambda e, hd=hd: e.activation(out=junk[:, 0:128], in_=osb[:, 128 * hd:128 * hd + 128], func=AF.Square, accum_out=hst[:, hd:hd + 1]), reads=[R_osb], writes=[R_junk, R_hst])
                    A("act", lambda e: e.activation(out=hst[:, 16:32], in_=hst[:, 0:16], func=AF.Ln, scale=1.0 / 128, bias=EPS), reads=[R_hst], writes=[R_hst])
                    A("act", lambda e: e.activation(out=hst[:, 16:32], in_=hst[:, 16:32], func=AF.Exp, scale=-0.5), reads=[R_hst], writes=[R_hst])
                    szt, R_szt = szs[ti]
                    for q_ in range(4):
                        yb_, R_yb = yb2[q_ // 2]
                        cb0 = 512 * (q_ % 2)
                        A("dve", lambda e, q_=q_, szt=szt: e.tensor_tensor(out=osb[:, 512 * q_:512 * q_ + 512], in0=osb[:, 512 * q_:512 * q_ + 512], in1=szt[:, 512 * q_:512 * q_ + 512], op=ALU.mult), reads=[R_osb, R_szt], writes=[R_osb])
                        A("dve", lambda e, q_=q_, yb_=yb_, cb0=cb0: e.tensor_tensor(out=yb_[:, cb0:cb0 + 512].rearrange("p (h c) -> p h c", c=128), in0=osb[:, 512 * q_:512 * q_ + 512].rearrange("p (h c) -> p h c", c=128),
                                                                                in1=hst[:, 16 + 4 * q_:20 + 4 * q_].unsqueeze(2).broadcast_to([128, 4, 128]), op=ALU.mult), reads=[R_osb, R_hst], writes=[R_yb])
                    for half in range(2):
                        yb_, R_yb = yb2[half]
                        yT_, R_yT = yT2[half]
                        pt, R_pt = pm[half]
                        ptb = pt[:].bitcast(BF16)
                        for kk in range(8):
                            A("pe", lambda e, kk=kk, yb_=yb_, ptb=ptb: e.transpose(out=ptb[:, 128 * kk:128 * kk + 128], in_=yb_[:, 128 * kk:128 * kk + 128], identity=idb[:]), reads=[R_yb, R_idb], writes=[R_pt])
                        A("act", lambda e, yT_=yT_, ptb=ptb: e.activation(out=yT_[:], in_=ptb.rearrange("p (k t) -> p k t", t=128), func=AF.Copy), reads=[R_pt], writes=[R_yT])
                    x_t, R_x = xt[ti % 2]
                    xrow = 64 + gidx * 256 + 128 * ti
                    A("sp", lambda e, xrow=xrow, x_t=x_t: e.dma_start(out=x_t[:], in_=src[xrow:xrow + 128, :]), writes=[R_x], chan="x%d" % (ti % 2))
                    for ch in range(2):
                        pt, R_pt = pm[2 + ch]
                        for kh in range(2):
                            i = wctr[0] % 2
                            wctr[0] += 1
                            wt_, R_wt = wt[i]
                            A("sp", lambda e, kh=kh, ch=ch, wt_=wt_: e.dma_start(out=wt_[:], in_=woutb[1024 * kh:1024 * kh + 1024, 512 * ch:512 * ch + 512].rearrange("(k p) n -> p k n", p=128)),
                              reads=[R_woutb], writes=[R_wt], chan="w%d" % i)
                            yT_, R_yT = yT2[kh]
                            for k in range(8):
                                A("pe", lambda e, k=k, kh=kh, pt=pt, yT_=yT_, wt_=wt_: e.matmul(pt[:], lhsT=yT_[:, k, :], rhs=wt_[:, k, :], start=(kh == 0 and k == 0), stop=(kh == 1 and k == 7)),
                                  reads=[R_yT, R_wt], writes=[R_pt])
                        A("dve", lambda e, ch=ch, pt=pt: e.tensor_tensor(out=hn[:, 512 * ch:512 * ch + 512], in0=pt[:], in1=gate[:, 512 * ch:512 * ch + 512], op=ALU.mult), reads=[R_pt, R_gate], writes=[R_hn])
                    A("dve", lambda e, x_t=x_t: e.tensor_tensor(out=hn[:], in0=hn[:], in1=x_t[:], op=ALU.add), reads=[R_hn, R_x], writes=[R_hn])
                    A("act", lambda e: e.activation(out=junk[:], in_=hn[:], func=AF.Square, accum_out=ssq[:, 0:1]), reads=[R_hn], writes=[R_junk, R_ssq])
                    A("act", lambda e: e.activation(out=ssq[:, 1:2], in_=ssq[:, 0:1], func=AF.Ln, scale=1.0 / D, bias=EPS), reads=[R_ssq], writes=[R_ssq])
                    A("act", lambda e: e.activation(out=ssq[:, 1:2], in_=ssq[:, 1:2], func=AF.Exp, scale=-0.5), reads=[R_ssq], writes=[R_ssq])
                    A("dve", lambda e: e.scalar_tensor_tensor(out=hn[:], in0=hn[:], scalar=ssq[:, 1:2], in1=fnw[:], op0=ALU.mult, op1=ALU.mult), reads=[R_hn, R_ssq, R_fnw], writes=[R_hn])
                    orow = (gidx - ng // 2) * 256 + 128 * ti
                    A("sp", lambda e, orow=orow: e.dma_start(out=out_d[orow:orow + 128, :], in_=hn[:]), reads=[R_hn], chan="out")
                if readout and s == 0:
                    srow = gidx * 256 + 128 * ti
                    A("sp", lambda e, srow=srow: e.dma_start(out=o1_d[srow:srow + 128, :], in_=osb[:]), reads=[R_osb], writes=[R_o1], chan="o1")

        for s in range(2 if do_s2 else 1):
            if s == 1:
                A("dve", lambda e: e.memset(SA[:], 0.0), writes=[R_SA])
                A("dve", lambda e: e.memset(SB[:], 0.0), writes=[R_SB])
                A("pool", lambda e: e.memset(SBb[:], 0.0), writes=[R_SBb])
            group(s, cx[s], 0, 2, 0, 1, False, -1)
            if s == 0:
                for g in range(ng // 2):
                    group(s, xs[s], 256 * g, 3, 64, 0, True, g)
            else:
                for g in range(ng):
                    group(s, xs[s], 256 * g, 3, 64, 0, g >= ng // 2, g)

        P.emit(st, final_waits=["o1"] + (["out"] if do_s2 else []))
    return nc


def make_inputs(x, c, ctx, c_ctx, norm_w, ada_w, ada_b, w_in, conv_w, hg_lb_logits, gdn_a_log,
                gdn_dt_bias, ha_norm_w, hb_norm_w, w_out, final_norm_w, ncores=8):
    f = lambda a: np.ascontiguousarray(np.asarray(a, dtype=np.float32))
    S = x.shape[1]
    Hh = S // 2
    cw = conv_w[0].reshape(9, 3072)
    cwn = cw.T.reshape(24, 128, 9).transpose(1, 0, 2)
    cwf = cw[::-1].T.reshape(24, 128, 9).transpose(1, 0, 2)
    lbl4 = hg_lb_logits.reshape(2, 2, 8, 128).transpose(3, 0, 1, 2)
    hnw = np.concatenate([ha_norm_w[0].reshape(-1), hb_norm_w[0].reshape(-1)])
    z = np.zeros((64, D), np.float32)
    in_maps = []
    for core in range(ncores):
        b, half = core // 2, core % 2
        d1, d2 = (0, 1) if half == 0 else (1, 0)
        xb = x[b]
        if half == 0:
            xs1 = np.concatenate([z, xb[:Hh], xb[Hh:Hh + 64]], axis=0)
            xs2 = np.concatenate([z, xb[::-1], z], axis=0)
            ctx1, ctx2 = ctx[b], ctx[b][::-1]
            cwp = np.stack([cwn, cwf], axis=1)
        else:
            xs1 = np.concatenate([z, xb[Hh:][::-1], xb[Hh - 64:Hh][::-1]], axis=0)
            xs2 = np.concatenate([z, xb, z], axis=0)
            ctx1, ctx2 = ctx[b][::-1], ctx[b]
            cwp = np.stack([cwf, cwn], axis=1)
        w = np.array(w_in[0], dtype=np.float32, copy=True)
        if half == 1:
            w[:, C_AF0:C_AF0 + 1024], w[:, C_AF1:C_AF1 + 1024] = w_in[0][:, C_AF1:C_AF1 + 1024], w_in[0][:, C_AF0:C_AF0 + 1024]
            w[:, C_BA:C_BA + 8], w[:, C_BA + 8:C_BA + 16] = w_in[0][:, C_BA + 8:C_BA + 16], w_in[0][:, C_BA:C_BA + 8]
            w[:, C_BB:C_BB + 8], w[:, C_BB + 8:C_BB + 16] = w_in[0][:, C_BB + 8:C_BB + 16], w_in[0][:, C_BB:C_BB + 8]
        lbl = lbl4[:, :, [d1, d2], :].reshape(128, 32)
        gpar = np.zeros((16, 16), np.float32)
        for si, d in enumerate((d1, d2)):
            gpar[:, 2 * si] = np.tile(gdn_a_log[0, d], 2)
            gpar[:, 2 * si + 1] = np.tile(gdn_dt_bias[0, d], 2)
        gpar[0:8, 4] = -1.0
        gpar[8:16, 5] = 1.0
        gpar[8:16, 6] = 1.0
        gpar[0:8, 7] = 1.0
        for k in range(16):
            gpar[k, 8 + k % 8] = 1.0
        gdr = np.concatenate([gdn_a_log[0, d1], gdn_a_log[0, d2], gdn_dt_bias[0, d1], gdn_dt_bias[0, d2]])
        ccol = np.concatenate([c[b].reshape(8, 128).T, c_ctx.reshape(8, 128).T], axis=1)
        in_maps.append({
            "xs1": f(xs1), "xs2": f(xs2), "ctx1": f(ctx1), "ctx2": f(ctx2),
            "ccol": f(ccol), "norm_w": f(norm_w[0]), "final_norm_w": f(final_norm_w),
            "ada_w": f(ada_w[0]), "ada_b": f(ada_b[0]), "w_in": f(w), "w_out": f(w_out[0]),
            "lbl": f(lbl), "cw": f(cwp), "hnwc": f(hnw.reshape(16, 128).T), "gpar": f(gpar), "gdr": f(gdr),
        })
    return in_maps


def assemble(results, nb):
    outs = []
    for b in range(nb):
        outs.append(np.concatenate([results[2 * b]["out"][::-1], results[2 * b + 1]["out"]], axis=0))
    return np.ascontiguousarray(np.stack(outs, axis=0).astype(np.float32))


_NC_CACHE = {}


def kernel(**inputs):
    inputs = {k: np.asarray(v) for k, v in inputs.items()}
    in_maps = make_inputs(**inputs)
    if "nc" not in _NC_CACHE:
        _NC_CACHE["nc"] = build()
    res = run_bass_kernel_spmd(_NC_CACHE["nc"], in_maps, core_ids=list(range(8)))
    return assemble(res.results, 4)
```

```python
import contextlib
import numpy as np
import concourse.bass as bass
import concourse.mybir as mybir
from concourse.bass_utils import run_bass_kernel_spmd

F32 = mybir.dt.float32
BF16 = mybir.dt.bfloat16
AF = mybir.ActivationFunctionType
ALU = mybir.AluOpType

ENGS = ("pe", "act", "dve", "pool", "sp")

D = 1024
SEQ = 8192
CTXL = 256
NIN = 9248
EPS = 1e-6
C_AQ, C_AF0, C_AF1, C_AI, C_AZ = 0, 1024, 2048, 3072, 4096
C_BQ, C_BK, C_BV, C_BZ, C_BA, C_BB = 5120, 6144, 7168, 8192, 9216, 9232


class Res:
    __slots__ = ("name", "w", "rs")

    def __init__(self, name):
        self.name = name
        self.w = None
        self.rs = {}


class Op:
    __slots__ = ("eng", "fn", "deps", "chan", "chan_idx", "need_sig", "sig", "idx")

    def __init__(self, eng, fn, chan):
        self.eng = eng
        self.fn = fn
        self.chan = chan
        self.chan_idx = 0
        self.deps = ()
        self.need_sig = False
        self.sig = 0
        self.idx = 0


class Prog:
    def __init__(self, nc):
        self.nc = nc
        self.eng_ops = {e: [] for e in ENGS}
        self.chan_cnt = {}
        self.n = 0

    def op(self, eng, fn, reads=(), writes=(), chan=None):
        o = Op(eng, fn, chan)
        o.idx = self.n
        self.n += 1
        if chan is not None:
            c = self.chan_cnt.get(chan, 0) + 1
            self.chan_cnt[chan] = c
            o.chan_idx = c
        deps = {}
        for r in reads:
            if r.w is not None:
                deps[id(r.w)] = r.w
        for w in writes:
            if w.w is not None:
                deps[id(w.w)] = w.w
            for x in w.rs.values():
                deps[id(x)] = x
        o.deps = list(deps.values())
        for r in reads:
            key = eng if chan is None else ("dma", o.idx)
            r.rs[key] = o
        for w in writes:
            w.w = o
            w.rs = {}
        self.eng_ops[eng].append(o)
        return o

    def emit(self, st, final_waits=()):
        nc = self.nc
        esem = {e: st.enter_context(nc.semaphore("s_" + e)) for e in ENGS}
        csem = {c: st.enter_context(nc.semaphore("c_" + str(c))) for c in self.chan_cnt}
        for e in ENGS:
            for o in self.eng_ops[e]:
                for d in o.deps:
                    if d.chan is None:
                        if d.eng == "pe" and o.eng == "pe":
                            continue
                        d.need_sig = True
        for e in ENGS:
            c = 0
            for o in self.eng_ops[e]:
                if o.chan is None and o.need_sig:
                    c += 1
                    o.sig = c
        block = st.enter_context(nc.Block())

        def run(e, h):
            known = {}
            for o in self.eng_ops[e]:
                for d in o.deps:
                    if d.chan is not None:
                        key, val, sem = ("c", d.chan), 16 * d.chan_idx, csem[d.chan]
                    else:
                        if d.eng == "pe" and e == "pe":
                            continue
                        key, val, sem = ("e", d.eng), d.sig, esem[d.eng]
                    if known.get(key, 0) >= val:
                        continue
                    known[key] = val
                    h.wait_ge(sem, val)
                ins = o.fn(h)
                if o.chan is not None:
                    ins.then_inc(csem[o.chan], 16)
                elif o.need_sig:
                    ins.then_inc(esem[e], 1)
            if e == "sp":
                for ch in final_waits:
                    h.wait_ge(csem[ch], 16 * self.chan_cnt[ch])

        @block.tensor
        def _(h):
            run("pe", h)

        @block.scalar
        def _(h):
            run("act", h)

        @block.vector
        def _(h):
            run("dve", h)

        @block.gpsimd
        def _(h):
            run("pool", h)

        @block.sync
        def _(h):
            run("sp", h)


def build(ng=32, debug=False, do_gdn=True, do_s2=True):
    nc = bass.Bass("TRN2", target_bir_lowering=False)
    NT = ng * 256
    dram = lambda n, s, dt=F32, kind="ExternalInput": nc.dram_tensor(n, s, dt, kind=kind).ap()
    NTH = NT // 2
    xs = [dram("xs1", [NTH + 128, D]), dram("xs2", [NT + 128, D])]
    cx = [dram("ctx1", [CTXL, D]), dram("ctx2", [CTXL, D])]
    ccol_d = dram("ccol", [128, 16])
    normw_d = dram("norm_w", [D])
    fnw_d = dram("final_norm_w", [D])
    adaw_d = dram("ada_w", [D, 3 * D])
    adab_d = dram("ada_b", [3 * D])
    win_d = dram("w_in", [D, NIN])
    wout_d = dram("w_out", [2 * D, D])
    lbl_d = dram("lbl", [128, 32])
    cw_d = dram("cw", [128, 2, 24, 9])
    hnw_d = dram("hnwc", [128, 16])
    gpar_d = dram("gpar", [16, 16])
    gdr_d = dram("gdr", [32])
    out_d = dram("out", [NTH, D], kind="ExternalOutput")
    winb = dram("winb", [D, NIN], BF16, kind="Internal")
    woutb = dram("woutb", [2 * D, D], BF16, kind="Internal")
    o1_d = dram("o1", [NTH, 2 * D], F32, kind="ExternalOutput" if debug else "Internal")
    dbg_d = dram("dbg", [128, 4096], F32, kind="ExternalOutput") if debug else None

    P = Prog(nc)
    R_o1 = Res("o1")
    st = contextlib.ExitStack()
    with st:
        def sb(name, shape, dt=F32):
            return st.enter_context(nc.sbuf_tensor("sb_" + name, shape, dt)), Res(name)

        def ps(name, shape, dt=F32):
            return st.enter_context(nc.psum_tensor("ps_" + name, shape, dt)), Res(name)

        A = P.op

        idf, R_idf = sb("idf", [128, 128])
        idb, R_idb = sb("idb", [128, 128], BF16)
        jf, R_jf = sb("jf", [128, 128])
        m01, R_m01 = sb("m01", [128, 128])
        rmask, R_rmask = sb("rmask", [128, 256])
        A("pool", lambda e: e.memset(idf[:], 1.0), writes=[R_idf])
        A("pool", lambda e: e.affine_select(out=idf[:], in_=idf[:], pattern=[[-1, 128]], compare_op=ALU.is_equal,
                                            fill=0.0, base=0, channel_multiplier=1), reads=[R_idf], writes=[R_idf])
        A("dve", lambda e: e.tensor_copy(out=idb[:], in_=idf[:]), reads=[R_idf], writes=[R_idb])
        A("pool", lambda e: e.memset(jf[:], 1.0), writes=[R_jf])
        A("pool", lambda e: e.affine_select(out=jf[:], in_=jf[:], pattern=[[1, 128]], compare_op=ALU.is_equal,
                                            fill=0.0, base=-127, channel_multiplier=1), reads=[R_jf], writes=[R_jf])
        A("pool", lambda e: e.memset(m01[:], 1.0), writes=[R_m01])
        A("pool", lambda e: e.affine_select(out=m01[:], in_=m01[:], pattern=[[1, 128]], compare_op=ALU.is_ge,
                                            fill=0.0, base=0, channel_multiplier=-1), reads=[R_m01], writes=[R_m01])
        A("pool", lambda e: e.memset(m01[0:64, 64:128], 0.0), reads=[R_m01], writes=[R_m01])
        A("pool", lambda e: e.memset(rmask[:], 1.0), writes=[R_rmask])
        A("pool", lambda e: e.memset(rmask[:].rearrange("p (c t) -> p c t", t=64)[:, :, 0:1], 0.0),
          reads=[R_rmask], writes=[R_rmask])

        R_winb, R_woutb = Res("winb"), Res("woutb")
        for i in range(8):
            A("pool", lambda e, i=i: e.dma_start(out=winb[128 * i:128 * i + 128, :], in_=win_d[128 * i:128 * i + 128, :]),
              writes=[R_winb], chan="wcv")

        xt = [sb("xt%d" % i, [128, D]) for i in range(2)]
        qA, R_qA = sb("qA", [128, 8, 256])
        osb, R_osb = sb("osb", [128, 2 * D])
        ccol, R_ccol = sb("ccol", [128, 16])
        scb, R_scb = qA[:].rearrange("p a b -> p (a b)").rearrange("p (j t) -> p j t", t=128), R_qA
        adaw, R_adaw = xt[0][0][:].rearrange("p (k n) -> p k n", n=128), xt[0][1]
        nwb, R_nwb = osb[:, 0:D], R_osb
        adab, R_adab = osb[:, D:D + 128], R_osb
        Amod, R_Amod = sb("Amod", [128, 2, D])
        shmod, R_shmod = sb("shmod", [128, 2, D])
        gate, R_gate = sb("gate", [128, D])
        A("sp", lambda e: e.dma_start(out=ccol[:], in_=ccol_d), writes=[R_ccol], chan="p0")
        A("sp", lambda e: e.dma_start(out=nwb, in_=normw_d.partition_broadcast(128)), writes=[R_nwb], chan="p2")
        A("act", lambda e: e.activation(out=ccol[:], in_=ccol[:], func=AF.Silu), reads=[R_ccol], writes=[R_ccol])
        A("dve", lambda e: e.tensor_copy(out=scb, in_=ccol[:].unsqueeze(2).broadcast_to([128, 16, 128])),
          reads=[R_ccol], writes=[R_scb])
        pm = [ps("pm%d" % i, [128, 512]) for i in range(8)]
        for blk in range(24):
            A("sp", lambda e, blk=blk: e.dma_start(out=adaw, in_=adaw_d[:, 128 * blk:128 * blk + 128].rearrange("(k p) n -> p k n", p=128)),
              writes=[R_adaw], chan="adaw")
            A("sp", lambda e, blk=blk: e.dma_start(out=adab, in_=adab_d[128 * blk:128 * blk + 128].partition_broadcast(128)), writes=[R_adab], chan="p1")
            which, off = blk // 8, (blk % 8) * 128
            for j in range(2):
                pt, R_pt = pm[j]
                for k in range(8):
                    A("pe", lambda e, j=j, k=k, pt=pt: e.matmul(pt[:, 0:128], lhsT=scb[:, j * 8 + k, :], rhs=adaw[:, k, :], start=(k == 0), stop=(k == 7)),
                      reads=[R_scb, R_adaw], writes=[R_pt])
                if which == 0:
                    A("dve", lambda e, j=j, pt=pt, off=off: e.tensor_tensor(out=shmod[:, j, off:off + 128], in0=pt[:, 0:128], in1=adab, op=ALU.add),
                      reads=[R_pt, R_adab], writes=[R_shmod])
                elif which == 1:
                    A("dve", lambda e, j=j, pt=pt, off=off: e.scalar_tensor_tensor(out=Amod[:, j, off:off + 128], in0=pt[:, 0:128], scalar=1.0, in1=adab, op0=ALU.add, op1=ALU.add),
                      reads=[R_pt, R_adab], writes=[R_Amod])
                    A("dve", lambda e, j=j, off=off: e.tensor_tensor(out=Amod[:, j, off:off + 128], in0=Amod[:, j, off:off + 128], in1=nwb[:, off:off + 128], op=ALU.mult),
                      reads=[R_Amod, R_nwb], writes=[R_Amod])
                elif j == 0:
                    A("dve", lambda e, pt=pt, off=off: e.tensor_tensor(out=gate[:, off:off + 128], in0=pt[:, 0:128], in1=adab, op=ALU.add),
                      reads=[R_pt, R_adab], writes=[R_gate])

        lbl, R_lbl = sb("lbl", [128, 32])
        lb, R_lb = sb("lb", [128, 16])
        oml, R_oml = sb("oml", [128, 16])
        lbm1, R_lbm1 = sb("lbm1", [128, 16])
        A("sp", lambda e: e.dma_start(out=lbl[:], in_=lbl_d), writes=[R_lbl], chan="p3")
        A("dve", lambda e: e.tensor_tensor(out=lb[:], in0=lbl[:, 0:16], in1=lbl[:, 16:32], op=ALU.subtract), reads=[R_lbl], writes=[R_lb])
        A("act", lambda e: e.activation(out=lb[:], in_=lb[:], func=AF.Sigmoid), reads=[R_lb], writes=[R_lb])
        A("dve", lambda e: e.tensor_scalar(out=oml[:], in0=lb[:], scalar1=-1.0, scalar2=1.0, op0=ALU.mult, op1=ALU.add), reads=[R_lb], writes=[R_oml])
        A("dve", lambda e: e.tensor_scalar(out=lbm1[:], in0=lb[:], scalar1=-1.0, scalar2=None, op0=ALU.add), reads=[R_lb], writes=[R_lbm1])

        junk, R_junk = sb("junk", [128, D], BF16)
        ssq, R_ssq = sb("ssq", [128, 2])
        hn, R_hn = sb("hn", [128, D])
        hb, R_hb = sb("hb", [128, D], BF16)
        hT, R_hT = sb("hT", [128, 8, 384], BF16)
        wt = [sb("wt%d" % i, [128, 8, 512], BF16) for i in range(2)]
        wctr = [0]
        tf = [sb("tf%d" % i, [128, 256]) for i in range(4)]
        qt, R_qt = sb("qt", [128, 8, 256], BF16)
        kdT, R_kdT = sb("kdT", [128, 8, 256], BF16)
        dA, R_dA = sb("dA", [128, 8, 4])
        kdm, R_kdm = sb("kdm", [128, 2, D], BF16)
        vA, R_vA = sb("vA", [128, 2, D], BF16)
        scT, R_scT = sb("scT", [128, 8, 128], BF16)
        SA, R_SA = sb("SA", [128, 8, 128])
        SAd, R_SAd = sb("SAd", [128, 8, 128])
        SAb, R_SAb = sb("SAb", [128, 8, 128], BF16)
        A("dve", lambda e: e.memset(SA[:], 0.0), writes=[R_SA])
        cws, R_cws = sb("cws", [128, 2, 24, 9])
        gp, R_gp = sb("gp", [16, 16])
        nA16, R_nA16 = sb("nA16", [16, 2])
        gdtr, R_gdtr = sb("gdtr", [128, 4, 8])
        nAr, R_nAr = sb("nAr", [128, 2, 8])
        onesb, R_onesb = sb("onesb", [128, 128], BF16)
        capI, R_capI = sb("capI", [128, 128])
        capS, R_capS = sb("capS", [128, 128])
        mU, R_mU = sb("mU", [128, 128])
        cind, R_cind = sb("cind", [128, 2, 128])
        Esel, R_Esel = sb("Esel", [16, 8, 128])
        A("sp", lambda e: e.dma_start(out=cws[:], in_=cw_d), writes=[R_cws], chan="p4")
        A("sp", lambda e: e.dma_start(out=gp[:], in_=gpar_d), writes=[R_gp], chan="p5")
        A("sp", lambda e: e.dma_start(out=gdtr[:].rearrange("p a h -> p (a h)"), in_=gdr_d.partition_broadcast(128)), writes=[R_gdtr], chan="p6")
        A("act", lambda e: e.activation(out=nA16[:, 0:1], in_=gp[:, 0:1], func=AF.Exp), reads=[R_gp], writes=[R_nA16])
        A("act", lambda e: e.activation(out=nA16[:, 1:2], in_=gp[:, 2:3], func=AF.Exp), reads=[R_gp], writes=[R_nA16])
        A("dve", lambda e: e.tensor_scalar(out=nA16[:], in0=nA16[:], scalar1=-1.0, scalar2=None, op0=ALU.mult), reads=[R_nA16], writes=[R_nA16])
        A("act", lambda e: e.activation(out=nAr[:], in_=gdtr[:, 0:2, :], func=AF.Exp), reads=[R_gdtr], writes=[R_nAr])
        A("dve", lambda e: e.tensor_scalar(out=nAr[:], in0=nAr[:], scalar1=-1.0, scalar2=None, op0=ALU.mult), reads=[R_nAr], writes=[R_nAr])
        A("pool", lambda e: e.memset(onesb[:], 1.0), writes=[R_onesb])
        A("dve", lambda e: e.tensor_scalar(out=capI[:], in0=m01[:], scalar1=-1.0, scalar2=30000.0, op0=ALU.add, op1=ALU.mult), reads=[R_m01], writes=[R_capI])
        A("dve", lambda e: e.tensor_tensor(out=capS[:], in0=m01[:], in1=idf[:], op=ALU.subtract), reads=[R_m01, R_idf], writes=[R_capS])
        A("dve", lambda e: e.tensor_scalar(out=capS[:], in0=capS[:], scalar1=-1.0, scalar2=30000.0, op0=ALU.add, op1=ALU.mult), reads=[R_capS], writes=[R_capS])
        A("pool", lambda e: e.memset(mU[:], 0.0), writes=[R_mU])
        A("pool", lambda e: e.memset(mU[0:64, 0:64], 1.0), reads=[R_mU], writes=[R_mU])
        A("pool", lambda e: e.memset(mU[64:128, 64:128], 1.0), reads=[R_mU], writes=[R_mU])
        A("dve", lambda e: e.tensor_tensor(out=mU[:], in0=mU[:], in1=m01[:], op=ALU.subtract), reads=[R_mU, R_m01], writes=[R_mU])
        A("pool", lambda e: e.memset(cind[:], 0.0), writes=[R_cind])
        A("pool", lambda e: e.memset(cind[0:64, 0, :], 1.0), reads=[R_cind], writes=[R_cind])
        A("pool", lambda e: e.memset(cind[64:128, 1, :], 1.0), reads=[R_cind], writes=[R_cind])
        A("dve", lambda e: e.tensor_scalar(out=Esel[:], in0=gp[:, 8:16].unsqueeze(2).broadcast_to([16, 8, 128]), scalar1=gp[:, 6:7], scalar2=None, op0=ALU.mult),
          reads=[R_gp], writes=[R_Esel])
        wab, R_wab = sb("wab", [128, 8, 32], BF16)
        dw, R_dw = sb("dw", [128, 9, 128], BF16)
        cpad, R_cpad = sb("cpad", [128, 400], BF16)
        cs, R_cs = sb("cs", [128, 256])
        sq, R_sq = sb("sq", [128, 256], BF16)
        rn, R_rn = sb("rn", [128, 256])
        qBf, R_qBf = sb("qBf", [128, 8, 256], BF16)
        kBf, R_kBf = sb("kBf", [128, 8, 256], BF16)
        vBf, R_vBf = sb("vBf", [128, 8, 256], BF16)
        qdB, R_qdB = sb("qdB", [128, 8, 256], BF16)
        rf = [sb("rf%d" % i, [16, 256]) for i in range(6)]
        rhsP, R_rhsP = sb("rhsP", [16, 8, 128])
        rhsA, R_rhsA = sb("rhsA", [16, 8, 128])
        tms = [sb("tms%d" % i, [128, 16]) for i in range(4)]
        bet, R_bet = sb("bet", [128, 8])
        bee, R_bee = sb("bee", [128, 8])
        ksf, R_ksf = sb("ksf", [128, 8])
        dB, R_dB = sb("dB", [128, 2, 8])
        kbe, R_kbe = sb("kbe", [128, 8, 128], BF16)
        kdc, R_kdc = sb("kdc", [128, 8, 128], BF16)
        bvt, R_bvt = sb("bvt", [128, 8, 128], BF16)
        eA, R_eA = sb("eA", [128, 8, 128])
        eP, R_eP = sb("eP", [128, 8, 128])
        XT = [sb("XT%d" % i, [128, 8, 128], BF16) for i in range(2)]
        XX = [sb("XX%d" % i, [128, 8, 128], BF16) for i in range(2)]
        QQ = [sb("QQ%d" % i, [128, 8, 128], BF16) for i in range(2)]
        PT, R_PT = sb("PT", [128, 8, 128], BF16)
        u0, R_u0 = sb("u0", [128, 8, 128])
        nwT, R_nwT = sb("nwT", [128, 8, 128], BF16)
        vnew, R_vnew = sb("vnew", [128, 8, 128], BF16)
        SB, R_SB = sb("SB", [128, 8, 128])
        SBb, R_SBb = sb("SBb", [128, 8, 128], BF16)
        A("dve", lambda e: e.memset(SB[:], 0.0), writes=[R_SB])
        A("pool", lambda e: e.memset(SBb[:], 0.0), writes=[R_SBb])
        fnw, R_fnw = sb("fnw", [128, D])
        hnwc, R_hnwc = sb("hnwc", [128, 16])
        hst, R_hst = sb("hst", [128, 32])
        A("sp", lambda e: e.dma_start(out=fnw[:], in_=fnw_d.partition_broadcast(128)), writes=[R_fnw], chan="p7")
        A("sp", lambda e: e.dma_start(out=hnwc[:], in_=hnw_d), writes=[R_hnwc], chan="p8")
        for kk in range(16):
            x_t, R_x = xt[1]
            A("sp", lambda e, kk=kk: e.dma_start(out=x_t[:], in_=wout_d[128 * kk:128 * kk + 128, :]), writes=[R_x], chan="x1")
            A("dve", lambda e, kk=kk: e.tensor_scalar(out=hb[:], in0=x_t[:], scalar1=hnwc[:, kk:kk + 1], scalar2=None, op0=ALU.mult), reads=[R_x, R_hnwc], writes=[R_hb])
            A("sp", lambda e, kk=kk: e.dma_start(out=woutb[128 * kk:128 * kk + 128, :], in_=hb[:]), reads=[R_hb], writes=[R_woutb], chan="wo_st")
        szs = [(Amod[:, 1, :].bitcast(BF16), R_Amod), (shmod[:, 1, :].bitcast(BF16), R_shmod)]
        yT2 = [XT[1], XX[1]]
        yb2 = [(kbe[:].rearrange("p h c -> p (h c)"), R_kbe), (kdc[:].rearrange("p h c -> p (h c)"), R_kdc)]
        dw2, R_dw2 = sb("dw2", [128, 9, 128], BF16)
        cpad2, R_cpad2 = sb("cpad2", [128, 400], BF16)
        cs2, R_cs2 = sb("cs2", [128, 256])
        sq2, R_sq2 = sb("sq2", [128, 256], BF16)
        rn2, R_rn2 = sb("rn2", [128, 256])
        NB = [(cs, R_cs, sq, R_sq, rn, R_rn), (cs2, R_cs2, sq2, R_sq2, rn2, R_rn2)]
        CP = [(cpad, R_cpad), (cpad2, R_cpad2)]
        DW = [(dw, R_dw), (dw2, R_dw2)]
        print("sbuf bytes remaining", nc.sbuf_bytes_remaining)


        def load_w(c0, ncol):
            i = wctr[0] % 2
            wctr[0] += 1
            t, R = wt[i]
            A("sp", lambda e: e.dma_start(out=t[:, :, 0:ncol], in_=winb[:, c0:c0 + ncol].rearrange("(k p) n -> p k n", p=128)),
              reads=[R_winb], writes=[R], chan="w%d" % i)
            return t, R

        def xprep(src, row0, i, col0, j):
            x_t, R_x = xt[i % 2]
            A("sp", lambda e: e.dma_start(out=x_t[:], in_=src[row0:row0 + 128, :]), writes=[R_x], chan="x%d" % (i % 2))
            A("act", lambda e: e.activation(out=junk[:], in_=x_t[:], func=AF.Square, accum_out=ssq[:, 0:1]),
              reads=[R_x], writes=[R_junk, R_ssq])
            A("act", lambda e: e.activation(out=ssq[:, 1:2], in_=ssq[:, 0:1], func=AF.Ln, scale=1.0 / D, bias=EPS), reads=[R_ssq], writes=[R_ssq])
            A("act", lambda e: e.activation(out=ssq[:, 1:2], in_=ssq[:, 1:2], func=AF.Exp, scale=-0.5), reads=[R_ssq], writes=[R_ssq])
            A("dve", lambda e: e.scalar_tensor_tensor(out=hn[:], in0=x_t[:], scalar=ssq[:, 1:2], in1=Amod[:, j, :], op0=ALU.mult, op1=ALU.mult),
              reads=[R_x, R_ssq, R_Amod], writes=[R_hn])
            A("pool", lambda e: e.tensor_tensor(out=hb[:], in0=hn[:], in1=shmod[:, j, :], op=ALU.add), reads=[R_hn, R_shmod], writes=[R_hb])
            pt, R_pt = pm[7]
            ptb = pt[:].bitcast(BF16)
            for k in range(8):
                A("pe", lambda e, k=k: e.transpose(out=ptb[:, 128 * k:128 * k + 128], in_=hb[:, 128 * k:128 * k + 128], identity=idb[:]),
                  reads=[R_hb, R_idb], writes=[R_pt])
            A("act", lambda e: e.activation(out=hT[:, :, col0:col0 + 128], in_=ptb.rearrange("p (k t) -> p k t", t=128), func=AF.Copy),
              reads=[R_pt], writes=[R_hT])

        def group(s, src, row0, ntile_w, off, j, readout, gidx):
            T = 256
            for i in range(ntile_w):
                if ntile_w == 3 and gidx >= 1 and i == 0:
                    A("pool", lambda e: e.tensor_copy(out=hT[:, :, 0:128], in_=hT[:, :, 256:384]), reads=[R_hT], writes=[R_hT])
                    continue
                xprep(src, row0 + 128 * i, i, 128 * i, j)
            main = slice(off, off + T)
            for half in (range(2) if readout else ()):
                w, R_w = load_w(C_AQ + 512 * half, 512)
                for hh in range(4):
                    h = 4 * half + hh
                    pt, R_pt = pm[hh % 2]
                    for k in range(8):
                        A("pe", lambda e, k=k, hh=hh, pt=pt, w=w: e.matmul(pt[:, 0:T], lhsT=w[:, k, 128 * hh:128 * hh + 128], rhs=hT[:, k, main], start=(k == 0), stop=(k == 7)),
                          reads=[R_w, R_hT], writes=[R_pt])
                    A("act", lambda e, h=h, pt=pt: e.activation(out=qA[:, h, :], in_=pt[:, 0:T], func=AF.Silu), reads=[R_pt], writes=[R_qA])
            caf = C_AF0 if s == 0 else C_AF1
            for half in range(2):
                w, R_w = load_w(caf + 512 * half, 512)
                for hh in range(4):
                    h = 4 * half + hh
                    li = s * 8 + h
                    pt, R_pt = pm[2 + hh % 2]
                    (sig, R_sig), (g, R_g), (b, R_b), (dl, R_dl) = tf
                    for k in range(8):
                        A("pe", lambda e, k=k, hh=hh, pt=pt, w=w: e.matmul(pt[:, 0:T], lhsT=w[:, k, 128 * hh:128 * hh + 128], rhs=hT[:, k, main], start=(k == 0), stop=(k == 7)),
                          reads=[R_w, R_hT], writes=[R_pt])
                    A("act", lambda e, pt=pt: e.activation(out=sig[:], in_=pt[:, 0:T], func=AF.Sigmoid), reads=[R_pt], writes=[R_sig])
                    A("act", lambda e, li=li: e.activation(out=g[:], in_=sig[:], func=AF.Ln, scale=oml[:, li:li + 1], bias=lb[:, li:li + 1]),
                      reads=[R_sig, R_oml, R_lb], writes=[R_g])
                    A("dve", lambda e, li=li: e.tensor_scalar(out=sig[:], in0=sig[:], scalar1=-1.0, scalar2=lbm1[:, li:li + 1], op0=ALU.add, op1=ALU.mult),
                      reads=[R_sig, R_lbm1], writes=[R_sig])
                    A("dve", lambda e: e.tensor_tensor_scan(out=b[:], data0=rmask[:, 0:T], data1=g[:], initial=0.0, op0=ALU.mult, op1=ALU.add),
                      reads=[R_rmask, R_g], writes=[R_b])
                    b3 = b[:].rearrange("p (c t) -> p c t", t=64)
                    A("dve", lambda e, b3=b3: e.tensor_tensor(out=dl[:].rearrange("p (c t) -> p c t", t=64), in0=b3, in1=b3[:, :, 63:64].broadcast_to([128, 4, 64]), op=ALU.subtract),
                      reads=[R_b], writes=[R_dl])
                    A("act", lambda e, h=h, b3=b3: e.activation(out=dA[:, h, :], in_=b3[:, :, 63], func=AF.Exp), reads=[R_b], writes=[R_dA])
                    if readout:
                        A("act", lambda e: e.activation(out=g[:], in_=dl[:], func=AF.Exp), reads=[R_dl], writes=[R_g])
                    A("act", lambda e: e.activation(out=b[:], in_=dl[:], func=AF.Exp, scale=-1.0), reads=[R_dl], writes=[R_b])
                    if readout:
                        A("dve", lambda e, h=h: e.scalar_tensor_tensor(out=qt[:, h, :], in0=qA[:, h, :], scalar=float(128 ** -0.5), in1=g[:], op0=ALU.mult, op1=ALU.mult),
                          reads=[R_qA, R_g], writes=[R_qt])
                    A("pool", lambda e, h=h: e.tensor_tensor(out=kdT[:, h, :], in0=sig[:], in1=b[:], op=ALU.mult),
                      reads=[R_sig, R_b], writes=[R_kdT])
            for half in range(2):
                w, R_w = load_w(C_AI + 512 * half, 512)
                for ti in range(2):
                    pt, R_pt = pm[4 + ti]
                    for k in range(8):
                        A("pe", lambda e, k=k, ti=ti, pt=pt, w=w: e.matmul(pt[:], lhsT=hT[:, k, off + 128 * ti:off + 128 * ti + 128], rhs=w[:, k, :], start=(k == 0), stop=(k == 7)),
                          reads=[R_w, R_hT], writes=[R_pt])
                    A("act", lambda e, ti=ti, pt=pt, half=half: e.activation(out=vA[:, ti, 512 * half:512 * half + 512], in_=pt[:], func=AF.Copy),
                      reads=[R_pt], writes=[R_vA])
            for ti in range(2):
                pt, R_pt = pm[6]
                ptb = pt[:].bitcast(BF16)
                for h in range(8):
                    A("pe", lambda e, h=h, ti=ti, ptb=ptb: e.transpose(out=ptb[:, 128 * h:128 * h + 128], in_=kdT[:, h, 128 * ti:128 * ti + 128], identity=idb[:]),
                      reads=[R_kdT, R_idb], writes=[R_pt])
                A("dve", lambda e, ti=ti, ptb=ptb: e.tensor_copy(out=kdm[:, ti, :], in_=ptb), reads=[R_pt], writes=[R_kdm])
            if do_gdn:
                lat = (ntile_w == 3)
                WN = 128 * ntile_w
                if gidx <= 0:
                    A("pool", lambda e: e.memset(cpad[:], 0.0), writes=[R_cpad])
                    A("pool", lambda e: e.memset(cpad2[:], 0.0), writes=[R_cpad2])
                if lat:
                    cpv = cpad[:, 0:396].rearrange("p (r c) -> p r c", c=66)
                def stageA(blk, w, R_w, hh):
                    cpd, Rcp = CP[blk % 2]
                    cpv_ = cpd[:, 0:396].rearrange("p (r c) -> p r c", c=66)
                    pt, R_pt = pm[blk % 2]
                    for k in range(8):
                        A("pe", lambda e, k=k, hh=hh, pt=pt, w=w: e.matmul(pt[:, 0:WN], lhsT=w[:, k, 128 * hh:128 * hh + 128], rhs=hT[:, k, 0:WN], start=(k == 0), stop=(k == 7)),
                          reads=[R_w, R_hT], writes=[R_pt])
                    if lat:
                        A("act", lambda e, pt=pt, cpv_=cpv_: e.activation(out=cpv_[:, :, 1:65], in_=pt[:, 0:384].rearrange("p (r c) -> p r c", c=64), func=AF.Copy),
                          reads=[R_pt], writes=[Rcp])
                        if gidx == 0:
                            A("pool", lambda e, cpv_=cpv_: e.memset(cpv_[:, 0, :], 0.0), reads=[Rcp], writes=[Rcp])
                        if s == 1 and gidx == ng - 1:
                            A("pool", lambda e, cpv_=cpv_: e.memset(cpv_[:, 5, :], 0.0), reads=[Rcp], writes=[Rcp])
                    else:
                        A("act", lambda e, pt=pt, cpd=cpd: e.activation(out=cpd[:, 1:257], in_=pt[:, 0:256], func=AF.Copy), reads=[R_pt], writes=[Rcp])
                    dwb, Rdw = DW[blk % 2]
                    A("pool", lambda e, blk=blk, dwb=dwb: e.tensor_tensor(out=dwb[:], in0=idb[:].unsqueeze(1).broadcast_to([128, 9, 128]),
                                                                          in1=cws[:, s, blk, :].unsqueeze(2).broadcast_to([128, 9, 128]), op=ALU.mult),
                      reads=[R_idb, R_cws], writes=[Rdw])
                    pc, R_pc = pm[2 + blk % 2]
                    taps = [(i, jj) for i in range(3) for jj in range(3)] if lat else [(1, jj) for jj in range(3)]
                    for n, (i, jj) in enumerate(taps):
                        if lat:
                            A("pe", lambda e, i=i, jj=jj, n=n, pc=pc, nt=len(taps), dwb=dwb, cpv_=cpv_: e.matmul(pc[:, 0:256].rearrange("p (r c) -> p r c", c=64), lhsT=dwb[:, 3 * i + jj, :], rhs=cpv_[:, i:i + 4, jj:jj + 64], start=(n == 0), stop=(n == nt - 1)),
                              reads=[Rdw, Rcp], writes=[R_pc])
                        else:
                            A("pe", lambda e, i=i, jj=jj, n=n, pc=pc, nt=len(taps), dwb=dwb, cpd=cpd: e.matmul(pc[:, 0:256], lhsT=dwb[:, 3 * i + jj, :], rhs=cpd[:, jj:jj + 256], start=(n == 0), stop=(n == nt - 1)),
                              reads=[Rdw, Rcp], writes=[R_pc])

                def stageB(blk):
                    kind, h = blk // 8, blk % 8
                    pc, R_pc = pm[2 + blk % 2]
                    if kind == 2:
                        A("act", lambda e, h=h, pc=pc: e.activation(out=vBf[:, h, :], in_=pc[:, 0:256], func=AF.Silu), reads=[R_pc], writes=[R_vBf])
                    else:
                        c_, Rc_, q_s, Rq_, r_, Rr_ = NB[blk % 2]
                        A("act", lambda e, pc=pc, c_=c_: e.activation(out=c_[:], in_=pc[:, 0:256], func=AF.Silu), reads=[R_pc], writes=[Rc_])
                        A("act", lambda e, c_=c_, q_s=q_s: e.activation(out=q_s[:], in_=c_[:], func=AF.Square), reads=[Rc_], writes=[Rq_])
                        pn, R_pn = pm[4 + blk % 2]
                        A("pe", lambda e, pn=pn, q_s=q_s: e.matmul(pn[:, 0:256], lhsT=onesb[:], rhs=q_s[:], start=True, stop=True), reads=[R_onesb, Rq_], writes=[R_pn])
                        A("act", lambda e, pn=pn, r_=r_: e.activation(out=r_[:], in_=pn[:, 0:256], func=AF.Ln, bias=EPS), reads=[R_pn], writes=[Rr_])
                        A("act", lambda e, r_=r_: e.activation(out=r_[:], in_=r_[:], func=AF.Exp, scale=-0.5), reads=[Rr_], writes=[Rr_])
                        dst, R_dst = (qBf, R_qBf) if kind == 0 else (kBf, R_kBf)
                        sc_ = float(128 ** -0.5) if kind == 0 else 1.0
                        A("dve", lambda e, h=h, dst=dst, sc_=sc_, c_=c_, r_=r_: e.scalar_tensor_tensor(out=dst[:, h, :], in0=c_[:], scalar=sc_, in1=r_[:], op0=ALU.mult, op1=ALU.mult),
                          reads=[Rc_, Rr_], writes=[R_dst])

                pend = None
                for m in range(6):
                    if m < 2 and not readout:
                        continue
                    w, R_w = load_w(C_BQ + 512 * m, 512)
                    for hh in range(4):
                        blk = 4 * m + hh
                        stageA(blk, w, R_w, hh)
                        if pend is not None:
                            stageB(pend)
                        pend = blk
                if pend is not None:
                    stageB(pend)
                if gidx <= 0:
                    ca_, cb_ = C_BA + 8 * s, C_BB + 8 * s
                    for q_, c_ in enumerate((ca_, ca_, cb_, cb_)):
                        A("sp", lambda e, q_=q_, c_=c_: e.dma_start(out=wab[:, :, 8 * q_:8 * q_ + 8], in_=winb[:, c_:c_ + 8].rearrange("(k p) n -> p k n", p=128)),
                          reads=[R_winb], writes=[R_wab], chan="wab")
                pz, R_pz = pm[6]
                for k in range(8):
                    A("pe", lambda e, k=k: e.matmul(pz[0:16, 0:256], lhsT=wab[:, k, 0:16], rhs=hT[:, k, main], start=(k == 0), stop=(k == 7)), reads=[R_wab, R_hT], writes=[R_pz])
                for k in range(8):
                    A("pe", lambda e, k=k: e.matmul(pz[0:16, 256:512], lhsT=wab[:, k, 16:32], rhs=hT[:, k, main], start=(k == 0), stop=(k == 7)), reads=[R_wab, R_hT], writes=[R_pz])
                (r0, R_r0), (r1, R_r1), (r2, R_r2), (r3, R_r3), (r4, R_r4), (r5, R_r5) = rf
                A("act", lambda e: e.activation(out=r0[:], in_=pz[0:16, 0:256], func=AF.Exp, bias=gp[:, 1 + 2 * s:2 + 2 * s]), reads=[R_pz, R_gp], writes=[R_r0])
                A("act", lambda e: e.activation(out=r0[:], in_=r0[:], func=AF.Ln, bias=1.0), reads=[R_r0], writes=[R_r0])
                A("dve", lambda e: e.tensor_scalar(out=r0[:], in0=r0[:], scalar1=nA16[:, s:s + 1], scalar2=None, op0=ALU.mult), reads=[R_r0, R_nA16], writes=[R_r0])
                A("act", lambda e: e.activation(out=r1[:], in_=pz[0:16, 256:512], func=AF.Exp, scale=-1.0), reads=[R_pz], writes=[R_r1])
                A("act", lambda e: e.activation(out=r1[:], in_=r1[:], func=AF.Ln, bias=1.0), reads=[R_r1], writes=[R_r1])
                A("dve", lambda e: e.tensor_tensor_scan(out=r2[:], data0=rmask[0:16, 0:256], data1=r0[:], initial=0.0, op0=ALU.mult, op1=ALU.add), reads=[R_rmask, R_r0], writes=[R_r2])
                A("dve", lambda e: e.tensor_tensor(out=r1[:], in0=r2[:], in1=r1[:], op=ALU.subtract), reads=[R_r2, R_r1], writes=[R_r1])
                A("dve", lambda e: e.tensor_scalar(out=r3[:], in0=r2[:], scalar1=gp[:, 4:5], scalar2=gp[:, 5:6], op0=ALU.mult, op1=ALU.add), reads=[R_r2, R_gp], writes=[R_r3])
                A("dve", lambda e: e.tensor_scalar(out=r4[:], in0=r2[:], scalar1=gp[:, 6:7], scalar2=gp[:, 7:8], op0=ALU.mult, op1=ALU.add), reads=[R_r2, R_gp], writes=[R_r4])
                A("dve", lambda e: e.tensor_scalar(out=r5[:], in0=r1[:], scalar1=gp[:, 6:7], scalar2=gp[:, 7:8], op0=ALU.mult, op1=ALU.add), reads=[R_r1, R_gp], writes=[R_r5])
                for h in (range(8) if readout else ()):
                    pb, R_pb = pm[h % 2]
                    A("pe", lambda e, h=h, pb=pb: e.matmul(pb[:, 0:256], lhsT=Esel[:, h, :], rhs=r4[:], start=True, stop=True), reads=[R_Esel, R_r4], writes=[R_pb])
                    tq_, Rtq_ = NB[h % 2][0], NB[h % 2][1]
                    A("act", lambda e, pb=pb, tq_=tq_: e.activation(out=tq_[:], in_=pb[:, 0:256], func=AF.Exp), reads=[R_pb], writes=[Rtq_])
                    A("dve", lambda e, h=h, tq_=tq_: e.tensor_tensor(out=qdB[:, h, :], in0=qBf[:, h, :], in1=tq_[:], op=ALU.mult), reads=[R_qBf, Rtq_], writes=[R_qdB])
            if s == 1 and readout:
                for q_ in range(4):
                    w, R_w = load_w((C_AZ if q_ < 2 else C_BZ) + 512 * (q_ % 2), 512)
                    for ti in range(2):
                        pt, R_pt = pm[4 + ti]
                        for k in range(8):
                            A("pe", lambda e, k=k, ti=ti, pt=pt, w=w: e.matmul(pt[:], lhsT=hT[:, k, off + 128 * ti:off + 128 * ti + 128], rhs=w[:, k, :], start=(k == 0), stop=(k == 7)),
                              reads=[R_w, R_hT], writes=[R_pt])
                        A("act", lambda e, ti=ti, pt=pt, q_=q_: e.activation(out=szs[ti][0][:, 512 * q_:512 * q_ + 512], in_=pt[:], func=AF.Silu), reads=[R_pt], writes=[szs[ti][1]])
            for ti in range(2):
                tok = slice(128 * ti, 128 * ti + 128)
                if readout:
                    for hq in range(2):
                        pt, R_pt = pm[hq]
                        for hh in range(4):
                            h = 4 * hq + hh
                            A("pe", lambda e, h=h, hh=hh, pt=pt, tok=tok: e.matmul(pt[:, 128 * hh:128 * hh + 128], lhsT=kdT[:, h, tok], rhs=qt[:, h, tok], start=True, stop=True),
                              reads=[R_kdT, R_qt], writes=[R_pt])
                        A("dve", lambda e, hq=hq, pt=pt: e.tensor_tensor(out=scT[:, 4 * hq:4 * hq + 4, :], in0=pt[:].rearrange("p (h t) -> p h t", t=128),
                                                                         in1=m01[:].unsqueeze(1).broadcast_to([128, 4, 128]), op=ALU.mult),
                          reads=[R_pt, R_m01], writes=[R_scT])
                po = [pm[2], pm[3]]
                pre = (s == 1 and readout)
                if pre:
                    nr = NTH - 128 - ((gidx - ng // 2) * 256 + 128 * ti)
                    A("sp", lambda e, nr=nr: e.dma_start(out=osb[:], in_=o1_d[nr:nr + 128, :]), reads=[R_o1], writes=[R_osb], chan="o1l")
                    for q_, (pt, R_pt) in enumerate((pm[2], pm[3], pm[6], pm[7])):
                        A("pe", lambda e, q_=q_, pt=pt: e.matmul(pt[:], lhsT=jf[:], rhs=osb[:, 512 * q_:512 * q_ + 512], start=True, stop=False), reads=[R_jf, R_osb], writes=[R_pt])
                for c in range(2):
                    rows = slice(64 * c, 64 * c + 64)
                    ctok = slice(128 * ti + 64 * c, 128 * ti + 64 * c + 64)
                    cg = 2 * ti + c
                    A("dve", lambda e, cg=cg: e.tensor_tensor(out=SAd[:], in0=SA[:], in1=dA[:, :, cg:cg + 1].broadcast_to([128, 8, 128]), op=ALU.mult),
                      reads=[R_SA, R_dA], writes=[R_SAd])
                    if readout:
                        A("act", lambda e: e.activation(out=SAb[:], in_=SAd[:], func=AF.Copy), reads=[R_SAd], writes=[R_SAb])
                        for h in range(8):
                            pt, R_pt = po[h // 4]
                            cols = slice(128 * (h % 4), 128 * (h % 4) + 128)
                            A("pe", lambda e, h=h, pt=pt, cols=cols, rows=rows, ctok=ctok: e.matmul(pt[rows, cols], lhsT=qt[:, h, ctok], rhs=SAb[:, h, :], start=(not pre), stop=False),
                              reads=[R_qt, R_SAb], writes=[R_pt])
                            A("pe", lambda e, h=h, pt=pt, cols=cols, rows=rows, ti=ti: e.matmul(pt[rows, cols], lhsT=scT[rows, h, rows], rhs=vA[rows, ti, 128 * h:128 * h + 128], start=False, stop=True),
                              reads=[R_scT, R_vA], writes=[R_pt])
                    for hq in range(2):
                        pt, R_pt = pm[4 + hq]
                        for hh in range(4):
                            h = 4 * hq + hh
                            A("pe", lambda e, h=h, hh=hh, pt=pt, rows=rows, ti=ti: e.matmul(pt[:, 128 * hh:128 * hh + 128], lhsT=kdm[rows, ti, 128 * h:128 * h + 128], rhs=vA[rows, ti, 128 * h:128 * h + 128], start=True, stop=True),
                              reads=[R_kdm, R_vA], writes=[R_pt])
                        A("dve", lambda e, hq=hq, pt=pt: e.tensor_tensor(out=SA[:, 4 * hq:4 * hq + 4, :], in0=pt[:].rearrange("p (h v) -> p h v", v=128), in1=SAd[:, 4 * hq:4 * hq + 4, :], op=ALU.add),
                          reads=[R_pt, R_SAd], writes=[R_SA])
                if readout:
                    for hq in range(2):
                        pt, R_pt = po[hq]
                        A("act", lambda e, hq=hq, pt=pt: e.activation(out=osb[:, 512 * hq:512 * hq + 512], in_=pt[:], func=AF.Copy), reads=[R_pt], writes=[R_osb])
                if do_gdn:
                    (t0, R_t0), (t1, R_t1), (t2, R_t2), (t3, R_t3) = tms
                    mtok = slice(off + 128 * ti, off + 128 * ti + 128)
                    pz, R_pz = pm[0]
                    for k in range(8):
                        A("pe", lambda e, k=k, mtok=mtok: e.matmul(pz[:, 0:16], lhsT=hT[:, k, mtok], rhs=wab[:, k, 8:24], start=(k == 0), stop=(k == 7)), reads=[R_wab, R_hT], writes=[R_pz])
                    A("dve", lambda e: e.tensor_tensor(out=t0[:, 0:8], in0=pz[:, 0:8], in1=gdtr[:, 2 + s, :], op=ALU.add), reads=[R_pz, R_gdtr], writes=[R_t0])
                    A("act", lambda e: e.activation(out=t0[:, 0:8], in_=t0[:, 0:8], func=AF.Exp), reads=[R_t0], writes=[R_t0])
                    A("act", lambda e: e.activation(out=t0[:, 0:8], in_=t0[:, 0:8], func=AF.Ln, bias=1.0), reads=[R_t0], writes=[R_t0])
                    A("dve", lambda e: e.tensor_tensor(out=t0[:, 0:8], in0=t0[:, 0:8], in1=nAr[:, s, :], op=ALU.mult), reads=[R_t0, R_nAr], writes=[R_t0])
                    A("act", lambda e: e.activation(out=t0[:, 8:16], in_=pz[:, 8:16], func=AF.Exp, scale=-1.0), reads=[R_pz], writes=[R_t0])
                    A("dve", lambda e: e.tensor_scalar(out=t0[:, 8:16], in0=t0[:, 8:16], scalar1=1.0, scalar2=None, op0=ALU.add), reads=[R_t0], writes=[R_t0])
                    A("dve", lambda e: e.reciprocal(out=bet[:], in_=t0[:, 8:16]), reads=[R_t0], writes=[R_bet])
                    pz2, R_pz2 = pm[1]
                    A("pe", lambda e: e.matmul(pz2[:, 0:8], lhsT=m01[:], rhs=t0[:, 0:8], start=True, stop=True), reads=[R_m01, R_t0], writes=[R_pz2])
                    A("pe", lambda e: e.matmul(pz2[:, 8:16], lhsT=mU[:], rhs=t0[:, 0:8], start=True, stop=True), reads=[R_mU, R_t0], writes=[R_pz2])
                    for c in range(2):
                        A("pe", lambda e, c=c: e.matmul(pz2[:, 16 + 8 * c:24 + 8 * c], lhsT=cind[:, c, :], rhs=t0[:, 0:8], start=True, stop=True), reads=[R_cind, R_t0], writes=[R_pz2])
                    A("act", lambda e: e.activation(out=bee[:], in_=pz2[:, 0:8], func=AF.Exp), reads=[R_pz2], writes=[R_bee])
                    A("act", lambda e: e.activation(out=ksf[:], in_=pz2[:, 8:16], func=AF.Exp), reads=[R_pz2], writes=[R_ksf])
                    A("act", lambda e: e.activation(out=dB[:].rearrange("p c h -> p (c h)"), in_=pz2[:, 16:32], func=AF.Exp), reads=[R_pz2], writes=[R_dB])
                    A("dve", lambda e: e.tensor_tensor(out=bee[:], in0=bee[:], in1=bet[:], op=ALU.mult), reads=[R_bee, R_bet], writes=[R_bee])
                    pk, R_pk = pm[0]
                    pkb = pk[:].bitcast(BF16)
                    for h in range(8):
                        A("pe", lambda e, h=h, tok=tok: e.transpose(out=pkb[:, 128 * h:128 * h + 128], in_=kBf[:, h, tok], identity=idb[:]), reads=[R_kBf, R_idb], writes=[R_pk])
                    pk3 = pkb.rearrange("p (h c) -> p h c", c=128)
                    A("dve", lambda e: e.tensor_tensor(out=kbe[:], in0=pk3, in1=bee[:].unsqueeze(2).broadcast_to([128, 8, 128]), op=ALU.mult), reads=[R_pk, R_bee], writes=[R_kbe])
                    A("dve", lambda e: e.tensor_tensor(out=kdc[:], in0=pk3, in1=ksf[:].unsqueeze(2).broadcast_to([128, 8, 128]), op=ALU.mult), reads=[R_pk, R_ksf], writes=[R_kdc])
                    pv_, R_pv_ = pm[1]
                    pvb = pv_[:].bitcast(BF16)
                    for h in range(8):
                        A("pe", lambda e, h=h, tok=tok: e.transpose(out=pvb[:, 128 * h:128 * h + 128], in_=vBf[:, h, tok], identity=idb[:]), reads=[R_vBf, R_idb], writes=[R_pv_])
                    A("dve", lambda e: e.tensor_tensor(out=bvt[:], in0=pvb.rearrange("p (h c) -> p h c", c=128), in1=bet[:].unsqueeze(2).broadcast_to([128, 8, 128]), op=ALU.mult),
                      reads=[R_pv_, R_bet], writes=[R_bvt])
                    A("dve", lambda e, tok=tok: e.tensor_tensor(out=rhsP[:], in0=rf[4][0][:, tok].unsqueeze(1).broadcast_to([16, 8, 128]), in1=gp[:, 8:16].unsqueeze(2).broadcast_to([16, 8, 128]), op=ALU.mult),
                      reads=[rf[4][1], R_gp], writes=[R_rhsP])
                    A("dve", lambda e, tok=tok: e.tensor_tensor(out=rhsA[:], in0=rf[5][0][:, tok].unsqueeze(1).broadcast_to([16, 8, 128]), in1=gp[:, 8:16].unsqueeze(2).broadcast_to([16, 8, 128]), op=ALU.mult),
                      reads=[rf[5][1], R_gp], writes=[R_rhsA])
                    for (rh, R_rh, cap, R_cap, ee, R_ee, b0) in (((rhsA, R_rhsA, capS, R_capS, eA, R_eA, 2), (rhsP, R_rhsP, capI, R_capI, eP, R_eP, 4)) if readout else ((rhsA, R_rhsA, capS, R_capS, eA, R_eA, 2),)):
                        for hq in range(2):
                            pd, R_pd = pm[b0 + hq]
                            for hh in range(4):
                                h = 4 * hq + hh
                                A("pe", lambda e, h=h, hh=hh, pd=pd, rh=rh, tok=tok: e.matmul(pd[:, 128 * hh:128 * hh + 128], lhsT=rf[3][0][:, tok], rhs=rh[:, h, :], start=True, stop=True),
                                  reads=[rf[3][1], R_rh], writes=[R_pd])
                            A("dve", lambda e, hq=hq, pd=pd, cap=cap, ee=ee: e.tensor_tensor(out=ee[:, 4 * hq:4 * hq + 4, :], in0=pd[:].rearrange("p (h t) -> p h t", t=128),
                                                                                           in1=cap[:].unsqueeze(1).broadcast_to([128, 4, 128]), op=ALU.min),
                              reads=[R_pd, R_cap], writes=[R_ee])
                        A("act", lambda e, ee=ee: e.activation(out=ee[:], in_=ee[:], func=AF.Exp), reads=[R_ee], writes=[R_ee])
                    (xt0_, R_xt0), (xx0_, R_xx0), (q0_, R_q0) = XT[0], XX[0], QQ[0]
                    for hq in range(2):
                        pa, R_pa = pm[2 + hq]
                        pb, R_pb = pm[4 + hq]
                        for hh in range(4):
                            h = 4 * hq + hh
                            A("pe", lambda e, h=h, hh=hh, pa=pa, tok=tok: e.matmul(pa[:, 128 * hh:128 * hh + 128], lhsT=kBf[:, h, tok], rhs=kBf[:, h, tok], start=True, stop=True), reads=[R_kBf], writes=[R_pa])
                            if readout:
                                A("pe", lambda e, h=h, hh=hh, pb=pb, tok=tok: e.matmul(pb[:, 128 * hh:128 * hh + 128], lhsT=kBf[:, h, tok], rhs=qBf[:, h, tok], start=True, stop=True), reads=[R_kBf, R_qBf], writes=[R_pb])
                        A("dve", lambda e, hq=hq, pa=pa: e.scalar_tensor_tensor(out=xt0_[:, 4 * hq:4 * hq + 4, :], in0=pa[:].rearrange("p (h t) -> p h t", t=128), scalar=-1.0, in1=eA[:, 4 * hq:4 * hq + 4, :], op0=ALU.mult, op1=ALU.mult),
                          reads=[R_pa, R_eA], writes=[R_xt0])
                        if readout:
                            A("dve", lambda e, hq=hq, pb=pb: e.tensor_tensor(out=PT[:, 4 * hq:4 * hq + 4, :], in0=pb[:].rearrange("p (h t) -> p h t", t=128), in1=eP[:, 4 * hq:4 * hq + 4, :], op=ALU.mult),
                              reads=[R_pb, R_eP], writes=[R_PT])
                    px, R_px = pm[0]
                    pxb = px[:].bitcast(BF16)
                    for h in range(8):
                        A("pe", lambda e, h=h: e.transpose(out=pxb[:, 128 * h:128 * h + 128], in_=xt0_[:, h, :], identity=idb[:]), reads=[R_xt0, R_idb], writes=[R_px])
                    A("act", lambda e: e.activation(out=xx0_[:], in_=pxb.rearrange("p (h c) -> p h c", c=128), func=AF.Copy), reads=[R_px], writes=[R_xx0])
                    A("pool", lambda e: e.tensor_tensor(out=q0_[:], in0=xt0_[:], in1=idb[:].unsqueeze(1).broadcast_to([128, 8, 128]), op=ALU.add), reads=[R_xt0, R_idb], writes=[R_q0])
                    for l in range(6):
                        (xtc, R_xtc), (xxc, R_xxc) = XT[l % 2], XX[l % 2]
                        (xtn, R_xtn), (xxn, R_xxn) = XT[(l + 1) % 2], XX[(l + 1) % 2]
                        (qc, R_qc), (qn, R_qn) = QQ[l % 2], QQ[(l + 1) % 2]
                        for hq in range(2):
                            hs = slice(4 * hq, 4 * hq + 4)
                            if l < 5:
                                p1, R_p1 = pm[2 + hq]
                                p2, R_p2 = pm[4 + hq]
                                for hh in range(4):
                                    h = 4 * hq + hh
                                    A("pe", lambda e, h=h, hh=hh, p1=p1, xxc=xxc, xtc=xtc: e.matmul(p1[:, 128 * hh:128 * hh + 128], lhsT=xxc[:, h, :], rhs=xtc[:, h, :], start=True, stop=True), reads=[R_xxc, R_xtc], writes=[R_p1])
                                    A("pe", lambda e, h=h, hh=hh, p2=p2, xxc=xxc, xtc=xtc: e.matmul(p2[:, 128 * hh:128 * hh + 128], lhsT=xtc[:, h, :], rhs=xxc[:, h, :], start=True, stop=True), reads=[R_xxc, R_xtc], writes=[R_p2])
                            if l >= 1:
                                p3, R_p3 = pm[hq]
                                for hh in range(4):
                                    h = 4 * hq + hh
                                    A("pe", lambda e, h=h, hh=hh, p3=p3, xxc=xxc, qc=qc: e.matmul(p3[:, 128 * hh:128 * hh + 128], lhsT=xxc[:, h, :], rhs=qc[:, h, :], start=True, stop=True), reads=[R_xxc, R_qc], writes=[R_p3])
                                A("dve", lambda e, hs=hs, p3=p3, qc=qc, qn=qn: e.tensor_tensor(out=qn[:, hs, :], in0=p3[:].rearrange("p (h t) -> p h t", t=128), in1=qc[:, hs, :], op=ALU.add),
                                  reads=[R_p3, R_qc], writes=[R_qn])
                            if l < 5:
                                A("act", lambda e, hs=hs, p1=p1, xtn=xtn: e.activation(out=xtn[:, hs, :], in_=p1[:].rearrange("p (h t) -> p h t", t=128), func=AF.Copy), reads=[R_p1], writes=[R_xtn])
                                A("pool" if False else "dve", lambda e, hs=hs, p2=p2, xxn=xxn: e.tensor_copy(out=xxn[:, hs, :], in_=p2[:].rearrange("p (h t) -> p h t", t=128)), reads=[R_p2], writes=[R_xxn])
                        if l == 0:
                            A("pool", lambda e: e.tensor_copy(out=QQ[1][0][:], in_=QQ[0][0][:]), reads=[QQ[0][1]], writes=[QQ[1][1]])
                    Qf, R_Qf = QQ[0]
                    for hq in range(2):
                        p1, R_p1 = pm[2 + hq]
                        p2, R_p2 = pm[4 + hq]
                        for hh in range(4):
                            h = 4 * hq + hh
                            A("pe", lambda e, h=h, hh=hh, p1=p1: e.matmul(p1[:, 128 * hh:128 * hh + 128], lhsT=Qf[:, h, :], rhs=bvt[:, h, :], start=True, stop=True), reads=[R_Qf, R_bvt], writes=[R_p1])
                            A("pe", lambda e, h=h, hh=hh, p2=p2: e.matmul(p2[:, 128 * hh:128 * hh + 128], lhsT=kbe[:, h, :], rhs=Qf[:, h, :], start=True, stop=True), reads=[R_Qf, R_kbe], writes=[R_p2])
                        A("act", lambda e, hq=hq, p1=p1: e.activation(out=u0[:, 4 * hq:4 * hq + 4, :], in_=p1[:].rearrange("p (h t) -> p h t", t=128), func=AF.Copy), reads=[R_p1], writes=[R_u0])
                        A("act", lambda e, hq=hq, p2=p2: e.activation(out=nwT[:, 4 * hq:4 * hq + 4, :], in_=p2[:].rearrange("p (h t) -> p h t", t=128), func=AF.Copy, scale=-1.0), reads=[R_p2], writes=[R_nwT])
                    pob = [pm[6], pm[7]]
                    for c in range(2):
                        rows = slice(64 * c, 64 * c + 64)
                        ctok = slice(128 * ti + 64 * c, 128 * ti + 64 * c + 64)
                        for hq in range(2):
                            p1, R_p1 = pm[hq]
                            for hh in range(4):
                                h = 4 * hq + hh
                                A("pe", lambda e, h=h, hh=hh, p1=p1, rows=rows: e.matmul(p1[rows, 128 * hh:128 * hh + 128], lhsT=nwT[:, h, rows], rhs=SBb[:, h, :], start=True, stop=True), reads=[R_nwT, R_SBb], writes=[R_p1])
                            A("dve", lambda e, hq=hq, p1=p1, rows=rows: e.tensor_tensor(out=vnew[rows, 4 * hq:4 * hq + 4, :], in0=p1[rows, :].rearrange("p (h t) -> p h t", t=128), in1=u0[rows, 4 * hq:4 * hq + 4, :], op=ALU.add),
                              reads=[R_p1, R_u0], writes=[R_vnew])
                        for hq in range(2):
                            if readout:
                                po_, R_po = pob[hq]
                                for hh in range(4):
                                    h = 4 * hq + hh
                                    A("pe", lambda e, h=h, hh=hh, po_=po_, rows=rows, ctok=ctok: e.matmul(po_[rows, 128 * hh:128 * hh + 128], lhsT=qdB[:, h, ctok], rhs=SBb[:, h, :], start=(not pre), stop=False), reads=[R_qdB, R_SBb], writes=[R_po])
                                    A("pe", lambda e, h=h, hh=hh, po_=po_, rows=rows: e.matmul(po_[rows, 128 * hh:128 * hh + 128], lhsT=PT[rows, h, rows], rhs=vnew[rows, h, :], start=False, stop=True), reads=[R_PT, R_vnew], writes=[R_po])
                            p2, R_p2 = pm[2 + hq]
                            for hh in range(4):
                                h = 4 * hq + hh
                                A("pe", lambda e, h=h, hh=hh, p2=p2, rows=rows: e.matmul(p2[:, 128 * hh:128 * hh + 128], lhsT=kdc[rows, h, :], rhs=vnew[rows, h, :], start=True, stop=True), reads=[R_kdc, R_vnew], writes=[R_p2])
                            hs = slice(4 * hq, 4 * hq + 4)
                            A("dve", lambda e, hs=hs, c=c: e.tensor_tensor(out=SB[:, hs, :], in0=SB[:, hs, :], in1=dB[:, c, hs].unsqueeze(2).broadcast_to([128, 4, 128]), op=ALU.mult), reads=[R_SB, R_dB], writes=[R_SB])
                            A("dve", lambda e, hs=hs, p2=p2: e.tensor_tensor(out=SB[:, hs, :], in0=p2[:].rearrange("p (h t) -> p h t", t=128), in1=SB[:, hs, :], op=ALU.add), reads=[R_p2, R_SB], writes=[R_SB])
                        A("act", lambda e: e.activation(out=SBb[:], in_=SB[:], func=AF.Copy), reads=[R_SB], writes=[R_SBb])
                    if readout:
                        for hq in range(2):
                            po_, R_po = pob[hq]
                            A("act", lambda e, hq=hq, po_=po_: e.activation(out=osb[:, D + 512 * hq:D + 512 * hq + 512], in_=po_[:], func=AF.Copy), reads=[R_po], writes=[R_osb])
                if readout and s == 1:
                    for hd in range(16):
                        A("act", lambda e, hd=hd: e.activation(out=junk[:, 0:128], in_=osb[:, 128 * hd:128 * hd + 128], func=AF.Square, accum_out=hst[:, hd:hd + 1]), reads=[R_osb], writes=[R_junk, R_hst])
                    A("act", lambda e: e.activation(out=hst[:, 16:32], in_=hst[:, 0:16], func=AF.Ln, scale=1.0 / 128, bias=EPS), reads=[R_hst], writes=[R_hst])
                    A("act", lambda e: e.activation(out=hst[:, 16:32], in_=hst[:, 16:32], func=AF.Exp, scale=-0.5), reads=[R_hst], writes=[R_hst])
                    szt, R_szt = szs[ti]
                    for q_ in range(4):
                        yb_, R_yb = yb2[q_ // 2]
                        cb0 = 512 * (q_ % 2)
                        A("dve", lambda e, q_=q_, szt=szt: e.tensor_tensor(out=osb[:, 512 * q_:512 * q_ + 512], in0=osb[:, 512 * q_:512 * q_ + 512], in1=szt[:, 512 * q_:512 * q_ + 512], op=ALU.mult), reads=[R_osb, R_szt], writes=[R_osb])
                        A("dve", lambda e, q_=q_, yb_=yb_, cb0=cb0: e.tensor_tensor(out=yb_[:, cb0:cb0 + 512].rearrange("p (h c) -> p h c", c=128), in0=osb[:, 512 * q_:512 * q_ + 512].rearrange("p (h c) -> p h c", c=128),
                                                                                in1=hst[:, 16 + 4 * q_:20 + 4 * q_].unsqueeze(2).broadcast_to([128, 4, 128]), op=ALU.mult), reads=[R_osb, R_hst], writes=[R_yb])
                    for half in range(2):
                        yb_, R_yb = yb2[half]
                        yT_, R_yT = yT2[half]
                        pt, R_pt = pm[half]
                        ptb = pt[:].bitcast(BF16)
                        for kk in range(8):
                            A("pe", lambda e, kk=kk, yb_=yb_, ptb=ptb: e.transpose(out=ptb[:, 128 * kk:128 * kk + 128], in_=yb_[:, 128 * kk:128 * kk + 128], identity=idb[:]), reads=[R_yb, R_idb], writes=[R_pt])
                        A("act", lambda e, yT_=yT_, ptb=ptb: e.activation(out=yT_[:], in_=ptb.rearrange("p (k t) -> p k t", t=128), func=AF.Copy), reads=[R_pt], writes=[R_yT])
                    x_t, R_x = xt[ti % 2]
                    xrow = 64 + gidx * 256 + 128 * ti
                    A("sp", lambda e, xrow=xrow, x_t=x_t: e.dma_start(out=x_t[:], in_=src[xrow:xrow + 128, :]), writes=[R_x], chan="x%d" % (ti % 2))
                    for ch in range(2):
                        pt, R_pt = pm[2 + ch]
                        for kh in range(2):
                            i = wctr[0] % 2
                            wctr[0] += 1
                            wt_, R_wt = wt[i]
                            A("sp", lambda e, kh=kh, ch=ch, wt_=wt_: e.dma_start(out=wt_[:], in_=woutb[1024 * kh:1024 * kh + 1024, 512 * ch:512 * ch + 512].rearrange("(k p) n -> p k n", p=128)),
                              reads=[R_woutb], writes=[R_wt], chan="w%d" % i)
                            yT_, R_yT = yT2[kh]
                            for k in range(8):
                                A("pe", lambda e, k=k, kh=kh, pt=pt, yT_=yT_, wt_=wt_: e.matmul(pt[:], lhsT=yT_[:, k, :], rhs=wt_[:, k, :], start=(kh == 0 and k == 0), stop=(kh == 1 and k == 7)),
                                  reads=[R_yT, R_wt], writes=[R_pt])
                        A("dve", lambda e, ch=ch, pt=pt: e.tensor_tensor(out=hn[:, 512 * ch:512 * ch + 512], in0=pt[:], in1=gate[:, 512 * ch:512 * ch + 512], op=ALU.mult), reads=[R_pt, R_gate], writes=[R_hn])
                    A("dve", lambda e, x_t=x_t: e.tensor_tensor(out=hn[:], in0=hn[:], in1=x_t[:], op=ALU.add), reads=[R_hn, R_x], writes=[R_hn])
                    A("act", lambda e: e.activation(out=junk[:], in_=hn[:], func=AF.Square, accum_out=ssq[:, 0:1]), reads=[R_hn], writes=[R_junk, R_ssq])
                    A("act", lambda e: e.activation(out=ssq[:, 1:2], in_=ssq[:, 0:1], func=AF.Ln, scale=1.0 / D, bias=EPS), reads=[R_ssq], writes=[R_ssq])
                    A("act", lambda e: e.activation(out=ssq[:, 1:2], in_=ssq[:, 1:2], func=AF.Exp, scale=-0.5), reads=[R_ssq], writes=[R_ssq])
                    A("dve", lambda e: e.scalar_tensor_tensor(out=hn[:], in0=hn[:], scalar=ssq[:, 1:2], in1=fnw[:], op0=ALU.mult, op1=ALU.mult), reads=[R_hn, R_ssq, R_fnw], writes=[R_hn])
                    orow = (gidx - ng // 2) * 256 + 128 * ti
                    A("sp", lambda e, orow=orow: e.dma_start(out=out_d[orow:orow + 128, :], in_=hn[:]), reads=[R_hn], chan="out")
                if readout and s == 0:
                    srow = gidx * 256 + 128 * ti
                    A("sp", lambda e, srow=srow: e.dma_start(out=o1_d[srow:srow + 128, :], in_=osb[:]), reads=[R_osb], writes=[R_o1], chan="o1")

        for s in range(2 if do_s2 else 1):
            if s == 1:
                A("dve", lambda e: e.memset(SA[:], 0.0), writes=[R_SA])
                A("dve", lambda e: e.memset(SB[:], 0.0), writes=[R_SB])
                A("pool", lambda e: e.memset(SBb[:], 0.0), writes=[R_SBb])
            group(s, cx[s], 0, 2, 0, 1, False, -1)
            if s == 0:
                for g in range(ng // 2):
                    group(s, xs[s], 256 * g, 3, 64, 0, True, g)
            else:
                for g in range(ng):
                    group(s, xs[s], 256 * g, 3, 64, 0, g >= ng // 2, g)

        P.emit(st, final_waits=["o1"] + (["out"] if do_s2 else []))
    return nc


def make_inputs(x, c, ctx, c_ctx, norm_w, ada_w, ada_b, w_in, conv_w, hg_lb_logits, gdn_a_log,
                gdn_dt_bias, ha_norm_w, hb_norm_w, w_out, final_norm_w, ncores=8):
    f = lambda a: np.ascontiguousarray(np.asarray(a, dtype=np.float32))
    S = x.shape[1]
    Hh = S // 2
    cw = conv_w[0].reshape(9, 3072)
    cwn = cw.T.reshape(24, 128, 9).transpose(1, 0, 2)
    cwf = cw[::-1].T.reshape(24, 128, 9).transpose(1, 0, 2)
    lbl4 = hg_lb_logits.reshape(2, 2, 8, 128).transpose(3, 0, 1, 2)
    hnw = np.concatenate([ha_norm_w[0].reshape(-1), hb_norm_w[0].reshape(-1)])
    z = np.zeros((64, D), np.float32)
    in_maps = []
    for core in range(ncores):
        b, half = core // 2, core % 2
        d1, d2 = (0, 1) if half == 0 else (1, 0)
        xb = x[b]
        if half == 0:
            xs1 = np.concatenate([z, xb[:Hh], xb[Hh:Hh + 64]], axis=0)
            xs2 = np.concatenate([z, xb[::-1], z], axis=0)
            ctx1, ctx2 = ctx[b], ctx[b][::-1]
            cwp = np.stack([cwn, cwf], axis=1)
        else:
            xs1 = np.concatenate([z, xb[Hh:][::-1], xb[Hh - 64:Hh][::-1]], axis=0)
            xs2 = np.concatenate([z, xb, z], axis=0)
            ctx1, ctx2 = ctx[b][::-1], ctx[b]
            cwp = np.stack([cwf, cwn], axis=1)
        w = np.array(w_in[0], dtype=np.float32, copy=True)
        if half == 1:
            w[:, C_AF0:C_AF0 + 1024], w[:, C_AF1:C_AF1 + 1024] = w_in[0][:, C_AF1:C_AF1 + 1024], w_in[0][:, C_AF0:C_AF0 + 1024]
            w[:, C_BA:C_BA + 8], w[:, C_BA + 8:C_BA + 16] = w_in[0][:, C_BA + 8:C_BA + 16], w_in[0][:, C_BA:C_BA + 8]
            w[:, C_BB:C_BB + 8], w[:, C_BB + 8:C_BB + 16] = w_in[0][:, C_BB + 8:C_BB + 16], w_in[0][:, C_BB:C_BB + 8]
        lbl = lbl4[:, :, [d1, d2], :].reshape(128, 32)
        gpar = np.zeros((16, 16), np.float32)
        for si, d in enumerate((d1, d2)):
            gpar[:, 2 * si] = np.tile(gdn_a_log[0, d], 2)
            gpar[:, 2 * si + 1] = np.tile(gdn_dt_bias[0, d], 2)
        gpar[0:8, 4] = -1.0
        gpar[8:16, 5] = 1.0
        gpar[8:16, 6] = 1.0
        gpar[0:8, 7] = 1.0
        for k in range(16):
            gpar[k, 8 + k % 8] = 1.0
        gdr = np.concatenate([gdn_a_log[0, d1], gdn_a_log[0, d2], gdn_dt_bias[0, d1], gdn_dt_bias[0, d2]])
        ccol = np.concatenate([c[b].reshape(8, 128).T, c_ctx.reshape(8, 128).T], axis=1)
        in_maps.append({
            "xs1": f(xs1), "xs2": f(xs2), "ctx1": f(ctx1), "ctx2": f(ctx2),
            "ccol": f(ccol), "norm_w": f(norm_w[0]), "final_norm_w": f(final_norm_w),
            "ada_w": f(ada_w[0]), "ada_b": f(ada_b[0]), "w_in": f(w), "w_out": f(w_out[0]),
            "lbl": f(lbl), "cw": f(cwp), "hnwc": f(hnw.reshape(16, 128).T), "gpar": f(gpar), "gdr": f(gdr),
        })
    return in_maps


def assemble(results, nb):
    outs = []
    for b in range(nb):
        outs.append(np.concatenate([results[2 * b]["out"][::-1], results[2 * b + 1]["out"]], axis=0))
    return np.ascontiguousarray(np.stack(outs, axis=0).astype(np.float32))


_NC_CACHE = {}


def kernel(**inputs):
    inputs = {k: np.asarray(v) for k, v in inputs.items()}
    in_maps = make_inputs(**inputs)
    if "nc" not in _NC_CACHE:
        _NC_CACHE["nc"] = build()
    res = run_bass_kernel_spmd(_NC_CACHE["nc"], in_maps, core_ids=list(range(8)))
    return assemble(res.results, 4)
```

```python
import contextlib
import numpy as np
import concourse.bass as bass
import concourse.mybir as mybir
from concourse.bass_utils import run_bass_kernel_spmd

F32 = mybir.dt.float32
BF16 = mybir.dt.bfloat16
AF = mybir.ActivationFunctionType
ALU = mybir.AluOpType

ENGS = ("pe", "act", "dve", "pool", "sp")

D = 1024
SEQ = 8192
CTXL = 256
NIN = 9248
EPS = 1e-6
C_AQ, C_AF0, C_AF1, C_AI, C_AZ = 0, 1024, 2048, 3072, 4096
C_BQ, C_BK, C_BV, C_BZ, C_BA, C_BB = 5120, 6144, 7168, 8192, 9216, 9232


class Res:
    __slots__ = ("name", "w", "rs")

    def __init__(self, name):
        self.name = name
        self.w = None
        self.rs = {}


class Op:
    __slots__ = ("eng", "fn", "deps", "chan", "chan_idx", "need_sig", "sig", "idx")

    def __init__(self, eng, fn, chan):
        self.eng = eng
        self.fn = fn
        self.chan = chan
        self.chan_idx = 0
        self.deps = ()
        self.need_sig = False
        self.sig = 0
        self.idx = 0


class Prog:
    def __init__(self, nc):
        self.nc = nc
        self.eng_ops = {e: [] for e in ENGS}
        self.chan_cnt = {}
        self.n = 0

    def op(self, eng, fn, reads=(), writes=(), chan=None):
        o = Op(eng, fn, chan)
        o.idx = self.n
        self.n += 1
        if chan is not None:
            c = self.chan_cnt.get(chan, 0) + 1
            self.chan_cnt[chan] = c
            o.chan_idx = c
        deps = {}
        for r in reads:
            if r.w is not None:
                deps[id(r.w)] = r.w
        for w in writes:
            if w.w is not None:
                deps[id(w.w)] = w.w
            for x in w.rs.values():
                deps[id(x)] = x
        o.deps = list(deps.values())
        for r in reads:
            key = eng if chan is None else ("dma", o.idx)
            r.rs[key] = o
        for w in writes:
            w.w = o
            w.rs = {}
        self.eng_ops[eng].append(o)
        return o

    def emit(self, st, final_waits=()):
        nc = self.nc
        esem = {e: st.enter_context(nc.semaphore("s_" + e)) for e in ENGS}
        csem = {c: st.enter_context(nc.semaphore("c_" + str(c))) for c in self.chan_cnt}
        for e in ENGS:
            for o in self.eng_ops[e]:
                for d in o.deps:
                    if d.chan is None:
                        if d.eng == "pe" and o.eng == "pe":
                            continue
                        d.need_sig = True
        for e in ENGS:
            c = 0
            for o in self.eng_ops[e]:
                if o.chan is None and o.need_sig:
                    c += 1
                    o.sig = c
        block = st.enter_context(nc.Block())

        def run(e, h):
            known = {}
            for o in self.eng_ops[e]:
                for d in o.deps:
                    if d.chan is not None:
                        key, val, sem = ("c", d.chan), 16 * d.chan_idx, csem[d.chan]
                    else:
                        if d.eng == "pe" and e == "pe":
                            continue
                        key, val, sem = ("e", d.eng), d.sig, esem[d.eng]
                    if known.get(key, 0) >= val:
                        continue
                    known[key] = val
                    h.wait_ge(sem, val)
                ins = o.fn(h)
                if o.chan is not None:
                    ins.then_inc(csem[o.chan], 16)
                elif o.need_sig:
                    ins.then_inc(esem[e], 1)
            if e == "sp":
                for ch in final_waits:
                    h.wait_ge(csem[ch], 16 * self.chan_cnt[ch])

        @block.tensor
        def _(h):
            run("pe", h)

        @block.scalar
        def _(h):
            run("act", h)

        @block.vector
        def _(h):
            run("dve", h)

        @block.gpsimd
        def _(h):
            run("pool", h)

        @block.sync
        def _(h):
            run("sp", h)


def build(ng=32, debug=False, do_gdn=True, do_s2=True):
    nc = bass.Bass("TRN2", target_bir_lowering=False)
    NT = ng * 256
    dram = lambda n, s, dt=F32, kind="ExternalInput": nc.dram_tensor(n, s, dt, kind=kind).ap()
    NTH = NT // 2
    xs = [dram("xs1", [NTH + 128, D]), dram("xs2", [NT + 128, D])]
    cx = [dram("ctx1", [CTXL, D]), dram("ctx2", [CTXL, D])]
    ccol_d = dram("ccol", [128, 16])
    normw_d = dram("norm_w", [D])
    fnw_d = dram("final_norm_w", [D])
    adaw_d = dram("ada_w", [D, 3 * D])
    adab_d = dram("ada_b", [3 * D])
    win_d = dram("w_in", [D, NIN])
    wout_d = dram("w_out", [2 * D, D])
    lbl_d = dram("lbl", [128, 32])
    cw_d = dram("cw", [128, 2, 24, 9])
    hnw_d = dram("hnwc", [128, 16])
    gpar_d = dram("gpar", [16, 16])
    gdr_d = dram("gdr", [32])
    out_d = dram("out", [NTH, D], kind="ExternalOutput")
    winb = dram("winb", [D, NIN], BF16, kind="Internal")
    woutb = dram("woutb", [2 * D, D], BF16, kind="Internal")
    o1_d = dram("o1", [NTH, 2 * D], F32, kind="ExternalOutput" if debug else "Internal")
    dbg_d = dram("dbg", [128, 4096], F32, kind="ExternalOutput") if debug else None

    P = Prog(nc)
    R_o1 = Res("o1")
    st = contextlib.ExitStack()
    with st:
        def sb(name, shape, dt=F32):
            return st.enter_context(nc.sbuf_tensor("sb_" + name, shape, dt)), Res(name)

        def ps(name, shape, dt=F32):
            return st.enter_context(nc.psum_tensor("ps_" + name, shape, dt)), Res(name)

        A = P.op

        idf, R_idf = sb("idf", [128, 128])
        idb, R_idb = sb("idb", [128, 128], BF16)
        jf, R_jf = sb("jf", [128, 128])
        m01, R_m01 = sb("m01", [128, 128])
        rmask, R_rmask = sb("rmask", [128, 256])
        A("pool", lambda e: e.memset(idf[:], 1.0), writes=[R_idf])
        A("pool", lambda e: e.affine_select(out=idf[:], in_=idf[:], pattern=[[-1, 128]], compare_op=ALU.is_equal,
                                            fill=0.0, base=0, channel_multiplier=1), reads=[R_idf], writes=[R_idf])
        A("dve", lambda e: e.tensor_copy(out=idb[:], in_=idf[:]), reads=[R_idf], writes=[R_idb])
        A("pool", lambda e: e.memset(jf[:], 1.0), writes=[R_jf])
        A("pool", lambda e: e.affine_select(out=jf[:], in_=jf[:], pattern=[[1, 128]], compare_op=ALU.is_equal,
                                            fill=0.0, base=-127, channel_multiplier=1), reads=[R_jf], writes=[R_jf])
        A("pool", lambda e: e.memset(m01[:], 1.0), writes=[R_m01])
        A("pool", lambda e: e.affine_select(out=m01[:], in_=m01[:], pattern=[[1, 128]], compare_op=ALU.is_ge,
                                            fill=0.0, base=0, channel_multiplier=-1), reads=[R_m01], writes=[R_m01])
        A("pool", lambda e: e.memset(m01[0:64, 64:128], 0.0), reads=[R_m01], writes=[R_m01])
        A("pool", lambda e: e.memset(rmask[:], 1.0), writes=[R_rmask])
        A("pool", lambda e: e.memset(rmask[:].rearrange("p (c t) -> p c t", t=64)[:, :, 0:1], 0.0),
          reads=[R_rmask], writes=[R_rmask])

        R_winb, R_woutb = Res("winb"), Res("woutb")
        for i in range(8):
            A("pool", lambda e, i=i: e.dma_start(out=winb[128 * i:128 * i + 128, :], in_=win_d[128 * i:128 * i + 128, :]),
              writes=[R_winb], chan="wcv")

        xt = [sb("xt%d" % i, [128, D]) for i in range(2)]
        qA, R_qA = sb("qA", [128, 8, 256])
        osb, R_osb = sb("osb", [128, 2 * D])
        ccol, R_ccol = sb("ccol", [128, 16])
        scb, R_scb = qA[:].rearrange("p a b -> p (a b)").rearrange("p (j t) -> p j t", t=128), R_qA
        adaw, R_adaw = xt[0][0][:].rearrange("p (k n) -> p k n", n=128), xt[0][1]
        nwb, R_nwb = osb[:, 0:D], R_osb
        adab, R_adab = osb[:, D:D + 128], R_osb
        Amod, R_Amod = sb("Amod", [128, 2, D])
        shmod, R_shmod = sb("shmod", [128, 2, D])
        gate, R_gate = sb("gate", [128, D])
        A("sp", lambda e: e.dma_start(out=ccol[:], in_=ccol_d), writes=[R_ccol], chan="p0")
        A("sp", lambda e: e.dma_start(out=nwb, in_=normw_d.partition_broadcast(128)), writes=[R_nwb], chan="p2")
        A("act", lambda e: e.activation(out=ccol[:], in_=ccol[:], func=AF.Silu), reads=[R_ccol], writes=[R_ccol])
        A("dve", lambda e: e.tensor_copy(out=scb, in_=ccol[:].unsqueeze(2).broadcast_to([128, 16, 128])),
          reads=[R_ccol], writes=[R_scb])
        pm = [ps("pm%d" % i, [128, 512]) for i in range(8)]
        for blk in range(24):
            A("sp", lambda e, blk=blk: e.dma_start(out=adaw, in_=adaw_d[:, 128 * blk:128 * blk + 128].rearrange("(k p) n -> p k n", p=128)),
              writes=[R_adaw], chan="adaw")
            A("sp", lambda e, blk=blk: e.dma_start(out=adab, in_=adab_d[128 * blk:128 * blk + 128].partition_broadcast(128)), writes=[R_adab], chan="p1")
            which, off = blk // 8, (blk % 8) * 128
            for j in range(2):
                pt, R_pt = pm[j]
                for k in range(8):
                    A("pe", lambda e, j=j, k=k, pt=pt: e.matmul(pt[:, 0:128], lhsT=scb[:, j * 8 + k, :], rhs=adaw[:, k, :], start=(k == 0), stop=(k == 7)),
                      reads=[R_scb, R_adaw], writes=[R_pt])
                if which == 0:
                    A("dve", lambda e, j=j, pt=pt, off=off: e.tensor_tensor(out=shmod[:, j, off:off + 128], in0=pt[:, 0:128], in1=adab, op=ALU.add),
                      reads=[R_pt, R_adab], writes=[R_shmod])
                elif which == 1:
                    A("dve", lambda e, j=j, pt=pt, off=off: e.scalar_tensor_tensor(out=Amod[:, j, off:off + 128], in0=pt[:, 0:128], scalar=1.0, in1=adab, op0=ALU.add, op1=ALU.add),
                      reads=[R_pt, R_adab], writes=[R_Amod])
                    A("dve", lambda e, j=j, off=off: e.tensor_tensor(out=Amod[:, j, off:off + 128], in0=Amod[:, j, off:off + 128], in1=nwb[:, off:off + 128], op=ALU.mult),
                      reads=[R_Amod, R_nwb], writes=[R_Amod])
                elif j == 0:
                    A("dve", lambda e, pt=pt, off=off: e.tensor_tensor(out=gate[:, off:off + 128], in0=pt[:, 0:128], in1=adab, op=ALU.add),
                      reads=[R_pt, R_adab], writes=[R_gate])

        lbl, R_lbl = sb("lbl", [128, 32])
        lb, R_lb = sb("lb", [128, 16])
        oml, R_oml = sb("oml", [128, 16])
        lbm1, R_lbm1 = sb("lbm1", [128, 16])
        A("sp", lambda e: e.dma_start(out=lbl[:], in_=lbl_d), writes=[R_lbl], chan="p3")
        A("dve", lambda e: e.tensor_tensor(out=lb[:], in0=lbl[:, 0:16], in1=lbl[:, 16:32], op=ALU.subtract), reads=[R_lbl], writes=[R_lb])
        A("act", lambda e: e.activation(out=lb[:], in_=lb[:], func=AF.Sigmoid), reads=[R_lb], writes=[R_lb])
        A("dve", lambda e: e.tensor_scalar(out=oml[:], in0=lb[:], scalar1=-1.0, scalar2=1.0, op0=ALU.mult, op1=ALU.add), reads=[R_lb], writes=[R_oml])
        A("dve", lambda e: e.tensor_scalar(out=lbm1[:], in0=lb[:], scalar1=-1.0, scalar2=None, op0=ALU.add), reads=[R_lb], writes=[R_lbm1])

        junk, R_junk = sb("junk", [128, D], BF16)
        ssq, R_ssq = sb("ssq", [128, 2])
        hn, R_hn = sb("hn", [128, D])
        hb, R_hb = sb("hb", [128, D], BF16)
        hT, R_hT = sb("hT", [128, 8, 384], BF16)
        wt = [sb("wt%d" % i, [128, 8, 512], BF16) for i in range(2)]
        wctr = [0]
        tf = [sb("tf%d" % i, [128, 256]) for i in range(4)]
        qt, R_qt = sb("qt", [128, 8, 256], BF16)
        kdT, R_kdT = sb("kdT", [128, 8, 256], BF16)
        dA, R_dA = sb("dA", [128, 8, 4])
        kdm, R_kdm = sb("kdm", [128, 2, D], BF16)
        vA, R_vA = sb("vA", [128, 2, D], BF16)
        scT, R_scT = sb("scT", [128, 8, 128], BF16)
        SA, R_SA = sb("SA", [128, 8, 128])
        SAd, R_SAd = sb("SAd", [128, 8, 128])
        SAb, R_SAb = sb("SAb", [128, 8, 128], BF16)
        A("dve", lambda e: e.memset(SA[:], 0.0), writes=[R_SA])
        cws, R_cws = sb("cws", [128, 2, 24, 9])
        gp, R_gp = sb("gp", [16, 16])
        nA16, R_nA16 = sb("nA16", [16, 2])
        gdtr, R_gdtr = sb("gdtr", [128, 4, 8])
        nAr, R_nAr = sb("nAr", [128, 2, 8])
        onesb, R_onesb = sb("onesb", [128, 128], BF16)
        capI, R_capI = sb("capI", [128, 128])
        capS, R_capS = sb("capS", [128, 128])
        mU, R_mU = sb("mU", [128, 128])
        cind, R_cind = sb("cind", [128, 2, 128])
        Esel, R_Esel = sb("Esel", [16, 8, 128])
        A("sp", lambda e: e.dma_start(out=cws[:], in_=cw_d), writes=[R_cws], chan="p4")
        A("sp", lambda e: e.dma_start(out=gp[:], in_=gpar_d), writes=[R_gp], chan="p5")
        A("sp", lambda e: e.dma_start(out=gdtr[:].rearrange("p a h -> p (a h)"), in_=gdr_d.partition_broadcast(128)), writes=[R_gdtr], chan="p6")
        A("act", lambda e: e.activation(out=nA16[:, 0:1], in_=gp[:, 0:1], func=AF.Exp), reads=[R_gp], writes=[R_nA16])
        A("act", lambda e: e.activation(out=nA16[:, 1:2], in_=gp[:, 2:3], func=AF.Exp), reads=[R_gp], writes=[R_nA16])
        A("dve", lambda e: e.tensor_scalar(out=nA16[:], in0=nA16[:], scalar1=-1.0, scalar2=None, op0=ALU.mult), reads=[R_nA16], writes=[R_nA16])
        A("act", lambda e: e.activation(out=nAr[:], in_=gdtr[:, 0:2, :], func=AF.Exp), reads=[R_gdtr], writes=[R_nAr])
        A("dve", lambda e: e.tensor_scalar(out=nAr[:], in0=nAr[:], scalar1=-1.0, scalar2=None, op0=ALU.mult), reads=[R_nAr], writes=[R_nAr])
        A("pool", lambda e: e.memset(onesb[:], 1.0), writes=[R_onesb])
        A("dve", lambda e: e.tensor_scalar(out=capI[:], in0=m01[:], scalar1=-1.0, scalar2=30000.0, op0=ALU.add, op1=ALU.mult), reads=[R_m01], writes=[R_capI])
        A("dve", lambda e: e.tensor_tensor(out=capS[:], in0=m01[:], in1=idf[:], op=ALU.subtract), reads=[R_m01, R_idf], writes=[R_capS])
        A("dve", lambda e: e.tensor_scalar(out=capS[:], in0=capS[:], scalar1=-1.0, scalar2=30000.0, op0=ALU.add, op1=ALU.mult), reads=[R_capS], writes=[R_capS])
        A("pool", lambda e: e.memset(mU[:], 0.0), writes=[R_mU])
        A("pool", lambda e: e.memset(mU[0:64, 0:64], 1.0), reads=[R_mU], writes=[R_mU])
        A("pool", lambda e: e.memset(mU[64:128, 64:128], 1.0), reads=[R_mU], writes=[R_mU])
        A("dve", lambda e: e.tensor_tensor(out=mU[:], in0=mU[:], in1=m01[:], op=ALU.subtract), reads=[R_mU, R_m01], writes=[R_mU])
        A("pool", lambda e: e.memset(cind[:], 0.0), writes=[R_cind])
        A("pool", lambda e: e.memset(cind[0:64, 0, :], 1.0), reads=[R_cind], writes=[R_cind])
        A("pool", lambda e: e.memset(cind[64:128, 1, :], 1.0), reads=[R_cind], writes=[R_cind])
        A("dve", lambda e: e.tensor_scalar(out=Esel[:], in0=gp[:, 8:16].unsqueeze(2).broadcast_to([16, 8, 128]), scalar1=gp[:, 6:7], scalar2=None, op0=ALU.mult),
          reads=[R_gp], writes=[R_Esel])
        wab, R_wab = sb("wab", [128, 8, 32], BF16)
        dw, R_dw = sb("dw", [128, 9, 128], BF16)
        cpad, R_cpad = sb("cpad", [128, 400], BF16)
        cs, R_cs = sb("cs", [128, 256])
        sq, R_sq = sb("sq", [128, 256], BF16)
        rn, R_rn = sb("rn", [128, 256])
        qBf, R_qBf = sb("qBf", [128, 8, 256], BF16)
        kBf, R_kBf = sb("kBf", [128, 8, 256], BF16)
        vBf, R_vBf = sb("vBf", [128, 8, 256], BF16)
        qdB, R_qdB = sb("qdB", [128, 8, 256], BF16)
        rf = [sb("rf%d" % i, [16, 256]) for i in range(6)]
        rhsP, R_rhsP = sb("rhsP", [16, 8, 128])
        rhsA, R_rhsA = sb("rhsA", [16, 8, 128])
        tms = [sb("tms%d" % i, [128, 16]) for i in range(4)]
        bet, R_bet = sb("bet", [128, 8])
        bee, R_bee = sb("bee", [128, 8])
        ksf, R_ksf = sb("ksf", [128, 8])
        dB, R_dB = sb("dB", [128, 2, 8])
        kbe, R_kbe = sb("kbe", [128, 8, 128], BF16)
        kdc, R_kdc = sb("kdc", [128, 8, 128], BF16)
        bvt, R_bvt = sb("bvt", [128, 8, 128], BF16)
        eA, R_eA = sb("eA", [128, 8, 128])
        eP, R_eP = sb("eP", [128, 8, 128])
        XT = [sb("XT%d" % i, [128, 8, 128], BF16) for i in range(2)]
        XX = [sb("XX%d" % i, [128, 8, 128], BF16) for i in range(2)]
        QQ = [sb("QQ%d" % i, [128, 8, 128], BF16) for i in range(2)]
        PT, R_PT = sb("PT", [128, 8, 128], BF16)
        u0, R_u0 = sb("u0", [128, 8, 128])
        nwT, R_nwT = sb("nwT", [128, 8, 128], BF16)
        vnew, R_vnew = sb("vnew", [128, 8, 128], BF16)
        SB, R_SB = sb("SB", [128, 8, 128])
        SBb, R_SBb = sb("SBb", [128, 8, 128], BF16)
        A("dve", lambda e: e.memset(SB[:], 0.0), writes=[R_SB])
        A("pool", lambda e: e.memset(SBb[:], 0.0), writes=[R_SBb])
        fnw, R_fnw = sb("fnw", [128, D])
        hnwc, R_hnwc = sb("hnwc", [128, 16])
        hst, R_hst = sb("hst", [128, 32])
        A("sp", lambda e: e.dma_start(out=fnw[:], in_=fnw_d.partition_broadcast(128)), writes=[R_fnw], chan="p7")
        A("sp", lambda e: e.dma_start(out=hnwc[:], in_=hnw_d), writes=[R_hnwc], chan="p8")
        for kk in range(16):
            x_t, R_x = xt[1]
            A("sp", lambda e, kk=kk: e.dma_start(out=x_t[:], in_=wout_d[128 * kk:128 * kk + 128, :]), writes=[R_x], chan="x1")
            A("dve", lambda e, kk=kk: e.tensor_scalar(out=hb[:], in0=x_t[:], scalar1=hnwc[:, kk:kk + 1], scalar2=None, op0=ALU.mult), reads=[R_x, R_hnwc], writes=[R_hb])
            A("sp", lambda e, kk=kk: e.dma_start(out=woutb[128 * kk:128 * kk + 128, :], in_=hb[:]), reads=[R_hb], writes=[R_woutb], chan="wo_st")
        szs = [(Amod[:, 1, :].bitcast(BF16), R_Amod), (shmod[:, 1, :].bitcast(BF16), R_shmod)]
        yT2 = [XT[1], XX[1]]
        yb2 = [(kbe[:].rearrange("p h c -> p (h c)"), R_kbe), (kdc[:].rearrange("p h c -> p (h c)"), R_kdc)]
        dw2, R_dw2 = sb("dw2", [128, 9, 128], BF16)
        cpad2, R_cpad2 = sb("cpad2", [128, 400], BF16)
        cs2, R_cs2 = sb("cs2", [128, 256])
        sq2, R_sq2 = sb("sq2", [128, 256], BF16)
        rn2, R_rn2 = sb("rn2", [128, 256])
        NB = [(cs, R_cs, sq, R_sq, rn, R_rn), (cs2, R_cs2, sq2, R_sq2, rn2, R_rn2)]
        CP = [(cpad, R_cpad), (cpad2, R_cpad2)]
        DW = [(dw, R_dw), (dw2, R_dw2)]
        print("sbuf bytes remaining", nc.sbuf_bytes_remaining)


        def load_w(c0, ncol):
            i = wctr[0] % 2
            wctr[0] += 1
            t, R = wt[i]
            A("sp", lambda e: e.dma_start(out=t[:, :, 0:ncol], in_=winb[:, c0:c0 + ncol].rearrange("(k p) n -> p k n", p=128)),
              reads=[R_winb], writes=[R], chan="w%d" % i)
            return t, R

        def xprep(src, row0, i, col0, j):
            x_t, R_x = xt[i % 2]
            A("sp", lambda e: e.dma_start(out=x_t[:], in_=src[row0:row0 + 128, :]), writes=[R_x], chan="x%d" % (i % 2))
            A("act", lambda e: e.activation(out=junk[:], in_=x_t[:], func=AF.Square, accum_out=ssq[:, 0:1]),
              reads=[R_x], writes=[R_junk, R_ssq])
            A("act", lambda e: e.activation(out=ssq[:, 1:2], in_=ssq[:, 0:1], func=AF.Ln, scale=1.0 / D, bias=EPS), reads=[R_ssq], writes=[R_ssq])
            A("act", lambda e: e.activation(out=ssq[:, 1:2], in_=ssq[:, 1:2], func=AF.Exp, scale=-0.5), reads=[R_ssq], writes=[R_ssq])
            A("dve", lambda e: e.scalar_tensor_tensor(out=hn[:], in0=x_t[:], scalar=ssq[:, 1:2], in1=Amod[:, j, :], op0=ALU.mult, op1=ALU.mult),
              reads=[R_x, R_ssq, R_Amod], writes=[R_hn])
            A("pool", lambda e: e.tensor_tensor(out=hb[:], in0=hn[:], in1=shmod[:, j, :], op=ALU.add), reads=[R_hn, R_shmod], writes=[R_hb])
            pt, R_pt = pm[7]
            ptb = pt[:].bitcast(BF16)
            for k in range(8):
                A("pe", lambda e, k=k: e.transpose(out=ptb[:, 128 * k:128 * k + 128], in_=hb[:, 128 * k:128 * k + 128], identity=idb[:]),
                  reads=[R_hb, R_idb], writes=[R_pt])
            A("act", lambda e: e.activation(out=hT[:, :, col0:col0 + 128], in_=ptb.rearrange("p (k t) -> p k t", t=128), func=AF.Copy),
              reads=[R_pt], writes=[R_hT])

        def group(s, src, row0, ntile_w, off, j, readout, gidx):
            T = 256
            for i in range(ntile_w):
                if ntile_w == 3 and gidx >= 1 and i == 0:
                    A("pool", lambda e: e.tensor_copy(out=hT[:, :, 0:128], in_=hT[:, :, 256:384]), reads=[R_hT], writes=[R_hT])
                    continue
                xprep(src, row0 + 128 * i, i, 128 * i, j)
            main = slice(off, off + T)
            for half in (range(2) if readout else ()):
                w, R_w = load_w(C_AQ + 512 * half, 512)
                for hh in range(4):
                    h = 4 * half + hh
                    pt, R_pt = pm[hh % 2]
                    for k in range(8):
                        A("pe", lambda e, k=k, hh=hh, pt=pt, w=w: e.matmul(pt[:, 0:T], lhsT=w[:, k, 128 * hh:128 * hh + 128], rhs=hT[:, k, main], start=(k == 0), stop=(k == 7)),
                          reads=[R_w, R_hT], writes=[R_pt])
                    A("act", lambda e, h=h, pt=pt: e.activation(out=qA[:, h, :], in_=pt[:, 0:T], func=AF.Silu), reads=[R_pt], writes=[R_qA])
            caf = C_AF0 if s == 0 else C_AF1
            for half in range(2):
                w, R_w = load_w(caf + 512 * half, 512)
                for hh in range(4):
                    h = 4 * half + hh
                    li = s * 8 + h
                    pt, R_pt = pm[2 + hh % 2]
                    (sig, R_sig), (g, R_g), (b, R_b), (dl, R_dl) = tf
                    for k in range(8):
                        A("pe", lambda e, k=k, hh=hh, pt=pt, w=w: e.matmul(pt[:, 0:T], lhsT=w[:, k, 128 * hh:128 * hh + 128], rhs=hT[:, k, main], start=(k == 0), stop=(k == 7)),
                          reads=[R_w, R_hT], writes=[R_pt])
                    A("act", lambda e, pt=pt: e.activation(out=sig[:], in_=pt[:, 0:T], func=AF.Sigmoid), reads=[R_pt], writes=[R_sig])
                    A("act", lambda e, li=li: e.activation(out=g[:], in_=sig[:], func=AF.Ln, scale=oml[:, li:li + 1], bias=lb[:, li:li + 1]),
                      reads=[R_sig, R_oml, R_lb], writes=[R_g])
                    A("dve", lambda e, li=li: e.tensor_scalar(out=sig[:], in0=sig[:], scalar1=-1.0, scalar2=lbm1[:, li:li + 1], op0=ALU.add, op1=ALU.mult),
                      reads=[R_sig, R_lbm1], writes=[R_sig])
                    A("dve", lambda e: e.tensor_tensor_scan(out=b[:], data0=rmask[:, 0:T], data1=g[:], initial=0.0, op0=ALU.mult, op1=ALU.add),
                      reads=[R_rmask, R_g], writes=[R_b])
                    b3 = b[:].rearrange("p (c t) -> p c t", t=64)
                    A("dve", lambda e, b3=b3: e.tensor_tensor(out=dl[:].rearrange("p (c t) -> p c t", t=64), in0=b3, in1=b3[:, :, 63:64].broadcast_to([128, 4, 64]), op=ALU.subtract),
                      reads=[R_b], writes=[R_dl])
                    A("act", lambda e, h=h, b3=b3: e.activation(out=dA[:, h, :], in_=b3[:, :, 63], func=AF.Exp), reads=[R_b], writes=[R_dA])
                    if readout:
                        A("act", lambda e: e.activation(out=g[:], in_=dl[:], func=AF.Exp), reads=[R_dl], writes=[R_g])
                    A("act", lambda e: e.activation(out=b[:], in_=dl[:], func=AF.Exp, scale=-1.0), reads=[R_dl], writes=[R_b])
                    if readout:
                        A("dve", lambda e, h=h: e.scalar_tensor_tensor(out=qt[:, h, :], in0=qA[:, h, :], scalar=float(128 ** -0.5), in1=g[:], op0=ALU.mult, op1=ALU.mult),
                          reads=[R_qA, R_g], writes=[R_qt])
                    A("pool", lambda e, h=h: e.tensor_tensor(out=kdT[:, h, :], in0=sig[:], in1=b[:], op=ALU.mult),
                      reads=[R_sig, R_b], writes=[R_kdT])
            for half in range(2):
                w, R_w = load_w(C_AI + 512 * half, 512)
                for ti in range(2):
                    pt, R_pt = pm[4 + ti]
                    for k in range(8):
                        A("pe", lambda e, k=k, ti=ti, pt=pt, w=w: e.matmul(pt[:], lhsT=hT[:, k, off + 128 * ti:off + 128 * ti + 128], rhs=w[:, k, :], start=(k == 0), stop=(k == 7)),
                          reads=[R_w, R_hT], writes=[R_pt])
                    A("act", lambda e, ti=ti, pt=pt, half=half: e.activation(out=vA[:, ti, 512 * half:512 * half + 512], in_=pt[:], func=AF.Copy),
                      reads=[R_pt], writes=[R_vA])
            for ti in range(2):
                pt, R_pt = pm[6]
                ptb = pt[:].bitcast(BF16)
                for h in range(8):
                    A("pe", lambda e, h=h, ti=ti, ptb=ptb: e.transpose(out=ptb[:, 128 * h:128 * h + 128], in_=kdT[:, h, 128 * ti:128 * ti + 128], identity=idb[:]),
                      reads=[R_kdT, R_idb], writes=[R_pt])
                A("dve", lambda e, ti=ti, ptb=ptb: e.tensor_copy(out=kdm[:, ti, :], in_=ptb), reads=[R_pt], writes=[R_kdm])
            if do_gdn:
                lat = (ntile_w == 3)
                WN = 128 * ntile_w
                if gidx <= 0:
                    A("pool", lambda e: e.memset(cpad[:], 0.0), writes=[R_cpad])
                    A("pool", lambda e: e.memset(cpad2[:], 0.0), writes=[R_cpad2])
                if lat:
                    cpv = cpad[:, 0:396].rearrange("p (r c) -> p r c", c=66)
                def stageA(blk, w, R_w, hh):
                    cpd, Rcp = CP[blk % 2]
                    cpv_ = cpd[:, 0:396].rearrange("p (r c) -> p r c", c=66)
                    pt, R_pt = pm[blk % 2]
                    for k in range(8):
                        A("pe", lambda e, k=k, hh=hh, pt=pt, w=w: e.matmul(pt[:, 0:WN], lhsT=w[:, k, 128 * hh:128 * hh + 128], rhs=hT[:, k, 0:WN], start=(k == 0), stop=(k == 7)),
                          reads=[R_w, R_hT], writes=[R_pt])
                    if lat:
                        A("dve", lambda e, pt=pt, cpv_=cpv_: e.tensor_copy(out=cpv_[:, :, 1:65], in_=pt[:, 0:384].rearrange("p (r c) -> p r c", c=64)),
                          reads=[R_pt], writes=[Rcp])
                        if gidx == 0:
                            A("pool", lambda e, cpv_=cpv_: e.memset(cpv_[:, 0, :], 0.0), reads=[Rcp], writes=[Rcp])
                        if s == 1 and gidx == ng - 1:
                            A("pool", lambda e, cpv_=cpv_: e.memset(cpv_[:, 5, :], 0.0), reads=[Rcp], writes=[Rcp])
                    else:
                        A("act", lambda e, pt=pt, cpd=cpd: e.activation(out=cpd[:, 1:257], in_=pt[:, 0:256], func=AF.Copy), reads=[R_pt], writes=[Rcp])
                    dwb, Rdw = DW[blk % 2]
                    A("pool", lambda e, blk=blk, dwb=dwb: e.tensor_tensor(out=dwb[:], in0=idb[:].unsqueeze(1).broadcast_to([128, 9, 128]),
                                                                          in1=cws[:, s, blk, :].unsqueeze(2).broadcast_to([128, 9, 128]), op=ALU.mult),
                      reads=[R_idb, R_cws], writes=[Rdw])
                    pc, R_pc = pm[2 + blk % 2]
                    taps = [(i, jj) for i in range(3) for jj in range(3)] if lat else [(1, jj) for jj in range(3)]
                    for n, (i, jj) in enumerate(taps):
                        if lat:
                            A("pe", lambda e, i=i, jj=jj, n=n, pc=pc, nt=len(taps), dwb=dwb, cpv_=cpv_: e.matmul(pc[:, 0:256].rearrange("p (r c) -> p r c", c=64), lhsT=dwb[:, 3 * i + jj, :], rhs=cpv_[:, i:i + 4, jj:jj + 64], start=(n == 0), stop=(n == nt - 1)),
                              reads=[Rdw, Rcp], writes=[R_pc])
                        else:
                            A("pe", lambda e, i=i, jj=jj, n=n, pc=pc, nt=len(taps), dwb=dwb, cpd=cpd: e.matmul(pc[:, 0:256], lhsT=dwb[:, 3 * i + jj, :], rhs=cpd[:, jj:jj + 256], start=(n == 0), stop=(n == nt - 1)),
                              reads=[Rdw, Rcp], writes=[R_pc])

                def stageB(blk):
                    kind, h = blk // 8, blk % 8
                    pc, R_pc = pm[2 + blk % 2]
                    if kind == 2:
                        A("act", lambda e, h=h, pc=pc: e.activation(out=vBf[:, h, :], in_=pc[:, 0:256], func=AF.Silu), reads=[R_pc], writes=[R_vBf])
                    else:
                        c_, Rc_, q_s, Rq_, r_, Rr_ = NB[blk % 2]
                        A("act", lambda e, pc=pc, c_=c_: e.activation(out=c_[:], in_=pc[:, 0:256], func=AF.Silu), reads=[R_pc], writes=[Rc_])
                        A("act", lambda e, c_=c_, q_s=q_s: e.activation(out=q_s[:], in_=c_[:], func=AF.Square), reads=[Rc_], writes=[Rq_])
                        pn, R_pn = pm[4 + blk % 2]
                        A("pe", lambda e, pn=pn, q_s=q_s: e.matmul(pn[:, 0:256], lhsT=onesb[:], rhs=q_s[:], start=True, stop=True), reads=[R_onesb, Rq_], writes=[R_pn])
                        A("act", lambda e, pn=pn, r_=r_: e.activation(out=r_[:], in_=pn[:, 0:256], func=AF.Ln, bias=EPS), reads=[R_pn], writes=[Rr_])
                        A("act", lambda e, r_=r_: e.activation(out=r_[:], in_=r_[:], func=AF.Exp, scale=-0.5), reads=[Rr_], writes=[Rr_])
                        dst, R_dst = (qBf, R_qBf) if kind == 0 else (kBf, R_kBf)
                        sc_ = float(128 ** -0.5) if kind == 0 else 1.0
                        A("dve", lambda e, h=h, dst=dst, sc_=sc_, c_=c_, r_=r_: e.scalar_tensor_tensor(out=dst[:, h, :], in0=c_[:], scalar=sc_, in1=r_[:], op0=ALU.mult, op1=ALU.mult),
                          reads=[Rc_, Rr_], writes=[R_dst])

                pend = None
                for m in range(6):
                    if m < 2 and not readout:
                        continue
                    w, R_w = load_w(C_BQ + 512 * m, 512)
                    for hh in range(4):
                        blk = 4 * m + hh
                        stageA(blk, w, R_w, hh)
                        if pend is not None:
                            stageB(pend)
                        pend = blk
                if pend is not None:
                    stageB(pend)
                if gidx <= 0:
                    ca_, cb_ = C_BA + 8 * s, C_BB + 8 * s
                    for q_, c_ in enumerate((ca_, ca_, cb_, cb_)):
                        A("sp", lambda e, q_=q_, c_=c_: e.dma_start(out=wab[:, :, 8 * q_:8 * q_ + 8], in_=winb[:, c_:c_ + 8].rearrange("(k p) n -> p k n", p=128)),
                          reads=[R_winb], writes=[R_wab], chan="wab")
                pz, R_pz = pm[6]
                for k in range(8):
                    A("pe", lambda e, k=k: e.matmul(pz[0:16, 0:256], lhsT=wab[:, k, 0:16], rhs=hT[:, k, main], start=(k == 0), stop=(k == 7)), reads=[R_wab, R_hT], writes=[R_pz])
                for k in range(8):
                    A("pe", lambda e, k=k: e.matmul(pz[0:16, 256:512], lhsT=wab[:, k, 16:32], rhs=hT[:, k, main], start=(k == 0), stop=(k == 7)), reads=[R_wab, R_hT], writes=[R_pz])
                (r0, R_r0), (r1, R_r1), (r2, R_r2), (r3, R_r3), (r4, R_r4), (r5, R_r5) = rf
                A("act", lambda e: e.activation(out=r0[:], in_=pz[0:16, 0:256], func=AF.Exp, bias=gp[:, 1 + 2 * s:2 + 2 * s]), reads=[R_pz, R_gp], writes=[R_r0])
                A("act", lambda e: e.activation(out=r0[:], in_=r0[:], func=AF.Ln, bias=1.0), reads=[R_r0], writes=[R_r0])
                A("dve", lambda e: e.tensor_scalar(out=r0[:], in0=r0[:], scalar1=nA16[:, s:s + 1], scalar2=None, op0=ALU.mult), reads=[R_r0, R_nA16], writes=[R_r0])
                A("act", lambda e: e.activation(out=r1[:], in_=pz[0:16, 256:512], func=AF.Exp, scale=-1.0), reads=[R_pz], writes=[R_r1])
                A("act", lambda e: e.activation(out=r1[:], in_=r1[:], func=AF.Ln, bias=1.0), reads=[R_r1], writes=[R_r1])
                A("dve", lambda e: e.tensor_tensor_scan(out=r2[:], data0=rmask[0:16, 0:256], data1=r0[:], initial=0.0, op0=ALU.mult, op1=ALU.add), reads=[R_rmask, R_r0], writes=[R_r2])
                A("dve", lambda e: e.tensor_tensor(out=r1[:], in0=r2[:], in1=r1[:], op=ALU.subtract), reads=[R_r2, R_r1], writes=[R_r1])
                A("dve", lambda e: e.tensor_scalar(out=r3[:], in0=r2[:], scalar1=gp[:, 4:5], scalar2=gp[:, 5:6], op0=ALU.mult, op1=ALU.add), reads=[R_r2, R_gp], writes=[R_r3])
                A("dve", lambda e: e.tensor_scalar(out=r4[:], in0=r2[:], scalar1=gp[:, 6:7], scalar2=gp[:, 7:8], op0=ALU.mult, op1=ALU.add), reads=[R_r2, R_gp], writes=[R_r4])
                A("dve", lambda e: e.tensor_scalar(out=r5[:], in0=r1[:], scalar1=gp[:, 6:7], scalar2=gp[:, 7:8], op0=ALU.mult, op1=ALU.add), reads=[R_r1, R_gp], writes=[R_r5])
                for h in (range(8) if readout else ()):
                    pb, R_pb = pm[h % 2]
                    A("pe", lambda e, h=h, pb=pb: e.matmul(pb[:, 0:256], lhsT=Esel[:, h, :], rhs=r4[:], start=True, stop=True), reads=[R_Esel, R_r4], writes=[R_pb])
                    tq_, Rtq_ = NB[h % 2][0], NB[h % 2][1]
                    A("act", lambda e, pb=pb, tq_=tq_: e.activation(out=tq_[:], in_=pb[:, 0:256], func=AF.Exp), reads=[R_pb], writes=[Rtq_])
                    A("dve", lambda e, h=h, tq_=tq_: e.tensor_tensor(out=qdB[:, h, :], in0=qBf[:, h, :], in1=tq_[:], op=ALU.mult), reads=[R_qBf, Rtq_], writes=[R_qdB])
            if s == 1 and readout:
                for q_ in range(4):
                    w, R_w = load_w((C_AZ if q_ < 2 else C_BZ) + 512 * (q_ % 2), 512)
                    for ti in range(2):
                        pt, R_pt = pm[4 + ti]
                        for k in range(8):
                            A("pe", lambda e, k=k, ti=ti, pt=pt, w=w: e.matmul(pt[:], lhsT=hT[:, k, off + 128 * ti:off + 128 * ti + 128], rhs=w[:, k, :], start=(k == 0), stop=(k == 7)),
                              reads=[R_w, R_hT], writes=[R_pt])
                        A("act", lambda e, ti=ti, pt=pt, q_=q_: e.activation(out=szs[ti][0][:, 512 * q_:512 * q_ + 512], in_=pt[:], func=AF.Silu), reads=[R_pt], writes=[szs[ti][1]])
            for ti in range(2):
                tok = slice(128 * ti, 128 * ti + 128)
                if readout:
                    for hq in range(2):
                        pt, R_pt = pm[hq]
                        for hh in range(4):
                            h = 4 * hq + hh
                            A("pe", lambda e, h=h, hh=hh, pt=pt, tok=tok: e.matmul(pt[:, 128 * hh:128 * hh + 128], lhsT=kdT[:, h, tok], rhs=qt[:, h, tok], start=True, stop=True),
                              reads=[R_kdT, R_qt], writes=[R_pt])
                        A("dve", lambda e, hq=hq, pt=pt: e.tensor_tensor(out=scT[:, 4 * hq:4 * hq + 4, :], in0=pt[:].rearrange("p (h t) -> p h t", t=128),
                                                                         in1=m01[:].unsqueeze(1).broadcast_to([128, 4, 128]), op=ALU.mult),
                          reads=[R_pt, R_m01], writes=[R_scT])
                po = [pm[2], pm[3]]
                pre = (s == 1 and readout)
                if pre:
                    nr = NTH - 128 - ((gidx - ng // 2) * 256 + 128 * ti)
                    A("sp", lambda e, nr=nr: e.dma_start(out=osb[:], in_=o1_d[nr:nr + 128, :]), reads=[R_o1], writes=[R_osb], chan="o1l")
                    for q_, (pt, R_pt) in enumerate((pm[2], pm[3], pm[6], pm[7])):
                        A("pe", lambda e, q_=q_, pt=pt: e.matmul(pt[:], lhsT=jf[:], rhs=osb[:, 512 * q_:512 * q_ + 512], start=True, stop=False), reads=[R_jf, R_osb], writes=[R_pt])
                for c in range(2):
                    rows = slice(64 * c, 64 * c + 64)
                    ctok = slice(128 * ti + 64 * c, 128 * ti + 64 * c + 64)
                    cg = 2 * ti + c
                    A("dve", lambda e, cg=cg: e.tensor_tensor(out=SAd[:], in0=SA[:], in1=dA[:, :, cg:cg + 1].broadcast_to([128, 8, 128]), op=ALU.mult),
                      reads=[R_SA, R_dA], writes=[R_SAd])
                    if readout:
                        A("act", lambda e: e.activation(out=SAb[:], in_=SAd[:], func=AF.Copy), reads=[R_SAd], writes=[R_SAb])
                        for h in range(8):
                            pt, R_pt = po[h // 4]
                            cols = slice(128 * (h % 4), 128 * (h % 4) + 128)
                            A("pe", lambda e, h=h, pt=pt, cols=cols, rows=rows, ctok=ctok: e.matmul(pt[rows, cols], lhsT=qt[:, h, ctok], rhs=SAb[:, h, :], start=(not pre), stop=False),
                              reads=[R_qt, R_SAb], writes=[R_pt])
                            A("pe", lambda e, h=h, pt=pt, cols=cols, rows=rows, ti=ti: e.matmul(pt[rows, cols], lhsT=scT[rows, h, rows], rhs=vA[rows, ti, 128 * h:128 * h + 128], start=False, stop=True),
                              reads=[R_scT, R_vA], writes=[R_pt])
                    for hq in range(2):
                        pt, R_pt = pm[4 + hq]
                        for hh in range(4):
                            h = 4 * hq + hh
                            A("pe", lambda e, h=h, hh=hh, pt=pt, rows=rows, ti=ti: e.matmul(pt[:, 128 * hh:128 * hh + 128], lhsT=kdm[rows, ti, 128 * h:128 * h + 128], rhs=vA[rows, ti, 128 * h:128 * h + 128], start=True, stop=True),
                              reads=[R_kdm, R_vA], writes=[R_pt])
                        A("dve", lambda e, hq=hq, pt=pt: e.tensor_tensor(out=SA[:, 4 * hq:4 * hq + 4, :], in0=pt[:].rearrange("p (h v) -> p h v", v=128), in1=SAd[:, 4 * hq:4 * hq + 4, :], op=ALU.add),
                          reads=[R_pt, R_SAd], writes=[R_SA])
                if readout:
                    for hq in range(2):
                        pt, R_pt = po[hq]
                        A("act", lambda e, hq=hq, pt=pt: e.activation(out=osb[:, 512 * hq:512 * hq + 512], in_=pt[:], func=AF.Copy), reads=[R_pt], writes=[R_osb])
                if do_gdn:
                    (t0, R_t0), (t1, R_t1), (t2, R_t2), (t3, R_t3) = tms
                    mtok = slice(off + 128 * ti, off + 128 * ti + 128)
                    pz, R_pz = pm[0]
                    for k in range(8):
                        A("pe", lambda e, k=k, mtok=mtok: e.matmul(pz[:, 0:16], lhsT=hT[:, k, mtok], rhs=wab[:, k, 8:24], start=(k == 0), stop=(k == 7)), reads=[R_wab, R_hT], writes=[R_pz])
                    A("dve", lambda e: e.tensor_tensor(out=t0[:, 0:8], in0=pz[:, 0:8], in1=gdtr[:, 2 + s, :], op=ALU.add), reads=[R_pz, R_gdtr], writes=[R_t0])
                    A("act", lambda e: e.activation(out=t0[:, 0:8], in_=t0[:, 0:8], func=AF.Exp), reads=[R_t0], writes=[R_t0])
                    A("act", lambda e: e.activation(out=t0[:, 0:8], in_=t0[:, 0:8], func=AF.Ln, bias=1.0), reads=[R_t0], writes=[R_t0])
                    A("dve", lambda e: e.tensor_tensor(out=t0[:, 0:8], in0=t0[:, 0:8], in1=nAr[:, s, :], op=ALU.mult), reads=[R_t0, R_nAr], writes=[R_t0])
                    A("act", lambda e: e.activation(out=t0[:, 8:16], in_=pz[:, 8:16], func=AF.Exp, scale=-1.0), reads=[R_pz], writes=[R_t0])
                    A("dve", lambda e: e.tensor_scalar(out=t0[:, 8:16], in0=t0[:, 8:16], scalar1=1.0, scalar2=None, op0=ALU.add), reads=[R_t0], writes=[R_t0])
                    A("dve", lambda e: e.reciprocal(out=bet[:], in_=t0[:, 8:16]), reads=[R_t0], writes=[R_bet])
                    pz2, R_pz2 = pm[1]
                    A("pe", lambda e: e.matmul(pz2[:, 0:8], lhsT=m01[:], rhs=t0[:, 0:8], start=True, stop=True), reads=[R_m01, R_t0], writes=[R_pz2])
                    A("pe", lambda e: e.matmul(pz2[:, 8:16], lhsT=mU[:], rhs=t0[:, 0:8], start=True, stop=True), reads=[R_mU, R_t0], writes=[R_pz2])
                    for c in range(2):
                        A("pe", lambda e, c=c: e.matmul(pz2[:, 16 + 8 * c:24 + 8 * c], lhsT=cind[:, c, :], rhs=t0[:, 0:8], start=True, stop=True), reads=[R_cind, R_t0], writes=[R_pz2])
                    A("act", lambda e: e.activation(out=bee[:], in_=pz2[:, 0:8], func=AF.Exp), reads=[R_pz2], writes=[R_bee])
                    A("act", lambda e: e.activation(out=ksf[:], in_=pz2[:, 8:16], func=AF.Exp), reads=[R_pz2], writes=[R_ksf])
                    A("act", lambda e: e.activation(out=dB[:].rearrange("p c h -> p (c h)"), in_=pz2[:, 16:32], func=AF.Exp), reads=[R_pz2], writes=[R_dB])
                    A("dve", lambda e: e.tensor_tensor(out=bee[:], in0=bee[:], in1=bet[:], op=ALU.mult), reads=[R_bee, R_bet], writes=[R_bee])
                    pk, R_pk = pm[0]
                    pkb = pk[:].bitcast(BF16)
                    for h in range(8):
                        A("pe", lambda e, h=h, tok=tok: e.transpose(out=pkb[:, 128 * h:128 * h + 128], in_=kBf[:, h, tok], identity=idb[:]), reads=[R_kBf, R_idb], writes=[R_pk])
                    pk3 = pkb.rearrange("p (h c) -> p h c", c=128)
                    A("dve", lambda e: e.tensor_tensor(out=kbe[:], in0=pk3, in1=bee[:].unsqueeze(2).broadcast_to([128, 8, 128]), op=ALU.mult), reads=[R_pk, R_bee], writes=[R_kbe])
                    A("dve", lambda e: e.tensor_tensor(out=kdc[:], in0=pk3, in1=ksf[:].unsqueeze(2).broadcast_to([128, 8, 128]), op=ALU.mult), reads=[R_pk, R_ksf], writes=[R_kdc])
                    pv_, R_pv_ = pm[1]
                    pvb = pv_[:].bitcast(BF16)
                    for h in range(8):
                        A("pe", lambda e, h=h, tok=tok: e.transpose(out=pvb[:, 128 * h:128 * h + 128], in_=vBf[:, h, tok], identity=idb[:]), reads=[R_vBf, R_idb], writes=[R_pv_])
                    A("dve", lambda e: e.tensor_tensor(out=bvt[:], in0=pvb.rearrange("p (h c) -> p h c", c=128), in1=bet[:].unsqueeze(2).broadcast_to([128, 8, 128]), op=ALU.mult),
                      reads=[R_pv_, R_bet], writes=[R_bvt])
                    A("dve", lambda e, tok=tok: e.tensor_tensor(out=rhsP[:], in0=rf[4][0][:, tok].unsqueeze(1).broadcast_to([16, 8, 128]), in1=gp[:, 8:16].unsqueeze(2).broadcast_to([16, 8, 128]), op=ALU.mult),
                      reads=[rf[4][1], R_gp], writes=[R_rhsP])
                    A("dve", lambda e, tok=tok: e.tensor_tensor(out=rhsA[:], in0=rf[5][0][:, tok].unsqueeze(1).broadcast_to([16, 8, 128]), in1=gp[:, 8:16].unsqueeze(2).broadcast_to([16, 8, 128]), op=ALU.mult),
                      reads=[rf[5][1], R_gp], writes=[R_rhsA])
                    for (rh, R_rh, cap, R_cap, ee, R_ee, b0) in (((rhsA, R_rhsA, capS, R_capS, eA, R_eA, 2), (rhsP, R_rhsP, capI, R_capI, eP, R_eP, 4)) if readout else ((rhsA, R_rhsA, capS, R_capS, eA, R_eA, 2),)):
                        for hq in range(2):
                            pd, R_pd = pm[b0 + hq]
                            for hh in range(4):
                                h = 4 * hq + hh
                                A("pe", lambda e, h=h, hh=hh, pd=pd, rh=rh, tok=tok: e.matmul(pd[:, 128 * hh:128 * hh + 128], lhsT=rf[3][0][:, tok], rhs=rh[:, h, :], start=True, stop=True),
                                  reads=[rf[3][1], R_rh], writes=[R_pd])
                            A("dve", lambda e, hq=hq, pd=pd, cap=cap, ee=ee: e.tensor_tensor(out=ee[:, 4 * hq:4 * hq + 4, :], in0=pd[:].rearrange("p (h t) -> p h t", t=128),
                                                                                           in1=cap[:].unsqueeze(1).broadcast_to([128, 4, 128]), op=ALU.min),
                              reads=[R_pd, R_cap], writes=[R_ee])
                        A("act", lambda e, ee=ee: e.activation(out=ee[:], in_=ee[:], func=AF.Exp), reads=[R_ee], writes=[R_ee])
                    (xt0_, R_xt0), (xx0_, R_xx0), (q0_, R_q0) = XT[0], XX[0], QQ[0]
                    for hq in range(2):
                        pa, R_pa = pm[2 + hq]
                        pb, R_pb = pm[4 + hq]
                        for hh in range(4):
                            h = 4 * hq + hh
                            A("pe", lambda e, h=h, hh=hh, pa=pa, tok=tok: e.matmul(pa[:, 128 * hh:128 * hh + 128], lhsT=kBf[:, h, tok], rhs=kBf[:, h, tok], start=True, stop=True), reads=[R_kBf], writes=[R_pa])
                            if readout:
                                A("pe", lambda e, h=h, hh=hh, pb=pb, tok=tok: e.matmul(pb[:, 128 * hh:128 * hh + 128], lhsT=kBf[:, h, tok], rhs=qBf[:, h, tok], start=True, stop=True), reads=[R_kBf, R_qBf], writes=[R_pb])
                        A("dve", lambda e, hq=hq, pa=pa: e.scalar_tensor_tensor(out=xt0_[:, 4 * hq:4 * hq + 4, :], in0=pa[:].rearrange("p (h t) -> p h t", t=128), scalar=-1.0, in1=eA[:, 4 * hq:4 * hq + 4, :], op0=ALU.mult, op1=ALU.mult),
                          reads=[R_pa, R_eA], writes=[R_xt0])
                        if readout:
                            A("dve", lambda e, hq=hq, pb=pb: e.tensor_tensor(out=PT[:, 4 * hq:4 * hq + 4, :], in0=pb[:].rearrange("p (h t) -> p h t", t=128), in1=eP[:, 4 * hq:4 * hq + 4, :], op=ALU.mult),
                              reads=[R_pb, R_eP], writes=[R_PT])
                    px, R_px = pm[0]
                    pxb = px[:].bitcast(BF16)
                    for h in range(8):
                        A("pe", lambda e, h=h: e.transpose(out=pxb[:, 128 * h:128 * h + 128], in_=xt0_[:, h, :], identity=idb[:]), reads=[R_xt0, R_idb], writes=[R_px])
                    A("act", lambda e: e.activation(out=xx0_[:], in_=pxb.rearrange("p (h c) -> p h c", c=128), func=AF.Copy), reads=[R_px], writes=[R_xx0])
                    A("pool", lambda e: e.tensor_tensor(out=q0_[:], in0=xt0_[:], in1=idb[:].unsqueeze(1).broadcast_to([128, 8, 128]), op=ALU.add), reads=[R_xt0, R_idb], writes=[R_q0])
                    for l in range(6):
                        (xtc, R_xtc), (xxc, R_xxc) = XT[l % 2], XX[l % 2]
                        (xtn, R_xtn), (xxn, R_xxn) = XT[(l + 1) % 2], XX[(l + 1) % 2]
                        (qc, R_qc), (qn, R_qn) = QQ[l % 2], QQ[(l + 1) % 2]
                        for hq in range(2):
                            hs = slice(4 * hq, 4 * hq + 4)
                            if l < 5:
                                p1, R_p1 = pm[2 + hq]
                                p2, R_p2 = pm[4 + hq]
                                for hh in range(4):
                                    h = 4 * hq + hh
                                    A("pe", lambda e, h=h, hh=hh, p1=p1, xxc=xxc, xtc=xtc: e.matmul(p1[:, 128 * hh:128 * hh + 128], lhsT=xxc[:, h, :], rhs=xtc[:, h, :], start=True, stop=True), reads=[R_xxc, R_xtc], writes=[R_p1])
                                    A("pe", lambda e, h=h, hh=hh, p2=p2, xxc=xxc, xtc=xtc: e.matmul(p2[:, 128 * hh:128 * hh + 128], lhsT=xtc[:, h, :], rhs=xxc[:, h, :], start=True, stop=True), reads=[R_xxc, R_xtc], writes=[R_p2])
                            if l >= 1:
                                p3, R_p3 = pm[hq]
                                for hh in range(4):
                                    h = 4 * hq + hh
                                    A("pe", lambda e, h=h, hh=hh, p3=p3, xxc=xxc, qc=qc: e.matmul(p3[:, 128 * hh:128 * hh + 128], lhsT=xxc[:, h, :], rhs=qc[:, h, :], start=True, stop=True), reads=[R_xxc, R_qc], writes=[R_p3])
                                A("dve", lambda e, hs=hs, p3=p3, qc=qc, qn=qn: e.tensor_tensor(out=qn[:, hs, :], in0=p3[:].rearrange("p (h t) -> p h t", t=128), in1=qc[:, hs, :], op=ALU.add),
                                  reads=[R_p3, R_qc], writes=[R_qn])
                            if l < 5:
                                A("act", lambda e, hs=hs, p1=p1, xtn=xtn: e.activation(out=xtn[:, hs, :], in_=p1[:].rearrange("p (h t) -> p h t", t=128), func=AF.Copy), reads=[R_p1], writes=[R_xtn])
                                A("pool" if False else "dve", lambda e, hs=hs, p2=p2, xxn=xxn: e.tensor_copy(out=xxn[:, hs, :], in_=p2[:].rearrange("p (h t) -> p h t", t=128)), reads=[R_p2], writes=[R_xxn])
                        if l == 0:
                            A("pool", lambda e: e.tensor_copy(out=QQ[1][0][:], in_=QQ[0][0][:]), reads=[QQ[0][1]], writes=[QQ[1][1]])
                    Qf, R_Qf = QQ[0]
                    for hq in range(2):
                        p1, R_p1 = pm[2 + hq]
                        p2, R_p2 = pm[4 + hq]
                        for hh in range(4):
                            h = 4 * hq + hh
                            A("pe", lambda e, h=h, hh=hh, p1=p1: e.matmul(p1[:, 128 * hh:128 * hh + 128], lhsT=Qf[:, h, :], rhs=bvt[:, h, :], start=True, stop=True), reads=[R_Qf, R_bvt], writes=[R_p1])
                            A("pe", lambda e, h=h, hh=hh, p2=p2: e.matmul(p2[:, 128 * hh:128 * hh + 128], lhsT=kbe[:, h, :], rhs=Qf[:, h, :], start=True, stop=True), reads=[R_Qf, R_kbe], writes=[R_p2])
                        A("act", lambda e, hq=hq, p1=p1: e.activation(out=u0[:, 4 * hq:4 * hq + 4, :], in_=p1[:].rearrange("p (h t) -> p h t", t=128), func=AF.Copy), reads=[R_p1], writes=[R_u0])
                        A("act", lambda e, hq=hq, p2=p2: e.activation(out=nwT[:, 4 * hq:4 * hq + 4, :], in_=p2[:].rearrange("p (h t) -> p h t", t=128), func=AF.Copy, scale=-1.0), reads=[R_p2], writes=[R_nwT])
                    pob = [pm[6], pm[7]]
                    for c in range(2):
                        rows = slice(64 * c, 64 * c + 64)
                        ctok = slice(128 * ti + 64 * c, 128 * ti + 64 * c + 64)
                        for hq in range(2):
                            p1, R_p1 = pm[hq]
                            for hh in range(4):
                                h = 4 * hq + hh
                                A("pe", lambda e, h=h, hh=hh, p1=p1, rows=rows: e.matmul(p1[rows, 128 * hh:128 * hh + 128], lhsT=nwT[:, h, rows], rhs=SBb[:, h, :], start=True, stop=True), reads=[R_nwT, R_SBb], writes=[R_p1])
                            A("dve", lambda e, hq=hq, p1=p1, rows=rows: e.tensor_tensor(out=vnew[rows, 4 * hq:4 * hq + 4, :], in0=p1[rows, :].rearrange("p (h t) -> p h t", t=128), in1=u0[rows, 4 * hq:4 * hq + 4, :], op=ALU.add),
                              reads=[R_p1, R_u0], writes=[R_vnew])
                        for hq in range(2):
                            if readout:
                                po_, R_po = pob[hq]
                                for hh in range(4):
                                    h = 4 * hq + hh
                                    A("pe", lambda e, h=h, hh=hh, po_=po_, rows=rows, ctok=ctok: e.matmul(po_[rows, 128 * hh:128 * hh + 128], lhsT=qdB[:, h, ctok], rhs=SBb[:, h, :], start=(not pre), stop=False), reads=[R_qdB, R_SBb], writes=[R_po])
                                    A("pe", lambda e, h=h, hh=hh, po_=po_, rows=rows: e.matmul(po_[rows, 128 * hh:128 * hh + 128], lhsT=PT[rows, h, rows], rhs=vnew[rows, h, :], start=False, stop=True), reads=[R_PT, R_vnew], writes=[R_po])
                            p2, R_p2 = pm[2 + hq]
                            for hh in range(4):
                                h = 4 * hq + hh
                                A("pe", lambda e, h=h, hh=hh, p2=p2, rows=rows: e.matmul(p2[:, 128 * hh:128 * hh + 128], lhsT=kdc[rows, h, :], rhs=vnew[rows, h, :], start=True, stop=True), reads=[R_kdc, R_vnew], writes=[R_p2])
                            hs = slice(4 * hq, 4 * hq + 4)
                            A("dve", lambda e, hs=hs, c=c: e.tensor_tensor(out=SB[:, hs, :], in0=SB[:, hs, :], in1=dB[:, c, hs].unsqueeze(2).broadcast_to([128, 4, 128]), op=ALU.mult), reads=[R_SB, R_dB], writes=[R_SB])
                            A("dve", lambda e, hs=hs, p2=p2: e.tensor_tensor(out=SB[:, hs, :], in0=p2[:].rearrange("p (h t) -> p h t", t=128), in1=SB[:, hs, :], op=ALU.add), reads=[R_p2, R_SB], writes=[R_SB])
                        A("act", lambda e: e.activation(out=SBb[:], in_=SB[:], func=AF.Copy), reads=[R_SB], writes=[R_SBb])
                    if readout:
                        for hq in range(2):
                            po_, R_po = pob[hq]
                            A("act", lambda e, hq=hq, po_=po_: e.activation(out=osb[:, D + 512 * hq:D + 512 * hq + 512], in_=po_[:], func=AF.Copy), reads=[R_po], writes=[R_osb])
                if readout and s == 1:
                    for hd in range(16):
                        A("act", lambda e, hd=hd: e.activation(out=junk[:, 0:128], in_=osb[:, 128 * hd:128 * hd + 128], func=AF.Square, accum_out=hst[:, hd:hd + 1]), reads=[R_osb], writes=[R_junk, R_hst])
                    A("act", lambda e: e.activation(out=hst[:, 16:32], in_=hst[:, 0:16], func=AF.Ln, scale=1.0 / 128, bias=EPS), reads=[R_hst], writes=[R_hst])
                    A("act", lambda e: e.activation(out=hst[:, 16:32], in_=hst[:, 16:32], func=AF.Exp, scale=-0.5), reads=[R_hst], writes=[R_hst])
                    szt, R_szt = szs[ti]
                    for q_ in range(4):
                        yb_, R_yb = yb2[q_ // 2]
                        cb0 = 512 * (q_ % 2)
                        A("dve", lambda e, q_=q_, szt=szt: e.tensor_tensor(out=osb[:, 512 * q_:512 * q_ + 512], in0=osb[:, 512 * q_:512 * q_ + 512], in1=szt[:, 512 * q_:512 * q_ + 512], op=ALU.mult), reads=[R_osb, R_szt], writes=[R_osb])
                        A("dve", lambda e, q_=q_, yb_=yb_, cb0=cb0: e.tensor_tensor(out=yb_[:, cb0:cb0 + 512].rearrange("p (h c) -> p h c", c=128), in0=osb[:, 512 * q_:512 * q_ + 512].rearrange("p (h c) -> p h c", c=128),
                                                                                in1=hst[:, 16 + 4 * q_:20 + 4 * q_].unsqueeze(2).broadcast_to([128, 4, 128]), op=ALU.mult), reads=[R_osb, R_hst], writes=[R_yb])
                    for half in range(2):
                        yb_, R_yb = yb2[half]
                        yT_, R_yT = yT2[half]
                        pt, R_pt = pm[half]
                        ptb = pt[:].bitcast(BF16)
                        for kk in range(8):
                            A("pe", lambda e, kk=kk, yb_=yb_, ptb=ptb: e.transpose(out=ptb[:, 128 * kk:128 * kk + 128], in_=yb_[:, 128 * kk:128 * kk + 128], identity=idb[:]), reads=[R_yb, R_idb], writes=[R_pt])
                        A("act", lambda e, yT_=yT_, ptb=ptb: e.activation(out=yT_[:], in_=ptb.rearrange("p (k t) -> p k t", t=128), func=AF.Copy), reads=[R_pt], writes=[R_yT])
                    x_t, R_x = xt[ti % 2]
                    xrow = 64 + gidx * 256 + 128 * ti
                    A("sp", lambda e, xrow=xrow, x_t=x_t: e.dma_start(out=x_t[:], in_=src[xrow:xrow + 128, :]), writes=[R_x], chan="x%d" % (ti % 2))
                    for ch in range(2):
                        pt, R_pt = pm[2 + ch]
                        for kh in range(2):
                            i = wctr[0] % 2
                            wctr[0] += 1
                            wt_, R_wt = wt[i]
                            A("sp", lambda e, kh=kh, ch=ch, wt_=wt_: e.dma_start(out=wt_[:], in_=woutb[1024 * kh:1024 * kh + 1024, 512 * ch:512 * ch + 512].rearrange("(k p) n -> p k n", p=128)),
                              reads=[R_woutb], writes=[R_wt], chan="w%d" % i)
                            yT_, R_yT = yT2[kh]
                            for k in range(8):
                                A("pe", lambda e, k=k, kh=kh, pt=pt, yT_=yT_, wt_=wt_: e.matmul(pt[:], lhsT=yT_[:, k, :], rhs=wt_[:, k, :], start=(kh == 0 and k == 0), stop=(kh == 1 and k == 7)),
                                  reads=[R_yT, R_wt], writes=[R_pt])
                        A("dve", lambda e, ch=ch, pt=pt: e.tensor_tensor(out=hn[:, 512 * ch:512 * ch + 512], in0=pt[:], in1=gate[:, 512 * ch:512 * ch + 512], op=ALU.mult), reads=[R_pt, R_gate], writes=[R_hn])
                    A("dve", lambda e, x_t=x_t: e.tensor_tensor(out=hn[:], in0=hn[:], in1=x_t[:], op=ALU.add), reads=[R_hn, R_x], writes=[R_hn])
                    A("act", lambda e: e.activation(out=junk[:], in_=hn[:], func=AF.Square, accum_out=ssq[:, 0:1]), reads=[R_hn], writes=[R_junk, R_ssq])
                    A("act", lambda e: e.activation(out=ssq[:, 1:2], in_=ssq[:, 0:1], func=AF.Ln, scale=1.0 / D, bias=EPS), reads=[R_ssq], writes=[R_ssq])
                    A("act", lambda e: e.activation(out=ssq[:, 1:2], in_=ssq[:, 1:2], func=AF.Exp, scale=-0.5), reads=[R_ssq], writes=[R_ssq])
                    A("dve", lambda e: e.scalar_tensor_tensor(out=hn[:], in0=hn[:], scalar=ssq[:, 1:2], in1=fnw[:], op0=ALU.mult, op1=ALU.mult), reads=[R_hn, R_ssq, R_fnw], writes=[R_hn])
                    orow = (gidx - ng // 2) * 256 + 128 * ti
                    A("sp", lambda e, orow=orow: e.dma_start(out=out_d[orow:orow + 128, :], in_=hn[:]), reads=[R_hn], chan="out")
                if readout and s == 0:
                    srow = gidx * 256 + 128 * ti
                    A("sp", lambda e, srow=srow: e.dma_start(out=o1_d[srow:srow + 128, :], in_=osb[:]), reads=[R_osb], writes=[R_o1], chan="o1")

        for s in range(2 if do_s2 else 1):
            if s == 1:
                A("dve", lambda e: e.memset(SA[:], 0.0), writes=[R_SA])
                A("dve", lambda e: e.memset(SB[:], 0.0), writes=[R_SB])
                A("pool", lambda e: e.memset(SBb[:], 0.0), writes=[R_SBb])
            group(s, cx[s], 0, 2, 0, 1, False, -1)
            if s == 0:
                for g in range(ng // 2):
                    group(s, xs[s], 256 * g, 3, 64, 0, True, g)
            else:
                for g in range(ng):
                    group(s, xs[s], 256 * g, 3, 64, 0, g >= ng // 2, g)

        P.emit(st, final_waits=["o1"] + (["out"] if do_s2 else []))
    return nc


def make_inputs(x, c, ctx, c_ctx, norm_w, ada_w, ada_b, w_in, conv_w, hg_lb_logits, gdn_a_log,
                gdn_dt_bias, ha_norm_w, hb_norm_w, w_out, final_norm_w, ncores=8):
    f = lambda a: np.ascontiguousarray(np.asarray(a, dtype=np.float32))
    S = x.shape[1]
    Hh = S // 2
    cw = conv_w[0].reshape(9, 3072)
    cwn = cw.T.reshape(24, 128, 9).transpose(1, 0, 2)
    cwf = cw[::-1].T.reshape(24, 128, 9).transpose(1, 0, 2)
    lbl4 = hg_lb_logits.reshape(2, 2, 8, 128).transpose(3, 0, 1, 2)
    hnw = np.concatenate([ha_norm_w[0].reshape(-1), hb_norm_w[0].reshape(-1)])
    z = np.zeros((64, D), np.float32)
    in_maps = []
    for core in range(ncores):
        b, half = core // 2, core % 2
        d1, d2 = (0, 1) if half == 0 else (1, 0)
        xb = x[b]
        if half == 0:
            xs1 = np.concatenate([z, xb[:Hh], xb[Hh:Hh + 64]], axis=0)
            xs2 = np.concatenate([z, xb[::-1], z], axis=0)
            ctx1, ctx2 = ctx[b], ctx[b][::-1]
            cwp = np.stack([cwn, cwf], axis=1)
        else:
            xs1 = np.concatenate([z, xb[Hh:][::-1], xb[Hh - 64:Hh][::-1]], axis=0)
            xs2 = np.concatenate([z, xb, z], axis=0)
            ctx1, ctx2 = ctx[b][::-1], ctx[b]
            cwp = np.stack([cwf, cwn], axis=1)
        w = np.array(w_in[0], dtype=np.float32, copy=True)
        if half == 1:
            w[:, C_AF0:C_AF0 + 1024], w[:, C_AF1:C_AF1 + 1024] = w_in[0][:, C_AF1:C_AF1 + 1024], w_in[0][:, C_AF0:C_AF0 + 1024]
            w[:, C_BA:C_BA + 8], w[:, C_BA + 8:C_BA + 16] = w_in[0][:, C_BA + 8:C_BA + 16], w_in[0][:, C_BA:C_BA + 8]
            w[:, C_BB:C_BB + 8], w[:, C_BB + 8:C_BB + 16] = w_in[0][:, C_BB + 8:C_BB + 16], w_in[0][:, C_BB:C_BB + 8]
        lbl = lbl4[:, :, [d1, d2], :].reshape(128, 32)
        gpar = np.zeros((16, 16), np.float32)
        for si, d in enumerate((d1, d2)):
            gpar[:, 2 * si] = np.tile(gdn_a_log[0, d], 2)
            gpar[:, 2 * si + 1] = np.tile(gdn_dt_bias[0, d], 2)
        gpar[0:8, 4] = -1.0
        gpar[8:16, 5] = 1.0
        gpar[8:16, 6] = 1.0
        gpar[0:8, 7] = 1.0
        for k in range(16):
            gpar[k, 8 + k % 8] = 1.0
        gdr = np.concatenate([gdn_a_log[0, d1], gdn_a_log[0, d2], gdn_dt_bias[0, d1], gdn_dt_bias[0, d2]])
        ccol = np.concatenate([c[b].reshape(8, 128).T, c_ctx.reshape(8, 128).T], axis=1)
        in_maps.append({
            "xs1": f(xs1), "xs2": f(xs2), "ctx1": f(ctx1), "ctx2": f(ctx2),
            "ccol": f(ccol), "norm_w": f(norm_w[0]), "final_norm_w": f(final_norm_w),
            "ada_w": f(ada_w[0]), "ada_b": f(ada_b[0]), "w_in": f(w), "w_out": f(w_out[0]),
            "lbl": f(lbl), "cw": f(cwp), "hnwc": f(hnw.reshape(16, 128).T), "gpar": f(gpar), "gdr": f(gdr),
        })
    return in_maps


def assemble(results, nb):
    outs = []
    for b in range(nb):
        outs.append(np.concatenate([results[2 * b]["out"][::-1], results[2 * b + 1]["out"]], axis=0))
    return np.ascontiguousarray(np.stack(outs, axis=0).astype(np.float32))


_NC_CACHE = {}


def kernel(**inputs):
    inputs = {k: np.asarray(v) for k, v in inputs.items()}
    in_maps = make_inputs(**inputs)
    if "nc" not in _NC_CACHE:
        _NC_CACHE["nc"] = build()
    res = run_bass_kernel_spmd(_NC_CACHE["nc"], in_maps, core_ids=list(range(8)))
    return assemble(res.results, 4)
```
